# Optimizing a Trainium2 kernel written in Bass

```python
import jax
import jax.numpy as jnp
from jax import lax
import numpy as np

D_MODEL = 1024
BATCH = 4
SEQ = 4096
DEPTH = 2
DEC_BATCH = 128
DEC_SEQ = 1
PAST_LEN = 2048
PAGE_SIZE = 128

N_A_LAYERS = DEPTH // 2
N_B_LAYERS = DEPTH - N_A_LAYERS
POOL_WINDOWS = (2, 4, 8, 16)
N_POOL_GROUPS = len(POOL_WINDOWS)
POOL_GROUP_DIM = D_MODEL // N_POOL_GROUPS
POOL_BUF = max(POOL_WINDOWS) - 1
N_HEADS = 16
HEAD_DIM = D_MODEL // N_HEADS
N_KV_HEADS = 4
Q_PER_KV = N_HEADS // N_KV_HEADS
N_BRANCH = 3
CMP_BLOCK = 32
CMP_STRIDE = 16
CMP_HIDDEN = 2 * HEAD_DIM
SEL_BLOCK = 64
N_SELECT = 16
WINDOW = 512
D_FF = 4 * D_MODEL
Q_BLOCK = 64
RMS_EPS = 1e-6
FORCED_PRIORITY = 1e6

kernel_name = 'yoco_pool_nsa_decoder_step'


def _rmsnorm(x, g):
    xf = x.astype(jnp.float32)
    y = xf * lax.rsqrt(jnp.mean(xf * xf, axis=-1, keepdims=True) + RMS_EPS)
    return (y * g.astype(jnp.float32)).astype(x.dtype)


def _mlp_block(x, g, w_up, w_down):
    h = jnp.square(jax.nn.relu(_rmsnorm(x, g) @ w_up))
    return x + h @ w_down


def _alibi_slopes():
    return jnp.exp2(-8.0 * (jnp.arange(N_HEADS, dtype=jnp.float32) + 1.0) / N_HEADS)


def _pool_mix(u, w, scale):
    b, l, _ = u.shape
    uf = u.astype(jnp.float32).reshape(b, l, N_POOL_GROUPS, POOL_GROUP_DIM)
    cs = jnp.concatenate([jnp.zeros_like(uf[:, :1]), jnp.cumsum(uf, axis=1)], axis=1)
    t = jnp.arange(l)
    means = []
    for gi, win in enumerate(POOL_WINDOWS):
        lo = jnp.maximum(t + 1 - win, 0)
        cnt = jnp.minimum(t + 1, win).astype(jnp.float32)
        means.append((cs[:, 1:, gi] - cs[:, lo, gi]) / cnt[None, :, None])
    diff = (jnp.stack(means, axis=2) - uf).astype(u.dtype)
    y = jnp.einsum('blgc,gcd->blgd', diff, w).reshape(b, l, D_MODEL)
    return y * scale


def _pool_layer(x, past_rows, g, w, scale):
    u = _rmsnorm(x, g)
    t = x.shape[1]
    ucat = u if past_rows is None else jnp.concatenate([past_rows.astype(u.dtype), u], axis=1)
    y = _pool_mix(ucat, w, scale)[:, -t:]
    return x + y, ucat[:, -POOL_BUF:]


def _shared_kv(x, g, w_kv):
    b, t, _ = x.shape
    return (_rmsnorm(x, g) @ w_kv).reshape(b, t, N_BRANCH, 2, N_KV_HEADS, HEAD_DIM)


def _compress(rows, pe, w1, w2):
    b, l, g, d = rows.shape
    r = CMP_BLOCK // CMP_STRIDE
    n_sub = l // CMP_STRIDE
    n_cmp = n_sub - r + 1
    sub = rows[:, : n_sub * CMP_STRIDE].reshape(b, n_sub, CMP_STRIDE, g, d)
    blocks = jnp.concatenate([sub[:, k:k + n_cmp] for k in range(r)], axis=2)
    blocks = blocks + pe[None, None, :, None, :].astype(rows.dtype)
    flat = blocks.transpose(0, 1, 3, 2, 4).reshape(b, n_cmp, g, CMP_BLOCK * d)
    return jax.nn.silu(flat @ w1) @ w2


def _sel_blocks(rows):
    b, l, g, d = rows.shape
    n_sel = -(-l // SEL_BLOCK)
    rows = jnp.pad(rows, ((0, 0), (0, n_sel * SEL_BLOCK - l), (0, 0), (0, 0)))
    return rows.reshape(b, n_sel, SEL_BLOCK, g, d).transpose(0, 3, 1, 2, 4)


def _kv_summaries(rows, cmp_pe, cmp_w1, cmp_w2):
    kc = _compress(rows[:, :, 0, 0], cmp_pe[0], cmp_w1[0], cmp_w2[0])
    vc = _compress(rows[:, :, 0, 1], cmp_pe[1], cmp_w1[1], cmp_w2[1])
    ks = _sel_blocks(rows[:, :, 1, 0])
    vs = _sel_blocks(rows[:, :, 1, 1])
    return kc, vc, ks, vs


def _masked_softmax(s, mask):
    s = jnp.where(mask, s, -jnp.inf)
    m = jnp.max(s, axis=-1, keepdims=True)
    m = jnp.where(jnp.isfinite(m), m, 0.0)
    e = jnp.where(mask, jnp.exp(s - m), 0.0)
    return e / jnp.maximum(jnp.sum(e, axis=-1, keepdims=True), 1e-30)


def _query_side(x, g, w_qg, b_gate):
    b, t, _ = x.shape
    h = _rmsnorm(x, g) @ w_qg
    q = h[..., : N_HEADS * HEAD_DIM].reshape(b, t, N_HEADS, HEAD_DIM)
    gate = jax.nn.sigmoid((h[..., N_HEADS * HEAD_DIM:] + b_gate).astype(jnp.float32))
    return q, gate.reshape(b, t, N_HEADS, N_BRANCH)


def _nsa_attend(q, gate, pos_q, kc, vc, ks_blk, vs_blk, kw, vw, pos_w, slopes):
    b, c = q.shape[:2]
    n_cmp = kc.shape[1]
    n_sel = ks_blk.shape[2]
    dt = q.dtype
    f32 = jnp.float32
    qg = (q * HEAD_DIM ** -0.5).reshape(b, c, N_KV_HEADS, Q_PER_KV, HEAD_DIM)
    sl = slopes.reshape(N_KV_HEADS, Q_PER_KV)[None, None, :, :, None]
    tq = pos_q[:, None]

    cmp_end = jnp.arange(n_cmp) * CMP_STRIDE + (CMP_BLOCK - 1)
    dist_c = (tq - cmp_end[None, :]).astype(f32)
    s_c = jnp.einsum('bcgrd,bngd->bcgrn', qg, kc).astype(f32) - sl * dist_c[None, :, None, None, :]
    p_c = _masked_softmax(s_c, (dist_c >= 0)[None, :, None, None, :])
    o_c = jnp.einsum('bcgrn,bngd->bcgrd', p_c.astype(dt), vc)

    sub = jnp.arange(n_cmp)[:, None] + jnp.arange(CMP_BLOCK // CMP_STRIDE)[None, :]
    cmp_to_sel = jax.nn.one_hot(sub * CMP_STRIDE // SEL_BLOCK, n_sel, dtype=f32).sum(axis=1)
    imp = jnp.einsum('bcgrn,ns->bcgs', p_c, cmp_to_sel)
    blk = jnp.arange(n_sel)[None, :]
    cur = (pos_q // SEL_BLOCK)[:, None]
    forced = (blk == 0) | (blk == cur) | (blk == cur - 1)
    valid = blk * SEL_BLOCK <= tq
    pri = jnp.where(valid[None, :, None, :], jnp.where(forced[None, :, None, :], FORCED_PRIORITY, imp), -1.0)
    _, idx = lax.top_k(pri, min(N_SELECT, n_sel))
    n_k = idx.shape[-1]
    bi = jnp.arange(b)[:, None, None, None]
    gi = jnp.arange(N_KV_HEADS)[None, None, :, None]
    kb = ks_blk[bi, gi, idx]
    vb = vs_blk[bi, gi, idx]
    kpos = idx[..., None] * SEL_BLOCK + jnp.arange(SEL_BLOCK)
    dist_s = (pos_q[None, :, None, None, None] - kpos).astype(f32)[:, :, :, None]
    s_s = jnp.einsum('bcgrd,bcgkjd->bcgrkj', qg, kb).astype(f32) - sl[..., None] * dist_s
    p_s = _masked_softmax(s_s.reshape(b, c, N_KV_HEADS, Q_PER_KV, n_k * SEL_BLOCK),
                          (dist_s >= 0).reshape(b, c, N_KV_HEADS, 1, n_k * SEL_BLOCK))
    o_s = jnp.einsum('bcgrm,bcgmd->bcgrd', p_s.astype(dt),
                     vb.reshape(b, c, N_KV_HEADS, n_k * SEL_BLOCK, HEAD_DIM))

    dist_w = tq - pos_w[None, :]
    mask_w = (dist_w >= 0) & (dist_w < WINDOW) & (pos_w[None, :] >= 0)
    s_w = jnp.einsum('bcgrd,blgd->bcgrl', qg, kw).astype(f32) - sl * dist_w.astype(f32)[None, :, None, None, :]
    p_w = _masked_softmax(s_w, mask_w[None, :, None, None, :])
    o_w = jnp.einsum('bcgrl,blgd->bcgrd', p_w.astype(dt), vw)

    g = gate.reshape(b, c, N_KV_HEADS, Q_PER_KV, N_BRANCH)
    o = (g[..., 0:1] * o_c.astype(f32) + g[..., 1:2] * o_s.astype(f32) + g[..., 2:3] * o_w.astype(f32))
    return o.astype(dt).reshape(b, c, N_HEADS * HEAD_DIM)


def _nsa_prompt(q, gate, kc, vc, ks_blk, vs_blk, win_rows, slopes):
    b, t = q.shape[:2]
    n_chunks = t // Q_BLOCK
    win_pad = jnp.pad(win_rows, ((0, 0), (WINDOW, 0), (0, 0), (0, 0), (0, 0)))

    def chunk(args):
        ci, qc, gc = args
        c0 = ci * Q_BLOCK
        w = lax.dynamic_slice_in_dim(win_pad, c0, WINDOW + Q_BLOCK, axis=1)
        pos_q = c0 + jnp.arange(Q_BLOCK)
        pos_w = c0 - WINDOW + jnp.arange(WINDOW + Q_BLOCK)
        return _nsa_attend(qc, gc, pos_q, kc, vc, ks_blk, vs_blk, w[:, :, 0], w[:, :, 1], pos_w, slopes)

    qc = q.reshape(b, n_chunks, Q_BLOCK, N_HEADS, HEAD_DIM).swapaxes(0, 1)
    gc = gate.reshape(b, n_chunks, Q_BLOCK, N_HEADS, N_BRANCH).swapaxes(0, 1)
    out = lax.map(chunk, (jnp.arange(n_chunks, dtype=jnp.int32), qc, gc))
    return out.swapaxes(0, 1).reshape(b, t, N_HEADS * HEAD_DIM)


def setup_inputs(seed: int = 0) -> dict:
    key = jax.random.key(seed)
    ks = jax.random.split(key, 24)
    nrm = jax.random.normal
    f32 = jnp.float32
    n_pages = PAST_LEN // PAGE_SIZE
    n_used = DEC_BATCH * n_pages
    n_phys = n_used + n_used // 4
    win_buf = min(WINDOW, PAST_LEN)
    qg_out = N_HEADS * HEAD_DIM + N_BRANCH * N_HEADS
    kv_out = N_BRANCH * 2 * N_KV_HEADS * HEAD_DIM
    page_table = jax.random.permutation(ks[5], n_phys)[:n_used].astype(jnp.int32).reshape(DEC_BATCH, n_pages)
    return {
        'x_prompt': nrm(ks[0], (BATCH, SEQ, D_MODEL), f32),
        'x_sample': nrm(ks[1], (DEC_BATCH, DEC_SEQ, D_MODEL), f32),
        'state_pool': nrm(ks[2], (DEC_BATCH, N_A_LAYERS, POOL_BUF, D_MODEL), f32),
        'cache_kv_pages': nrm(ks[3], (n_phys, PAGE_SIZE, 2, 2, N_KV_HEADS, HEAD_DIM), f32),
        'state_win': nrm(ks[4], (DEC_BATCH, win_buf, 2, N_KV_HEADS, HEAD_DIM), f32),
        'page_table': page_table,
        'norm_mix': 1.0 + 0.02 * nrm(ks[6], (DEPTH, D_MODEL), f32),
        'norm_mlp': 1.0 + 0.02 * nrm(ks[7], (DEPTH, D_MODEL), f32),
        'w_up': nrm(ks[8], (DEPTH, D_MODEL, D_FF), f32) * D_MODEL ** -0.5,
        'w_down': nrm(ks[9], (DEPTH, D_FF, D_MODEL), f32) * D_FF ** -0.5,
        'pool_w': nrm(ks[10], (N_A_LAYERS, N_POOL_GROUPS, POOL_GROUP_DIM, POOL_GROUP_DIM), f32) * POOL_GROUP_DIM ** -0.5,
        'pool_scale': 1.0 + 0.02 * nrm(ks[11], (N_A_LAYERS, D_MODEL), f32),
        'norm_kv': 1.0 + 0.02 * nrm(ks[12], (D_MODEL,), f32),
        'w_kv': nrm(ks[13], (D_MODEL, kv_out), f32) * D_MODEL ** -0.5,
        'cmp_pe': 0.1 * nrm(ks[14], (2, CMP_BLOCK, HEAD_DIM), f32),
        'cmp_w1': nrm(ks[15], (2, CMP_BLOCK * HEAD_DIM, CMP_HIDDEN), f32) * (CMP_BLOCK * HEAD_DIM) ** -0.5,
        'cmp_w2': nrm(ks[16], (2, CMP_HIDDEN, HEAD_DIM), f32) * CMP_HIDDEN ** -0.5,
        'w_qg': nrm(ks[17], (N_B_LAYERS, D_MODEL, qg_out), f32) * D_MODEL ** -0.5,
        'b_gate': 0.01 * nrm(ks[18], (N_B_LAYERS, N_BRANCH * N_HEADS), f32),
        'w_o': nrm(ks[19], (N_B_LAYERS, N_HEADS * HEAD_DIM, D_MODEL), f32) * (N_HEADS * HEAD_DIM) ** -0.5,
        'norm_final': 1.0 + 0.02 * nrm(ks[20], (D_MODEL,), f32),
    }


def reference(x_prompt, x_sample, state_pool, cache_kv_pages, state_win, page_table,
              norm_mix, norm_mlp, w_up, w_down, pool_w, pool_scale, norm_kv, w_kv,
              cmp_pe, cmp_w1, cmp_w2, w_qg, b_gate, w_o, norm_final):
    slopes = _alibi_slopes()

    xp = x_prompt
    pool_p = []
    for layer in range(DEPTH):
        if layer < N_A_LAYERS:
            xp, rows = _pool_layer(xp, None, norm_mix[layer], pool_w[layer], pool_scale[layer])
            pool_p.append(rows)
        else:
            if layer == N_A_LAYERS:
                kv_p = _shared_kv(xp, norm_kv, w_kv)
                kv_rows_p = kv_p[:, :, :2]
                kc_p, vc_p, ks_p, vs_p = _kv_summaries(kv_rows_p, cmp_pe, cmp_w1, cmp_w2)
                win_rows_p = kv_p[:, :, 2]
                win_new_p = win_rows_p[:, -min(WINDOW, xp.shape[1]):]
            j = layer - N_A_LAYERS
            q, gate = _query_side(xp, norm_mix[layer], w_qg[j], b_gate[j])
            o = _nsa_prompt(q, gate, kc_p, vc_p, ks_p, vs_p, win_rows_p, slopes)
            xp = xp + o @ w_o[j]
        xp = _mlp_block(xp, norm_mlp[layer], w_up[layer], w_down[layer])
    y_prompt = _rmsnorm(xp, norm_final)

    xs = x_sample
    dec_b, dec_t, _ = xs.shape
    past_len = page_table.shape[1] * cache_kv_pages.shape[1]
    pos_s = past_len + jnp.arange(dec_t)
    wb = state_win.shape[1]
    pool_s = []
    for layer in range(DEPTH):
        if layer < N_A_LAYERS:
            xs, rows = _pool_layer(xs, state_pool[:, layer], norm_mix[layer], pool_w[layer], pool_scale[layer])
            pool_s.append(rows)
        else:
            if layer == N_A_LAYERS:
                kv_s = _shared_kv(xs, norm_kv, w_kv)
                kv_rows_s = kv_s[:, :, :2]
                past = cache_kv_pages[page_table].reshape((dec_b, past_len) + cache_kv_pages.shape[2:]).astype(xs.dtype)
                kc_s, vc_s, ks_s, vs_s = _kv_summaries(jnp.concatenate([past, kv_rows_s], axis=1), cmp_pe, cmp_w1, cmp_w2)
                win_all = jnp.concatenate([state_win.astype(xs.dtype), kv_s[:, :, 2]], axis=1)
                win_new_s = win_all[:, -wb:]
                pos_w = past_len - wb + jnp.arange(wb + dec_t)
            j = layer - N_A_LAYERS
            q, gate = _query_side(xs, norm_mix[layer], w_qg[j], b_gate[j])
            o = _nsa_attend(q, gate, pos_s, kc_s, vc_s, ks_s, vs_s, win_all[:, :, 0], win_all[:, :, 1], pos_w, slopes)
            xs = xs + o @ w_o[j]
        xs = _mlp_block(xs, norm_mlp[layer], w_up[layer], w_down[layer])
    y_sample = _rmsnorm(xs, norm_final)

    return (y_prompt, y_sample, jnp.stack(pool_p, axis=1), jnp.stack(pool_s, axis=1),
            kv_rows_p, kv_rows_s, win_new_p, win_new_s)
```

```python
import contextlib
import numpy as np
import ml_dtypes
import concourse.bass as bass
import concourse.mybir as mybir
from concourse.bass_utils import run_bass_kernel_spmd

F32 = mybir.dt.float32
BF16 = mybir.dt.bfloat16
AF = mybir.ActivationFunctionType
ALU = mybir.AluOpType
AX = mybir.AxisListType
NPBF = ml_dtypes.bfloat16

NCORES = 8
D = 1024
DFF = 4096
SEQ = 4096
NB = 4
GT = 1024
HALO = 16
NCOL = HALO + GT
NS = 16
NCX = NCOL + NS
EPS = 1e-6
KVW = 1536
NWB = 3
SIG_EPOCH = 30000
NEGPOS = -8192.0
NPAGES = 2560
_DEV = {}
STAGE = 4


class Tk:
    __slots__ = ("w", "r", "name")

    def __init__(self, name=""):
        self.w = None
        self.r = []
        self.name = name


class Op:
    __slots__ = ("eng", "fn", "deps", "dma", "sig", "signum", "slot", "slotval", "idx")

    def __init__(self, eng, fn, dma):
        self.eng = eng
        self.fn = fn
        self.deps = []
        self.dma = dma
        self.sig = False
        self.signum = None
        self.slot = None
        self.slotval = None


class Sched:
    ENGS = ("pe", "act", "dve", "pool", "sp")

    def __init__(self, nslot=8):
        self.q = {e: [] for e in self.ENGS}
        self.nslot = nslot
        self.pending = {}

    def barrier(self):
        fr = []
        for e in self.ENGS:
            comp = [o for o in self.q[e] if not o.dma]
            if comp:
                fr.append(comp[-1])
            dm = [o for o in self.q[e] if o.dma]
            fr.extend(dm[-self.nslot:])
        self.pending = {e: list(fr) for e in self.ENGS}

    def add(self, eng, fn, reads=(), writes=(), dma=False, extra=()):
        op = Op(eng, fn, dma)
        deps = []
        seen = set()
        if self.pending.get(eng):
            extra = list(extra) + self.pending.pop(eng)
        for d in extra:
            if id(d) not in seen:
                seen.add(id(d)); deps.append(d)
        for t in reads:
            if t.w is not None and id(t.w) not in seen:
                seen.add(id(t.w)); deps.append(t.w)
        for t in writes:
            for r in t.r:
                if id(r) not in seen:
                    seen.add(id(r)); deps.append(r)
            if t.w is not None and id(t.w) not in seen:
                seen.add(id(t.w)); deps.append(t.w)
        op.deps = [d for d in deps if d is not op]
        for t in reads:
            t.r.append(op)
        for t in writes:
            t.w = op
            t.r = []
        op.idx = len(self.q[eng])
        self.q[eng].append(op)
        return op

    def prepare(self, nc):
        nslot = self.nslot
        for e in self.ENGS:
            for op in self.q[e]:
                for d in op.deps:
                    if d.dma:
                        continue
                    if d.eng == "pe" and op.eng == "pe" and not op.dma:
                        continue
                    d.sig = True
        nsig = {}
        for e in self.ENGS:
            k = 0
            i = 0
            for op in self.q[e]:
                if op.dma:
                    op.slot = i % nslot
                    op.slotval = 16 * (i // nslot + 1)
                    i += 1
                elif op.sig:
                    k += 1
                    op.signum = k
            nsig[e] = k
        st = contextlib.ExitStack()
        csem = {}
        for e in self.ENGS:
            nep = nsig[e] // SIG_EPOCH + 1
            csem[e] = [st.enter_context(nc.semaphore("c_%s_%d" % (e, j))) for j in range(nep)]
        dsem = {}
        for e in ("sp", "pool"):
            dsem[e] = [st.enter_context(nc.semaphore("d_%s_%d" % (e, j))) for j in range(nslot)]
        self._st = st
        self.csem = csem
        self.dsem = dsem

    def emit(self, block):
        csem, dsem = self.csem, self.dsem

        def run(e, eng):
            waited = {}

            def wait(sem, key, val):
                if waited.get(key, -1) >= val:
                    return
                waited[key] = val
                eng.wait_ge(sem, val)

            for op in self.q[e]:
                need = {}
                for d in op.deps:
                    if d.dma:
                        key, sem, val = ("d", d.eng, d.slot), dsem[d.eng][d.slot], d.slotval
                    else:
                        if d.eng == "pe" and e == "pe" and not op.dma:
                            continue
                        ep = (d.signum - 1) // SIG_EPOCH
                        key, sem, val = ("c", d.eng, ep), csem[d.eng][ep], d.signum - ep * SIG_EPOCH
                    if key not in need or need[key][1] < val:
                        need[key] = (sem, val)
                for key, (sem, val) in need.items():
                    wait(sem, key, val)
                if op.dma:
                    if op.slotval > 16:
                        wait(dsem[e][op.slot], ("d", e, op.slot), op.slotval - 16)
                    ins = op.fn(eng)
                    ins.then_inc(dsem[e][op.slot], 16)
                else:
                    ins = op.fn(eng)
                    if op.sig:
                        ep = (op.signum - 1) // SIG_EPOCH
                        ins.then_inc(csem[e][ep], 1)
            if e in dsem:
                last = {}
                for op in self.q[e]:
                    if op.dma:
                        last[op.slot] = op.slotval
                for s, v in last.items():
                    wait(dsem[e][s], ("d", e, s), v)

        @block.tensor
        def _(eng):
            run("pe", eng)

        @block.scalar
        def _(eng):
            run("act", eng)

        @block.vector
        def _(eng):
            run("dve", eng)

        @block.gpsimd
        def _(eng):
            run("pool", eng)

        @block.sync
        def _(eng):
            run("sp", eng)


L0_GROUPS = [(16, False, None), (24, False, None), (0, True, 0), (8, True, 1)]


def build_nc():
    nc = bass.Bass("TRN2", target_bir_lowering=False)

    def din(name, shape, dt=F32):
        return nc.dram_tensor(name, list(shape), dt, kind="ExternalInput").ap()

    def dout(name, shape, dt=F32):
        return nc.dram_tensor(name, list(shape), dt, kind="ExternalOutput").ap()

    def dscr(name, shape, dt):
        return nc.dram_tensor(name, list(shape), dt).ap()

    xg = din("xg", [4, NCOL, D])
    corr = din("corr", [4, 64])
    vecs = din("vecs", [128, 56])
    ident_d = din("ident", [128, 128])
    w_up = din("w_up", [2, D, DFF])
    w_down = din("w_down", [2, DFF, D])
    pool_w = din("pool_w", [4, 256, 256])
    w_kv = din("w_kv", [D, KVW])
    w_qg = din("w_qg", [D, 1072])
    w_o = din("w_o", [D, D])
    b_gate = din("b_gate", [1, 48])
    cmp_w1 = din("cmp_w1", [2, 2048, 128])
    cmp_w2 = din("cmp_w2", [2, 128, 64])
    cmp_pe = din("cmp_pe", [2, 32, 64])
    kaug_d = din("kaug", [32, 6, 128], BF16)
    qaug_d = din("qaug", [6, 4, 4, 2048], BF16)
    caug_d = din("caug", [16, 6, 256], BF16)
    posq_d = din("posq", [2048])
    posk_d = din("posk", [128, 32])
    cend_d = din("cend", [128, 2])
    privn_d = din("privn", [2048, 64])
    pribias_d = din("pribias", [2048, 64])
    mc2s_d = din("mc2s", [128, 2, 64], BF16)
    esel_d = din("esel", [64, 4096], BF16)
    tri_d = din("tri", [128, 128], BF16)
    identb_d = din("identb", [128, 128], BF16)

    cache_d = din("cache", [NPAGES * 128, 1024])
    pt_d = din("pt", [NS, 16], mybir.dt.int32)
    iotap_d = din("iotap", [128, 1])
    kaugS_d = din("kaugS", [21, 6, 128], BF16)
    qaugS_d = din("qaugS", [6, 4, 4, NS], BF16)
    caugS_d = din("caugS", [6, 128], BF16)
    maskS_d = din("maskS", [128, 8])
    privnS_d = din("privnS", [NS, 64])
    pribiasS_d = din("pribiasS", [NS, 64])
    mc2sS_d = din("mc2sS", [128, 64], BF16)
    xs_d = din("xs", [NS, D])
    spool_d = din("spool", [NS * 15, D])
    swin_d = din("swin", [NS, 512, 512])
    selw_d = din("selw", [128, 2, 64])
    pool_s_out = dout("pool_s_out", [NS, 15, D])
    kv_s_out = dout("kv_s_out", [NS, 1024])
    win_s_out = dout("win_s_out", [NS, 512, 512])
    y_s_out = dout("y_s_out", [NS, D])
    kv_out = dout("kv_out", [2048, 1024])
    win_out = dout("win_out", [512, 512])
    pool_out = dout("pool_out", [16, D])
    y_out = dout("y_out", [2048, D])

    x1_d = dscr("x1_d", [2, 128, 8 * (GT + NS)], F32)
    KsT_d = dscr("KsT_d", [32, 70, 4, 128], BF16)
    KwT_d = dscr("KwT_d", [32, 70, 4, 128], BF16)
    Vs_d = dscr("Vs_d", [32, 128, 4, 65], BF16)
    Vw_d = dscr("Vw_d", [32, 128, 4, 65], BF16)
    KsTS_d = dscr("KsTS_d", [NS, 16, 70, 4, 128], BF16)
    VsS_d = dscr("VsS_d", [NS, 16, 128, 4, 65], BF16)
    KwTS_d = dscr("KwTS_d", [NS, 4, 70, 4, 128], BF16)
    VwS_d = dscr("VwS_d", [NS, 4, 128, 4, 65], BF16)
    kcTS_d = dscr("kcTS_d", [NS, 64, 4, 128], BF16)
    vcS_d = dscr("vcS_d", [NS, 128, 4, 64], BF16)

    es = contextlib.ExitStack()

    def sb(name, shape, dt):
        return es.enter_context(nc.sbuf_tensor(name, list(shape), dt))

    xT = sb("xT", [128, 8, NCX], F32)
    uT = sb("uT", [128, 8, NCX], BF16)
    hT = sb("hT", [128, 16, GT], BF16)
    wb = [sb("wb%d" % i, [128, 8 * 512], BF16) for i in range(NWB)]
    xin = [sb("xin%d" % i, [128, D], F32) for i in range(2)]
    tA = sb("tA", [128, NCX], F32)
    rstd = sb("rstd", [128, NCX], F32)
    kvst = [sb("kvst%d" % i, [128, KVW], F32) for i in range(2)]
    vec_sb = sb("vec_sb", [128, 56], F32)
    corr_sb = sb("corr_sb", [128, 4 * 64], F32)
    pw_sb = sb("pw_sb", [128, 4 * 2 * 256], BF16)
    ones_bf = sb("ones_bf", [128, 128], BF16)
    utail = sb("utail", [128, 8, 16], F32)
    selw = sb("selw_sb", [128, 2, 64], F32)
    diffS = sb("diffS", [128, 8, NS], BF16)
    hTs = sb("hTs", [128, 16, NS], BF16)
    rs_t = sb("rs_t", [128, NS], F32)
    QTs = sb("QTs", [70, 4, 4, NS], BF16)
    gate_s = sb("gate_s", [NS, 48], F32)
    knT = sb("knT", [70, 2, 4, NS], BF16)
    vnew = sb("vnew", [NS, 2, 4, 65], BF16)
    knew_bf = sb("knew_bf", [NS, 512], BF16)
    kcTs = [sb("kcTs%d" % i, [70, 4, 128], BF16) for i in range(2)]
    vcMs = [sb("vcMs%d" % i, [128, 4, 128], BF16) for i in range(2)]
    Pz = [sb("Pz%d" % i, [128, 4, NS], BF16) for i in range(4)]
    maskS = sb("maskS_sb", [128, 8], F32)
    privnS = sb("privnS_sb", [NS, 64], F32)
    pribiasS = sb("pribiasS_sb", [NS, 64], F32)
    selTs = sb("selTs", [64, 4, NS], BF16)
    epsb = sb("epsb", [128, 1], F32)
    ident = sb("ident_sb", [128, 128], F32)
    identb = sb("identb_sb", [128, 128], BF16)
    vst = [sb("vst%d" % i, [128, 2, 4, 65], BF16) for i in range(2)]
    ktst = [sb("ktst%d" % i, [64, GT], BF16) for i in range(2)]
    w1_sb = sb("w1_sb", [64, 2, 32, 128], BF16)
    w2_sb = sb("w2_sb", [128, 2, 64], BF16)
    peT = sb("peT", [64, 2, 32], BF16)
    pe_nat = sb("pe_nat", [64, 64], F32)
    ABs = sb("ABs", [128, 2, 4, 2, 4], F32)
    hs_all = sb("hs_all", [128, 2, 4, 256], BF16)
    hpreb = sb("hpreb", [128, 4], F32)
    cvec = sb("cvec", [128, 2], F32)
    kcT = sb("kcT", [70, 4, 256], BF16)
    vcM = sb("vcM", [128, 2, 4, 128], BF16)
    kbuf = [sb("kbuf%d" % i, [70, 4, 128], BF16) for i in range(3)]
    vbuf = [sb("vbuf%d" % i, [128, 4, 65], BF16) for i in range(3)]
    pT = [sb("pT%d" % i, [128, 4, 128], BF16) for i in range(4)]
    posq_bc = [sb("posq_bc%d" % i, [128, 128], F32) for i in range(2)]
    posk_sb = sb("posk_sb", [128, 32], F32)
    cend_sb = sb("cend_sb", [128, 2], F32)
    mtmp = [sb("mtmp%d" % i, [128, 128], F32) for i in range(2)]
    mask_sb = [sb("mask%d" % i, [128, 128], BF16) for i in range(3)]
    tri_sb = sb("tri_sb", [128, 128], BF16)
    esel_sb = sb("esel_sb", [64, 4096], BF16)
    privn_sb = [sb("privn%d" % i, [128, 64], F32) for i in range(2)]
    pribias_sb = [sb("pribias%d" % i, [128, 64], F32) for i in range(2)]
    pri = sb("pri", [128, 4, 64], F32)
    pri2 = sb("pri2", [128, 64], F32)
    m8a = sb("m8a", [128, 8], F32)
    m8b = sb("m8b", [128, 8], F32)
    sel_sb = sb("sel_sb", [128, 4, 64], BF16)
    selT = sb("selT", [64, 4, 128], BF16)
    gate_sb = sb("gate_sb", [128, 8, 48], F32)
    bg_sb = sb("bg_sb", [128, 48], F32)
    rsum = sb("rsum", [128, 4], F32)
    fsc = sb("fsc", [128, 4], F32)
    oacc = pw_sb[:, :].bitcast(F32).rearrange("p (h d) -> p h d", h=16)
    hs_flat = hs_all[:, :, :, :].rearrange("p a b c -> p (a b c)")
    obf = hs_flat[:, 0:1024]
    otmp = hs_flat[:, 1024:1536].bitcast(F32).rearrange("p (r d) -> p r d", r=4)
    yst = [kvst[i][:, 0:D] for i in range(2)]
    ptail = xin[0][0:16, :]
    tB = rstd
    xct = ktst
    psum = [es.enter_context(nc.psum_tensor("ps%d" % i, [128, 512], F32)) for i in range(8)]

    S = Sched()
    k_xT = [Tk("xT%d" % i) for i in range(8)]
    k_uT = Tk("uT")
    k_h = [Tk("h%d" % i) for i in range(16)]
    k_wb = [Tk() for _ in range(NWB)]
    k_xin = [Tk(), Tk()]
    k_tA, k_rstd = Tk(), Tk()
    k_tA1 = Tk()
    k_tB = k_rstd
    k_kvst = [Tk(), Tk()]
    k_ps = [Tk() for _ in range(8)]
    k_const = Tk("const")
    k_utail = Tk()
    k_diffS, k_hTs, k_rs = Tk(), Tk(), Tk()
    k_QTs, k_gates, k_knT, k_vnew, k_knew = Tk(), Tk(), Tk(), Tk(), Tk()
    k_kcTs = [Tk(), Tk()]
    k_vcMs = [Tk(), Tk()]
    k_Pz = [Tk() for _ in range(4)]
    k_selTs = Tk()
    k_G = [Tk() for _ in range(8)]
    k_X = [[Tk() for _ in range(4)] for _ in range(2)]
    k_idx = Tk()
    k_sscr = Tk("sample scratch")
    k_ptail = k_xin[0]
    k_out = Tk("out")
    k_vst = [Tk(), Tk()]
    k_ktst = [Tk(), Tk()]
    k_xct = k_ktst
    k_AB = Tk()
    k_scr = Tk("scratch")
    k_x1 = [Tk(), Tk()]
    k_cmp = Tk()
    k_cmp2 = Tk()
    k_kc = Tk()
    k_vc = Tk()
    k_kbuf = [Tk() for _ in range(3)]
    k_vbuf = [Tk() for _ in range(3)]
    k_pT = [Tk() for _ in range(4)]
    k_posq = [Tk(), Tk()]
    k_mtmp = [Tk(), Tk()]
    k_mask = [Tk() for _ in range(3)]
    k_privn = [Tk(), Tk()]
    k_pri, k_pri2, k_m8, k_sel, k_selT = Tk(), Tk(), Tk(), Tk(), Tk()
    k_gate, k_rsum, k_fsc, k_oacc, k_otmp, k_obf = Tk(), Tk(), Tk(), Tk(), Tk(), Tk()
    k_yst = k_kvst

    state = {"ps": 0, "wb": 0, "xin": 0, "kvst": 0, "vst": 0, "ktst": 0,
             "kb": 0, "pT": 0, "mask": 0, "pq": 0, "yst": 0, "psS": 0,
             "G": 0, "pz": 0, "kc2": 0}

    def rot(key, n):
        i = state[key]
        state[key] = (i + 1) % n
        return i

    def next_ps():
        return rot("ps", 8)

    pw4 = pw_sb[:, :].rearrange("p (g k c) -> p g k c", g=4, k=2)
    VG0, VG1, VGKV, VPS, VGQ, VG1B, VGF = 0, 1, 2, 3, 4, 5, 6

    def vcol(v, dc):
        return vec_sb[:, v * 8 + dc: v * 8 + dc + 1]

    def cload(eng, out_ap, in_ap):
        S.add(eng, lambda e: e.dma_start(out=out_ap, in_=in_ap), writes=[k_const], dma=True)

    cload("sp", vec_sb[:, :], vecs[:, :])
    cload("sp", corr_sb[:, :], corr.rearrange("g c -> (g c)").partition_broadcast(128))
    cload("pool", pw4, pool_w.rearrange("g (k p) c -> p g k c", p=128))
    cload("sp", ident[:, :], ident_d[:, :])
    cload("sp", identb[:, :], identb_d[:, :])
    cload("sp", selw[:, :, :], selw_d[:, :, :])
    S.add("sp", lambda e: e.dma_start(out=pool_s_out[:, 0:14, :], in_=spool_d.rearrange("(s r) d -> s r d", r=15)[:, 1:15, :]),
          writes=[k_out], dma=True)
    for s_ in range(NS):
        S.add("sp", lambda e, s_=s_: e.dma_start(out=win_s_out[s_, 0:511, :], in_=swin_d[s_, 1:512, :]), writes=[k_out], dma=True)
    if STAGE >= 3:
        cload("sp", tri_sb[:, :], tri_d[:, :])
        cload("sp", esel_sb[:, :], esel_d[:, :])
        cload("sp", posk_sb[:, :], posk_d[:, :])
        cload("sp", cend_sb[:, :], cend_d[:, :])
        cload("sp", bg_sb[:, :], b_gate[0, :].partition_broadcast(128))
        cload("pool", w1_sb[:, :, :, :], cmp_w1.rearrange("k (p d) h -> d k p h", d=64))
        cload("pool", w2_sb[:, :, :], cmp_w2.rearrange("k h d -> h k d"))
        cload("sp", pe_nat[:, :], cmp_pe.rearrange("k p d -> (k p) d"))
        for g in range(4):
            cload("sp", vcM[:, :, g, 64:128], mc2s_d[:, :, :])
    if STAGE >= 4:
        cload("sp", maskS[:, :], maskS_d[:, :])
        cload("sp", privnS[:, :], privnS_d[:, :])
        cload("sp", pribiasS[:, :], pribiasS_d[:, :])
        for i_ in range(2):
            cload("sp", kcTs[i_][64:70, :, :], caugS_d.unsqueeze(1).to_broadcast([6, 4, 128]))
            for g in range(4):
                cload("sp", vcMs[i_][:, g, 64:128], mc2sS_d[:, :])
        cload("sp", QTs[64:70, :, :, :], qaugS_d[:, :, :, :])
        for b_ in range(2):
            for g in range(4):
                cload("sp", knT[64:70, b_, g, :], kaugS_d[20, :, 0:NS])
        S.add("dve", lambda e: e.memset(vnew[:, :, :, 64:65], 1.0), writes=[k_vnew])
        for i_ in range(4):
            S.add("dve", lambda e, i_=i_: e.memset(Pz[i_][:, :, :], 0.0), writes=[k_Pz[i_]])
        for g in range(4):
            for s_ in range(NS):
                S.add("sp", lambda e, g=g, s_=s_: e.dma_start(
                    out=KsTS_d[s_, :, 64:70, g, :], in_=kaugS_d[0:16]), writes=[k_sscr], dma=True)
                S.add("sp", lambda e, g=g, s_=s_: e.dma_start(
                    out=KwTS_d[s_, :, 64:70, g, :], in_=kaugS_d[16:20]), writes=[k_sscr], dma=True)
    S.add("dve", lambda e: e.memset(ones_bf[:, :], 1.0), writes=[k_const])
    S.add("dve", lambda e: e.memset(epsb[:, :], EPS), writes=[k_const])
    if STAGE >= 3:
        for i in range(2):
            S.add("dve", lambda e, i=i: e.memset(vst[i][:, :, :, 64:65], 1.0), writes=[k_vst[i]])
        for g in range(4):
            S.add("sp", lambda e, g=g: e.dma_start(out=KsT_d[:, 64:70, g, :], in_=kaug_d[:, :, :]), writes=[k_scr], dma=True)
            S.add("sp", lambda e, g=g: e.dma_start(out=KwT_d[:, 64:70, g, :], in_=kaug_d[:, :, :]), writes=[k_scr], dma=True)
        pi = next_ps()
        S.add("pe", lambda e, pi=pi: e.transpose(psum[pi][0:64, 0:64], pe_nat[:, :], ident[0:64, 0:64]),
              reads=[k_const], writes=[k_ps[pi]])
        S.add("act", lambda e, pi=pi: e.activation(out=peT[:, :, :], in_=psum[pi][0:64, 0:64].rearrange("d (k p) -> d k p", k=2), func=AF.Copy),
              reads=[k_ps[pi]], writes=[k_const])
        for kv in range(2):
            pi = next_ps()
            for p in range(32):
                S.add("pe", lambda e, pi=pi, kv=kv, p=p: e.matmul(
                    psum[pi][:, 0:1], w1_sb[:, kv, p, :], peT[:, kv, p:p + 1], start=(p == 0), stop=(p == 31)),
                    reads=[k_const], writes=[k_ps[pi]])
            S.add("act", lambda e, pi=pi, kv=kv: e.activation(out=cvec[:, kv:kv + 1], in_=psum[pi][:, 0:1], func=AF.Copy),
                  reads=[k_ps[pi]], writes=[k_cmp])

    def wload(src_ap, shape3):
        i = rot("wb", NWB)
        a, b = shape3
        view = wb[i][:, 0:a * b].rearrange("p (a b) -> p a b", a=a)
        S.add("pool", lambda e: e.dma_start(out=view, in_=src_ap), writes=[k_wb[i]], dma=True)
        return i, view

    COLR = [(0, 16), (16, 528), (528, 1040)]
    MAINR = [(16, 528), (528, 1040)]
    sq = hT[:, :, :].rearrange("p a b -> p (a b)")[:, 0:8 * NCX].rearrange("p (a b) -> p a b", a=8)

    SR = (NCOL, NCX)

    def norm(vidx, ranges, with_tail=False, final=False, stail=False):
        for dc in range(8):
            S.add("act", lambda e, dc=dc: e.activation(out=sq[:, dc, :], in_=xT[:, dc, :], func=AF.Square),
                  reads=[k_xT[dc]], writes=list(k_h))
        for (c0, c1) in ranges:
            pi = next_ps()
            n = c1 - c0
            for dc in range(8):
                S.add("pe", lambda e, dc=dc, pi=pi, c0=c0, c1=c1, n=n: e.matmul(
                    psum[pi][:, 0:n], ones_bf[:, :], sq[:, dc, c0:c1], start=(dc == 0), stop=(dc == 7)),
                    reads=list(k_h) + [k_const], writes=[k_ps[pi]])
            S.add("act", lambda e, pi=pi, c0=c0, c1=c1, n=n: e.activation(
                out=rstd[:, c0:c1], in_=psum[pi][:, 0:n], func=AF.Sqrt, bias=epsb[:, 0:1], scale=1.0 / D),
                reads=[k_ps[pi], k_const], writes=[k_rstd])
            S.add("dve", lambda e, c0=c0, c1=c1: e.reciprocal(out=rstd[:, c0:c1], in_=rstd[:, c0:c1]),
                  reads=[k_rstd], writes=[k_rstd])
        lo = ranges[0][0]
        hi = ranges[-1][1]
        for dc in range(8):
            if final:
                S.add("dve", lambda e, dc=dc: e.scalar_tensor_tensor(
                    out=xT[:, dc, lo:hi], in0=xT[:, dc, lo:hi], scalar=vcol(vidx, dc), in1=rstd[:, lo:hi],
                    op0=ALU.mult, op1=ALU.mult), reads=[k_xT[dc], k_rstd, k_const], writes=[k_xT[dc]])
            else:
                S.add("dve", lambda e, dc=dc: e.scalar_tensor_tensor(
                    out=uT[:, dc, lo:hi], in0=xT[:, dc, lo:hi], scalar=vcol(vidx, dc), in1=rstd[:, lo:hi],
                    op0=ALU.mult, op1=ALU.mult), reads=[k_xT[dc], k_rstd, k_const], writes=[k_uT])
        if with_tail or stail:
            t0 = NCX - 16 if stail else NCOL - 16
            for dc in range(8):
                S.add("dve", lambda e, dc=dc, t0=t0: e.scalar_tensor_tensor(
                    out=utail[:, dc, :], in0=xT[:, dc, t0:t0 + 16], scalar=vcol(vidx, dc),
                    in1=rstd[:, t0:t0 + 16], op0=ALU.mult, op1=ALU.mult),
                    reads=[k_xT[dc], k_rstd, k_const], writes=[k_utail])

    def rows16_out(src3, ksrc, dst_ap):
        pi = next_ps()
        pi2 = next_ps()
        for dc in range(8):
            pp = pi if dc < 4 else pi2
            S.add("pe", lambda e, dc=dc, pp=pp: e.transpose(
                psum[pp][0:16, (dc % 4) * 128:(dc % 4 + 1) * 128], src3[:, dc, :], ident[:, :]),
                reads=ksrc + [k_const], writes=[k_ps[pp]])
        S.add("act", lambda e, pi=pi: e.activation(out=ptail[:, 0:512], in_=psum[pi][0:16, :], func=AF.Copy),
              reads=[k_ps[pi]], writes=[k_ptail])
        S.add("act", lambda e, pi2=pi2: e.activation(out=ptail[:, 512:1024], in_=psum[pi2][0:16, :], func=AF.Copy),
              reads=[k_ps[pi2]], writes=[k_ptail])
        S.add("sp", lambda e: e.dma_start(out=dst_ap, in_=ptail[:, :]), reads=[k_ptail], writes=[k_out], dma=True)

    def mlp(layer, has_s=False):
        for hh in range(2):
            for fb in range(4):
                col0 = hh * 2048 + fb * 512
                wi, wv = wload(w_up[layer].rearrange("(dc p) f -> p dc f", p=128)[:, :, col0:col0 + 512], (8, 512))
                for fc in range(4):
                    fcl = fb * 4 + fc
                    for th in range(2):
                        pi = next_ps()
                        c0 = 16 + th * 512
                        for dc in range(8):
                            S.add("pe", lambda e, pi=pi, wv=wv, dc=dc, fc=fc, c0=c0: e.matmul(
                                psum[pi][:, :], wv[:, dc, fc * 128:(fc + 1) * 128], uT[:, dc, c0:c0 + 512],
                                start=(dc == 0), stop=(dc == 7)),
                                reads=[k_wb[wi], k_uT], writes=[k_ps[pi]])
                        tt = tA[:, th * 512:(th + 1) * 512]
                        kt = k_tA if th == 0 else k_tA1
                        S.add("act", lambda e, pi=pi, tt=tt: e.activation(out=tt, in_=psum[pi][:, :], func=AF.Relu),
                              reads=[k_ps[pi]], writes=[kt])
                        S.add("dve", lambda e, tt=tt, fcl=fcl, th=th: e.tensor_tensor(
                            out=hT[:, fcl, th * 512:(th + 1) * 512], in0=tt, in1=tt, op=ALU.mult),
                            reads=[kt], writes=[k_h[fcl]])
                    if has_s:
                        pi = next_ps()
                        for dc in range(8):
                            S.add("pe", lambda e, pi=pi, wv=wv, dc=dc, fc=fc: e.matmul(
                                psum[pi][:, 0:NS], wv[:, dc, fc * 128:(fc + 1) * 128], uT[:, dc, NCOL:NCX],
                                start=(dc == 0), stop=(dc == 7)),
                                reads=[k_wb[wi], k_uT], writes=[k_ps[pi]])
                        S.add("act", lambda e, pi=pi: e.activation(out=rs_t[:, :], in_=psum[pi][:, 0:NS], func=AF.Relu),
                              reads=[k_ps[pi]], writes=[k_rs])
                        S.add("dve", lambda e, fcl=fcl: e.tensor_tensor(
                            out=hTs[:, fcl, :], in0=rs_t[:, :], in1=rs_t[:, :], op=ALU.mult),
                            reads=[k_rs], writes=[k_hTs])
            for dmp in range(4):
                wi, wv = wload(w_down[layer][hh * 2048:(hh + 1) * 2048, dmp * 256:(dmp + 1) * 256]
                               .rearrange("(f p) c -> p f c", p=128), (16, 256))
                for dmc in range(2):
                    dca = dmp * 2 + dmc
                    for th in range(2):
                        pi = next_ps()
                        for fcl in range(16):
                            S.add("pe", lambda e, pi=pi, wv=wv, fcl=fcl, dmc=dmc, th=th: e.matmul(
                                psum[pi][:, :], wv[:, fcl, dmc * 128:(dmc + 1) * 128], hT[:, fcl, th * 512:(th + 1) * 512],
                                start=(fcl == 0), stop=(fcl == 15)),
                                reads=[k_wb[wi], k_h[fcl]], writes=[k_ps[pi]])
                        c0 = 16 + th * 512
                        S.add("dve", lambda e, pi=pi, dca=dca, c0=c0: e.tensor_tensor(
                            out=xT[:, dca, c0:c0 + 512], in0=xT[:, dca, c0:c0 + 512], in1=psum[pi][:, :], op=ALU.add),
                            reads=[k_ps[pi], k_xT[dca]], writes=[k_xT[dca]])
                    if has_s:
                        pi = next_ps()
                        for fcl in range(16):
                            S.add("pe", lambda e, pi=pi, wv=wv, fcl=fcl, dmc=dmc: e.matmul(
                                psum[pi][:, 0:NS], wv[:, fcl, dmc * 128:(dmc + 1) * 128], hTs[:, fcl, :],
                                start=(fcl == 0), stop=(fcl == 15)),
                                reads=[k_wb[wi], k_hTs], writes=[k_ps[pi]])
                        S.add("dve", lambda e, pi=pi, dca=dca: e.tensor_tensor(
                            out=xT[:, dca, NCOL:NCX], in0=xT[:, dca, NCOL:NCX], in1=psum[pi][:, 0:NS], op=ALU.add),
                            reads=[k_ps[pi], k_xT[dca]], writes=[k_xT[dca]])

    if STAGE >= 4 and not _DEV.get("skip_prep"):
        I32 = mybir.dt.int32
        hflat = hT[:, :, :].rearrange("p a b -> p (a b)")
        XcT = hflat[0:64, :].rearrange("p (k g n) -> p k g n", k=2, g=4)
        uflat = uT[:, :, :].rearrange("p a b -> p (a b)")
        Gb = [uflat[:, i * 1024:(i + 1) * 1024] for i in range(8)]
        ptb_i = xin[0][:, 0:256].bitcast(I32)
        ptb_f = xin[0][:, 256:512]
        idx_f = xin[0][:, 512:768]
        idx_i = xin[1][:, 0:256].bitcast(I32)
        iotap = xin[1][:, 256:257]
        kcs_st = ktst[0][:, 0:512].rearrange("d (g n) -> d g n", g=4)
        vcs_st = xin[1][:, 512:640].bitcast(BF16).rearrange("n (g d) -> n g d", g=4)
        k_kcs, k_vcs = Tk(), Tk()
        hs_s = hs_all[:, 0, 0, 0:128]
        S.add("sp", lambda e: e.dma_start(out=ptb_i, in_=pt_d.rearrange("s k -> (s k)").partition_broadcast(128)), writes=[k_idx], dma=True)
        S.add("sp", lambda e: e.dma_start(out=iotap, in_=iotap_d[:, :]), writes=[k_idx], dma=True)
        S.add("dve", lambda e: e.tensor_copy(out=ptb_f, in_=ptb_i), reads=[k_idx], writes=[k_idx])
        S.add("dve", lambda e: e.tensor_scalar(out=idx_f, in0=ptb_f, scalar1=128.0, scalar2=iotap, op0=ALU.mult, op1=ALU.add),
              reads=[k_idx], writes=[k_idx])
        S.add("dve", lambda e: e.tensor_copy(out=idx_i, in_=idx_f), reads=[k_idx], writes=[k_idx])
        S.add("dve", lambda e: e.memset(kcs_st, 0.0), writes=[k_kcs])
        S.add("dve", lambda e: e.memset(vcs_st, 0.0), writes=[k_vcs])
        for s_ in range(_DEV.get("prep_ns", NS)):
            for kt in range(16):
                gi_ = rot("G", 8)
                col = s_ * 16 + kt
                S.add("pool", lambda e, gi_=gi_, col=col: e.indirect_dma_start(
                    out=Gb[gi_], out_offset=None, in_=cache_d[:, :],
                    in_offset=bass.IndirectOffsetOnAxis(ap=idx_i[:, col:col + 1], axis=0)),
                    reads=[k_idx], writes=[k_G[gi_]], dma=True)
                if _DEV.get("prep_level", 9) < 2:
                    continue
                pa, pb = next_ps(), next_ps()
                psa = psum[pa][:, :].bitcast(BF16)
                psb = psum[pb][:, :].bitcast(BF16)
                for j in range(4):
                    c_ = 512 + j * 64
                    S.add("pe", lambda e, psa=psa, j=j, c_=c_, gi_=gi_: e.transpose(
                        psa[0:64, j * 128:(j + 1) * 128], Gb[gi_][:, c_:c_ + 64], identb[:, :]),
                        reads=[k_G[gi_], k_const], writes=[k_ps[pa]])
                for j in range(8):
                    c_ = j * 64
                    S.add("pe", lambda e, psb=psb, j=j, c_=c_, gi_=gi_: e.transpose(
                        psb[0:64, j * 128:(j + 1) * 128], Gb[gi_][:, c_:c_ + 64], identb[:, :]),
                        reads=[k_G[gi_], k_const], writes=[k_ps[pb]])
                if _DEV.get("no_evac"):
                    continue
                kst = ktst[1][:, 0:512].rearrange("d (g n) -> d g n", g=4)
                if not _DEV.get("no_e1"):
                    S.add("act", lambda e, psa=psa, kst=kst: e.activation(
                        out=kst, in_=psa[0:64, 0:512].rearrange("d (g n) -> d g n", g=4), func=AF.Copy),
                        reads=[k_ps[pa]], writes=[k_ktst[1]])
                if not _DEV.get("no_store"):
                    S.add("sp", lambda e, kst=kst, s_=s_, kt=kt: e.dma_start(out=KsTS_d[s_, kt, 0:64, :, :], in_=kst),
                          reads=[k_ktst[1]], writes=[k_sscr], dma=True)
                if not _DEV.get("no_e2"):
                    S.add("dve", lambda e, psb=psb, kt=kt: e.tensor_copy(
                        out=XcT[:, :, :, kt * 128:(kt + 1) * 128], in_=psb[0:64, 0:1024].rearrange("d (k g n) -> d k g n", k=2, g=4)),
                        reads=[k_ps[pb]], writes=k_X[0] + k_X[1])
                vi = rot("vst", 2)
                if not _DEV.get("no_e4"):
                    S.add("act", lambda e, vi=vi, gi_=gi_: e.activation(
                        out=vst[vi][:, 0, :, 0:64], in_=Gb[gi_][:, 768:1024].rearrange("p (g d) -> p g d", g=4), func=AF.Copy),
                        reads=[k_G[gi_]], writes=[k_vst[vi]])
                if not _DEV.get("no_store"):
                    S.add("sp", lambda e, vi=vi, s_=s_, kt=kt: e.dma_start(out=VsS_d[s_, kt], in_=vst[vi][:, 0, :, :]),
                          reads=[k_vst[vi]], writes=[k_sscr], dma=True)
            for kv in range(2 if _DEV.get("prep_level", 9) >= 3 else 0):
                for g in range(4):
                    pi = next_ps()
                    for pos in range(32):
                        S.add("pe", lambda e, pi=pi, kv=kv, g=g, pos=pos: e.matmul(
                            psum[pi][:, 0:127], w1_sb[:, kv, pos, :], XcT[:, kv, g, pos:pos + 2017:16],
                            start=(pos == 0), stop=(pos == 31)),
                            reads=[k_X[kv][g], k_const], writes=[k_ps[pi]])
                    S.add("act", lambda e, pi=pi, kv=kv: e.activation(
                        out=hs_s[:, 0:127], in_=psum[pi][:, 0:127], func=AF.Silu, bias=cvec[:, kv:kv + 1]),
                        reads=[k_ps[pi], k_cmp], writes=[k_AB])
                    pj = next_ps()
                    if kv == 0:
                        S.add("pe", lambda e, pj=pj: e.matmul(psum[pj][0:64, 0:127], w2_sb[:, 0, :], hs_s[:, 0:127], start=True, stop=True),
                              reads=[k_AB, k_const], writes=[k_ps[pj]])
                        S.add("act", lambda e, pj=pj, g=g: e.activation(out=kcs_st[:, g, 0:127], in_=psum[pj][0:64, 0:127], func=AF.Copy),
                              reads=[k_ps[pj]], writes=[k_kcs])
                    else:
                        S.add("pe", lambda e, pj=pj: e.matmul(psum[pj][0:127, 0:64], hs_s[:, 0:127], w2_sb[:, 1, :], start=True, stop=True),
                              reads=[k_AB, k_const], writes=[k_ps[pj]])
                        S.add("act", lambda e, pj=pj, g=g: e.activation(out=vcs_st[0:127, g, :], in_=psum[pj][0:127, 0:64], func=AF.Copy),
                              reads=[k_ps[pj]], writes=[k_vcs])
            if _DEV.get("prep_level", 9) < 4:
                continue
            S.add("sp", lambda e, s_=s_: e.dma_start(out=kcTS_d[s_], in_=kcs_st), reads=[k_kcs], writes=[k_sscr], dma=True)
            S.add("sp", lambda e, s_=s_: e.dma_start(out=vcS_d[s_], in_=vcs_st), reads=[k_vcs], writes=[k_sscr], dma=True)
            for t_ in range(4):
                gi_ = rot("G", 8)
                S.add("pool", lambda e, gi_=gi_, s_=s_, t_=t_: e.dma_start(
                    out=Gb[gi_][:, 0:512], in_=swin_d[s_, t_ * 128:(t_ + 1) * 128, :]), writes=[k_G[gi_]], dma=True)
                pa = next_ps()
                psa = psum[pa][:, :].bitcast(BF16)
                for j in range(4):
                    S.add("pe", lambda e, psa=psa, j=j, gi_=gi_: e.transpose(
                        psa[0:64, j * 128:(j + 1) * 128], Gb[gi_][:, j * 64:(j + 1) * 64], identb[:, :]),
                        reads=[k_G[gi_], k_const], writes=[k_ps[pa]])
                kst = ktst[1][:, 0:512].rearrange("d (g n) -> d g n", g=4)
                S.add("act", lambda e, psa=psa, kst=kst: e.activation(
                    out=kst, in_=psa[0:64, 0:512].rearrange("d (g n) -> d g n", g=4), func=AF.Copy),
                    reads=[k_ps[pa]], writes=[k_ktst[1]])
                S.add("sp", lambda e, kst=kst, s_=s_, t_=t_: e.dma_start(out=KwTS_d[s_, t_, 0:64, :, :], in_=kst),
                      reads=[k_ktst[1]], writes=[k_sscr], dma=True)
                vi = rot("vst", 2)
                S.add("act", lambda e, vi=vi, gi_=gi_: e.activation(
                    out=vst[vi][:, 0, :, 0:64], in_=Gb[gi_][:, 256:512].rearrange("p (g d) -> p g d", g=4), func=AF.Copy),
                    reads=[k_G[gi_]], writes=[k_vst[vi]])
                S.add("sp", lambda e, vi=vi, s_=s_, t_=t_: e.dma_start(out=VwS_d[s_, t_], in_=vst[vi][:, 0, :, :]),
                      reads=[k_vst[vi]], writes=[k_sscr], dma=True)
        S.barrier()

    for gq, (slot0, is_own, ogi) in enumerate(L0_GROUPS):
        if _DEV.get("prep_only"):
            continue
        if STAGE < 3 and not is_own:
            continue
        last = is_own and ogi == 1
        has_s = is_own and ogi == 0
        RNG0 = COLR + ([SR] if has_s else [])
        RNG = MAINR + ([SR] if has_s else [])
        for ti in range(10 if has_s else 9):
            r0, nr = (0, 16) if ti == 0 else ((NCOL, NS) if ti == 9 else (16 + (ti - 1) * 128, 128))
            xi = rot("xin", 2)
            if ti == 9:
                S.add("sp", lambda e, xi=xi: e.dma_start(out=xin[xi][0:NS, :], in_=xs_d[:, :]),
                      writes=[k_xin[xi]], dma=True)
            else:
                S.add("sp", lambda e, xi=xi, r0=r0, nr=nr, gq=gq: e.dma_start(out=xin[xi][0:nr, :], in_=xg[gq, r0:r0 + nr, :]),
                      writes=[k_xin[xi]], dma=True)
            for hb in range(2):
                pi = next_ps()
                for j in range(4):
                    dc = hb * 4 + j
                    S.add("pe", lambda e, xi=xi, pi=pi, j=j, dc=dc, nr=nr: e.transpose(
                        psum[pi][:, j * 128:j * 128 + nr], xin[xi][0:nr, dc * 128:(dc + 1) * 128], ident[0:nr, 0:nr]),
                        reads=[k_xin[xi], k_const], writes=[k_ps[pi]])
                src = psum[pi][:, :].rearrange("p (j t) -> p j t", j=4)[:, :, 0:nr]
                if hb == 0:
                    S.add("act", lambda e, src=src, hb=hb, r0=r0, nr=nr: e.activation(
                        out=xT[:, hb * 4:hb * 4 + 4, r0:r0 + nr], in_=src, func=AF.Copy),
                        reads=[k_ps[pi]], writes=k_xT[hb * 4:hb * 4 + 4])
                else:
                    S.add("dve", lambda e, src=src, hb=hb, r0=r0, nr=nr: e.tensor_copy(
                        out=xT[:, hb * 4:hb * 4 + 4, r0:r0 + nr], in_=src),
                        reads=[k_ps[pi]], writes=k_xT[hb * 4:hb * 4 + 4])
        norm(VG0, RNG0, with_tail=last, stail=has_s)
        if last:
            rows16_out(utail, [k_utail], pool_out[:, :])
        if has_s:
            rows16_out(utail, [k_utail], pool_s_out[:, 14, :])
            stx = [rot("xin", 2), None]
            stx[1] = rot("xin", 2)
            for t_, (r0_, nr_) in enumerate(((0, 128), (128, 112))):
                S.add("sp", lambda e, xi=stx[t_], r0_=r0_, nr_=nr_: e.dma_start(out=xin[xi][0:nr_, :], in_=spool_d[r0_:r0_ + nr_, :]),
                      writes=[k_xin[stx[t_]]], dma=True)
        diffT = hT[:, 0:8, :]
        for dc in range(8):
            g = dc // 2
            nst = g + 1
            bufs = [tA, tB]
            kb_ = [k_tA, k_tB]
            kbw_ = [[k_tA, k_tA1], [k_tB]]
            cur = None
            for s in range(nst):
                sh = 1 << s
                lo = (1 << (s + 1))
                o = bufs[s % 2]
                ko = kb_[s % 2]
                kow = kbw_[s % 2]
                if s == 0:
                    S.add("dve", lambda e, o=o, dc=dc, lo=lo, sh=sh: e.tensor_tensor(
                        out=o[:, lo:NCOL], in0=uT[:, dc, lo:NCOL], in1=uT[:, dc, lo - sh:NCOL - sh], op=ALU.add),
                        reads=[k_uT], writes=kow)
                else:
                    i_ = bufs[(s - 1) % 2]
                    ki = kb_[(s - 1) % 2]
                    S.add("dve", lambda e, o=o, i_=i_, lo=lo, sh=sh: e.tensor_tensor(
                        out=o[:, lo:NCOL], in0=i_[:, lo:NCOL], in1=i_[:, lo - sh:NCOL - sh], op=ALU.add),
                        reads=[ki], writes=kow)
                cur = (o, ko, kow)
            o, ko, kow = cur
            w = 1 << (g + 1)
            if has_s:
                pi = next_ps()
                for t_, nr_ in enumerate((128, 112)):
                    S.add("pe", lambda e, pi=pi, t_=t_, nr_=nr_, dc=dc, g=g: e.matmul(
                        psum[pi][:, 0:NS], xin[stx[t_]][0:nr_, dc * 128:(dc + 1) * 128], selw[0:nr_, t_, g * 16:(g + 1) * 16],
                        start=(t_ == 0), stop=(t_ == 1)),
                        reads=[k_xin[stx[t_]], k_const], writes=[k_ps[pi]])
                S.add("dve", lambda e, o=o, pi=pi, dc=dc: e.tensor_tensor(
                    out=o[:, NCOL:NCX], in0=psum[pi][:, 0:NS], in1=uT[:, dc, NCOL:NCX], op=ALU.add),
                    reads=[k_ps[pi], k_uT, ko], writes=kow)
                S.add("dve", lambda e, o=o, dc=dc, w=w: e.scalar_tensor_tensor(
                    out=diffS[:, dc, :], in0=o[:, NCOL:NCX], scalar=1.0 / w, in1=uT[:, dc, NCOL:NCX],
                    op0=ALU.mult, op1=ALU.subtract), reads=[ko, k_uT], writes=[k_diffS])
            S.add("dve", lambda e, o=o, g=g, gq=gq: e.tensor_tensor(
                out=o[:, 16:32], in0=o[:, 16:32], in1=corr_sb[:, gq * 64 + g * 16: gq * 64 + g * 16 + 16], op=ALU.mult),
                reads=[ko, k_const], writes=kow)
            S.add("dve", lambda e, o=o, dc=dc, w=w: e.scalar_tensor_tensor(
                out=diffT[:, dc, :], in0=o[:, 16:NCOL], scalar=1.0 / w, in1=uT[:, dc, 16:NCOL],
                op0=ALU.mult, op1=ALU.subtract), reads=[ko, k_uT], writes=[k_h[dc]])
        for oc in range(8):
            g = oc // 2
            for th in range(2):
                pi = next_ps()
                for kk in range(2):
                    S.add("pe", lambda e, pi=pi, g=g, kk=kk, oc=oc, th=th: e.matmul(
                        psum[pi][:, :], pw4[:, g, kk, (oc % 2) * 128:(oc % 2) * 128 + 128],
                        diffT[:, 2 * g + kk, th * 512:(th + 1) * 512], start=(kk == 0), stop=(kk == 1)),
                        reads=[k_h[2 * g + kk], k_const], writes=[k_ps[pi]])
                c0 = 16 + th * 512
                S.add("dve", lambda e, pi=pi, oc=oc, c0=c0: e.scalar_tensor_tensor(
                    out=xT[:, oc, c0:c0 + 512], in0=psum[pi][:, :], scalar=vcol(VPS, oc), in1=xT[:, oc, c0:c0 + 512],
                    op0=ALU.mult, op1=ALU.add), reads=[k_ps[pi], k_xT[oc], k_const], writes=[k_xT[oc]])
            if has_s:
                pi = next_ps()
                for kk in range(2):
                    S.add("pe", lambda e, pi=pi, g=g, kk=kk, oc=oc: e.matmul(
                        psum[pi][:, 0:NS], pw4[:, g, kk, (oc % 2) * 128:(oc % 2) * 128 + 128],
                        diffS[:, 2 * g + kk, :], start=(kk == 0), stop=(kk == 1)),
                        reads=[k_diffS, k_const], writes=[k_ps[pi]])
                S.add("dve", lambda e, pi=pi, oc=oc: e.scalar_tensor_tensor(
                    out=xT[:, oc, NCOL:NCX], in0=psum[pi][:, 0:NS], scalar=vcol(VPS, oc), in1=xT[:, oc, NCOL:NCX],
                    op0=ALU.mult, op1=ALU.add), reads=[k_ps[pi], k_xT[oc], k_const], writes=[k_xT[oc]])
        norm(VG1, RNG)
        mlp(0, has_s)
        if is_own and STAGE >= 2:
            S.add("sp", lambda e, ogi=ogi: e.dma_start(
                out=x1_d[ogi].rearrange("p (a b) -> p a b", a=8), in_=xT[:, :, 16:NCX]),
                reads=list(k_xT), writes=[k_x1[ogi]], dma=True)
        norm(VGKV, RNG)
        kvw = []
        for cb in range(3):
            kvw.append(wload(w_kv.rearrange("(dc p) f -> p dc f", p=128)[:, :, cb * 512:(cb + 1) * 512], (8, 512)))
        for ti in range(8):
            ks = rot("kvst", 2)
            c0 = 16 + ti * 128
            for cb in range(3):
                wi, wv = kvw[cb]
                pi = next_ps()
                for dc in range(8):
                    S.add("pe", lambda e, pi=pi, wv=wv, dc=dc, c0=c0: e.matmul(
                        psum[pi][:, :], uT[:, dc, c0:c0 + 128], wv[:, dc, :], start=(dc == 0), stop=(dc == 7)),
                        reads=[k_wb[wi], k_uT], writes=[k_ps[pi]])
                S.add("act", lambda e, pi=pi, ks=ks, cb=cb: e.activation(
                    out=kvst[ks][:, cb * 512:(cb + 1) * 512], in_=psum[pi][:, :], func=AF.Copy),
                    reads=[k_ps[pi]], writes=[k_kvst[ks]])
            if is_own:
                row0 = ogi * GT + ti * 128
                S.add("sp", lambda e, ks=ks, row0=row0: e.dma_start(out=kv_out[row0:row0 + 128, :], in_=kvst[ks][:, 0:1024]),
                      reads=[k_kvst[ks]], writes=[k_out], dma=True)
                if last and ti >= 4:
                    S.add("sp", lambda e, ks=ks, ti=ti: e.dma_start(
                        out=win_out[(ti - 4) * 128:(ti - 3) * 128, :], in_=kvst[ks][:, 1024:1536]),
                        reads=[k_kvst[ks]], writes=[k_out], dma=True)
            if STAGE >= 3:
                vi = rot("vst", 2)
                S.add("dve", lambda e, ks=ks, vi=vi: e.tensor_copy(
                    out=vst[vi][:, 0, :, 0:64], in_=kvst[ks][:, 768:1024].rearrange("p (g d) -> p g d", g=4)),
                    reads=[k_kvst[ks]], writes=[k_vst[vi]])
                S.add("dve", lambda e, ks=ks, vi=vi: e.tensor_copy(
                    out=vst[vi][:, 1, :, 0:64], in_=kvst[ks][:, 1280:1536].rearrange("p (g d) -> p g d", g=4)),
                    reads=[k_kvst[ks]], writes=[k_vst[vi]])
                st_ = slot0 + ti
                S.add("sp", lambda e, vi=vi, st_=st_: e.dma_start(out=Vs_d[st_], in_=vst[vi][:, 0, :, :]),
                      reads=[k_vst[vi]], writes=[k_scr], dma=True)
                S.add("sp", lambda e, vi=vi, st_=st_: e.dma_start(out=Vw_d[st_], in_=vst[vi][:, 1, :, :]),
                      reads=[k_vst[vi]], writes=[k_scr], dma=True)
        if has_s:
            ks = rot("kvst", 2)
            for cb in range(3):
                wi, wv = kvw[cb]
                pi = next_ps()
                for dc in range(8):
                    S.add("pe", lambda e, pi=pi, wv=wv, dc=dc: e.matmul(
                        psum[pi][0:NS, :], uT[:, dc, NCOL:NCX], wv[:, dc, :], start=(dc == 0), stop=(dc == 7)),
                        reads=[k_wb[wi], k_uT], writes=[k_ps[pi]])
                S.add("act", lambda e, pi=pi, ks=ks, cb=cb: e.activation(
                    out=kvst[ks][0:NS, cb * 512:(cb + 1) * 512], in_=psum[pi][0:NS, :], func=AF.Copy),
                    reads=[k_ps[pi]], writes=[k_kvst[ks]])
            S.add("sp", lambda e, ks=ks: e.dma_start(out=kv_s_out[:, :], in_=kvst[ks][0:NS, 0:1024]),
                  reads=[k_kvst[ks]], writes=[k_out], dma=True)
            S.add("sp", lambda e, ks=ks: e.dma_start(out=win_s_out[:, 511, :], in_=kvst[ks][0:NS, 1024:1536]),
                  reads=[k_kvst[ks]], writes=[k_out], dma=True)
            if STAGE >= 4:
                for bi_, (kc0, vc0) in enumerate(((512, 768), (1024, 1280))):
                    S.add("dve", lambda e, ks=ks, bi_=bi_, vc0=vc0: e.tensor_copy(
                        out=vnew[:, bi_, :, 0:64], in_=kvst[ks][0:NS, vc0:vc0 + 256].rearrange("p (g d) -> p g d", g=4)),
                        reads=[k_kvst[ks]], writes=[k_vnew])
                    S.add("dve", lambda e, ks=ks, bi_=bi_, kc0=kc0: e.tensor_copy(
                        out=knew_bf[:, bi_ * 256:(bi_ + 1) * 256], in_=kvst[ks][0:NS, kc0:kc0 + 256]),
                        reads=[k_kvst[ks]], writes=[k_knew])
                pi = next_ps()
                pbf = psum[pi][:, :].bitcast(BF16)
                for j in range(8):
                    S.add("pe", lambda e, pbf=pbf, j=j: e.transpose(
                        pbf[0:64, j * NS:(j + 1) * NS], knew_bf[:, j * 64:(j + 1) * 64], identb[0:NS, 0:NS]),
                        reads=[k_knew, k_const], writes=[k_ps[pi]])
                S.add("act", lambda e, pbf=pbf: e.activation(
                    out=knT[0:64, :, :, :], in_=pbf[0:64, 0:8 * NS].rearrange("d (b g n) -> d b g n", b=2, g=4), func=AF.Copy),
                    reads=[k_ps[pi]], writes=[k_knT])
        if STAGE >= 3:
            for (cb, dst) in ((1, KsT_d), (2, KwT_d)):
                wi, wv = kvw[cb]
                for g in range(4):
                    ki = rot("ktst", 2)
                    for th in range(2):
                        pi = next_ps()
                        c0 = 16 + th * 512
                        for dc in range(8):
                            S.add("pe", lambda e, pi=pi, wv=wv, dc=dc, g=g, c0=c0: e.matmul(
                                psum[pi][0:64, :], wv[:, dc, g * 64:(g + 1) * 64], uT[:, dc, c0:c0 + 512],
                                start=(dc == 0), stop=(dc == 7)),
                                reads=[k_wb[wi], k_uT], writes=[k_ps[pi]])
                        S.add("act", lambda e, pi=pi, ki=ki, th=th: e.activation(
                            out=ktst[ki][:, th * 512:(th + 1) * 512], in_=psum[pi][0:64, :], func=AF.Copy),
                            reads=[k_ps[pi]], writes=[k_ktst[ki]])
                    S.add("sp", lambda e, ki=ki, g=g, dst=dst, slot0=slot0: e.dma_start(
                        out=dst[slot0:slot0 + 8, 0:64, g, :].rearrange("t d k -> d t k"),
                        in_=ktst[ki][:, :].rearrange("d (t k) -> d t k", t=8)),
                        reads=[k_ktst[ki]], writes=[k_scr], dma=True)
            wi, wv = kvw[0]
            for kv in range(2):
                for g in range(4):
                    xi = rot("ktst", 2)
                    for th in range(2):
                        pi = next_ps()
                        c0 = 16 + th * 512
                        for dc in range(8):
                            S.add("pe", lambda e, pi=pi, wv=wv, dc=dc, g=g, kv=kv, c0=c0: e.matmul(
                                psum[pi][0:64, :], wv[:, dc, kv * 256 + g * 64: kv * 256 + (g + 1) * 64], uT[:, dc, c0:c0 + 512],
                                start=(dc == 0), stop=(dc == 7)),
                                reads=[k_wb[wi], k_uT], writes=[k_ps[pi]])
                        S.add("act", lambda e, pi=pi, xi=xi, th=th: e.activation(
                            out=xct[xi][:, th * 512:(th + 1) * 512], in_=psum[pi][0:64, :], func=AF.Copy),
                            reads=[k_ps[pi]], writes=[k_xct[xi]])
                    pi = next_ps()
                    xb = xct[xi]
                    for pos in range(32):
                        S.add("pe", lambda e, pi=pi, kv=kv, pos=pos, xb=xb: e.matmul(
                            psum[pi][:, 0:63], w1_sb[:, kv, pos, :], xb[:, pos:pos + 993:16],
                            start=(pos == 0), stop=(pos == 31)),
                            reads=[k_xct[xi], k_const], writes=[k_ps[pi]])
                    for p in range(16):
                        S.add("pe", lambda e, pi=pi, kv=kv, p=p, xb=xb: e.matmul(
                            psum[pi][:, 64:65], w1_sb[:, kv, p, :], xb[:, 1008 + p:1009 + p],
                            start=(p == 0), stop=(p == 15)),
                            reads=[k_xct[xi], k_const], writes=[k_ps[pi]])
                    for p in range(16):
                        S.add("pe", lambda e, pi=pi, kv=kv, p=p, xb=xb: e.matmul(
                            psum[pi][:, 65:66], w1_sb[:, kv, 16 + p, :], xb[:, p:p + 1],
                            start=(p == 0), stop=(p == 15)),
                            reads=[k_xct[xi], k_const], writes=[k_ps[pi]])
                    sb0 = (slot0 * 8) % 256
                    po = sb0 // 64
                    S.add("act", lambda e, pi=pi, kv=kv, g=g, sb0=sb0: e.activation(
                        out=hs_all[:, kv, g, sb0:sb0 + 63], in_=psum[pi][:, 0:63], func=AF.Silu, bias=cvec[:, kv:kv + 1]),
                        reads=[k_ps[pi], k_cmp], writes=[k_AB])
                    S.add("act", lambda e, pi=pi, kv=kv, g=g, po=po: e.activation(
                        out=ABs[:, kv, g, :, po], in_=psum[pi][:, 64:66], func=AF.Copy),
                        reads=[k_ps[pi]], writes=[k_AB])

    if STAGE >= 3 and not _DEV.get("prep_only"):
        for kv in range(2):
            for g in range(4):
                S.add("dve", lambda e, kv=kv, g=g: e.tensor_tensor(
                    out=hpreb[:, 0:3], in0=ABs[:, kv, g, 0, 0:3], in1=ABs[:, kv, g, 1, 1:4], op=ALU.add),
                    reads=[k_AB], writes=[k_cmp2])
                S.add("dve", lambda e, kv=kv, g=g: e.tensor_tensor(
                    out=hpreb[:, 3:4], in0=ABs[:, kv, g, 0, 3:4], in1=ABs[:, kv, g, 1, 0:1], op=ALU.add),
                    reads=[k_AB], writes=[k_cmp2])
                S.add("act", lambda e, kv=kv, g=g: e.activation(
                    out=hs_all[:, kv, g, 63:256:64], in_=hpreb[:, :], func=AF.Silu, bias=cvec[:, kv:kv + 1]),
                    reads=[k_cmp2, k_cmp], writes=[k_AB])
                pi = next_ps()
                if kv == 0:
                    S.add("pe", lambda e, pi=pi, g=g: e.matmul(psum[pi][0:64, 0:256], w2_sb[:, 0, :], hs_all[:, 0, g, :], start=True, stop=True),
                          reads=[k_AB, k_const], writes=[k_ps[pi]])
                    S.add("act", lambda e, pi=pi, g=g: e.activation(out=kcT[0:64, g, :], in_=psum[pi][0:64, 0:256], func=AF.Copy),
                          reads=[k_ps[pi]], writes=[k_kc])
                else:
                    for nt in range(2):
                        S.add("pe", lambda e, pi=pi, nt=nt, g=g: e.matmul(
                            psum[pi][:, nt * 64:(nt + 1) * 64], hs_all[:, 1, g, nt * 128:(nt + 1) * 128], w2_sb[:, 1, :], start=True, stop=True),
                            reads=[k_AB, k_const], writes=[k_ps[pi]])
                    S.add("act", lambda e, pi=pi, g=g: e.activation(
                        out=vcM[:, :, g, 0:64], in_=psum[pi][:, 0:128].rearrange("n (t d) -> n t d", t=2), func=AF.Copy),
                        reads=[k_ps[pi]], writes=[k_vc])

    PS_S = [0, 1, 2, 3]
    PS_ACC = [4, 5]
    PS_M = [6, 7]
    QT = hT[:, :, :].rearrange("p a b -> p (a b)").rearrange("p (g r t) -> p g r t", g=4, r=4)

    def attention_tile(i, ogi):
        tl = i % 8
        c0 = tl * 128
        pq = rot("pq", 2)
        pqb = posq_bc[pq]
        kpq = k_posq[pq]
        S.add("sp", lambda e: e.dma_start(out=pqb[:, :], in_=posq_d[i * 128:(i + 1) * 128].partition_broadcast(128)),
              writes=[kpq], dma=True)
        S.add("sp", lambda e: e.dma_start(out=privn_sb[pq][:, :], in_=privn_d[i * 128:(i + 1) * 128, :]),
              writes=[k_privn[pq]], dma=True)
        S.add("sp", lambda e: e.dma_start(out=pribias_sb[pq][:, :], in_=pribias_d[i * 128:(i + 1) * 128, :]),
              writes=[k_privn[pq]], dma=True)
        S.add("sp", lambda e: e.dma_start(out=kcT[64:70, :, :], in_=caug_d[i].unsqueeze(1).to_broadcast([6, 4, 256])),
              writes=[k_kc], dma=True)

        def branch_core(gp, br, slots, kind):
            gs = [2 * gp, 2 * gp + 1]
            nkt = len(slots)

            def front(ki_, slot):
                cx = {"first": ki_ == 0, "lastk": ki_ == nkt - 1, "slot": slot, "pti": {}, "kb": None}
                kb = None
                if kind == "c":
                    kt_r = [k_kc]
                    cx["vt_r"] = [k_vc]
                else:
                    kb = rot("kb", 3)
                    ksrc = KsT_d if kind == "s" else KwT_d
                    vsrc = Vs_d if kind == "s" else Vw_d
                    S.add("sp", lambda e, kb=kb, ksrc=ksrc, slot=slot: e.dma_start(out=kbuf[kb][:, :, :], in_=ksrc[slot]),
                          reads=[k_scr], writes=[k_kbuf[kb]], dma=True)
                    S.add("sp", lambda e, kb=kb, vsrc=vsrc, slot=slot: e.dma_start(out=vbuf[kb][:, :, :], in_=vsrc[slot]),
                          reads=[k_scr], writes=[k_vbuf[kb]], dma=True)
                    kt_r = [k_kbuf[kb]]
                    cx["vt_r"] = [k_vbuf[kb]]
                cx["kb"] = kb
                psm = PS_M[ki_ % 2]
                if kind == "c":
                    mi = rot("mask", 3)
                    S.add("dve", lambda e, mi=mi, slot=slot: e.tensor_scalar(
                        out=mask_sb[mi][:, :], in0=pqb[:, :], scalar1=cend_sb[:, slot:slot + 1], scalar2=None, op0=ALU.is_ge),
                        reads=[kpq, k_const], writes=[k_mask[mi]])
                    mk = ("sb", mi)
                elif kind == "w":
                    mi = rot("mask", 3)
                    S.add("dve", lambda e, slot=slot: e.tensor_scalar(
                        out=mtmp[0][:, :], in0=pqb[:, :], scalar1=posk_sb[:, slot:slot + 1], scalar2=0.0,
                        op0=ALU.subtract, op1=ALU.is_ge), reads=[kpq, k_const], writes=[k_mtmp[0]])
                    S.add("dve", lambda e, slot=slot: e.tensor_scalar(
                        out=mtmp[1][:, :], in0=pqb[:, :], scalar1=posk_sb[:, slot:slot + 1], scalar2=512.0,
                        op0=ALU.subtract, op1=ALU.is_lt), reads=[kpq, k_const], writes=[k_mtmp[1]])
                    S.add("dve", lambda e, mi=mi: e.tensor_tensor(
                        out=mask_sb[mi][:, :], in0=mtmp[0][:, :], in1=mtmp[1][:, :], op=ALU.mult),
                        reads=[k_mtmp[0], k_mtmp[1]], writes=[k_mask[mi]])
                    mk = ("sb", mi)
                else:
                    for g in gs:
                        S.add("pe", lambda e, g=g, slot=slot, psm=psm: e.matmul(
                            psum[psm][:, g * 128:(g + 1) * 128], esel_sb[:, slot * 128:(slot + 1) * 128], selT[:, g, :],
                            start=True, stop=True), reads=[k_selT, k_const], writes=[k_ps[psm]])
                    mk = ("ps", None)
                for g in gs:
                    si = PS_S[rot("psS", 4)]
                    pti = rot("pT", 4)
                    cx["pti"][g] = pti
                    if kind == "c":
                        lhs = kcT[:, g, slot * 128:(slot + 1) * 128]
                    else:
                        lhs = kbuf[kb][:, g, :]
                    S.add("pe", lambda e, si=si, lhs=lhs, g=g: e.matmul(
                        psum[si][:, :].rearrange("k (r t) -> k r t", r=4), lhs, QT[0:70, g, :, c0:c0 + 128], start=True, stop=True),
                        reads=kt_r + list(k_h), writes=[k_ps[si]])
                    S.add("act", lambda e, si=si, pti=pti: e.activation(
                        out=pT[pti][:, :, :], in_=psum[si][:, :].rearrange("k (r t) -> k r t", r=4), func=AF.Exp),
                        reads=[k_ps[si]], writes=[k_pT[pti]])
                    if mk[0] == "sb":
                        m_ap = mask_sb[mk[1]][:, :].unsqueeze(1).to_broadcast([128, 4, 128])
                        m_r = [k_mask[mk[1]]]
                    else:
                        m_ap = psum[psm][:, g * 128:(g + 1) * 128].unsqueeze(1).to_broadcast([128, 4, 128])
                        m_r = [k_ps[psm]]
                    S.add("dve", lambda e, pti=pti, m_ap=m_ap: e.tensor_tensor(
                        out=pT[pti][:, :, :], in0=pT[pti][:, :, :], in1=m_ap, op=ALU.mult),
                        reads=[k_pT[pti]] + m_r, writes=[k_pT[pti]])
                    if kind == "s" and slot == i:
                        S.add("dve", lambda e, pti=pti: e.tensor_tensor(
                            out=pT[pti][:, :, :], in0=pT[pti][:, :, :],
                            in1=tri_sb[:, :].unsqueeze(1).to_broadcast([128, 4, 128]), op=ALU.mult),
                            reads=[k_pT[pti], k_const], writes=[k_pT[pti]])
                return cx

            def back(cx):
                slot, kb, first, lastk = cx["slot"], cx["kb"], cx["first"], cx["lastk"]
                for g in gs:
                    pti = cx["pti"][g]
                    ai = PS_ACC[g % 2]
                    ncol = 128 if kind == "c" else 65
                    for r in range(4):
                        if kind == "c":
                            rhs = vcM[:, slot, g, :]
                        else:
                            rhs = vbuf[kb][:, g, :]
                        S.add("pe", lambda e, ai=ai, pti=pti, r=r, rhs=rhs, ncol=ncol, first=first, lastk=lastk: e.matmul(
                            psum[ai][:, r * ncol:(r + 1) * ncol], pT[pti][:, r, :], rhs, start=(first and r == 0), stop=lastk,
                            skip_group_check=True),
                            reads=[k_pT[pti]] + cx["vt_r"], writes=[k_ps[ai]])

            pend = None
            for ki_, slot in enumerate(slots):
                cx = front(ki_, slot)
                if pend is not None:
                    back(pend)
                pend = cx
            back(pend)
            for g in gs:
                ai = PS_ACC[g % 2]
                if kind == "c":
                    acc3 = psum[ai][:, :].rearrange("t (r c) -> t r c", r=4)
                    S.add("dve", lambda e, acc3=acc3: e.tensor_reduce(
                        out=rsum[:, :], in_=acc3[:, :, 64:128], axis=AX.X, op=ALU.add),
                        reads=[k_ps[ai]], writes=[k_rsum])
                    S.add("dve", lambda e: e.tensor_scalar(out=rsum[:, :], in0=rsum[:, :], scalar1=0.5, scalar2=1e-30,
                                                           op0=ALU.mult, op1=ALU.max), reads=[k_rsum], writes=[k_rsum])
                else:
                    acc3 = psum[ai][:, 0:260].rearrange("t (r c) -> t r c", r=4)
                    S.add("dve", lambda e, acc3=acc3: e.tensor_scalar(
                        out=rsum[:, :], in0=acc3[:, :, 64], scalar1=1e-30, scalar2=None, op0=ALU.max),
                        reads=[k_ps[ai]], writes=[k_rsum])
                S.add("dve", lambda e: e.reciprocal(out=rsum[:, :], in_=rsum[:, :]), reads=[k_rsum], writes=[k_rsum])
                if kind == "c":
                    for r in range(4):
                        if r == 0:
                            S.add("dve", lambda e, acc3=acc3, g=g: e.tensor_scalar(
                                out=pri[:, g, :], in0=acc3[:, 0, 64:128], scalar1=rsum[:, 0:1], scalar2=None, op0=ALU.mult),
                                reads=[k_ps[ai], k_rsum], writes=[k_pri])
                        else:
                            S.add("dve", lambda e, acc3=acc3, g=g, r=r: e.scalar_tensor_tensor(
                                out=pri[:, g, :], in0=acc3[:, r, 64:128], scalar=rsum[:, r:r + 1], in1=pri[:, g, :],
                                op0=ALU.mult, op1=ALU.add), reads=[k_ps[ai], k_rsum, k_pri], writes=[k_pri])
                S.add("dve", lambda e, g=g, br=br: e.tensor_tensor(
                    out=fsc[:, :], in0=rsum[:, :],
                    in1=gate_sb[:, tl, :].rearrange("t (h b) -> t h b", b=3)[:, 4 * g:4 * g + 4, br], op=ALU.mult),
                    reads=[k_rsum, k_gate], writes=[k_fsc])
                if br == 0:
                    S.add("dve", lambda e, acc3=acc3, g=g: e.tensor_tensor(
                        out=oacc[:, 4 * g:4 * g + 4, :], in0=acc3[:, :, 0:64],
                        in1=fsc[:, :].unsqueeze(2).to_broadcast([128, 4, 64]), op=ALU.mult),
                        reads=[k_ps[ai], k_fsc], writes=[k_oacc])
                else:
                    S.add("dve", lambda e, acc3=acc3: e.tensor_tensor(
                        out=otmp[:, :, :], in0=acc3[:, :, 0:64],
                        in1=fsc[:, :].unsqueeze(2).to_broadcast([128, 4, 64]), op=ALU.mult),
                        reads=[k_ps[ai], k_fsc], writes=[k_otmp])
                    S.add("dve", lambda e, g=g: e.tensor_tensor(
                        out=oacc[:, 4 * g:4 * g + 4, :], in0=oacc[:, 4 * g:4 * g + 4, :], in1=otmp[:, :, :], op=ALU.add),
                        reads=[k_otmp, k_oacc], writes=[k_oacc])

        for gp in range(2):
            branch_core(gp, 0, [0, 1], "c")
        for g in range(4):
            S.add("dve", lambda e, g=g: e.tensor_tensor(out=pri[:, g, :], in0=pri[:, g, :], in1=privn_sb[pq][:, :], op=ALU.mult),
                  reads=[k_pri, k_privn[pq]], writes=[k_pri])
            S.add("dve", lambda e, g=g: e.tensor_tensor(out=pri[:, g, :], in0=pri[:, g, :], in1=pribias_sb[pq][:, :], op=ALU.add),
                  reads=[k_pri, k_privn[pq]], writes=[k_pri])
            S.add("dve", lambda e, g=g: e.max(out=m8a[:, :], in_=pri[:, g, :]), reads=[k_pri], writes=[k_m8])
            S.add("dve", lambda e, g=g: e.match_replace(out=pri2[:, :], in_to_replace=m8a[:, :], in_values=pri[:, g, :], imm_value=-1e30),
                  reads=[k_pri, k_m8], writes=[k_pri2])
            S.add("dve", lambda e: e.max(out=m8b[:, :], in_=pri2[:, :]), reads=[k_pri2], writes=[k_m8])
            S.add("dve", lambda e: e.tensor_scalar(out=m8b[:, 7:8], in0=m8b[:, 7:8], scalar1=0.0, scalar2=None, op0=ALU.max),
                  reads=[k_m8], writes=[k_m8])
            S.add("dve", lambda e, g=g: e.tensor_scalar(out=sel_sb[:, g, :], in0=pri[:, g, :], scalar1=m8b[:, 7:8], scalar2=None, op0=ALU.is_ge),
                  reads=[k_pri, k_m8], writes=[k_sel])
        PS_X = PS_S[rot("psS", 4)]
        psx_bf = psum[PS_X][:, :].bitcast(BF16)
        for g in range(4):
            S.add("pe", lambda e, g=g: e.transpose(psx_bf[0:64, g * 128:(g + 1) * 128], sel_sb[:, g, :], identb[:, :]),
                  reads=[k_sel, k_const], writes=[k_ps[PS_X]])
        S.add("act", lambda e: e.activation(out=selT[:, :, :], in_=psx_bf[0:64, 0:512].rearrange("s (g t) -> s g t", g=4), func=AF.Copy),
              reads=[k_ps[PS_X]], writes=[k_selT])
        sel_slots = list(range(0, i + 1)) + list(range(16, 32))
        for gp in range(2):
            branch_core(gp, 1, sel_slots, "s")
        if i >= 4:
            win_slots = list(range(i - 4, i + 1))
        else:
            win_slots = list(range(28 + i, 32)) + list(range(0, i + 1))
        for gp in range(2):
            branch_core(gp, 2, win_slots, "w")
        S.add("act", lambda e: e.activation(out=obf[:, :], in_=oacc[:, :, :].rearrange("t h d -> t (h d)"), func=AF.Copy),
              reads=[k_oacc], writes=[k_obf])
        PS_X2 = PS_S[rot("psS", 4)]
        psx2_bf = psum[PS_X2][:, :].bitcast(BF16)
        for fc in range(8):
            S.add("pe", lambda e, fc=fc: e.transpose(psx2_bf[:, fc * 128:(fc + 1) * 128], obf[:, fc * 128:(fc + 1) * 128], identb[:, :]),
                  reads=[k_obf, k_const], writes=[k_ps[PS_X2]])
        S.add("dve", lambda e: e.tensor_copy(out=uT[:, :, 16 + c0:16 + c0 + 128],
                                             in_=psx2_bf[:, :].rearrange("f (c t) -> f c t", c=8)),
              reads=[k_ps[PS_X2]], writes=[k_uT])

    def sample_attention():
        P_ = NS
        PS_X = PS_S[0]
        psx_bf = psum[PS_X][:, :].bitcast(BF16)

        def run_branch(gp, br, kind):
            gs = [2 * gp, 2 * gp + 1]
            started = {g: False for g in gs}
            steps = []
            for s_ in range(NS):
                if kind == "c":
                    tiles = [("c", 0)]
                elif kind == "s":
                    tiles = [("k", kt) for kt in range(16)] + [("n", 0)]
                else:
                    tiles = [("k", t_) for t_ in range(4)] + [("n", 1)]
                for (tk, ti_) in tiles:
                    steps.append((s_, tk, ti_))

            def front(n_, s_, tk, ti_):
                cx = {"s": s_, "nk": 128, "pz": {}}
                psm = PS_M[n_ % 2]
                if tk == "c":
                    ci = rot("kc2", 2)
                    S.add("sp", lambda e, ci=ci, s_=s_: e.dma_start(out=kcTs[ci][0:64, :, :], in_=kcTS_d[s_]),
                          reads=[k_sscr], writes=[k_kcTs[ci]], dma=True)
                    S.add("sp", lambda e, ci=ci, s_=s_: e.dma_start(out=vcMs[ci][:, :, 0:64], in_=vcS_d[s_]),
                          reads=[k_sscr], writes=[k_vcMs[ci]], dma=True)
                    kt_r, cx["vt_r"] = [k_kcTs[ci]], [k_vcMs[ci]]
                    klhs = lambda g, ci=ci: kcTs[ci][:, g, :]
                    cx["vrhs"] = lambda g, ci=ci: vcMs[ci][:, g, :]
                    mcol = maskS[:, 0:1]
                    m_r = [k_const]
                elif tk == "k":
                    kb = rot("kb", 3)
                    ksrc = KsTS_d if kind == "s" else KwTS_d
                    vsrc = VsS_d if kind == "s" else VwS_d
                    S.add("sp", lambda e, kb=kb, ksrc=ksrc, s_=s_, ti_=ti_: e.dma_start(out=kbuf[kb][:, :, :], in_=ksrc[s_, ti_]),
                          reads=[k_sscr], writes=[k_kbuf[kb]], dma=True)
                    S.add("sp", lambda e, kb=kb, vsrc=vsrc, s_=s_, ti_=ti_: e.dma_start(out=vbuf[kb][:, :, :], in_=vsrc[s_, ti_]),
                          reads=[k_sscr], writes=[k_vbuf[kb]], dma=True)
                    kt_r, cx["vt_r"] = [k_kbuf[kb]], [k_vbuf[kb]]
                    klhs = lambda g, kb=kb: kbuf[kb][:, g, :]
                    cx["vrhs"] = lambda g, kb=kb: vbuf[kb][:, g, :]
                    if kind == "w":
                        mcol = maskS[:, 1 + ti_:2 + ti_]
                        m_r = [k_const]
                    else:
                        for g in gs:
                            S.add("pe", lambda e, g=g, ti_=ti_, s_=s_, psm=psm: e.matmul(
                                psum[psm][:, g:g + 1], esel_sb[:, ti_ * 128:(ti_ + 1) * 128], selTs[:, g, s_:s_ + 1],
                                start=True, stop=True), reads=[k_selTs, k_const], writes=[k_ps[psm]])
                        mcol = None
                        m_r = [k_ps[psm]]
                else:
                    cx["nk"] = NS
                    bi_ = ti_
                    kt_r, cx["vt_r"] = [k_knT], [k_vnew]
                    klhs = lambda g, bi_=bi_: knT[:, bi_, g, :]
                    cx["vrhs"] = lambda g, bi_=bi_: vnew[:, bi_, g, :]
                    mcol = ident[0:NS, s_:s_ + 1]
                    m_r = [k_const]
                nk = cx["nk"]
                for g in gs:
                    si = PS_S[rot("psS", 4)]
                    pz = rot("pz", 4)
                    cx["pz"][g] = pz
                    S.add("pe", lambda e, si=si, g=g, klhs=klhs, nk=nk, s_=s_: e.matmul(
                        psum[si][0:nk, 0:4], klhs(g), QTs[0:70, g, :, s_], start=True, stop=True),
                        reads=kt_r + [k_QTs], writes=[k_ps[si]])
                    S.add("act", lambda e, si=si, pz=pz, nk=nk, s_=s_: e.activation(
                        out=Pz[pz][0:nk, :, s_], in_=psum[si][0:nk, 0:4], func=AF.Exp),
                        reads=[k_ps[si]], writes=[k_Pz[pz]])
                    mc = mcol if mcol is not None else psum[psm][:, g:g + 1]
                    S.add("dve", lambda e, pz=pz, nk=nk, s_=s_, mc=mc: e.tensor_scalar(
                        out=Pz[pz][0:nk, :, s_], in0=Pz[pz][0:nk, :, s_], scalar1=mc[0:nk, :] if mc.shape[0] != nk else mc, scalar2=None, op0=ALU.mult),
                        reads=[k_Pz[pz]] + m_r, writes=[k_Pz[pz]])
                return cx

            def back(cx):
                s_, nk = cx["s"], cx["nk"]
                for g in gs:
                    pz = cx["pz"][g]
                    vr = cx["vrhs"](g)
                    ai = PS_ACC[g % 2]
                    ncol = 128 if kind == "c" else 65
                    for r in range(4):
                        st_flag = (not started[g])
                        started[g] = True
                        S.add("pe", lambda e, ai=ai, pz=pz, r=r, vr=vr, ncol=ncol, nk=nk, st_flag=st_flag: e.matmul(
                            psum[ai][0:NS, r * ncol:(r + 1) * ncol], Pz[pz][0:nk, r, :], vr[0:nk, :] if nk != 128 else vr,
                            start=st_flag, stop=False, skip_group_check=True),
                            reads=[k_Pz[pz]] + cx["vt_r"], writes=[k_ps[ai]])
                    S.add("dve", lambda e, pz=pz, nk=nk, s_=s_: e.memset(Pz[pz][0:nk, :, s_], 0.0),
                          writes=[k_Pz[pz]])

            pend = None
            for n_, (s_, tk, ti_) in enumerate(steps):
                cx = front(n_, s_, tk, ti_)
                if pend is not None:
                    back(pend)
                pend = cx
            back(pend)
            for g in gs:
                ai = PS_ACC[g % 2]
                if kind == "c":
                    acc3 = psum[ai][0:P_, :].rearrange("t (r c) -> t r c", r=4)
                    S.add("dve", lambda e, acc3=acc3: e.tensor_reduce(
                        out=rsum[0:P_, :], in_=acc3[:, :, 64:128], axis=AX.X, op=ALU.add), reads=[k_ps[ai]], writes=[k_rsum])
                    S.add("dve", lambda e: e.tensor_scalar(out=rsum[0:P_, :], in0=rsum[0:P_, :], scalar1=0.5, scalar2=1e-30,
                                                           op0=ALU.mult, op1=ALU.max), reads=[k_rsum], writes=[k_rsum])
                else:
                    acc3 = psum[ai][0:P_, 0:260].rearrange("t (r c) -> t r c", r=4)
                    S.add("dve", lambda e, acc3=acc3: e.tensor_scalar(
                        out=rsum[0:P_, :], in0=acc3[:, :, 64], scalar1=1e-30, scalar2=None, op0=ALU.max),
                        reads=[k_ps[ai]], writes=[k_rsum])
                S.add("dve", lambda e: e.reciprocal(out=rsum[0:P_, :], in_=rsum[0:P_, :]), reads=[k_rsum], writes=[k_rsum])
                if kind == "c":
                    for r in range(4):
                        if r == 0:
                            S.add("dve", lambda e, acc3=acc3, g=g: e.tensor_scalar(
                                out=pri[0:P_, g, :], in0=acc3[:, 0, 64:128], scalar1=rsum[0:P_, 0:1], scalar2=None, op0=ALU.mult),
                                reads=[k_ps[ai], k_rsum], writes=[k_pri])
                        else:
                            S.add("dve", lambda e, acc3=acc3, g=g, r=r: e.scalar_tensor_tensor(
                                out=pri[0:P_, g, :], in0=acc3[:, r, 64:128], scalar=rsum[0:P_, r:r + 1], in1=pri[0:P_, g, :],
                                op0=ALU.mult, op1=ALU.add), reads=[k_ps[ai], k_rsum, k_pri], writes=[k_pri])
                S.add("dve", lambda e, g=g, br=br: e.tensor_tensor(
                    out=fsc[0:P_, :], in0=rsum[0:P_, :],
                    in1=gate_s[:, :].rearrange("t (h b) -> t h b", b=3)[:, 4 * g:4 * g + 4, br], op=ALU.mult),
                    reads=[k_rsum, k_gates], writes=[k_fsc])
                if br == 0:
                    S.add("dve", lambda e, acc3=acc3, g=g: e.tensor_tensor(
                        out=oacc[0:P_, 4 * g:4 * g + 4, :], in0=acc3[:, :, 0:64],
                        in1=fsc[0:P_, :].unsqueeze(2).to_broadcast([P_, 4, 64]), op=ALU.mult),
                        reads=[k_ps[ai], k_fsc], writes=[k_oacc])
                else:
                    S.add("dve", lambda e, acc3=acc3: e.tensor_tensor(
                        out=otmp[0:P_, :, :], in0=acc3[:, :, 0:64],
                        in1=fsc[0:P_, :].unsqueeze(2).to_broadcast([P_, 4, 64]), op=ALU.mult),
                        reads=[k_ps[ai], k_fsc], writes=[k_otmp])
                    S.add("dve", lambda e, g=g: e.tensor_tensor(
                        out=oacc[0:P_, 4 * g:4 * g + 4, :], in0=oacc[0:P_, 4 * g:4 * g + 4, :], in1=otmp[0:P_, :, :], op=ALU.add),
                        reads=[k_otmp, k_oacc], writes=[k_oacc])

        for gp in range(2):
            run_branch(gp, 0, "c")
        for g in range(4):
            S.add("dve", lambda e, g=g: e.tensor_tensor(out=pri[0:P_, g, :], in0=pri[0:P_, g, :], in1=privnS[:, :], op=ALU.mult),
                  reads=[k_pri, k_const], writes=[k_pri])
            S.add("dve", lambda e, g=g: e.tensor_tensor(out=pri[0:P_, g, :], in0=pri[0:P_, g, :], in1=pribiasS[:, :], op=ALU.add),
                  reads=[k_pri, k_const], writes=[k_pri])
            S.add("dve", lambda e, g=g: e.max(out=m8a[0:P_, :], in_=pri[0:P_, g, :]), reads=[k_pri], writes=[k_m8])
            S.add("dve", lambda e, g=g: e.match_replace(out=pri2[0:P_, :], in_to_replace=m8a[0:P_, :], in_values=pri[0:P_, g, :], imm_value=-1e30),
                  reads=[k_pri, k_m8], writes=[k_pri2])
            S.add("dve", lambda e: e.max(out=m8b[0:P_, :], in_=pri2[0:P_, :]), reads=[k_pri2], writes=[k_m8])
            S.add("dve", lambda e: e.tensor_scalar(out=m8b[0:P_, 7:8], in0=m8b[0:P_, 7:8], scalar1=0.0, scalar2=None, op0=ALU.max),
                  reads=[k_m8], writes=[k_m8])
            S.add("dve", lambda e, g=g: e.tensor_scalar(out=sel_sb[0:P_, g, :], in0=pri[0:P_, g, :], scalar1=m8b[0:P_, 7:8], scalar2=None, op0=ALU.is_ge),
                  reads=[k_pri, k_m8], writes=[k_sel])
        for g in range(4):
            S.add("pe", lambda e, g=g: e.transpose(psx_bf[0:64, g * NS:(g + 1) * NS], sel_sb[0:P_, g, :], identb[0:P_, 0:P_]),
                  reads=[k_sel, k_const], writes=[k_ps[PS_X]])
        S.add("act", lambda e: e.activation(out=selTs[:, :, :], in_=psx_bf[0:64, 0:4 * NS].rearrange("s (g t) -> s g t", g=4), func=AF.Copy),
              reads=[k_ps[PS_X]], writes=[k_selTs])
        for gp in range(2):
            run_branch(gp, 1, "s")
        for gp in range(2):
            run_branch(gp, 2, "w")
        S.add("act", lambda e: e.activation(out=obf[0:P_, :], in_=oacc[0:P_, :, :].rearrange("t h d -> t (h d)"), func=AF.Copy),
              reads=[k_oacc], writes=[k_obf])
        for fc in range(8):
            S.add("pe", lambda e, fc=fc: e.transpose(psx_bf[:, fc * NS:(fc + 1) * NS], obf[0:P_, fc * 128:(fc + 1) * 128], identb[0:P_, 0:P_]),
                  reads=[k_obf, k_const], writes=[k_ps[PS_X]])
        S.add("dve", lambda e: e.tensor_copy(out=uT[:, :, NCOL:NCX], in_=psx_bf[:, 0:8 * NS].rearrange("f (c t) -> f c t", c=8)),
              reads=[k_ps[PS_X]], writes=[k_uT])

    if STAGE >= 2 and not _DEV.get("prep_only"):
        for ogi in range(2):
            has_s = (ogi == 0)
            RNG = MAINR + ([SR] if has_s else [])
            S.add("sp", lambda e, ogi=ogi: e.dma_start(
                out=xT[:, :, 16:NCX], in_=x1_d[ogi].rearrange("p (a b) -> p a b", a=8)),
                reads=[k_x1[ogi]], writes=list(k_xT), dma=True)
            if STAGE >= 3:
                norm(VGQ, MAINR)
                wq = [wload(w_qg.rearrange("(dc p) f -> p dc f", p=128)[:, :, cb * 512:(cb + 1) * 512], (8, 512)) for cb in range(2)]
                wgi, wgv = wload(w_qg.rearrange("(dc p) f -> p dc f", p=128)[:, :, 1024:1072], (8, 48))
                for h in range(16):
                    g, r = h // 4, h % 4
                    wi, wv = wq[h // 8]
                    for th in range(2):
                        pi = next_ps()
                        cc = 16 + th * 512
                        for dc in range(8):
                            S.add("pe", lambda e, pi=pi, wv=wv, dc=dc, h=h, cc=cc: e.matmul(
                                psum[pi][0:64, :], wv[:, dc, (h % 8) * 64:(h % 8 + 1) * 64], uT[:, dc, cc:cc + 512],
                                start=(dc == 0), stop=(dc == 7)), reads=[k_wb[wi], k_uT], writes=[k_ps[pi]])
                        S.add("act", lambda e, pi=pi, g=g, r=r, th=th: e.activation(
                            out=QT[0:64, g, r, th * 512:(th + 1) * 512], in_=psum[pi][0:64, :], func=AF.Copy, scale=0.125),
                            reads=[k_ps[pi]], writes=list(k_h))
                    if has_s and STAGE >= 4:
                        pi = next_ps()
                        for dc in range(8):
                            S.add("pe", lambda e, pi=pi, wv=wv, dc=dc, h=h: e.matmul(
                                psum[pi][0:64, 0:NS], wv[:, dc, (h % 8) * 64:(h % 8 + 1) * 64], uT[:, dc, NCOL:NCX],
                                start=(dc == 0), stop=(dc == 7)), reads=[k_wb[wi], k_uT], writes=[k_ps[pi]])
                        S.add("act", lambda e, pi=pi, g=g, r=r: e.activation(
                            out=QTs[0:64, g, r, :], in_=psum[pi][0:64, 0:NS], func=AF.Copy, scale=0.125),
                            reads=[k_ps[pi]], writes=[k_QTs])
                S.add("sp", lambda e, ogi=ogi: e.dma_start(
                    out=QT[64:70, :, :, :], in_=qaug_d[:, :, :, ogi * GT:(ogi + 1) * GT]), writes=list(k_h), dma=True)
                for tl in range(8):
                    pi = next_ps()
                    cc = 16 + tl * 128
                    for dc in range(8):
                        S.add("pe", lambda e, pi=pi, dc=dc, cc=cc: e.matmul(
                            psum[pi][:, 0:48], uT[:, dc, cc:cc + 128], wgv[:, dc, :], start=(dc == 0), stop=(dc == 7)),
                            reads=[k_wb[wgi], k_uT], writes=[k_ps[pi]])
                    S.add("dve", lambda e, pi=pi, tl=tl: e.tensor_tensor(
                        out=gate_sb[:, tl, :], in0=psum[pi][:, 0:48], in1=bg_sb[:, :], op=ALU.add),
                        reads=[k_ps[pi], k_const], writes=[k_gate])
                S.add("act", lambda e: e.activation(out=gate_sb[:, :, :], in_=gate_sb[:, :, :], func=AF.Sigmoid),
                      reads=[k_gate], writes=[k_gate])
                if has_s and STAGE >= 4:
                    pi = next_ps()
                    for dc in range(8):
                        S.add("pe", lambda e, pi=pi, dc=dc: e.matmul(
                            psum[pi][0:NS, 0:48], uT[:, dc, NCOL:NCX], wgv[:, dc, :], start=(dc == 0), stop=(dc == 7)),
                            reads=[k_wb[wgi], k_uT], writes=[k_ps[pi]])
                    S.add("dve", lambda e, pi=pi: e.tensor_tensor(
                        out=gate_s[:, :], in0=psum[pi][0:NS, 0:48], in1=bg_sb[0:NS, :], op=ALU.add),
                        reads=[k_ps[pi], k_const], writes=[k_gates])
                    S.add("act", lambda e: e.activation(out=gate_s[:, :], in_=gate_s[:, :], func=AF.Sigmoid),
                          reads=[k_gates], writes=[k_gates])
                for tl in range(8):
                    attention_tile(ogi * 8 + tl, ogi)
                if has_s and STAGE >= 4 and not _DEV.get("skip_sa"):
                    sample_attention()
                wo = [wload(w_o.rearrange("(fc p) f -> p fc f", p=128)[:, :, cb * 512:(cb + 1) * 512], (8, 512)) for cb in range(2)]
                for dmc in range(8):
                    wi, wv = wo[dmc // 4]
                    for th in range(2):
                        pi = next_ps()
                        cc = 16 + th * 512
                        for fc in range(8):
                            S.add("pe", lambda e, pi=pi, wv=wv, fc=fc, dmc=dmc, cc=cc: e.matmul(
                                psum[pi][:, :], wv[:, fc, (dmc % 4) * 128:(dmc % 4 + 1) * 128], uT[:, fc, cc:cc + 512],
                                start=(fc == 0), stop=(fc == 7)), reads=[k_wb[wi], k_uT], writes=[k_ps[pi]])
                        S.add("dve", lambda e, pi=pi, dmc=dmc, cc=cc: e.tensor_tensor(
                            out=xT[:, dmc, cc:cc + 512], in0=xT[:, dmc, cc:cc + 512], in1=psum[pi][:, :], op=ALU.add),
                            reads=[k_ps[pi], k_xT[dmc]], writes=[k_xT[dmc]])
                    if has_s and STAGE >= 4:
                        pi = next_ps()
                        for fc in range(8):
                            S.add("pe", lambda e, pi=pi, wv=wv, fc=fc, dmc=dmc: e.matmul(
                                psum[pi][:, 0:NS], wv[:, fc, (dmc % 4) * 128:(dmc % 4 + 1) * 128], uT[:, fc, NCOL:NCX],
                                start=(fc == 0), stop=(fc == 7)), reads=[k_wb[wi], k_uT], writes=[k_ps[pi]])
                        S.add("dve", lambda e, pi=pi, dmc=dmc: e.tensor_tensor(
                            out=xT[:, dmc, NCOL:NCX], in0=xT[:, dmc, NCOL:NCX], in1=psum[pi][:, 0:NS], op=ALU.add),
                            reads=[k_ps[pi], k_xT[dmc]], writes=[k_xT[dmc]])
            norm(VG1B, RNG)
            mlp(1, has_s)
            norm(VGF, RNG, final=True)
            if has_s:
                rows16_out(xT[:, :, NCOL:NCX], list(k_xT), y_s_out[:, :])
            for tl in range(8):
                yi = rot("yst", 2)
                cc = 16 + tl * 128
                for hb in range(2):
                    pi = next_ps()
                    for j in range(4):
                        dc = hb * 4 + j
                        S.add("pe", lambda e, pi=pi, j=j, dc=dc, cc=cc: e.transpose(
                            psum[pi][:, j * 128:(j + 1) * 128], xT[:, dc, cc:cc + 128], ident[:, :]),
                            reads=[k_xT[dc], k_const], writes=[k_ps[pi]])
                    if hb == 0:
                        S.add("act", lambda e, pi=pi, yi=yi: e.activation(out=yst[yi][:, 0:512], in_=psum[pi][:, :], func=AF.Copy),
                              reads=[k_ps[pi]], writes=[k_yst[yi]])
                    else:
                        S.add("dve", lambda e, pi=pi, yi=yi: e.tensor_copy(out=yst[yi][:, 512:1024], in_=psum[pi][:, :]),
                              reads=[k_ps[pi]], writes=[k_yst[yi]])
                row0 = ogi * GT + tl * 128
                S.add("sp", lambda e, yi=yi, row0=row0: e.dma_start(out=y_out[row0:row0 + 128, :], in_=yst[yi][:, :]),
                      reads=[k_yst[yi]], writes=[k_out], dma=True)

    S.prepare(nc)
    with nc.Block() as block:
        S.emit(block)
    S._st.close()
    es.close()
    return nc


def _bf(x):
    return np.asarray(x, np.float32).astype(NPBF)


def _hilo(x):
    x = np.asarray(x, np.float32)
    hi = x.astype(NPBF).astype(np.float32)
    lo = (x - hi).astype(NPBF).astype(np.float32)
    return hi, lo


def core_meta(half):
    seqtile = np.concatenate([16 * half + np.arange(16), 16 * (1 - half) + np.arange(16)])
    posk = (seqtile[None, :] * 128 + np.arange(128)[:, None]).astype(np.float32)
    other_visible = (half == 1)
    eff = posk.copy()
    if not other_visible:
        eff[:, 16:] = NEGPOS
    hi, lo = _hilo(eff)
    kaug = np.zeros((32, 6, 128), np.float32)
    kaug[:, 0] = hi.T; kaug[:, 1] = lo.T; kaug[:, 2] = hi.T; kaug[:, 3] = lo.T; kaug[:, 4] = 1.0; kaug[:, 5] = 1.0
    posq = (half * 2048 + np.arange(2048)).astype(np.float32)
    slopes = np.exp2(-8.0 * (np.arange(16, dtype=np.float32) + 1.0) / 16).astype(np.float32)
    shi, slo = _hilo(slopes)
    tref = (half * 2048 + (np.arange(2048) // 128) * 128 + 64).astype(np.float32)
    qaug = np.zeros((6, 4, 4, 2048), np.float32)
    for g in range(4):
        for r in range(4):
            h = g * 4 + r
            c = (-slopes[h] * tref).astype(np.float32)
            chi, clo = _hilo(c)
            qaug[0, g, r] = shi[h]; qaug[1, g, r] = shi[h]; qaug[2, g, r] = slo[h]; qaug[3, g, r] = slo[h]
            qaug[4, g, r] = chi; qaug[5, g, r] = clo
    seqsub = (seqtile[:, None] * 8 + np.arange(8)[None, :]).reshape(-1)
    nxt = np.roll(seqsub, -1)
    valid = (nxt == seqsub + 1)
    cend_true = np.where(valid, seqsub * 16 + 31, 10 ** 9).astype(np.float64)
    cend = cend_true.astype(np.float32).reshape(2, 128).T.copy()
    caug = np.zeros((16, 6, 256), np.float32)
    for i in range(16):
        tr = half * 2048 + i * 128 + 64
        e = np.where(valid, np.minimum(cend_true, tr + 63), NEGPOS).astype(np.float32)
        ehi, elo = _hilo(e)
        caug[i, 0] = ehi; caug[i, 1] = elo; caug[i, 2] = ehi; caug[i, 3] = elo; caug[i, 4] = 1.0; caug[i, 5] = 1.0
    seqblk = (seqtile[:, None] * 2 + np.arange(2)[None, :]).reshape(-1)
    blk2slot = np.zeros(64, np.int64)
    blk2slot[seqblk] = np.arange(64)
    mc2s = np.zeros((256, 64), np.float32)
    for n in range(256):
        if valid[n]:
            nb = seqsub[n]
            for k in range(2):
                sb_ = ((nb + k) * 16) // 64
                mc2s[n, blk2slot[sb_]] += 1.0
    mc2s = mc2s.reshape(2, 128, 64).transpose(1, 0, 2).copy()
    tq = posq.astype(np.int64)
    cur = tq // 64
    bs = seqblk[None, :]
    validb = (bs * 64 <= tq[:, None])
    forced = (bs == 0) | (bs == cur[:, None]) | (bs == cur[:, None] - 1)
    privn = (validb & ~forced).astype(np.float32)
    pribias = np.where(validb, np.where(forced, 1e6, 0.0), -1.0).astype(np.float32)
    esel = (np.arange(4096)[None, :] // 64 == np.arange(64)[:, None]).astype(np.float32)
    tri = (np.arange(128)[:, None] <= np.arange(128)[None, :]).astype(np.float32)
    return {
        "kaug": _bf(kaug), "qaug": _bf(qaug), "caug": _bf(caug), "posq": posq, "posk": posk, "cend": cend,
        "privn": privn, "pribias": pribias, "mc2s": _bf(mc2s), "esel": _bf(esel), "tri": _bf(tri),
        "identb": _bf(np.eye(128)), "ident": np.eye(128, dtype=np.float32),
    }


def sample_meta():
    slopes = np.exp2(-8.0 * (np.arange(16, dtype=np.float32) + 1.0) / 16).astype(np.float32)
    shi, slo = _hilo(slopes)
    pos = np.zeros((21, 128), np.float32)
    for kt in range(16):
        pos[kt] = kt * 128 + np.arange(128)
    for t in range(4):
        pos[16 + t] = 1536 + t * 128 + np.arange(128)
    pos[20] = 2048.0
    hi, lo = _hilo(pos)
    kaugS = np.zeros((21, 6, 128), np.float32)
    kaugS[:, 0] = hi; kaugS[:, 1] = lo; kaugS[:, 2] = hi; kaugS[:, 3] = lo; kaugS[:, 4] = 1.0; kaugS[:, 5] = 1.0
    qaugS = np.zeros((6, 4, 4, 16), np.float32)
    for g in range(4):
        for r in range(4):
            h = g * 4 + r
            chi, clo = _hilo(np.float32(-slopes[h] * 2048.0))
            qaugS[0, g, r] = shi[h]; qaugS[1, g, r] = shi[h]; qaugS[2, g, r] = slo[h]; qaugS[3, g, r] = slo[h]
            qaugS[4, g, r] = chi; qaugS[5, g, r] = clo
    cend = np.where(np.arange(128) < 127, np.arange(128) * 16 + 31, NEGPOS).astype(np.float32)
    ehi, elo = _hilo(cend)
    caugS = np.zeros((6, 128), np.float32)
    caugS[0] = ehi; caugS[1] = elo; caugS[2] = ehi; caugS[3] = elo; caugS[4] = 1.0; caugS[5] = 1.0
    maskS = np.zeros((128, 8), np.float32)
    maskS[:127, 0] = 1.0
    for t in range(4):
        d = 2048 - (1536 + t * 128 + np.arange(128))
        maskS[:, 1 + t] = ((d >= 0) & (d < 512)).astype(np.float32)
    blk = np.arange(64)
    valid = blk <= 32
    forced = (blk == 0) | (blk == 31) | (blk == 32)
    privn = np.tile((valid & ~forced).astype(np.float32)[None], (16, 1))
    pribias = np.tile(np.where(valid, np.where(forced, 1e6, 0.0), -1.0).astype(np.float32)[None], (16, 1))
    mc2s = np.zeros((128, 64), np.float32)
    for n in range(127):
        for k in range(2):
            mc2s[n, ((n + k) * 16) // 64] += 1.0
    return {"kaugS": _bf(kaugS), "qaugS": _bf(qaugS), "caugS": _bf(caugS), "maskS": maskS, "privnS": privn,
            "pribiasS": pribias, "mc2sS": _bf(mc2s), "iotap": np.arange(128, dtype=np.float32).reshape(128, 1)}


_NC_CACHE = {}


def kernel(**inp):
    f32 = lambda a: np.ascontiguousarray(np.asarray(a, dtype=np.float32))
    x_prompt = f32(inp["x_prompt"])
    vec_list = [inp["norm_mix"][0], inp["norm_mlp"][0], inp["norm_kv"], inp["pool_scale"][0],
                inp["norm_mix"][1], inp["norm_mlp"][1], inp["norm_final"]]
    vecs = np.ascontiguousarray(np.concatenate([f32(v).reshape(8, 128).T for v in vec_list], axis=1))
    shared = {
        "vecs": vecs, "w_up": f32(inp["w_up"]), "w_down": f32(inp["w_down"]), "pool_w": f32(inp["pool_w"])[0],
        "w_kv": f32(inp["w_kv"]), "w_qg": f32(inp["w_qg"])[0], "w_o": f32(inp["w_o"])[0], "b_gate": f32(inp["b_gate"]),
        "cmp_w1": f32(inp["cmp_w1"]), "cmp_w2": f32(inp["cmp_w2"]), "cmp_pe": f32(inp["cmp_pe"]),
    }
    metas = [core_meta(0), core_meta(1)]
    x_sample = f32(inp["x_sample"]).reshape(128, D)
    state_pool = f32(inp["state_pool"]).reshape(128, 15, D)
    state_win = f32(inp["state_win"]).reshape(128, 512, 512)
    selw = np.zeros((256, 4, 16), np.float32)
    for s_ in range(16):
        for r_ in range(15):
            for g_ in range(4):
                if r_ >= 16 - (2 << g_):
                    selw[s_ * 15 + r_, g_, s_] = 1.0
    selw = np.ascontiguousarray(selw.reshape(2, 128, 64).transpose(1, 0, 2))
    smeta = sample_meta()
    cache2d = f32(inp["cache_kv_pages"]).reshape(2560 * 128, 1024)
    page_table = np.ascontiguousarray(np.asarray(inp["page_table"], dtype=np.int32))
    in_maps = []
    for c in range(NCORES):
        b, half = c // 2, c % 2
        xg = np.zeros((4, NCOL, D), np.float32)
        corr = np.ones((4, 4, 16), np.float32)
        for gq, (slot0, is_own, ogi) in enumerate(L0_GROUPS):
            hf = half if is_own else 1 - half
            st = hf * 2048 + (slot0 % 16) * 128
            if st > 0:
                xg[gq, 1:16] = x_prompt[b, st - 15:st]
            xg[gq, 16:] = x_prompt[b, st:st + GT]
            for g in range(4):
                w = 2 << g
                t = st + np.arange(16)
                corr[gq, g] = w / np.minimum(t + 1, w)
        m = dict(shared)
        m.update(metas[half])
        m["xs"] = np.ascontiguousarray(x_sample[16 * c:16 * c + 16])
        m["spool"] = np.ascontiguousarray(state_pool[16 * c:16 * c + 16].reshape(240, D))
        m["swin"] = np.ascontiguousarray(state_win[16 * c:16 * c + 16])
        m["selw"] = selw
        m.update(smeta)
        m["cache"] = cache2d
        m["pt"] = np.ascontiguousarray(page_table[16 * c:16 * c + 16])
        m["xg"] = xg
        m["corr"] = corr.reshape(4, 64)
        in_maps.append(m)
    if "nc" not in _NC_CACHE:
        _NC_CACHE["nc"] = build_nc()
    nc = _NC_CACHE["nc"]
    res = run_bass_kernel_spmd(nc, in_maps, core_ids=list(range(NCORES)))
    R = res.results
    y_prompt = np.zeros((NB, SEQ, D), np.float32)
    y_sample = np.zeros((128, 1, D), np.float32)
    pool_prompt = np.zeros((NB, 1, 15, D), np.float32)
    pool_sample = np.zeros((128, 1, 15, D), np.float32)
    kv_rows_prompt = np.zeros((NB, SEQ, 2, 2, 4, 64), np.float32)
    kv_rows_sample = np.zeros((128, 1, 2, 2, 4, 64), np.float32)
    win_prompt = np.zeros((NB, 512, 2, 4, 64), np.float32)
    win_sample = np.zeros((128, 512, 2, 4, 64), np.float32)
    for c in range(NCORES):
        b, half = c // 2, c % 2
        kv_rows_prompt[b, half * 2048:(half + 1) * 2048] = R[c]["kv_out"].reshape(2048, 2, 2, 4, 64)
        y_sample[16 * c:16 * c + 16, 0] = R[c]["y_s_out"]
        pool_sample[16 * c:16 * c + 16, 0] = R[c]["pool_s_out"]
        kv_rows_sample[16 * c:16 * c + 16, 0] = R[c]["kv_s_out"].reshape(16, 2, 2, 4, 64)
        win_sample[16 * c:16 * c + 16] = R[c]["win_s_out"].reshape(16, 512, 2, 4, 64)
        y_prompt[b, half * 2048:(half + 1) * 2048] = R[c]["y_out"]
        if half == 1:
            win_prompt[b] = R[c]["win_out"].reshape(512, 2, 4, 64)
            pool_prompt[b, 0] = R[c]["pool_out"][1:16]
    return (y_prompt, y_sample, pool_prompt, pool_sample, kv_rows_prompt, kv_rows_sample, win_prompt, win_sample)
```

```python
import contextlib
import numpy as np
import ml_dtypes
import concourse.bass as bass
import concourse.mybir as mybir
from concourse.bass_utils import run_bass_kernel_spmd

F32 = mybir.dt.float32
BF16 = mybir.dt.bfloat16
AF = mybir.ActivationFunctionType
ALU = mybir.AluOpType
AX = mybir.AxisListType
NPBF = ml_dtypes.bfloat16

NCORES = 8
D = 1024
DFF = 4096
SEQ = 4096
NB = 4
GT = 1024
HALO = 16
NCOL = HALO + GT
NS = 16
NCX = NCOL + NS
EPS = 1e-6
KVW = 1536
NWB = 3
SIG_EPOCH = 30000
NEGPOS = -8192.0
NPAGES = 2560
_DEV = {}
STAGE = 4


class Tk:
    __slots__ = ("w", "r", "name")

    def __init__(self, name=""):
        self.w = None
        self.r = []
        self.name = name


class Op:
    __slots__ = ("eng", "fn", "deps", "dma", "sig", "signum", "slot", "slotval", "idx")

    def __init__(self, eng, fn, dma):
        self.eng = eng
        self.fn = fn
        self.deps = []
        self.dma = dma
        self.sig = False
        self.signum = None
        self.slot = None
        self.slotval = None


class Sched:
    ENGS = ("pe", "act", "dve", "pool", "sp")

    def __init__(self, nslot=8):
        self.q = {e: [] for e in self.ENGS}
        self.nslot = nslot
        self.pending = {}

    def barrier(self):
        fr = []
        for e in self.ENGS:
            comp = [o for o in self.q[e] if not o.dma]
            if comp:
                fr.append(comp[-1])
            dm = [o for o in self.q[e] if o.dma]
            fr.extend(dm[-self.nslot:])
        self.pending = {e: list(fr) for e in self.ENGS}

    def add(self, eng, fn, reads=(), writes=(), dma=False, extra=()):
        op = Op(eng, fn, dma)
        deps = []
        seen = set()
        if self.pending.get(eng):
            extra = list(extra) + self.pending.pop(eng)
        for d in extra:
            if id(d) not in seen:
                seen.add(id(d)); deps.append(d)
        for t in reads:
            if t.w is not None and id(t.w) not in seen:
                seen.add(id(t.w)); deps.append(t.w)
        for t in writes:
            for r in t.r:
                if id(r) not in seen:
                    seen.add(id(r)); deps.append(r)
            if t.w is not None and id(t.w) not in seen:
                seen.add(id(t.w)); deps.append(t.w)
        op.deps = [d for d in deps if d is not op]
        for t in reads:
            t.r.append(op)
        for t in writes:
            t.w = op
            t.r = []
        op.idx = len(self.q[eng])
        self.q[eng].append(op)
        return op

    def prepare(self, nc):
        nslot = self.nslot
        for e in self.ENGS:
            for op in self.q[e]:
                for d in op.deps:
                    if d.dma:
                        continue
                    if d.eng == "pe" and op.eng == "pe" and not op.dma:
                        continue
                    d.sig = True
        nsig = {}
        for e in self.ENGS:
            k = 0
            i = 0
            for op in self.q[e]:
                if op.dma:
                    op.slot = i % nslot
                    op.slotval = 16 * (i // nslot + 1)
                    i += 1
                elif op.sig:
                    k += 1
                    op.signum = k
            nsig[e] = k
        st = contextlib.ExitStack()
        csem = {}
        for e in self.ENGS:
            nep = nsig[e] // SIG_EPOCH + 1
            csem[e] = [st.enter_context(nc.semaphore("c_%s_%d" % (e, j))) for j in range(nep)]
        dsem = {}
        for e in ("sp", "pool"):
            dsem[e] = [st.enter_context(nc.semaphore("d_%s_%d" % (e, j))) for j in range(nslot)]
        self._st = st
        self.csem = csem
        self.dsem = dsem

    def emit(self, block):
        csem, dsem = self.csem, self.dsem

        def run(e, eng):
            waited = {}

            def wait(sem, key, val):
                if waited.get(key, -1) >= val:
                    return
                waited[key] = val
                eng.wait_ge(sem, val)

            for op in self.q[e]:
                need = {}
                for d in op.deps:
                    if d.dma:
                        key, sem, val = ("d", d.eng, d.slot), dsem[d.eng][d.slot], d.slotval
                    else:
                        if d.eng == "pe" and e == "pe" and not op.dma:
                            continue
                        ep = (d.signum - 1) // SIG_EPOCH
                        key, sem, val = ("c", d.eng, ep), csem[d.eng][ep], d.signum - ep * SIG_EPOCH
                    if key not in need or need[key][1] < val:
                        need[key] = (sem, val)
                for key, (sem, val) in need.items():
                    wait(sem, key, val)
                if op.dma:
                    if op.slotval > 16:
                        wait(dsem[e][op.slot], ("d", e, op.slot), op.slotval - 16)
                    ins = op.fn(eng)
                    ins.then_inc(dsem[e][op.slot], 16)
                else:
                    ins = op.fn(eng)
                    if op.sig:
                        ep = (op.signum - 1) // SIG_EPOCH
                        ins.then_inc(csem[e][ep], 1)
            if e in dsem:
                last = {}
                for op in self.q[e]:
                    if op.dma:
                        last[op.slot] = op.slotval
                for s, v in last.items():
                    wait(dsem[e][s], ("d", e, s), v)

        @block.tensor
        def _(eng):
            run("pe", eng)

        @block.scalar
        def _(eng):
            run("act", eng)

        @block.vector
        def _(eng):
            run("dve", eng)

        @block.gpsimd
        def _(eng):
            run("pool", eng)

        @block.sync
        def _(eng):
            run("sp", eng)


L0_GROUPS = [(16, False, None), (24, False, None), (0, True, 0), (8, True, 1)]


def build_nc():
    nc = bass.Bass("TRN2", target_bir_lowering=False)

    def din(name, shape, dt=F32):
        return nc.dram_tensor(name, list(shape), dt, kind="ExternalInput").ap()

    def dout(name, shape, dt=F32):
        return nc.dram_tensor(name, list(shape), dt, kind="ExternalOutput").ap()

    def dscr(name, shape, dt):
        return nc.dram_tensor(name, list(shape), dt).ap()

    xg = din("xg", [4, NCOL, D])
    corr = din("corr", [4, 64])
    vecs = din("vecs", [128, 56])
    ident_d = din("ident", [128, 128])
    w_up = din("w_up", [2, D, DFF])
    w_down = din("w_down", [2, DFF, D])
    pool_w = din("pool_w", [4, 256, 256])
    w_kv = din("w_kv", [D, KVW])
    w_qg = din("w_qg", [D, 1072])
    w_o = din("w_o", [D, D])
    b_gate = din("b_gate", [1, 48])
    cmp_w1 = din("cmp_w1", [2, 2048, 128])
    cmp_w2 = din("cmp_w2", [2, 128, 64])
    cmp_pe = din("cmp_pe", [2, 32, 64])
    kaug_d = din("kaug", [32, 6, 128], BF16)
    qaug_d = din("qaug", [6, 4, 4, 2048], BF16)
    caug_d = din("caug", [16, 6, 256], BF16)
    posq_d = din("posq", [2048])
    posk_d = din("posk", [128, 32])
    cend_d = din("cend", [128, 2])
    privn_d = din("privn", [2048, 64])
    pribias_d = din("pribias", [2048, 64])
    mc2s_d = din("mc2s", [128, 2, 64], BF16)
    esel_d = din("esel", [64, 4096], BF16)
    tri_d = din("tri", [128, 128], BF16)
    identb_d = din("identb", [128, 128], BF16)

    cache_d = din("cache", [NPAGES * 128, 1024])
    pt_d = din("pt", [NS, 16], mybir.dt.int32)
    iotap_d = din("iotap", [128, 1])
    kaugS_d = din("kaugS", [21, 6, 128], BF16)
    qaugS_d = din("qaugS", [6, 4, 4, NS], BF16)
    caugS_d = din("caugS", [6, 128], BF16)
    maskS_d = din("maskS", [128, 8])
    privnS_d = din("privnS", [NS, 64])
    pribiasS_d = din("pribiasS", [NS, 64])
    mc2sS_d = din("mc2sS", [128, 64], BF16)
    xs_d = din("xs", [NS, D])
    spool_d = din("spool", [NS * 15, D])
    swin_d = din("swin", [NS, 512, 512])
    selw_d = din("selw", [128, 2, 64])
    pool_s_out = dout("pool_s_out", [NS, 15, D])
    kv_s_out = dout("kv_s_out", [NS, 1024])
    win_s_out = dout("win_s_out", [NS, 512, 512])
    y_s_out = dout("y_s_out", [NS, D])
    kv_out = dout("kv_out", [2048, 1024])
    win_out = dout("win_out", [512, 512])
    pool_out = dout("pool_out", [16, D])
    y_out = dout("y_out", [2048, D])

    x1_d = dscr("x1_d", [2, 128, 8 * (GT + NS)], F32)
    KsT_d = dscr("KsT_d", [32, 70, 4, 128], BF16)
    KwT_d = dscr("KwT_d", [32, 70, 4, 128], BF16)
    Vs_d = dscr("Vs_d", [32, 128, 4, 65], BF16)
    Vw_d = dscr("Vw_d", [32, 128, 4, 65], BF16)
    KsTS_d = dscr("KsTS_d", [NS, 16, 70, 4, 128], BF16)
    VsS_d = dscr("VsS_d", [NS, 16, 128, 4, 65], BF16)
    KwTS_d = dscr("KwTS_d", [NS, 4, 70, 4, 128], BF16)
    VwS_d = dscr("VwS_d", [NS, 4, 128, 4, 65], BF16)
    kcTS_d = dscr("kcTS_d", [NS, 64, 4, 128], BF16)
    vcS_d = dscr("vcS_d", [NS, 128, 4, 64], BF16)

    es = contextlib.ExitStack()

    def sb(name, shape, dt):
        return es.enter_context(nc.sbuf_tensor(name, list(shape), dt))

    xT = sb("xT", [128, 8, NCX], F32)
    uT = sb("uT", [128, 8, NCX], BF16)
    hT = sb("hT", [128, 16, GT], BF16)
    wb = [sb("wb%d" % i, [128, 8 * 512], BF16) for i in range(NWB)]
    xin = [sb("xin%d" % i, [128, D], F32) for i in range(2)]
    tA = sb("tA", [128, NCX], F32)
    rstd = sb("rstd", [128, NCX], F32)
    kvst = [sb("kvst%d" % i, [128, KVW], F32) for i in range(2)]
    vec_sb = sb("vec_sb", [128, 56], F32)
    corr_sb = sb("corr_sb", [128, 4 * 64], F32)
    pw_sb = sb("pw_sb", [128, 4 * 2 * 256], BF16)
    ones_bf = sb("ones_bf", [128, 128], BF16)
    utail = sb("utail", [128, 8, 16], F32)
    selw = sb("selw_sb", [128, 2, 64], F32)
    diffS = sb("diffS", [128, 8, NS], BF16)
    hTs = sb("hTs", [128, 16, NS], BF16)
    rs_t = sb("rs_t", [128, NS], F32)
    QTs = sb("QTs", [70, 4, 4, NS], BF16)
    gate_s = sb("gate_s", [NS, 48], F32)
    knT = sb("knT", [70, 2, 4, NS], BF16)
    vnew = sb("vnew", [NS, 2, 4, 65], BF16)
    knew_bf = sb("knew_bf", [NS, 512], BF16)
    kcTs = [sb("kcTs%d" % i, [70, 4, 128], BF16) for i in range(2)]
    vcMs = [sb("vcMs%d" % i, [128, 4, 128], BF16) for i in range(2)]
    Pz = [sb("Pz%d" % i, [128, 4, NS], BF16) for i in range(4)]
    maskS = sb("maskS_sb", [128, 8], F32)
    privnS = sb("privnS_sb", [NS, 64], F32)
    pribiasS = sb("pribiasS_sb", [NS, 64], F32)
    selTs = sb("selTs", [64, 4, NS], BF16)
    epsb = sb("epsb", [128, 1], F32)
    ident = sb("ident_sb", [128, 128], F32)
    identb = sb("identb_sb", [128, 128], BF16)
    vst = [sb("vst%d" % i, [128, 2, 4, 65], BF16) for i in range(2)]
    ktst = [sb("ktst%d" % i, [64, GT], BF16) for i in range(2)]
    w1_sb = sb("w1_sb", [64, 2, 32, 128], BF16)
    w2_sb = sb("w2_sb", [128, 2, 64], BF16)
    peT = sb("peT", [64, 2, 32], BF16)
    pe_nat = sb("pe_nat", [64, 64], F32)
    ABs = sb("ABs", [128, 2, 4, 2, 4], F32)
    hs_all = sb("hs_all", [128, 2, 4, 256], BF16)
    hpreb = sb("hpreb", [128, 4], F32)
    cvec = sb("cvec", [128, 2], F32)
    kcT = sb("kcT", [70, 4, 256], BF16)
    vcM = sb("vcM", [128, 2, 4, 128], BF16)
    kbuf = [sb("kbuf%d" % i, [70, 4, 128], BF16) for i in range(3)]
    vbuf = [sb("vbuf%d" % i, [128, 4, 65], BF16) for i in range(3)]
    pT = [sb("pT%d" % i, [128, 4, 128], BF16) for i in range(4)]
    posq_bc = [sb("posq_bc%d" % i, [128, 128], F32) for i in range(2)]
    posk_sb = sb("posk_sb", [128, 32], F32)
    cend_sb = sb("cend_sb", [128, 2], F32)
    mtmp = [sb("mtmp%d" % i, [128, 128], F32) for i in range(2)]
    mask_sb = [sb("mask%d" % i, [128, 128], BF16) for i in range(3)]
    tri_sb = sb("tri_sb", [128, 128], BF16)
    esel_sb = sb("esel_sb", [64, 4096], BF16)
    privn_sb = [sb("privn%d" % i, [128, 64], F32) for i in range(2)]
    pribias_sb = [sb("pribias%d" % i, [128, 64], F32) for i in range(2)]
    pri = sb("pri", [128, 4, 64], F32)
    pri2 = sb("pri2", [128, 64], F32)
    m8a = sb("m8a", [128, 8], F32)
    m8b = sb("m8b", [128, 8], F32)
    sel_sb = sb("sel_sb", [128, 4, 64], BF16)
    selT = sb("selT", [64, 4, 128], BF16)
    gate_sb = sb("gate_sb", [128, 8, 48], F32)
    bg_sb = sb("bg_sb", [128, 48], F32)
    rsum = sb("rsum", [128, 4], F32)
    fsc = sb("fsc", [128, 4], F32)
    oacc = pw_sb[:, :].bitcast(F32).rearrange("p (h d) -> p h d", h=16)
    hs_flat = hs_all[:, :, :, :].rearrange("p a b c -> p (a b c)")
    obf = hs_flat[:, 0:1024]
    otmp = hs_flat[:, 1024:1536].bitcast(F32).rearrange("p (r d) -> p r d", r=4)
    yst = [kvst[i][:, 0:D] for i in range(2)]
    ptail = xin[0][0:16, :]
    tB = rstd
    xct = ktst
    psum = [es.enter_context(nc.psum_tensor("ps%d" % i, [128, 512], F32)) for i in range(8)]

    S = Sched(nslot=16)
    k_xT = [Tk("xT%d" % i) for i in range(8)]
    k_uT = Tk("uT")
    k_h = [Tk("h%d" % i) for i in range(16)]
    k_wb = [Tk() for _ in range(NWB)]
    k_xin = [Tk(), Tk()]
    k_tA, k_rstd = Tk(), Tk()
    k_tA1 = Tk()
    k_tB = k_rstd
    k_kvst = [Tk(), Tk()]
    k_ps = [Tk() for _ in range(8)]
    k_const = Tk("const")
    k_utail = Tk()
    k_diffS, k_hTs, k_rs = Tk(), Tk(), Tk()
    k_QTs, k_gates, k_knT, k_vnew, k_knew = Tk(), Tk(), Tk(), Tk(), Tk()
    k_kcTs = [Tk(), Tk()]
    k_vcMs = [Tk(), Tk()]
    k_Pz = [Tk() for _ in range(4)]
    k_selTs = Tk()
    k_G = [Tk() for _ in range(24)]
    k_X = [[Tk() for _ in range(4)] for _ in range(2)]
    k_idx = Tk()
    k_sscr = Tk("sample scratch")
    k_ptail = k_xin[0]
    k_out = Tk("out")
    k_vst = [Tk(), Tk()]
    k_ktst = [Tk(), Tk()]
    k_xct = k_ktst
    k_AB = Tk()
    k_scr = Tk("scratch")
    k_x1 = [Tk(), Tk()]
    k_cmp = Tk()
    k_cmp2 = Tk()
    k_kc = Tk()
    k_vc = Tk()
    k_kbuf = [Tk() for _ in range(3)]
    k_vbuf = [Tk() for _ in range(3)]
    k_pT = [Tk() for _ in range(4)]
    k_posq = [Tk(), Tk()]
    k_mtmp = [Tk(), Tk()]
    k_mask = [Tk() for _ in range(3)]
    k_privn = [Tk(), Tk()]
    k_pri, k_pri2, k_m8, k_sel, k_selT = Tk(), Tk(), Tk(), Tk(), Tk()
    k_gate, k_rsum, k_fsc, k_oacc, k_otmp, k_obf = Tk(), Tk(), Tk(), Tk(), Tk(), Tk()
    k_yst = k_kvst

    state = {"ps": 0, "wb": 0, "xin": 0, "kvst": 0, "vst": 0, "ktst": 0,
             "kb": 0, "pT": 0, "mask": 0, "pq": 0, "yst": 0, "psS": 0,
             "G": 0, "pz": 0, "kc2": 0}

    def rot(key, n):
        i = state[key]
        state[key] = (i + 1) % n
        return i

    def next_ps():
        return rot("ps", 8)

    pw4 = pw_sb[:, :].rearrange("p (g k c) -> p g k c", g=4, k=2)
    VG0, VG1, VGKV, VPS, VGQ, VG1B, VGF = 0, 1, 2, 3, 4, 5, 6

    def vcol(v, dc):
        return vec_sb[:, v * 8 + dc: v * 8 + dc + 1]

    def cload(eng, out_ap, in_ap):
        S.add(eng, lambda e: e.dma_start(out=out_ap, in_=in_ap), writes=[k_const], dma=True)

    cload("sp", vec_sb[:, :], vecs[:, :])
    cload("sp", corr_sb[:, :], corr.rearrange("g c -> (g c)").partition_broadcast(128))
    cload("pool", pw4, pool_w.rearrange("g (k p) c -> p g k c", p=128))
    cload("sp", ident[:, :], ident_d[:, :])
    cload("sp", identb[:, :], identb_d[:, :])
    cload("sp", selw[:, :, :], selw_d[:, :, :])
    S.add("sp", lambda e: e.dma_start(out=pool_s_out[:, 0:14, :], in_=spool_d.rearrange("(s r) d -> s r d", r=15)[:, 1:15, :]),
          writes=[k_out], dma=True)
    for s_ in range(NS):
        S.add("sp", lambda e, s_=s_: e.dma_start(out=win_s_out[s_, 0:511, :], in_=swin_d[s_, 1:512, :]), writes=[k_out], dma=True)
    if STAGE >= 3:
        cload("sp", tri_sb[:, :], tri_d[:, :])
        cload("sp", esel_sb[:, :], esel_d[:, :])
        cload("sp", posk_sb[:, :], posk_d[:, :])
        cload("sp", cend_sb[:, :], cend_d[:, :])
        cload("sp", bg_sb[:, :], b_gate[0, :].partition_broadcast(128))
        cload("pool", w1_sb[:, :, :, :], cmp_w1.rearrange("k (p d) h -> d k p h", d=64))
        cload("pool", w2_sb[:, :, :], cmp_w2.rearrange("k h d -> h k d"))
        cload("sp", pe_nat[:, :], cmp_pe.rearrange("k p d -> (k p) d"))
        for g in range(4):
            cload("sp", vcM[:, :, g, 64:128], mc2s_d[:, :, :])
    if STAGE >= 4:
        cload("sp", maskS[:, :], maskS_d[:, :])
        cload("sp", privnS[:, :], privnS_d[:, :])
        cload("sp", pribiasS[:, :], pribiasS_d[:, :])
        for i_ in range(2):
            cload("sp", kcTs[i_][64:70, :, :], caugS_d.unsqueeze(1).to_broadcast([6, 4, 128]))
            for g in range(4):
                cload("sp", vcMs[i_][:, g, 64:128], mc2sS_d[:, :])
        cload("sp", QTs[64:70, :, :, :], qaugS_d[:, :, :, :])
        for b_ in range(2):
            for g in range(4):
                cload("sp", knT[64:70, b_, g, :], kaugS_d[20, :, 0:NS])
        S.add("dve", lambda e: e.memset(vnew[:, :, :, 64:65], 1.0), writes=[k_vnew])
        for i_ in range(4):
            S.add("dve", lambda e, i_=i_: e.memset(Pz[i_][:, :, :], 0.0), writes=[k_Pz[i_]])
        for g in range(4):
            for s_ in range(NS):
                S.add("sp", lambda e, g=g, s_=s_: e.dma_start(
                    out=KsTS_d[s_, :, 64:70, g, :], in_=kaugS_d[0:16]), writes=[k_sscr], dma=True)
                S.add("sp", lambda e, g=g, s_=s_: e.dma_start(
                    out=KwTS_d[s_, :, 64:70, g, :], in_=kaugS_d[16:20]), writes=[k_sscr], dma=True)
    S.add("dve", lambda e: e.memset(ones_bf[:, :], 1.0), writes=[k_const])
    S.add("dve", lambda e: e.memset(epsb[:, :], EPS), writes=[k_const])
    if STAGE >= 3:
        for i in range(2):
            S.add("dve", lambda e, i=i: e.memset(vst[i][:, :, :, 64:65], 1.0), writes=[k_vst[i]])
        for g in range(4):
            S.add("sp", lambda e, g=g: e.dma_start(out=KsT_d[:, 64:70, g, :], in_=kaug_d[:, :, :]), writes=[k_scr], dma=True)
            S.add("sp", lambda e, g=g: e.dma_start(out=KwT_d[:, 64:70, g, :], in_=kaug_d[:, :, :]), writes=[k_scr], dma=True)
        pi = next_ps()
        S.add("pe", lambda e, pi=pi: e.transpose(psum[pi][0:64, 0:64], pe_nat[:, :], ident[0:64, 0:64]),
              reads=[k_const], writes=[k_ps[pi]])
        S.add("act", lambda e, pi=pi: e.activation(out=peT[:, :, :], in_=psum[pi][0:64, 0:64].rearrange("d (k p) -> d k p", k=2), func=AF.Copy),
              reads=[k_ps[pi]], writes=[k_const])
        for kv in range(2):
            pi = next_ps()
            for p in range(32):
                S.add("pe", lambda e, pi=pi, kv=kv, p=p: e.matmul(
                    psum[pi][:, 0:1], w1_sb[:, kv, p, :], peT[:, kv, p:p + 1], start=(p == 0), stop=(p == 31)),
                    reads=[k_const], writes=[k_ps[pi]])
            S.add("act", lambda e, pi=pi, kv=kv: e.activation(out=cvec[:, kv:kv + 1], in_=psum[pi][:, 0:1], func=AF.Copy),
                  reads=[k_ps[pi]], writes=[k_cmp])

    def wload(src_ap, shape3):
        i = rot("wb", NWB)
        a, b = shape3
        view = wb[i][:, 0:a * b].rearrange("p (a b) -> p a b", a=a)
        S.add("pool", lambda e: e.dma_start(out=view, in_=src_ap), writes=[k_wb[i]], dma=True)
        return i, view

    COLR = [(0, 16), (16, 528), (528, 1040)]
    MAINR = [(16, 528), (528, 1040)]
    sq = hT[:, :, :].rearrange("p a b -> p (a b)")[:, 0:8 * NCX].rearrange("p (a b) -> p a b", a=8)

    SR = (NCOL, NCX)

    def norm(vidx, ranges, with_tail=False, final=False, stail=False):
        for dc in range(8):
            S.add("act", lambda e, dc=dc: e.activation(out=sq[:, dc, :], in_=xT[:, dc, :], func=AF.Square),
                  reads=[k_xT[dc]], writes=list(k_h))
        for (c0, c1) in ranges:
            pi = next_ps()
            n = c1 - c0
            for dc in range(8):
                S.add("pe", lambda e, dc=dc, pi=pi, c0=c0, c1=c1, n=n: e.matmul(
                    psum[pi][:, 0:n], ones_bf[:, :], sq[:, dc, c0:c1], start=(dc == 0), stop=(dc == 7)),
                    reads=list(k_h) + [k_const], writes=[k_ps[pi]])
            S.add("act", lambda e, pi=pi, c0=c0, c1=c1, n=n: e.activation(
                out=rstd[:, c0:c1], in_=psum[pi][:, 0:n], func=AF.Sqrt, bias=epsb[:, 0:1], scale=1.0 / D),
                reads=[k_ps[pi], k_const], writes=[k_rstd])
            S.add("dve", lambda e, c0=c0, c1=c1: e.reciprocal(out=rstd[:, c0:c1], in_=rstd[:, c0:c1]),
                  reads=[k_rstd], writes=[k_rstd])
        lo = ranges[0][0]
        hi = ranges[-1][1]
        for dc in range(8):
            if final:
                S.add("dve", lambda e, dc=dc: e.scalar_tensor_tensor(
                    out=xT[:, dc, lo:hi], in0=xT[:, dc, lo:hi], scalar=vcol(vidx, dc), in1=rstd[:, lo:hi],
                    op0=ALU.mult, op1=ALU.mult), reads=[k_xT[dc], k_rstd, k_const], writes=[k_xT[dc]])
            else:
                S.add("dve", lambda e, dc=dc: e.scalar_tensor_tensor(
                    out=uT[:, dc, lo:hi], in0=xT[:, dc, lo:hi], scalar=vcol(vidx, dc), in1=rstd[:, lo:hi],
                    op0=ALU.mult, op1=ALU.mult), reads=[k_xT[dc], k_rstd, k_const], writes=[k_uT])
        if with_tail or stail:
            t0 = NCX - 16 if stail else NCOL - 16
            for dc in range(8):
                S.add("dve", lambda e, dc=dc, t0=t0: e.scalar_tensor_tensor(
                    out=utail[:, dc, :], in0=xT[:, dc, t0:t0 + 16], scalar=vcol(vidx, dc),
                    in1=rstd[:, t0:t0 + 16], op0=ALU.mult, op1=ALU.mult),
                    reads=[k_xT[dc], k_rstd, k_const], writes=[k_utail])

    def rows16_out(src3, ksrc, dst_ap):
        pi = next_ps()
        pi2 = next_ps()
        for dc in range(8):
            pp = pi if dc < 4 else pi2
            S.add("pe", lambda e, dc=dc, pp=pp: e.transpose(
                psum[pp][0:16, (dc % 4) * 128:(dc % 4 + 1) * 128], src3[:, dc, :], ident[:, :]),
                reads=ksrc + [k_const], writes=[k_ps[pp]])
        S.add("act", lambda e, pi=pi: e.activation(out=ptail[:, 0:512], in_=psum[pi][0:16, :], func=AF.Copy),
              reads=[k_ps[pi]], writes=[k_ptail])
        S.add("act", lambda e, pi2=pi2: e.activation(out=ptail[:, 512:1024], in_=psum[pi2][0:16, :], func=AF.Copy),
              reads=[k_ps[pi2]], writes=[k_ptail])
        S.add("sp", lambda e: e.dma_start(out=dst_ap, in_=ptail[:, :]), reads=[k_ptail], writes=[k_out], dma=True)

    def mlp(layer, has_s=False):
        for hh in range(2):
            for fb in range(4):
                col0 = hh * 2048 + fb * 512
                wi, wv = wload(w_up[layer].rearrange("(dc p) f -> p dc f", p=128)[:, :, col0:col0 + 512], (8, 512))
                for fc in range(4):
                    fcl = fb * 4 + fc
                    for th in range(2):
                        pi = next_ps()
                        c0 = 16 + th * 512
                        for dc in range(8):
                            S.add("pe", lambda e, pi=pi, wv=wv, dc=dc, fc=fc, c0=c0: e.matmul(
                                psum[pi][:, :], wv[:, dc, fc * 128:(fc + 1) * 128], uT[:, dc, c0:c0 + 512],
                                start=(dc == 0), stop=(dc == 7)),
                                reads=[k_wb[wi], k_uT], writes=[k_ps[pi]])
                        tt = tA[:, th * 512:(th + 1) * 512]
                        kt = k_tA if th == 0 else k_tA1
                        S.add("act", lambda e, pi=pi, tt=tt: e.activation(out=tt, in_=psum[pi][:, :], func=AF.Relu),
                              reads=[k_ps[pi]], writes=[kt])
                        S.add("dve", lambda e, tt=tt, fcl=fcl, th=th: e.tensor_tensor(
                            out=hT[:, fcl, th * 512:(th + 1) * 512], in0=tt, in1=tt, op=ALU.mult),
                            reads=[kt], writes=[k_h[fcl]])
                    if has_s:
                        pi = next_ps()
                        for dc in range(8):
                            S.add("pe", lambda e, pi=pi, wv=wv, dc=dc, fc=fc: e.matmul(
                                psum[pi][:, 0:NS], wv[:, dc, fc * 128:(fc + 1) * 128], uT[:, dc, NCOL:NCX],
                                start=(dc == 0), stop=(dc == 7)),
                                reads=[k_wb[wi], k_uT], writes=[k_ps[pi]])
                        S.add("act", lambda e, pi=pi: e.activation(out=rs_t[:, :], in_=psum[pi][:, 0:NS], func=AF.Relu),
                              reads=[k_ps[pi]], writes=[k_rs])
                        S.add("dve", lambda e, fcl=fcl: e.tensor_tensor(
                            out=hTs[:, fcl, :], in0=rs_t[:, :], in1=rs_t[:, :], op=ALU.mult),
                            reads=[k_rs], writes=[k_hTs])
            for dmp in range(4):
                wi, wv = wload(w_down[layer][hh * 2048:(hh + 1) * 2048, dmp * 256:(dmp + 1) * 256]
                               .rearrange("(f p) c -> p f c", p=128), (16, 256))
                for dmc in range(2):
                    dca = dmp * 2 + dmc
                    for th in range(2):
                        pi = next_ps()
                        for fcl in range(16):
                            S.add("pe", lambda e, pi=pi, wv=wv, fcl=fcl, dmc=dmc, th=th: e.matmul(
                                psum[pi][:, :], wv[:, fcl, dmc * 128:(dmc + 1) * 128], hT[:, fcl, th * 512:(th + 1) * 512],
                                start=(fcl == 0), stop=(fcl == 15)),
                                reads=[k_wb[wi], k_h[fcl]], writes=[k_ps[pi]])
                        c0 = 16 + th * 512
                        S.add("dve", lambda e, pi=pi, dca=dca, c0=c0: e.tensor_tensor(
                            out=xT[:, dca, c0:c0 + 512], in0=xT[:, dca, c0:c0 + 512], in1=psum[pi][:, :], op=ALU.add),
                            reads=[k_ps[pi], k_xT[dca]], writes=[k_xT[dca]])
                    if has_s:
                        pi = next_ps()
                        for fcl in range(16):
                            S.add("pe", lambda e, pi=pi, wv=wv, fcl=fcl, dmc=dmc: e.matmul(
                                psum[pi][:, 0:NS], wv[:, fcl, dmc * 128:(dmc + 1) * 128], hTs[:, fcl, :],
                                start=(fcl == 0), stop=(fcl == 15)),
                                reads=[k_wb[wi], k_hTs], writes=[k_ps[pi]])
                        S.add("dve", lambda e, pi=pi, dca=dca: e.tensor_tensor(
                            out=xT[:, dca, NCOL:NCX], in0=xT[:, dca, NCOL:NCX], in1=psum[pi][:, 0:NS], op=ALU.add),
                            reads=[k_ps[pi], k_xT[dca]], writes=[k_xT[dca]])

    if STAGE >= 4 and not _DEV.get("skip_prep"):
        I32 = mybir.dt.int32
        hflat = hT[:, :, :].rearrange("p a b -> p (a b)")
        XcT = hflat[0:64, :].rearrange("p (k g n) -> p k g n", k=2, g=4)
        uflat = uT[:, :, :].rearrange("p a b -> p (a b)")
        xflat_bf = xT[:, :, :].rearrange("p a b -> p (a b)").bitcast(BF16)
        Gb = [uflat[:, i * 1024:(i + 1) * 1024] for i in range(8)] + \
             [xflat_bf[:, i * 1024:(i + 1) * 1024] for i in range(16)]
        ptb_i = xin[0][:, 0:256].bitcast(I32)
        ptb_f = xin[0][:, 256:512]
        idx_f = xin[0][:, 512:768]
        idx_i = xin[1][:, 0:256].bitcast(I32)
        iotap = xin[1][:, 256:257]
        kcs_st = ktst[0][:, 0:512].rearrange("d (g n) -> d g n", g=4)
        vcs_st = xin[1][:, 512:640].bitcast(BF16).rearrange("n (g d) -> n g d", g=4)
        k_kcs, k_vcs = Tk(), Tk()
        hs_s = hs_all[:, 0, 0, 0:128]
        S.add("sp", lambda e: e.dma_start(out=ptb_i, in_=pt_d.rearrange("s k -> (s k)").partition_broadcast(128)), writes=[k_idx], dma=True)
        S.add("sp", lambda e: e.dma_start(out=iotap, in_=iotap_d[:, :]), writes=[k_idx], dma=True)
        S.add("dve", lambda e: e.tensor_copy(out=ptb_f, in_=ptb_i), reads=[k_idx], writes=[k_idx])
        S.add("dve", lambda e: e.tensor_scalar(out=idx_f, in0=ptb_f, scalar1=128.0, scalar2=iotap, op0=ALU.mult, op1=ALU.add),
              reads=[k_idx], writes=[k_idx])
        S.add("dve", lambda e: e.tensor_copy(out=idx_i, in_=idx_f), reads=[k_idx], writes=[k_idx])
        S.add("dve", lambda e: e.memset(kcs_st, 0.0), writes=[k_kcs])
        S.add("dve", lambda e: e.memset(vcs_st, 0.0), writes=[k_vcs])
        for s_ in range(_DEV.get("prep_ns", NS)):
            for kt in range(16):
                gi_ = rot("G", 24)
                col = s_ * 16 + kt
                S.add("pool", lambda e, gi_=gi_, col=col: e.indirect_dma_start(
                    out=Gb[gi_], out_offset=None, in_=cache_d[:, :],
                    in_offset=bass.IndirectOffsetOnAxis(ap=idx_i[:, col:col + 1], axis=0)),
                    reads=[k_idx], writes=[k_G[gi_]], dma=True)
                if _DEV.get("prep_level", 9) < 2:
                    continue
                pa, pb = next_ps(), next_ps()
                psa = psum[pa][:, :].bitcast(BF16)
                psb = psum[pb][:, :].bitcast(BF16)
                for j in range(4):
                    c_ = 512 + j * 64
                    S.add("pe", lambda e, psa=psa, j=j, c_=c_, gi_=gi_: e.transpose(
                        psa[0:64, j * 128:(j + 1) * 128], Gb[gi_][:, c_:c_ + 64], identb[:, :]),
                        reads=[k_G[gi_], k_const], writes=[k_ps[pa]])
                for j in range(8):
                    c_ = j * 64
                    S.add("pe", lambda e, psb=psb, j=j, c_=c_, gi_=gi_: e.transpose(
                        psb[0:64, j * 128:(j + 1) * 128], Gb[gi_][:, c_:c_ + 64], identb[:, :]),
                        reads=[k_G[gi_], k_const], writes=[k_ps[pb]])
                if _DEV.get("no_evac"):
                    continue
                kst = ktst[1][:, 0:512].rearrange("d (g n) -> d g n", g=4)
                if not _DEV.get("no_e1"):
                    S.add("act", lambda e, psa=psa, kst=kst: e.activation(
                        out=kst, in_=psa[0:64, 0:512].rearrange("d (g n) -> d g n", g=4), func=AF.Copy),
                        reads=[k_ps[pa]], writes=[k_ktst[1]])
                if not _DEV.get("no_store"):
                    S.add("sp", lambda e, kst=kst, s_=s_, kt=kt: e.dma_start(out=KsTS_d[s_, kt, 0:64, :, :], in_=kst),
                          reads=[k_ktst[1]], writes=[k_sscr], dma=True)
                if not _DEV.get("no_e2"):
                    S.add("dve", lambda e, psb=psb, kt=kt: e.tensor_copy(
                        out=XcT[:, :, :, kt * 128:(kt + 1) * 128], in_=psb[0:64, 0:1024].rearrange("d (k g n) -> d k g n", k=2, g=4)),
                        reads=[k_ps[pb]], writes=k_X[0] + k_X[1])
                vi = rot("vst", 2)
                if not _DEV.get("no_e4"):
                    S.add("act", lambda e, vi=vi, gi_=gi_: e.activation(
                        out=vst[vi][:, 0, :, 0:64], in_=Gb[gi_][:, 768:1024].rearrange("p (g d) -> p g d", g=4), func=AF.Copy),
                        reads=[k_G[gi_]], writes=[k_vst[vi]])
                if not _DEV.get("no_store"):
                    S.add("sp", lambda e, vi=vi, s_=s_, kt=kt: e.dma_start(out=VsS_d[s_, kt], in_=vst[vi][:, 0, :, :]),
                          reads=[k_vst[vi]], writes=[k_sscr], dma=True)
            for kv in range(2 if _DEV.get("prep_level", 9) >= 3 else 0):
                for g in range(4):
                    pi = next_ps()
                    for pos in range(32):
                        S.add("pe", lambda e, pi=pi, kv=kv, g=g, pos=pos: e.matmul(
                            psum[pi][:, 0:127], w1_sb[:, kv, pos, :], XcT[:, kv, g, pos:pos + 2017:16],
                            start=(pos == 0), stop=(pos == 31)),
                            reads=[k_X[kv][g], k_const], writes=[k_ps[pi]])
                    S.add("act", lambda e, pi=pi, kv=kv: e.activation(
                        out=hs_s[:, 0:127], in_=psum[pi][:, 0:127], func=AF.Silu, bias=cvec[:, kv:kv + 1]),
                        reads=[k_ps[pi], k_cmp], writes=[k_AB])
                    pj = next_ps()
                    if kv == 0:
                        S.add("pe", lambda e, pj=pj: e.matmul(psum[pj][0:64, 0:127], w2_sb[:, 0, :], hs_s[:, 0:127], start=True, stop=True),
                              reads=[k_AB, k_const], writes=[k_ps[pj]])
                        S.add("act", lambda e, pj=pj, g=g: e.activation(out=kcs_st[:, g, 0:127], in_=psum[pj][0:64, 0:127], func=AF.Copy),
                              reads=[k_ps[pj]], writes=[k_kcs])
                    else:
                        S.add("pe", lambda e, pj=pj: e.matmul(psum[pj][0:127, 0:64], hs_s[:, 0:127], w2_sb[:, 1, :], start=True, stop=True),
                              reads=[k_AB, k_const], writes=[k_ps[pj]])
                        S.add("act", lambda e, pj=pj, g=g: e.activation(out=vcs_st[0:127, g, :], in_=psum[pj][0:127, 0:64], func=AF.Copy),
                              reads=[k_ps[pj]], writes=[k_vcs])
            if _DEV.get("prep_level", 9) < 4:
                continue
            S.add("sp", lambda e, s_=s_: e.dma_start(out=kcTS_d[s_], in_=kcs_st), reads=[k_kcs], writes=[k_sscr], dma=True)
            S.add("sp", lambda e, s_=s_: e.dma_start(out=vcS_d[s_], in_=vcs_st), reads=[k_vcs], writes=[k_sscr], dma=True)
            for t_ in range(4):
                gi_ = rot("G", 24)
                S.add("pool", lambda e, gi_=gi_, s_=s_, t_=t_: e.dma_start(
                    out=Gb[gi_][:, 0:512], in_=swin_d[s_, t_ * 128:(t_ + 1) * 128, :]), writes=[k_G[gi_]], dma=True)
                pa = next_ps()
                psa = psum[pa][:, :].bitcast(BF16)
                for j in range(4):
                    S.add("pe", lambda e, psa=psa, j=j, gi_=gi_: e.transpose(
                        psa[0:64, j * 128:(j + 1) * 128], Gb[gi_][:, j * 64:(j + 1) * 64], identb[:, :]),
                        reads=[k_G[gi_], k_const], writes=[k_ps[pa]])
                kst = ktst[1][:, 0:512].rearrange("d (g n) -> d g n", g=4)
                S.add("act", lambda e, psa=psa, kst=kst: e.activation(
                    out=kst, in_=psa[0:64, 0:512].rearrange("d (g n) -> d g n", g=4), func=AF.Copy),
                    reads=[k_ps[pa]], writes=[k_ktst[1]])
                S.add("sp", lambda e, kst=kst, s_=s_, t_=t_: e.dma_start(out=KwTS_d[s_, t_, 0:64, :, :], in_=kst),
                      reads=[k_ktst[1]], writes=[k_sscr], dma=True)
                vi = rot("vst", 2)
                S.add("act", lambda e, vi=vi, gi_=gi_: e.activation(
                    out=vst[vi][:, 0, :, 0:64], in_=Gb[gi_][:, 256:512].rearrange("p (g d) -> p g d", g=4), func=AF.Copy),
                    reads=[k_G[gi_]], writes=[k_vst[vi]])
                S.add("sp", lambda e, vi=vi, s_=s_, t_=t_: e.dma_start(out=VwS_d[s_, t_], in_=vst[vi][:, 0, :, :]),
                      reads=[k_vst[vi]], writes=[k_sscr], dma=True)
        S.barrier()

    for gq, (slot0, is_own, ogi) in enumerate(L0_GROUPS):
        if _DEV.get("prep_only"):
            continue
        if STAGE < 3 and not is_own:
            continue
        last = is_own and ogi == 1
        has_s = is_own and ogi == 0
        RNG0 = COLR + ([SR] if has_s else [])
        RNG = MAINR + ([SR] if has_s else [])
        for ti in range(10 if has_s else 9):
            r0, nr = (0, 16) if ti == 0 else ((NCOL, NS) if ti == 9 else (16 + (ti - 1) * 128, 128))
            xi = rot("xin", 2)
            if ti == 9:
                S.add("sp", lambda e, xi=xi: e.dma_start(out=xin[xi][0:NS, :], in_=xs_d[:, :]),
                      writes=[k_xin[xi]], dma=True)
            else:
                S.add("sp", lambda e, xi=xi, r0=r0, nr=nr, gq=gq: e.dma_start(out=xin[xi][0:nr, :], in_=xg[gq, r0:r0 + nr, :]),
                      writes=[k_xin[xi]], dma=True)
            for hb in range(2):
                pi = next_ps()
                for j in range(4):
                    dc = hb * 4 + j
                    S.add("pe", lambda e, xi=xi, pi=pi, j=j, dc=dc, nr=nr: e.transpose(
                        psum[pi][:, j * 128:j * 128 + nr], xin[xi][0:nr, dc * 128:(dc + 1) * 128], ident[0:nr, 0:nr]),
                        reads=[k_xin[xi], k_const], writes=[k_ps[pi]])
                src = psum[pi][:, :].rearrange("p (j t) -> p j t", j=4)[:, :, 0:nr]
                if hb == 0:
                    S.add("act", lambda e, src=src, hb=hb, r0=r0, nr=nr: e.activation(
                        out=xT[:, hb * 4:hb * 4 + 4, r0:r0 + nr], in_=src, func=AF.Copy),
                        reads=[k_ps[pi]], writes=k_xT[hb * 4:hb * 4 + 4])
                else:
                    S.add("dve", lambda e, src=src, hb=hb, r0=r0, nr=nr: e.tensor_copy(
                        out=xT[:, hb * 4:hb * 4 + 4, r0:r0 + nr], in_=src),
                        reads=[k_ps[pi]], writes=k_xT[hb * 4:hb * 4 + 4])
        norm(VG0, RNG0, with_tail=last, stail=has_s)
        if last:
            rows16_out(utail, [k_utail], pool_out[:, :])
        if has_s:
            rows16_out(utail, [k_utail], pool_s_out[:, 14, :])
            stx = [rot("xin", 2), None]
            stx[1] = rot("xin", 2)
            for t_, (r0_, nr_) in enumerate(((0, 128), (128, 112))):
                S.add("sp", lambda e, xi=stx[t_], r0_=r0_, nr_=nr_: e.dma_start(out=xin[xi][0:nr_, :], in_=spool_d[r0_:r0_ + nr_, :]),
                      writes=[k_xin[stx[t_]]], dma=True)
        diffT = hT[:, 0:8, :]
        for dc in range(8):
            g = dc // 2
            nst = g + 1
            bufs = [tA, tB]
            kb_ = [k_tA, k_tB]
            kbw_ = [[k_tA, k_tA1], [k_tB]]
            cur = None
            for s in range(nst):
                sh = 1 << s
                lo = (1 << (s + 1))
                o = bufs[s % 2]
                ko = kb_[s % 2]
                kow = kbw_[s % 2]
                if s == 0:
                    S.add("dve", lambda e, o=o, dc=dc, lo=lo, sh=sh: e.tensor_tensor(
                        out=o[:, lo:NCOL], in0=uT[:, dc, lo:NCOL], in1=uT[:, dc, lo - sh:NCOL - sh], op=ALU.add),
                        reads=[k_uT], writes=kow)
                else:
                    i_ = bufs[(s - 1) % 2]
                    ki = kb_[(s - 1) % 2]
                    S.add("dve", lambda e, o=o, i_=i_, lo=lo, sh=sh: e.tensor_tensor(
                        out=o[:, lo:NCOL], in0=i_[:, lo:NCOL], in1=i_[:, lo - sh:NCOL - sh], op=ALU.add),
                        reads=[ki], writes=kow)
                cur = (o, ko, kow)
            o, ko, kow = cur
            w = 1 << (g + 1)
            if has_s:
                pi = next_ps()
                for t_, nr_ in enumerate((128, 112)):
                    S.add("pe", lambda e, pi=pi, t_=t_, nr_=nr_, dc=dc, g=g: e.matmul(
                        psum[pi][:, 0:NS], xin[stx[t_]][0:nr_, dc * 128:(dc + 1) * 128], selw[0:nr_, t_, g * 16:(g + 1) * 16],
                        start=(t_ == 0), stop=(t_ == 1)),
                        reads=[k_xin[stx[t_]], k_const], writes=[k_ps[pi]])
                S.add("dve", lambda e, o=o, pi=pi, dc=dc: e.tensor_tensor(
                    out=o[:, NCOL:NCX], in0=psum[pi][:, 0:NS], in1=uT[:, dc, NCOL:NCX], op=ALU.add),
                    reads=[k_ps[pi], k_uT, ko], writes=kow)
                S.add("dve", lambda e, o=o, dc=dc, w=w: e.scalar_tensor_tensor(
                    out=diffS[:, dc, :], in0=o[:, NCOL:NCX], scalar=1.0 / w, in1=uT[:, dc, NCOL:NCX],
                    op0=ALU.mult, op1=ALU.subtract), reads=[ko, k_uT], writes=[k_diffS])
            S.add("dve", lambda e, o=o, g=g, gq=gq: e.tensor_tensor(
                out=o[:, 16:32], in0=o[:, 16:32], in1=corr_sb[:, gq * 64 + g * 16: gq * 64 + g * 16 + 16], op=ALU.mult),
                reads=[ko, k_const], writes=kow)
            S.add("dve", lambda e, o=o, dc=dc, w=w: e.scalar_tensor_tensor(
                out=diffT[:, dc, :], in0=o[:, 16:NCOL], scalar=1.0 / w, in1=uT[:, dc, 16:NCOL],
                op0=ALU.mult, op1=ALU.subtract), reads=[ko, k_uT], writes=[k_h[dc]])
        for oc in range(8):
            g = oc // 2
            for th in range(2):
                pi = next_ps()
                for kk in range(2):
                    S.add("pe", lambda e, pi=pi, g=g, kk=kk, oc=oc, th=th: e.matmul(
                        psum[pi][:, :], pw4[:, g, kk, (oc % 2) * 128:(oc % 2) * 128 + 128],
                        diffT[:, 2 * g + kk, th * 512:(th + 1) * 512], start=(kk == 0), stop=(kk == 1)),
                        reads=[k_h[2 * g + kk], k_const], writes=[k_ps[pi]])
                c0 = 16 + th * 512
                S.add("dve", lambda e, pi=pi, oc=oc, c0=c0: e.scalar_tensor_tensor(
                    out=xT[:, oc, c0:c0 + 512], in0=psum[pi][:, :], scalar=vcol(VPS, oc), in1=xT[:, oc, c0:c0 + 512],
                    op0=ALU.mult, op1=ALU.add), reads=[k_ps[pi], k_xT[oc], k_const], writes=[k_xT[oc]])
            if has_s:
                pi = next_ps()
                for kk in range(2):
                    S.add("pe", lambda e, pi=pi, g=g, kk=kk, oc=oc: e.matmul(
                        psum[pi][:, 0:NS], pw4[:, g, kk, (oc % 2) * 128:(oc % 2) * 128 + 128],
                        diffS[:, 2 * g + kk, :], start=(kk == 0), stop=(kk == 1)),
                        reads=[k_diffS, k_const], writes=[k_ps[pi]])
                S.add("dve", lambda e, pi=pi, oc=oc: e.scalar_tensor_tensor(
                    out=xT[:, oc, NCOL:NCX], in0=psum[pi][:, 0:NS], scalar=vcol(VPS, oc), in1=xT[:, oc, NCOL:NCX],
                    op0=ALU.mult, op1=ALU.add), reads=[k_ps[pi], k_xT[oc], k_const], writes=[k_xT[oc]])
        norm(VG1, RNG)
        mlp(0, has_s)
        if is_own and STAGE >= 2:
            S.add("sp", lambda e, ogi=ogi: e.dma_start(
                out=x1_d[ogi].rearrange("p (a b) -> p a b", a=8), in_=xT[:, :, 16:NCX]),
                reads=list(k_xT), writes=[k_x1[ogi]], dma=True)
        norm(VGKV, RNG)
        kvw = []
        for cb in range(3):
            kvw.append(wload(w_kv.rearrange("(dc p) f -> p dc f", p=128)[:, :, cb * 512:(cb + 1) * 512], (8, 512)))
        for ti in range(8):
            ks = rot("kvst", 2)
            c0 = 16 + ti * 128
            for cb in range(3):
                wi, wv = kvw[cb]
                pi = next_ps()
                for dc in range(8):
                    S.add("pe", lambda e, pi=pi, wv=wv, dc=dc, c0=c0: e.matmul(
                        psum[pi][:, :], uT[:, dc, c0:c0 + 128], wv[:, dc, :], start=(dc == 0), stop=(dc == 7)),
                        reads=[k_wb[wi], k_uT], writes=[k_ps[pi]])
                S.add("act", lambda e, pi=pi, ks=ks, cb=cb: e.activation(
                    out=kvst[ks][:, cb * 512:(cb + 1) * 512], in_=psum[pi][:, :], func=AF.Copy),
                    reads=[k_ps[pi]], writes=[k_kvst[ks]])
            if is_own:
                row0 = ogi * GT + ti * 128
                S.add("sp", lambda e, ks=ks, row0=row0: e.dma_start(out=kv_out[row0:row0 + 128, :], in_=kvst[ks][:, 0:1024]),
                      reads=[k_kvst[ks]], writes=[k_out], dma=True)
                if last and ti >= 4:
                    S.add("sp", lambda e, ks=ks, ti=ti: e.dma_start(
                        out=win_out[(ti - 4) * 128:(ti - 3) * 128, :], in_=kvst[ks][:, 1024:1536]),
                        reads=[k_kvst[ks]], writes=[k_out], dma=True)
            if STAGE >= 3:
                vi = rot("vst", 2)
                S.add("dve", lambda e, ks=ks, vi=vi: e.tensor_copy(
                    out=vst[vi][:, 0, :, 0:64], in_=kvst[ks][:, 768:1024].rearrange("p (g d) -> p g d", g=4)),
                    reads=[k_kvst[ks]], writes=[k_vst[vi]])
                S.add("dve", lambda e, ks=ks, vi=vi: e.tensor_copy(
                    out=vst[vi][:, 1, :, 0:64], in_=kvst[ks][:, 1280:1536].rearrange("p (g d) -> p g d", g=4)),
                    reads=[k_kvst[ks]], writes=[k_vst[vi]])
                st_ = slot0 + ti
                S.add("sp", lambda e, vi=vi, st_=st_: e.dma_start(out=Vs_d[st_], in_=vst[vi][:, 0, :, :]),
                      reads=[k_vst[vi]], writes=[k_scr], dma=True)
                S.add("sp", lambda e, vi=vi, st_=st_: e.dma_start(out=Vw_d[st_], in_=vst[vi][:, 1, :, :]),
                      reads=[k_vst[vi]], writes=[k_scr], dma=True)
        if has_s:
            ks = rot("kvst", 2)
            for cb in range(3):
                wi, wv = kvw[cb]
                pi = next_ps()
                for dc in range(8):
                    S.add("pe", lambda e, pi=pi, wv=wv, dc=dc: e.matmul(
                        psum[pi][0:NS, :], uT[:, dc, NCOL:NCX], wv[:, dc, :], start=(dc == 0), stop=(dc == 7)),
                        reads=[k_wb[wi], k_uT], writes=[k_ps[pi]])
                S.add("act", lambda e, pi=pi, ks=ks, cb=cb: e.activation(
                    out=kvst[ks][0:NS, cb * 512:(cb + 1) * 512], in_=psum[pi][0:NS, :], func=AF.Copy),
                    reads=[k_ps[pi]], writes=[k_kvst[ks]])
            S.add("sp", lambda e, ks=ks: e.dma_start(out=kv_s_out[:, :], in_=kvst[ks][0:NS, 0:1024]),
                  reads=[k_kvst[ks]], writes=[k_out], dma=True)
            S.add("sp", lambda e, ks=ks: e.dma_start(out=win_s_out[:, 511, :], in_=kvst[ks][0:NS, 1024:1536]),
                  reads=[k_kvst[ks]], writes=[k_out], dma=True)
            if STAGE >= 4:
                for bi_, (kc0, vc0) in enumerate(((512, 768), (1024, 1280))):
                    S.add("dve", lambda e, ks=ks, bi_=bi_, vc0=vc0: e.tensor_copy(
                        out=vnew[:, bi_, :, 0:64], in_=kvst[ks][0:NS, vc0:vc0 + 256].rearrange("p (g d) -> p g d", g=4)),
                        reads=[k_kvst[ks]], writes=[k_vnew])
                    S.add("dve", lambda e, ks=ks, bi_=bi_, kc0=kc0: e.tensor_copy(
                        out=knew_bf[:, bi_ * 256:(bi_ + 1) * 256], in_=kvst[ks][0:NS, kc0:kc0 + 256]),
                        reads=[k_kvst[ks]], writes=[k_knew])
                pi = next_ps()
                pbf = psum[pi][:, :].bitcast(BF16)
                for j in range(8):
                    S.add("pe", lambda e, pbf=pbf, j=j: e.transpose(
                        pbf[0:64, j * NS:(j + 1) * NS], knew_bf[:, j * 64:(j + 1) * 64], identb[0:NS, 0:NS]),
                        reads=[k_knew, k_const], writes=[k_ps[pi]])
                S.add("act", lambda e, pbf=pbf: e.activation(
                    out=knT[0:64, :, :, :], in_=pbf[0:64, 0:8 * NS].rearrange("d (b g n) -> d b g n", b=2, g=4), func=AF.Copy),
                    reads=[k_ps[pi]], writes=[k_knT])
        if STAGE >= 3:
            for (cb, dst) in ((1, KsT_d), (2, KwT_d)):
                wi, wv = kvw[cb]
                for g in range(4):
                    ki = rot("ktst", 2)
                    for th in range(2):
                        pi = next_ps()
                        c0 = 16 + th * 512
                        for dc in range(8):
                            S.add("pe", lambda e, pi=pi, wv=wv, dc=dc, g=g, c0=c0: e.matmul(
                                psum[pi][0:64, :], wv[:, dc, g * 64:(g + 1) * 64], uT[:, dc, c0:c0 + 512],
                                start=(dc == 0), stop=(dc == 7)),
                                reads=[k_wb[wi], k_uT], writes=[k_ps[pi]])
                        S.add("act", lambda e, pi=pi, ki=ki, th=th: e.activation(
                            out=ktst[ki][:, th * 512:(th + 1) * 512], in_=psum[pi][0:64, :], func=AF.Copy),
                            reads=[k_ps[pi]], writes=[k_ktst[ki]])
                    S.add("sp", lambda e, ki=ki, g=g, dst=dst, slot0=slot0: e.dma_start(
                        out=dst[slot0:slot0 + 8, 0:64, g, :].rearrange("t d k -> d t k"),
                        in_=ktst[ki][:, :].rearrange("d (t k) -> d t k", t=8)),
                        reads=[k_ktst[ki]], writes=[k_scr], dma=True)
            wi, wv = kvw[0]
            for kv in range(2):
                for g in range(4):
                    xi = rot("ktst", 2)
                    for th in range(2):
                        pi = next_ps()
                        c0 = 16 + th * 512
                        for dc in range(8):
                            S.add("pe", lambda e, pi=pi, wv=wv, dc=dc, g=g, kv=kv, c0=c0: e.matmul(
                                psum[pi][0:64, :], wv[:, dc, kv * 256 + g * 64: kv * 256 + (g + 1) * 64], uT[:, dc, c0:c0 + 512],
                                start=(dc == 0), stop=(dc == 7)),
                                reads=[k_wb[wi], k_uT], writes=[k_ps[pi]])
                        S.add("act", lambda e, pi=pi, xi=xi, th=th: e.activation(
                            out=xct[xi][:, th * 512:(th + 1) * 512], in_=psum[pi][0:64, :], func=AF.Copy),
                            reads=[k_ps[pi]], writes=[k_xct[xi]])
                    pi = next_ps()
                    xb = xct[xi]
                    for pos in range(32):
                        S.add("pe", lambda e, pi=pi, kv=kv, pos=pos, xb=xb: e.matmul(
                            psum[pi][:, 0:63], w1_sb[:, kv, pos, :], xb[:, pos:pos + 993:16],
                            start=(pos == 0), stop=(pos == 31)),
                            reads=[k_xct[xi], k_const], writes=[k_ps[pi]])
                    for p in range(16):
                        S.add("pe", lambda e, pi=pi, kv=kv, p=p, xb=xb: e.matmul(
                            psum[pi][:, 64:65], w1_sb[:, kv, p, :], xb[:, 1008 + p:1009 + p],
                            start=(p == 0), stop=(p == 15)),
                            reads=[k_xct[xi], k_const], writes=[k_ps[pi]])
                    for p in range(16):
                        S.add("pe", lambda e, pi=pi, kv=kv, p=p, xb=xb: e.matmul(
                            psum[pi][:, 65:66], w1_sb[:, kv, 16 + p, :], xb[:, p:p + 1],
                            start=(p == 0), stop=(p == 15)),
                            reads=[k_xct[xi], k_const], writes=[k_ps[pi]])
                    sb0 = (slot0 * 8) % 256
                    po = sb0 // 64
                    S.add("act", lambda e, pi=pi, kv=kv, g=g, sb0=sb0: e.activation(
                        out=hs_all[:, kv, g, sb0:sb0 + 63], in_=psum[pi][:, 0:63], func=AF.Silu, bias=cvec[:, kv:kv + 1]),
                        reads=[k_ps[pi], k_cmp], writes=[k_AB])
                    S.add("act", lambda e, pi=pi, kv=kv, g=g, po=po: e.activation(
                        out=ABs[:, kv, g, :, po], in_=psum[pi][:, 64:66], func=AF.Copy),
                        reads=[k_ps[pi]], writes=[k_AB])

    if STAGE >= 3 and not _DEV.get("prep_only"):
        for kv in range(2):
            for g in range(4):
                S.add("dve", lambda e, kv=kv, g=g: e.tensor_tensor(
                    out=hpreb[:, 0:3], in0=ABs[:, kv, g, 0, 0:3], in1=ABs[:, kv, g, 1, 1:4], op=ALU.add),
                    reads=[k_AB], writes=[k_cmp2])
                S.add("dve", lambda e, kv=kv, g=g: e.tensor_tensor(
                    out=hpreb[:, 3:4], in0=ABs[:, kv, g, 0, 3:4], in1=ABs[:, kv, g, 1, 0:1], op=ALU.add),
                    reads=[k_AB], writes=[k_cmp2])
                S.add("act", lambda e, kv=kv, g=g: e.activation(
                    out=hs_all[:, kv, g, 63:256:64], in_=hpreb[:, :], func=AF.Silu, bias=cvec[:, kv:kv + 1]),
                    reads=[k_cmp2, k_cmp], writes=[k_AB])
                pi = next_ps()
                if kv == 0:
                    S.add("pe", lambda e, pi=pi, g=g: e.matmul(psum[pi][0:64, 0:256], w2_sb[:, 0, :], hs_all[:, 0, g, :], start=True, stop=True),
                          reads=[k_AB, k_const], writes=[k_ps[pi]])
                    S.add("act", lambda e, pi=pi, g=g: e.activation(out=kcT[0:64, g, :], in_=psum[pi][0:64, 0:256], func=AF.Copy),
                          reads=[k_ps[pi]], writes=[k_kc])
                else:
                    for nt in range(2):
                        S.add("pe", lambda e, pi=pi, nt=nt, g=g: e.matmul(
                            psum[pi][:, nt * 64:(nt + 1) * 64], hs_all[:, 1, g, nt * 128:(nt + 1) * 128], w2_sb[:, 1, :], start=True, stop=True),
                            reads=[k_AB, k_const], writes=[k_ps[pi]])
                    S.add("act", lambda e, pi=pi, g=g: e.activation(
                        out=vcM[:, :, g, 0:64], in_=psum[pi][:, 0:128].rearrange("n (t d) -> n t d", t=2), func=AF.Copy),
                        reads=[k_ps[pi]], writes=[k_vc])

    PS_S = [0, 1, 2, 3]
    PS_ACC = [4, 5]
    PS_M = [6, 7]
    QT = hT[:, :, :].rearrange("p a b -> p (a b)").rearrange("p (g r t) -> p g r t", g=4, r=4)

    def attention_tile(i, ogi):
        tl = i % 8
        c0 = tl * 128
        pq = rot("pq", 2)
        pqb = posq_bc[pq]
        kpq = k_posq[pq]
        S.add("sp", lambda e: e.dma_start(out=pqb[:, :], in_=posq_d[i * 128:(i + 1) * 128].partition_broadcast(128)),
              writes=[kpq], dma=True)
        S.add("sp", lambda e: e.dma_start(out=privn_sb[pq][:, :], in_=privn_d[i * 128:(i + 1) * 128, :]),
              writes=[k_privn[pq]], dma=True)
        S.add("sp", lambda e: e.dma_start(out=pribias_sb[pq][:, :], in_=pribias_d[i * 128:(i + 1) * 128, :]),
              writes=[k_privn[pq]], dma=True)
        S.add("sp", lambda e: e.dma_start(out=kcT[64:70, :, :], in_=caug_d[i].unsqueeze(1).to_broadcast([6, 4, 256])),
              writes=[k_kc], dma=True)

        def branch_core(gp, br, slots, kind):
            gs = [2 * gp, 2 * gp + 1]
            nkt = len(slots)

            def front(ki_, slot):
                cx = {"first": ki_ == 0, "lastk": ki_ == nkt - 1, "slot": slot, "pti": {}, "kb": None}
                kb = None
                if kind == "c":
                    kt_r = [k_kc]
                    cx["vt_r"] = [k_vc]
                else:
                    kb = rot("kb", 3)
                    ksrc = KsT_d if kind == "s" else KwT_d
                    vsrc = Vs_d if kind == "s" else Vw_d
                    S.add("sp", lambda e, kb=kb, ksrc=ksrc, slot=slot: e.dma_start(out=kbuf[kb][:, :, :], in_=ksrc[slot]),
                          reads=[k_scr], writes=[k_kbuf[kb]], dma=True)
                    S.add("sp", lambda e, kb=kb, vsrc=vsrc, slot=slot: e.dma_start(out=vbuf[kb][:, :, :], in_=vsrc[slot]),
                          reads=[k_scr], writes=[k_vbuf[kb]], dma=True)
                    kt_r = [k_kbuf[kb]]
                    cx["vt_r"] = [k_vbuf[kb]]
                cx["kb"] = kb
                psm = PS_M[ki_ % 2]
                if kind == "c":
                    mi = rot("mask", 3)
                    S.add("dve", lambda e, mi=mi, slot=slot: e.tensor_scalar(
                        out=mask_sb[mi][:, :], in0=pqb[:, :], scalar1=cend_sb[:, slot:slot + 1], scalar2=None, op0=ALU.is_ge),
                        reads=[kpq, k_const], writes=[k_mask[mi]])
                    mk = ("sb", mi)
                elif kind == "w":
                    mi = rot("mask", 3)
                    S.add("dve", lambda e, slot=slot: e.tensor_scalar(
                        out=mtmp[0][:, :], in0=pqb[:, :], scalar1=posk_sb[:, slot:slot + 1], scalar2=0.0,
                        op0=ALU.subtract, op1=ALU.is_ge), reads=[kpq, k_const], writes=[k_mtmp[0]])
                    S.add("dve", lambda e, slot=slot: e.tensor_scalar(
                        out=mtmp[1][:, :], in0=pqb[:, :], scalar1=posk_sb[:, slot:slot + 1], scalar2=512.0,
                        op0=ALU.subtract, op1=ALU.is_lt), reads=[kpq, k_const], writes=[k_mtmp[1]])
                    S.add("dve", lambda e, mi=mi: e.tensor_tensor(
                        out=mask_sb[mi][:, :], in0=mtmp[0][:, :], in1=mtmp[1][:, :], op=ALU.mult),
                        reads=[k_mtmp[0], k_mtmp[1]], writes=[k_mask[mi]])
                    mk = ("sb", mi)
                else:
                    for g in gs:
                        S.add("pe", lambda e, g=g, slot=slot, psm=psm: e.matmul(
                            psum[psm][:, g * 128:(g + 1) * 128], esel_sb[:, slot * 128:(slot + 1) * 128], selT[:, g, :],
                            start=True, stop=True), reads=[k_selT, k_const], writes=[k_ps[psm]])
                    mk = ("ps", None)
                for g in gs:
                    si = PS_S[rot("psS", 4)]
                    pti = rot("pT", 4)
                    cx["pti"][g] = pti
                    if kind == "c":
                        lhs = kcT[:, g, slot * 128:(slot + 1) * 128]
                    else:
                        lhs = kbuf[kb][:, g, :]
                    S.add("pe", lambda e, si=si, lhs=lhs, g=g: e.matmul(
                        psum[si][:, :].rearrange("k (r t) -> k r t", r=4), lhs, QT[0:70, g, :, c0:c0 + 128], start=True, stop=True),
                        reads=kt_r + list(k_h), writes=[k_ps[si]])
                    S.add("act", lambda e, si=si, pti=pti: e.activation(
                        out=pT[pti][:, :, :], in_=psum[si][:, :].rearrange("k (r t) -> k r t", r=4), func=AF.Exp),
                        reads=[k_ps[si]], writes=[k_pT[pti]])
                    if mk[0] == "sb":
                        m_ap = mask_sb[mk[1]][:, :].unsqueeze(1).to_broadcast([128, 4, 128])
                        m_r = [k_mask[mk[1]]]
                    else:
                        m_ap = psum[psm][:, g * 128:(g + 1) * 128].unsqueeze(1).to_broadcast([128, 4, 128])
                        m_r = [k_ps[psm]]
                    S.add("dve", lambda e, pti=pti, m_ap=m_ap: e.tensor_tensor(
                        out=pT[pti][:, :, :], in0=pT[pti][:, :, :], in1=m_ap, op=ALU.mult),
                        reads=[k_pT[pti]] + m_r, writes=[k_pT[pti]])
                    if kind == "s" and slot == i:
                        S.add("dve", lambda e, pti=pti: e.tensor_tensor(
                            out=pT[pti][:, :, :], in0=pT[pti][:, :, :],
                            in1=tri_sb[:, :].unsqueeze(1).to_broadcast([128, 4, 128]), op=ALU.mult),
                            reads=[k_pT[pti], k_const], writes=[k_pT[pti]])
                return cx

            def back(cx):
                slot, kb, first, lastk = cx["slot"], cx["kb"], cx["first"], cx["lastk"]
                for g in gs:
                    pti = cx["pti"][g]
                    ai = PS_ACC[g % 2]
                    ncol = 128 if kind == "c" else 65
                    for r in range(4):
                        if kind == "c":
                            rhs = vcM[:, slot, g, :]
                        else:
                            rhs = vbuf[kb][:, g, :]
                        S.add("pe", lambda e, ai=ai, pti=pti, r=r, rhs=rhs, ncol=ncol, first=first, lastk=lastk: e.matmul(
                            psum[ai][:, r * ncol:(r + 1) * ncol], pT[pti][:, r, :], rhs, start=(first and r == 0), stop=lastk,
                            skip_group_check=True),
                            reads=[k_pT[pti]] + cx["vt_r"], writes=[k_ps[ai]])

            pend = None
            for ki_, slot in enumerate(slots):
                cx = front(ki_, slot)
                if pend is not None:
                    back(pend)
                pend = cx
            back(pend)
            for g in gs:
                ai = PS_ACC[g % 2]
                if kind == "c":
                    acc3 = psum[ai][:, :].rearrange("t (r c) -> t r c", r=4)
                    S.add("dve", lambda e, acc3=acc3: e.tensor_reduce(
                        out=rsum[:, :], in_=acc3[:, :, 64:128], axis=AX.X, op=ALU.add),
                        reads=[k_ps[ai]], writes=[k_rsum])
                    S.add("dve", lambda e: e.tensor_scalar(out=rsum[:, :], in0=rsum[:, :], scalar1=0.5, scalar2=1e-30,
                                                           op0=ALU.mult, op1=ALU.max), reads=[k_rsum], writes=[k_rsum])
                else:
                    acc3 = psum[ai][:, 0:260].rearrange("t (r c) -> t r c", r=4)
                    S.add("dve", lambda e, acc3=acc3: e.tensor_scalar(
                        out=rsum[:, :], in0=acc3[:, :, 64], scalar1=1e-30, scalar2=None, op0=ALU.max),
                        reads=[k_ps[ai]], writes=[k_rsum])
                S.add("dve", lambda e: e.reciprocal(out=rsum[:, :], in_=rsum[:, :]), reads=[k_rsum], writes=[k_rsum])
                if kind == "c":
                    for r in range(4):
                        if r == 0:
                            S.add("dve", lambda e, acc3=acc3, g=g: e.tensor_scalar(
                                out=pri[:, g, :], in0=acc3[:, 0, 64:128], scalar1=rsum[:, 0:1], scalar2=None, op0=ALU.mult),
                                reads=[k_ps[ai], k_rsum], writes=[k_pri])
                        else:
                            S.add("dve", lambda e, acc3=acc3, g=g, r=r: e.scalar_tensor_tensor(
                                out=pri[:, g, :], in0=acc3[:, r, 64:128], scalar=rsum[:, r:r + 1], in1=pri[:, g, :],
                                op0=ALU.mult, op1=ALU.add), reads=[k_ps[ai], k_rsum, k_pri], writes=[k_pri])
                S.add("dve", lambda e, g=g, br=br: e.tensor_tensor(
                    out=fsc[:, :], in0=rsum[:, :],
                    in1=gate_sb[:, tl, :].rearrange("t (h b) -> t h b", b=3)[:, 4 * g:4 * g + 4, br], op=ALU.mult),
                    reads=[k_rsum, k_gate], writes=[k_fsc])
                if br == 0:
                    S.add("dve", lambda e, acc3=acc3, g=g: e.tensor_tensor(
                        out=oacc[:, 4 * g:4 * g + 4, :], in0=acc3[:, :, 0:64],
                        in1=fsc[:, :].unsqueeze(2).to_broadcast([128, 4, 64]), op=ALU.mult),
                        reads=[k_ps[ai], k_fsc], writes=[k_oacc])
                else:
                    S.add("dve", lambda e, acc3=acc3: e.tensor_tensor(
                        out=otmp[:, :, :], in0=acc3[:, :, 0:64],
                        in1=fsc[:, :].unsqueeze(2).to_broadcast([128, 4, 64]), op=ALU.mult),
                        reads=[k_ps[ai], k_fsc], writes=[k_otmp])
                    S.add("dve", lambda e, g=g: e.tensor_tensor(
                        out=oacc[:, 4 * g:4 * g + 4, :], in0=oacc[:, 4 * g:4 * g + 4, :], in1=otmp[:, :, :], op=ALU.add),
                        reads=[k_otmp, k_oacc], writes=[k_oacc])

        for gp in range(2):
            branch_core(gp, 0, [0, 1], "c")
        for g in range(4):
            S.add("dve", lambda e, g=g: e.tensor_tensor(out=pri[:, g, :], in0=pri[:, g, :], in1=privn_sb[pq][:, :], op=ALU.mult),
                  reads=[k_pri, k_privn[pq]], writes=[k_pri])
            S.add("dve", lambda e, g=g: e.tensor_tensor(out=pri[:, g, :], in0=pri[:, g, :], in1=pribias_sb[pq][:, :], op=ALU.add),
                  reads=[k_pri, k_privn[pq]], writes=[k_pri])
            S.add("dve", lambda e, g=g: e.max(out=m8a[:, :], in_=pri[:, g, :]), reads=[k_pri], writes=[k_m8])
            S.add("dve", lambda e, g=g: e.match_replace(out=pri2[:, :], in_to_replace=m8a[:, :], in_values=pri[:, g, :], imm_value=-1e30),
                  reads=[k_pri, k_m8], writes=[k_pri2])
            S.add("dve", lambda e: e.max(out=m8b[:, :], in_=pri2[:, :]), reads=[k_pri2], writes=[k_m8])
            S.add("dve", lambda e: e.tensor_scalar(out=m8b[:, 7:8], in0=m8b[:, 7:8], scalar1=0.0, scalar2=None, op0=ALU.max),
                  reads=[k_m8], writes=[k_m8])
            S.add("dve", lambda e, g=g: e.tensor_scalar(out=sel_sb[:, g, :], in0=pri[:, g, :], scalar1=m8b[:, 7:8], scalar2=None, op0=ALU.is_ge),
                  reads=[k_pri, k_m8], writes=[k_sel])
        PS_X = PS_S[rot("psS", 4)]
        psx_bf = psum[PS_X][:, :].bitcast(BF16)
        for g in range(4):
            S.add("pe", lambda e, g=g: e.transpose(psx_bf[0:64, g * 128:(g + 1) * 128], sel_sb[:, g, :], identb[:, :]),
                  reads=[k_sel, k_const], writes=[k_ps[PS_X]])
        S.add("act", lambda e: e.activation(out=selT[:, :, :], in_=psx_bf[0:64, 0:512].rearrange("s (g t) -> s g t", g=4), func=AF.Copy),
              reads=[k_ps[PS_X]], writes=[k_selT])
        sel_slots = list(range(0, i + 1)) + list(range(16, 32))
        for gp in range(2):
            branch_core(gp, 1, sel_slots, "s")
        if i >= 4:
            win_slots = list(range(i - 4, i + 1))
        else:
            win_slots = list(range(28 + i, 32)) + list(range(0, i + 1))
        for gp in range(2):
            branch_core(gp, 2, win_slots, "w")
        S.add("act", lambda e: e.activation(out=obf[:, :], in_=oacc[:, :, :].rearrange("t h d -> t (h d)"), func=AF.Copy),
              reads=[k_oacc], writes=[k_obf])
        PS_X2 = PS_S[rot("psS", 4)]
        psx2_bf = psum[PS_X2][:, :].bitcast(BF16)
        for fc in range(8):
            S.add("pe", lambda e, fc=fc: e.transpose(psx2_bf[:, fc * 128:(fc + 1) * 128], obf[:, fc * 128:(fc + 1) * 128], identb[:, :]),
                  reads=[k_obf, k_const], writes=[k_ps[PS_X2]])
        S.add("dve", lambda e: e.tensor_copy(out=uT[:, :, 16 + c0:16 + c0 + 128],
                                             in_=psx2_bf[:, :].rearrange("f (c t) -> f c t", c=8)),
              reads=[k_ps[PS_X2]], writes=[k_uT])

    def sample_attention():
        P_ = NS
        PS_X = PS_S[0]
        psx_bf = psum[PS_X][:, :].bitcast(BF16)

        def run_branch(gp, br, kind):
            gs = [2 * gp, 2 * gp + 1]
            started = {g: False for g in gs}
            steps = []
            for s_ in range(NS):
                if kind == "c":
                    tiles = [("c", 0)]
                elif kind == "s":
                    tiles = [("k", kt) for kt in range(16)] + [("n", 0)]
                else:
                    tiles = [("k", t_) for t_ in range(4)] + [("n", 1)]
                for (tk, ti_) in tiles:
                    steps.append((s_, tk, ti_))

            def front(n_, s_, tk, ti_):
                cx = {"s": s_, "nk": 128, "pz": {}}
                psm = PS_M[n_ % 2]
                if tk == "c":
                    ci = rot("kc2", 2)
                    S.add("sp", lambda e, ci=ci, s_=s_: e.dma_start(out=kcTs[ci][0:64, :, :], in_=kcTS_d[s_]),
                          reads=[k_sscr], writes=[k_kcTs[ci]], dma=True)
                    S.add("sp", lambda e, ci=ci, s_=s_: e.dma_start(out=vcMs[ci][:, :, 0:64], in_=vcS_d[s_]),
                          reads=[k_sscr], writes=[k_vcMs[ci]], dma=True)
                    kt_r, cx["vt_r"] = [k_kcTs[ci]], [k_vcMs[ci]]
                    klhs = lambda g, ci=ci: kcTs[ci][:, g, :]
                    cx["vrhs"] = lambda g, ci=ci: vcMs[ci][:, g, :]
                    mcol = maskS[:, 0:1]
                    m_r = [k_const]
                elif tk == "k":
                    kb = rot("kb", 3)
                    ksrc = KsTS_d if kind == "s" else KwTS_d
                    vsrc = VsS_d if kind == "s" else VwS_d
                    S.add("sp", lambda e, kb=kb, ksrc=ksrc, s_=s_, ti_=ti_: e.dma_start(out=kbuf[kb][:, :, :], in_=ksrc[s_, ti_]),
                          reads=[k_sscr], writes=[k_kbuf[kb]], dma=True)
                    S.add("sp", lambda e, kb=kb, vsrc=vsrc, s_=s_, ti_=ti_: e.dma_start(out=vbuf[kb][:, :, :], in_=vsrc[s_, ti_]),
                          reads=[k_sscr], writes=[k_vbuf[kb]], dma=True)
                    kt_r, cx["vt_r"] = [k_kbuf[kb]], [k_vbuf[kb]]
                    klhs = lambda g, kb=kb: kbuf[kb][:, g, :]
                    cx["vrhs"] = lambda g, kb=kb: vbuf[kb][:, g, :]
                    if kind == "w":
                        mcol = maskS[:, 1 + ti_:2 + ti_]
                        m_r = [k_const]
                    else:
                        for g in gs:
                            S.add("pe", lambda e, g=g, ti_=ti_, s_=s_, psm=psm: e.matmul(
                                psum[psm][:, g:g + 1], esel_sb[:, ti_ * 128:(ti_ + 1) * 128], selTs[:, g, s_:s_ + 1],
                                start=True, stop=True), reads=[k_selTs, k_const], writes=[k_ps[psm]])
                        mcol = None
                        m_r = [k_ps[psm]]
                else:
                    cx["nk"] = NS
                    bi_ = ti_
                    kt_r, cx["vt_r"] = [k_knT], [k_vnew]
                    klhs = lambda g, bi_=bi_: knT[:, bi_, g, :]
                    cx["vrhs"] = lambda g, bi_=bi_: vnew[:, bi_, g, :]
                    mcol = ident[0:NS, s_:s_ + 1]
                    m_r = [k_const]
                nk = cx["nk"]
                for g in gs:
                    si = PS_S[rot("psS", 4)]
                    pz = rot("pz", 4)
                    cx["pz"][g] = pz
                    S.add("pe", lambda e, si=si, g=g, klhs=klhs, nk=nk, s_=s_: e.matmul(
                        psum[si][0:nk, 0:4], klhs(g), QTs[0:70, g, :, s_], start=True, stop=True),
                        reads=kt_r + [k_QTs], writes=[k_ps[si]])
                    S.add("act", lambda e, si=si, pz=pz, nk=nk, s_=s_: e.activation(
                        out=Pz[pz][0:nk, :, s_], in_=psum[si][0:nk, 0:4], func=AF.Exp),
                        reads=[k_ps[si]], writes=[k_Pz[pz]])
                    mc = mcol if mcol is not None else psum[psm][:, g:g + 1]
                    S.add("dve", lambda e, pz=pz, nk=nk, s_=s_, mc=mc: e.tensor_scalar(
                        out=Pz[pz][0:nk, :, s_], in0=Pz[pz][0:nk, :, s_], scalar1=mc[0:nk, :] if mc.shape[0] != nk else mc, scalar2=None, op0=ALU.mult),
                        reads=[k_Pz[pz]] + m_r, writes=[k_Pz[pz]])
                return cx

            def back(cx):
                s_, nk = cx["s"], cx["nk"]
                for g in gs:
                    pz = cx["pz"][g]
                    vr = cx["vrhs"](g)
                    ai = PS_ACC[g % 2]
                    ncol = 128 if kind == "c" else 65
                    for r in range(4):
                        st_flag = (not started[g])
                        started[g] = True
                        S.add("pe", lambda e, ai=ai, pz=pz, r=r, vr=vr, ncol=ncol, nk=nk, st_flag=st_flag: e.matmul(
                            psum[ai][0:NS, r * ncol:(r + 1) * ncol], Pz[pz][0:nk, r, :], vr[0:nk, :] if nk != 128 else vr,
                            start=st_flag, stop=False, skip_group_check=True),
                            reads=[k_Pz[pz]] + cx["vt_r"], writes=[k_ps[ai]])
                    S.add("dve", lambda e, pz=pz, nk=nk, s_=s_: e.memset(Pz[pz][0:nk, :, s_], 0.0),
                          writes=[k_Pz[pz]])

            pend = None
            for n_, (s_, tk, ti_) in enumerate(steps):
                cx = front(n_, s_, tk, ti_)
                if pend is not None:
                    back(pend)
                pend = cx
            back(pend)
            for g in gs:
                ai = PS_ACC[g % 2]
                if kind == "c":
                    acc3 = psum[ai][0:P_, :].rearrange("t (r c) -> t r c", r=4)
                    S.add("dve", lambda e, acc3=acc3: e.tensor_reduce(
                        out=rsum[0:P_, :], in_=acc3[:, :, 64:128], axis=AX.X, op=ALU.add), reads=[k_ps[ai]], writes=[k_rsum])
                    S.add("dve", lambda e: e.tensor_scalar(out=rsum[0:P_, :], in0=rsum[0:P_, :], scalar1=0.5, scalar2=1e-30,
                                                           op0=ALU.mult, op1=ALU.max), reads=[k_rsum], writes=[k_rsum])
                else:
                    acc3 = psum[ai][0:P_, 0:260].rearrange("t (r c) -> t r c", r=4)
                    S.add("dve", lambda e, acc3=acc3: e.tensor_scalar(
                        out=rsum[0:P_, :], in0=acc3[:, :, 64], scalar1=1e-30, scalar2=None, op0=ALU.max),
                        reads=[k_ps[ai]], writes=[k_rsum])
                S.add("dve", lambda e: e.reciprocal(out=rsum[0:P_, :], in_=rsum[0:P_, :]), reads=[k_rsum], writes=[k_rsum])
                if kind == "c":
                    for r in range(4):
                        if r == 0:
                            S.add("dve", lambda e, acc3=acc3, g=g: e.tensor_scalar(
                                out=pri[0:P_, g, :], in0=acc3[:, 0, 64:128], scalar1=rsum[0:P_, 0:1], scalar2=None, op0=ALU.mult),
                                reads=[k_ps[ai], k_rsum], writes=[k_pri])
                        else:
                            S.add("dve", lambda e, acc3=acc3, g=g, r=r: e.scalar_tensor_tensor(
                                out=pri[0:P_, g, :], in0=acc3[:, r, 64:128], scalar=rsum[0:P_, r:r + 1], in1=pri[0:P_, g, :],
                                op0=ALU.mult, op1=ALU.add), reads=[k_ps[ai], k_rsum, k_pri], writes=[k_pri])
                S.add("dve", lambda e, g=g, br=br: e.tensor_tensor(
                    out=fsc[0:P_, :], in0=rsum[0:P_, :],
                    in1=gate_s[:, :].rearrange("t (h b) -> t h b", b=3)[:, 4 * g:4 * g + 4, br], op=ALU.mult),
                    reads=[k_rsum, k_gates], writes=[k_fsc])
                if br == 0:
                    S.add("dve", lambda e, acc3=acc3, g=g: e.tensor_tensor(
                        out=oacc[0:P_, 4 * g:4 * g + 4, :], in0=acc3[:, :, 0:64],
                        in1=fsc[0:P_, :].unsqueeze(2).to_broadcast([P_, 4, 64]), op=ALU.mult),
                        reads=[k_ps[ai], k_fsc], writes=[k_oacc])
                else:
                    S.add("dve", lambda e, acc3=acc3: e.tensor_tensor(
                        out=otmp[0:P_, :, :], in0=acc3[:, :, 0:64],
                        in1=fsc[0:P_, :].unsqueeze(2).to_broadcast([P_, 4, 64]), op=ALU.mult),
                        reads=[k_ps[ai], k_fsc], writes=[k_otmp])
                    S.add("dve", lambda e, g=g: e.tensor_tensor(
                        out=oacc[0:P_, 4 * g:4 * g + 4, :], in0=oacc[0:P_, 4 * g:4 * g + 4, :], in1=otmp[0:P_, :, :], op=ALU.add),
                        reads=[k_otmp, k_oacc], writes=[k_oacc])

        for gp in range(2):
            run_branch(gp, 0, "c")
        for g in range(4):
            S.add("dve", lambda e, g=g: e.tensor_tensor(out=pri[0:P_, g, :], in0=pri[0:P_, g, :], in1=privnS[:, :], op=ALU.mult),
                  reads=[k_pri, k_const], writes=[k_pri])
            S.add("dve", lambda e, g=g: e.tensor_tensor(out=pri[0:P_, g, :], in0=pri[0:P_, g, :], in1=pribiasS[:, :], op=ALU.add),
                  reads=[k_pri, k_const], writes=[k_pri])
            S.add("dve", lambda e, g=g: e.max(out=m8a[0:P_, :], in_=pri[0:P_, g, :]), reads=[k_pri], writes=[k_m8])
            S.add("dve", lambda e, g=g: e.match_replace(out=pri2[0:P_, :], in_to_replace=m8a[0:P_, :], in_values=pri[0:P_, g, :], imm_value=-1e30),
                  reads=[k_pri, k_m8], writes=[k_pri2])
            S.add("dve", lambda e: e.max(out=m8b[0:P_, :], in_=pri2[0:P_, :]), reads=[k_pri2], writes=[k_m8])
            S.add("dve", lambda e: e.tensor_scalar(out=m8b[0:P_, 7:8], in0=m8b[0:P_, 7:8], scalar1=0.0, scalar2=None, op0=ALU.max),
                  reads=[k_m8], writes=[k_m8])
            S.add("dve", lambda e, g=g: e.tensor_scalar(out=sel_sb[0:P_, g, :], in0=pri[0:P_, g, :], scalar1=m8b[0:P_, 7:8], scalar2=None, op0=ALU.is_ge),
                  reads=[k_pri, k_m8], writes=[k_sel])
        for g in range(4):
            S.add("pe", lambda e, g=g: e.transpose(psx_bf[0:64, g * NS:(g + 1) * NS], sel_sb[0:P_, g, :], identb[0:P_, 0:P_]),
                  reads=[k_sel, k_const], writes=[k_ps[PS_X]])
        S.add("act", lambda e: e.activation(out=selTs[:, :, :], in_=psx_bf[0:64, 0:4 * NS].rearrange("s (g t) -> s g t", g=4), func=AF.Copy),
              reads=[k_ps[PS_X]], writes=[k_selTs])
        for gp in range(2):
            run_branch(gp, 1, "s")
        for gp in range(2):
            run_branch(gp, 2, "w")
        S.add("act", lambda e: e.activation(out=obf[0:P_, :], in_=oacc[0:P_, :, :].rearrange("t h d -> t (h d)"), func=AF.Copy),
              reads=[k_oacc], writes=[k_obf])
        for fc in range(8):
            S.add("pe", lambda e, fc=fc: e.transpose(psx_bf[:, fc * NS:(fc + 1) * NS], obf[0:P_, fc * 128:(fc + 1) * 128], identb[0:P_, 0:P_]),
                  reads=[k_obf, k_const], writes=[k_ps[PS_X]])
        S.add("dve", lambda e: e.tensor_copy(out=uT[:, :, NCOL:NCX], in_=psx_bf[:, 0:8 * NS].rearrange("f (c t) -> f c t", c=8)),
              reads=[k_ps[PS_X]], writes=[k_uT])

    if STAGE >= 2 and not _DEV.get("prep_only"):
        for ogi in range(2):
            has_s = (ogi == 0)
            RNG = MAINR + ([SR] if has_s else [])
            S.add("sp", lambda e, ogi=ogi: e.dma_start(
                out=xT[:, :, 16:NCX], in_=x1_d[ogi].rearrange("p (a b) -> p a b", a=8)),
                reads=[k_x1[ogi]], writes=list(k_xT), dma=True)
            if STAGE >= 3:
                norm(VGQ, MAINR)
                wq = [wload(w_qg.rearrange("(dc p) f -> p dc f", p=128)[:, :, cb * 512:(cb + 1) * 512], (8, 512)) for cb in range(2)]
                wgi, wgv = wload(w_qg.rearrange("(dc p) f -> p dc f", p=128)[:, :, 1024:1072], (8, 48))
                for h in range(16):
                    g, r = h // 4, h % 4
                    wi, wv = wq[h // 8]
                    for th in range(2):
                        pi = next_ps()
                        cc = 16 + th * 512
                        for dc in range(8):
                            S.add("pe", lambda e, pi=pi, wv=wv, dc=dc, h=h, cc=cc: e.matmul(
                                psum[pi][0:64, :], wv[:, dc, (h % 8) * 64:(h % 8 + 1) * 64], uT[:, dc, cc:cc + 512],
                                start=(dc == 0), stop=(dc == 7)), reads=[k_wb[wi], k_uT], writes=[k_ps[pi]])
                        S.add("act", lambda e, pi=pi, g=g, r=r, th=th: e.activation(
                            out=QT[0:64, g, r, th * 512:(th + 1) * 512], in_=psum[pi][0:64, :], func=AF.Copy, scale=0.125),
                            reads=[k_ps[pi]], writes=list(k_h))
                    if has_s and STAGE >= 4:
                        pi = next_ps()
                        for dc in range(8):
                            S.add("pe", lambda e, pi=pi, wv=wv, dc=dc, h=h: e.matmul(
                                psum[pi][0:64, 0:NS], wv[:, dc, (h % 8) * 64:(h % 8 + 1) * 64], uT[:, dc, NCOL:NCX],
                                start=(dc == 0), stop=(dc == 7)), reads=[k_wb[wi], k_uT], writes=[k_ps[pi]])
                        S.add("act", lambda e, pi=pi, g=g, r=r: e.activation(
                            out=QTs[0:64, g, r, :], in_=psum[pi][0:64, 0:NS], func=AF.Copy, scale=0.125),
                            reads=[k_ps[pi]], writes=[k_QTs])
                S.add("sp", lambda e, ogi=ogi: e.dma_start(
                    out=QT[64:70, :, :, :], in_=qaug_d[:, :, :, ogi * GT:(ogi + 1) * GT]), writes=list(k_h), dma=True)
                for tl in range(8):
                    pi = next_ps()
                    cc = 16 + tl * 128
                    for dc in range(8):
                        S.add("pe", lambda e, pi=pi, dc=dc, cc=cc: e.matmul(
                            psum[pi][:, 0:48], uT[:, dc, cc:cc + 128], wgv[:, dc, :], start=(dc == 0), stop=(dc == 7)),
                            reads=[k_wb[wgi], k_uT], writes=[k_ps[pi]])
                    S.add("dve", lambda e, pi=pi, tl=tl: e.tensor_tensor(
                        out=gate_sb[:, tl, :], in0=psum[pi][:, 0:48], in1=bg_sb[:, :], op=ALU.add),
                        reads=[k_ps[pi], k_const], writes=[k_gate])
                S.add("act", lambda e: e.activation(out=gate_sb[:, :, :], in_=gate_sb[:, :, :], func=AF.Sigmoid),
                      reads=[k_gate], writes=[k_gate])
                if has_s and STAGE >= 4:
                    pi = next_ps()
                    for dc in range(8):
                        S.add("pe", lambda e, pi=pi, dc=dc: e.matmul(
                            psum[pi][0:NS, 0:48], uT[:, dc, NCOL:NCX], wgv[:, dc, :], start=(dc == 0), stop=(dc == 7)),
                            reads=[k_wb[wgi], k_uT], writes=[k_ps[pi]])
                    S.add("dve", lambda e, pi=pi: e.tensor_tensor(
                        out=gate_s[:, :], in0=psum[pi][0:NS, 0:48], in1=bg_sb[0:NS, :], op=ALU.add),
                        reads=[k_ps[pi], k_const], writes=[k_gates])
                    S.add("act", lambda e: e.activation(out=gate_s[:, :], in_=gate_s[:, :], func=AF.Sigmoid),
                          reads=[k_gates], writes=[k_gates])
                for tl in range(8):
                    attention_tile(ogi * 8 + tl, ogi)
                if has_s and STAGE >= 4 and not _DEV.get("skip_sa"):
                    sample_attention()
                wo = [wload(w_o.rearrange("(fc p) f -> p fc f", p=128)[:, :, cb * 512:(cb + 1) * 512], (8, 512)) for cb in range(2)]
                for dmc in range(8):
                    wi, wv = wo[dmc // 4]
                    for th in range(2):
                        pi = next_ps()
                        cc = 16 + th * 512
                        for fc in range(8):
                            S.add("pe", lambda e, pi=pi, wv=wv, fc=fc, dmc=dmc, cc=cc: e.matmul(
                                psum[pi][:, :], wv[:, fc, (dmc % 4) * 128:(dmc % 4 + 1) * 128], uT[:, fc, cc:cc + 512],
                                start=(fc == 0), stop=(fc == 7)), reads=[k_wb[wi], k_uT], writes=[k_ps[pi]])
                        S.add("dve", lambda e, pi=pi, dmc=dmc, cc=cc: e.tensor_tensor(
                            out=xT[:, dmc, cc:cc + 512], in0=xT[:, dmc, cc:cc + 512], in1=psum[pi][:, :], op=ALU.add),
                            reads=[k_ps[pi], k_xT[dmc]], writes=[k_xT[dmc]])
                    if has_s and STAGE >= 4:
                        pi = next_ps()
                        for fc in range(8):
                            S.add("pe", lambda e, pi=pi, wv=wv, fc=fc, dmc=dmc: e.matmul(
                                psum[pi][:, 0:NS], wv[:, fc, (dmc % 4) * 128:(dmc % 4 + 1) * 128], uT[:, fc, NCOL:NCX],
                                start=(fc == 0), stop=(fc == 7)), reads=[k_wb[wi], k_uT], writes=[k_ps[pi]])
                        S.add("dve", lambda e, pi=pi, dmc=dmc: e.tensor_tensor(
                            out=xT[:, dmc, NCOL:NCX], in0=xT[:, dmc, NCOL:NCX], in1=psum[pi][:, 0:NS], op=ALU.add),
                            reads=[k_ps[pi], k_xT[dmc]], writes=[k_xT[dmc]])
            norm(VG1B, RNG)
            mlp(1, has_s)
            norm(VGF, RNG, final=True)
            if has_s:
                rows16_out(xT[:, :, NCOL:NCX], list(k_xT), y_s_out[:, :])
            for tl in range(8):
                yi = rot("yst", 2)
                cc = 16 + tl * 128
                for hb in range(2):
                    pi = next_ps()
                    for j in range(4):
                        dc = hb * 4 + j
                        S.add("pe", lambda e, pi=pi, j=j, dc=dc, cc=cc: e.transpose(
                            psum[pi][:, j * 128:(j + 1) * 128], xT[:, dc, cc:cc + 128], ident[:, :]),
                            reads=[k_xT[dc], k_const], writes=[k_ps[pi]])
                    if hb == 0:
                        S.add("act", lambda e, pi=pi, yi=yi: e.activation(out=yst[yi][:, 0:512], in_=psum[pi][:, :], func=AF.Copy),
                              reads=[k_ps[pi]], writes=[k_yst[yi]])
                    else:
                        S.add("dve", lambda e, pi=pi, yi=yi: e.tensor_copy(out=yst[yi][:, 512:1024], in_=psum[pi][:, :]),
                              reads=[k_ps[pi]], writes=[k_yst[yi]])
                row0 = ogi * GT + tl * 128
                S.add("sp", lambda e, yi=yi, row0=row0: e.dma_start(out=y_out[row0:row0 + 128, :], in_=yst[yi][:, :]),
                      reads=[k_yst[yi]], writes=[k_out], dma=True)

    S.prepare(nc)
    with nc.Block() as block:
        S.emit(block)
    S._st.close()
    es.close()
    return nc


def _bf(x):
    return np.asarray(x, np.float32).astype(NPBF)


def _hilo(x):
    x = np.asarray(x, np.float32)
    hi = x.astype(NPBF).astype(np.float32)
    lo = (x - hi).astype(NPBF).astype(np.float32)
    return hi, lo


def core_meta(half):
    seqtile = np.concatenate([16 * half + np.arange(16), 16 * (1 - half) + np.arange(16)])
    posk = (seqtile[None, :] * 128 + np.arange(128)[:, None]).astype(np.float32)
    other_visible = (half == 1)
    eff = posk.copy()
    if not other_visible:
        eff[:, 16:] = NEGPOS
    hi, lo = _hilo(eff)
    kaug = np.zeros((32, 6, 128), np.float32)
    kaug[:, 0] = hi.T; kaug[:, 1] = lo.T; kaug[:, 2] = hi.T; kaug[:, 3] = lo.T; kaug[:, 4] = 1.0; kaug[:, 5] = 1.0
    posq = (half * 2048 + np.arange(2048)).astype(np.float32)
    slopes = np.exp2(-8.0 * (np.arange(16, dtype=np.float32) + 1.0) / 16).astype(np.float32)
    shi, slo = _hilo(slopes)
    tref = (half * 2048 + (np.arange(2048) // 128) * 128 + 64).astype(np.float32)
    qaug = np.zeros((6, 4, 4, 2048), np.float32)
    for g in range(4):
        for r in range(4):
            h = g * 4 + r
            c = (-slopes[h] * tref).astype(np.float32)
            chi, clo = _hilo(c)
            qaug[0, g, r] = shi[h]; qaug[1, g, r] = shi[h]; qaug[2, g, r] = slo[h]; qaug[3, g, r] = slo[h]
            qaug[4, g, r] = chi; qaug[5, g, r] = clo
    seqsub = (seqtile[:, None] * 8 + np.arange(8)[None, :]).reshape(-1)
    nxt = np.roll(seqsub, -1)
    valid = (nxt == seqsub + 1)
    cend_true = np.where(valid, seqsub * 16 + 31, 10 ** 9).astype(np.float64)
    cend = cend_true.astype(np.float32).reshape(2, 128).T.copy()
    caug = np.zeros((16, 6, 256), np.float32)
    for i in range(16):
        tr = half * 2048 + i * 128 + 64
        e = np.where(valid, np.minimum(cend_true, tr + 63), NEGPOS).astype(np.float32)
        ehi, elo = _hilo(e)
        caug[i, 0] = ehi; caug[i, 1] = elo; caug[i, 2] = ehi; caug[i, 3] = elo; caug[i, 4] = 1.0; caug[i, 5] = 1.0
    seqblk = (seqtile[:, None] * 2 + np.arange(2)[None, :]).reshape(-1)
    blk2slot = np.zeros(64, np.int64)
    blk2slot[seqblk] = np.arange(64)
    mc2s = np.zeros((256, 64), np.float32)
    for n in range(256):
        if valid[n]:
            nb = seqsub[n]
            for k in range(2):
                sb_ = ((nb + k) * 16) // 64
                mc2s[n, blk2slot[sb_]] += 1.0
    mc2s = mc2s.reshape(2, 128, 64).transpose(1, 0, 2).copy()
    tq = posq.astype(np.int64)
    cur = tq // 64
    bs = seqblk[None, :]
    validb = (bs * 64 <= tq[:, None])
    forced = (bs == 0) | (bs == cur[:, None]) | (bs == cur[:, None] - 1)
    privn = (validb & ~forced).astype(np.float32)
    pribias = np.where(validb, np.where(forced, 1e6, 0.0), -1.0).astype(np.float32)
    esel = (np.arange(4096)[None, :] // 64 == np.arange(64)[:, None]).astype(np.float32)
    tri = (np.arange(128)[:, None] <= np.arange(128)[None, :]).astype(np.float32)
    return {
        "kaug": _bf(kaug), "qaug": _bf(qaug), "caug": _bf(caug), "posq": posq, "posk": posk, "cend": cend,
        "privn": privn, "pribias": pribias, "mc2s": _bf(mc2s), "esel": _bf(esel), "tri": _bf(tri),
        "identb": _bf(np.eye(128)), "ident": np.eye(128, dtype=np.float32),
    }


def sample_meta():
    slopes = np.exp2(-8.0 * (np.arange(16, dtype=np.float32) + 1.0) / 16).astype(np.float32)
    shi, slo = _hilo(slopes)
    pos = np.zeros((21, 128), np.float32)
    for kt in range(16):
        pos[kt] = kt * 128 + np.arange(128)
    for t in range(4):
        pos[16 + t] = 1536 + t * 128 + np.arange(128)
    pos[20] = 2048.0
    hi, lo = _hilo(pos)
    kaugS = np.zeros((21, 6, 128), np.float32)
    kaugS[:, 0] = hi; kaugS[:, 1] = lo; kaugS[:, 2] = hi; kaugS[:, 3] = lo; kaugS[:, 4] = 1.0; kaugS[:, 5] = 1.0
    qaugS = np.zeros((6, 4, 4, 16), np.float32)
    for g in range(4):
        for r in range(4):
            h = g * 4 + r
            chi, clo = _hilo(np.float32(-slopes[h] * 2048.0))
            qaugS[0, g, r] = shi[h]; qaugS[1, g, r] = shi[h]; qaugS[2, g, r] = slo[h]; qaugS[3, g, r] = slo[h]
            qaugS[4, g, r] = chi; qaugS[5, g, r] = clo
    cend = np.where(np.arange(128) < 127, np.arange(128) * 16 + 31, NEGPOS).astype(np.float32)
    ehi, elo = _hilo(cend)
    caugS = np.zeros((6, 128), np.float32)
    caugS[0] = ehi; caugS[1] = elo; caugS[2] = ehi; caugS[3] = elo; caugS[4] = 1.0; caugS[5] = 1.0
    maskS = np.zeros((128, 8), np.float32)
    maskS[:127, 0] = 1.0
    for t in range(4):
        d = 2048 - (1536 + t * 128 + np.arange(128))
        maskS[:, 1 + t] = ((d >= 0) & (d < 512)).astype(np.float32)
    blk = np.arange(64)
    valid = blk <= 32
    forced = (blk == 0) | (blk == 31) | (blk == 32)
    privn = np.tile((valid & ~forced).astype(np.float32)[None], (16, 1))
    pribias = np.tile(np.where(valid, np.where(forced, 1e6, 0.0), -1.0).astype(np.float32)[None], (16, 1))
    mc2s = np.zeros((128, 64), np.float32)
    for n in range(127):
        for k in range(2):
            mc2s[n, ((n + k) * 16) // 64] += 1.0
    return {"kaugS": _bf(kaugS), "qaugS": _bf(qaugS), "caugS": _bf(caugS), "maskS": maskS, "privnS": privn,
            "pribiasS": pribias, "mc2sS": _bf(mc2s), "iotap": np.arange(128, dtype=np.float32).reshape(128, 1)}


_NC_CACHE = {}


def kernel(**inp):
    f32 = lambda a: np.ascontiguousarray(np.asarray(a, dtype=np.float32))
    x_prompt = f32(inp["x_prompt"])
    vec_list = [inp["norm_mix"][0], inp["norm_mlp"][0], inp["norm_kv"], inp["pool_scale"][0],
                inp["norm_mix"][1], inp["norm_mlp"][1], inp["norm_final"]]
    vecs = np.ascontiguousarray(np.concatenate([f32(v).reshape(8, 128).T for v in vec_list], axis=1))
    shared = {
        "vecs": vecs, "w_up": f32(inp["w_up"]), "w_down": f32(inp["w_down"]), "pool_w": f32(inp["pool_w"])[0],
        "w_kv": f32(inp["w_kv"]), "w_qg": f32(inp["w_qg"])[0], "w_o": f32(inp["w_o"])[0], "b_gate": f32(inp["b_gate"]),
        "cmp_w1": f32(inp["cmp_w1"]), "cmp_w2": f32(inp["cmp_w2"]), "cmp_pe": f32(inp["cmp_pe"]),
    }
    metas = [core_meta(0), core_meta(1)]
    x_sample = f32(inp["x_sample"]).reshape(128, D)
    state_pool = f32(inp["state_pool"]).reshape(128, 15, D)
    state_win = f32(inp["state_win"]).reshape(128, 512, 512)
    selw = np.zeros((256, 4, 16), np.float32)
    for s_ in range(16):
        for r_ in range(15):
            for g_ in range(4):
                if r_ >= 16 - (2 << g_):
                    selw[s_ * 15 + r_, g_, s_] = 1.0
    selw = np.ascontiguousarray(selw.reshape(2, 128, 64).transpose(1, 0, 2))
    smeta = sample_meta()
    cache2d = f32(inp["cache_kv_pages"]).reshape(2560 * 128, 1024)
    page_table = np.ascontiguousarray(np.asarray(inp["page_table"], dtype=np.int32))
    in_maps = []
    for c in range(NCORES):
        b, half = c // 2, c % 2
        xg = np.zeros((4, NCOL, D), np.float32)
        corr = np.ones((4, 4, 16), np.float32)
        for gq, (slot0, is_own, ogi) in enumerate(L0_GROUPS):
            hf = half if is_own else 1 - half
            st = hf * 2048 + (slot0 % 16) * 128
            if st > 0:
                xg[gq, 1:16] = x_prompt[b, st - 15:st]
            xg[gq, 16:] = x_prompt[b, st:st + GT]
            for g in range(4):
                w = 2 << g
                t = st + np.arange(16)
                corr[gq, g] = w / np.minimum(t + 1, w)
        m = dict(shared)
        m.update(metas[half])
        m["xs"] = np.ascontiguousarray(x_sample[16 * c:16 * c + 16])
        m["spool"] = np.ascontiguousarray(state_pool[16 * c:16 * c + 16].reshape(240, D))
        m["swin"] = np.ascontiguousarray(state_win[16 * c:16 * c + 16])
        m["selw"] = selw
        m.update(smeta)
        m["cache"] = cache2d
        m["pt"] = np.ascontiguousarray(page_table[16 * c:16 * c + 16])
        m["xg"] = xg
        m["corr"] = corr.reshape(4, 64)
        in_maps.append(m)
    if "nc" not in _NC_CACHE:
        _NC_CACHE["nc"] = build_nc()
    nc = _NC_CACHE["nc"]
    res = run_bass_kernel_spmd(nc, in_maps, core_ids=list(range(NCORES)))
    R = res.results
    y_prompt = np.zeros((NB, SEQ, D), np.float32)
    y_sample = np.zeros((128, 1, D), np.float32)
    pool_prompt = np.zeros((NB, 1, 15, D), np.float32)
    pool_sample = np.zeros((128, 1, 15, D), np.float32)
    kv_rows_prompt = np.zeros((NB, SEQ, 2, 2, 4, 64), np.float32)
    kv_rows_sample = np.zeros((128, 1, 2, 2, 4, 64), np.float32)
    win_prompt = np.zeros((NB, 512, 2, 4, 64), np.float32)
    win_sample = np.zeros((128, 512, 2, 4, 64), np.float32)
    for c in range(NCORES):
        b, half = c // 2, c % 2
        kv_rows_prompt[b, half * 2048:(half + 1) * 2048] = R[c]["kv_out"].reshape(2048, 2, 2, 4, 64)
        y_sample[16 * c:16 * c + 16, 0] = R[c]["y_s_out"]
        pool_sample[16 * c:16 * c + 16, 0] = R[c]["pool_s_out"]
        kv_rows_sample[16 * c:16 * c + 16, 0] = R[c]["kv_s_out"].reshape(16, 2, 2, 4, 64)
        win_sample[16 * c:16 * c + 16] = R[c]["win_s_out"].reshape(16, 512, 2, 4, 64)
        y_prompt[b, half * 2048:(half + 1) * 2048] = R[c]["y_out"]
        if half == 1:
            win_prompt[b] = R[c]["win_out"].reshape(512, 2, 4, 64)
            pool_prompt[b, 0] = R[c]["pool_out"][1:16]
    return (y_prompt, y_sample, pool_prompt, pool_sample, kv_rows_prompt, kv_rows_sample, win_prompt, win_sample)
```

```python
import contextlib
import numpy as np
import ml_dtypes
import concourse.bass as bass
import concourse.mybir as mybir
from concourse.bass_utils import run_bass_kernel_spmd

F32 = mybir.dt.float32
BF16 = mybir.dt.bfloat16
AF = mybir.ActivationFunctionType
ALU = mybir.AluOpType
AX = mybir.AxisListType
NPBF = ml_dtypes.bfloat16

NCORES = 8
D = 1024
DFF = 4096
SEQ = 4096
NB = 4
GT = 1024
HALO = 16
NCOL = HALO + GT
NS = 16
NCX = NCOL + NS
EPS = 1e-6
KVW = 1536
NWB = 3
SIG_EPOCH = 30000
NEGPOS = -8192.0
NPAGES = 2560
_DEV = {}
STAGE = 4


class Tk:
    __slots__ = ("w", "r", "name")

    def __init__(self, name=""):
        self.w = None
        self.r = []
        self.name = name


class Op:
    __slots__ = ("eng", "fn", "deps", "dma", "sig", "signum", "slot", "slotval", "idx")

    def __init__(self, eng, fn, dma):
        self.eng = eng
        self.fn = fn
        self.deps = []
        self.dma = dma
        self.sig = False
        self.signum = None
        self.slot = None
        self.slotval = None


class Sched:
    ENGS = ("pe", "act", "dve", "pool", "sp")

    def __init__(self, nslot=8):
        self.q = {e: [] for e in self.ENGS}
        self.nslot = nslot
        self.pending = {}

    def barrier(self):
        fr = []
        for e in self.ENGS:
            comp = [o for o in self.q[e] if not o.dma]
            if comp:
                fr.append(comp[-1])
            dm = [o for o in self.q[e] if o.dma]
            fr.extend(dm[-self.nslot:])
        self.pending = {e: list(fr) for e in self.ENGS}

    def add(self, eng, fn, reads=(), writes=(), dma=False, extra=()):
        op = Op(eng, fn, dma)
        deps = []
        seen = set()
        if self.pending.get(eng):
            extra = list(extra) + self.pending.pop(eng)
        for d in extra:
            if id(d) not in seen:
                seen.add(id(d)); deps.append(d)
        for t in reads:
            if t.w is not None and id(t.w) not in seen:
                seen.add(id(t.w)); deps.append(t.w)
        for t in writes:
            for r in t.r:
                if id(r) not in seen:
                    seen.add(id(r)); deps.append(r)
            if t.w is not None and id(t.w) not in seen:
                seen.add(id(t.w)); deps.append(t.w)
        op.deps = [d for d in deps if d is not op]
        for t in reads:
            t.r.append(op)
        for t in writes:
            t.w = op
            t.r = []
        op.idx = len(self.q[eng])
        self.q[eng].append(op)
        return op

    def prepare(self, nc):
        nslot = self.nslot
        for e in self.ENGS:
            for op in self.q[e]:
                for d in op.deps:
                    if d.dma:
                        continue
                    if d.eng == "pe" and op.eng == "pe" and not op.dma:
                        continue
                    d.sig = True
        nsig = {}
        for e in self.ENGS:
            k = 0
            i = 0
            for op in self.q[e]:
                if op.dma:
                    op.slot = i % nslot
                    op.slotval = 16 * (i // nslot + 1)
                    i += 1
                elif op.sig:
                    k += 1
                    op.signum = k
            nsig[e] = k
        st = contextlib.ExitStack()
        csem = {}
        for e in self.ENGS:
            nep = nsig[e] // SIG_EPOCH + 1
            csem[e] = [st.enter_context(nc.semaphore("c_%s_%d" % (e, j))) for j in range(nep)]
        dsem = {}
        for e in ("sp", "pool"):
            dsem[e] = [st.enter_context(nc.semaphore("d_%s_%d" % (e, j))) for j in range(nslot)]
        self._st = st
        self.csem = csem
        self.dsem = dsem

    def emit(self, block):
        csem, dsem = self.csem, self.dsem

        def run(e, eng):
            waited = {}

            def wait(sem, key, val):
                if waited.get(key, -1) >= val:
                    return
                waited[key] = val
                eng.wait_ge(sem, val)

            for op in self.q[e]:
                need = {}
                for d in op.deps:
                    if d.dma:
                        key, sem, val = ("d", d.eng, d.slot), dsem[d.eng][d.slot], d.slotval
                    else:
                        if d.eng == "pe" and e == "pe" and not op.dma:
                            continue
                        ep = (d.signum - 1) // SIG_EPOCH
                        key, sem, val = ("c", d.eng, ep), csem[d.eng][ep], d.signum - ep * SIG_EPOCH
                    if key not in need or need[key][1] < val:
                        need[key] = (sem, val)
                for key, (sem, val) in need.items():
                    wait(sem, key, val)
                if op.dma:
                    if op.slotval > 16:
                        wait(dsem[e][op.slot], ("d", e, op.slot), op.slotval - 16)
                    ins = op.fn(eng)
                    ins.then_inc(dsem[e][op.slot], 16)
                else:
                    ins = op.fn(eng)
                    if op.sig:
                        ep = (op.signum - 1) // SIG_EPOCH
                        ins.then_inc(csem[e][ep], 1)
            if e in dsem:
                last = {}
                for op in self.q[e]:
                    if op.dma:
                        last[op.slot] = op.slotval
                for s, v in last.items():
                    wait(dsem[e][s], ("d", e, s), v)

        @block.tensor
        def _(eng):
            run("pe", eng)

        @block.scalar
        def _(eng):
            run("act", eng)

        @block.vector
        def _(eng):
            run("dve", eng)

        @block.gpsimd
        def _(eng):
            run("pool", eng)

        @block.sync
        def _(eng):
            run("sp", eng)


L0_GROUPS = [(16, False, None), (24, False, None), (0, True, 0), (8, True, 1)]


def build_nc():
    nc = bass.Bass("TRN2", target_bir_lowering=False)

    def din(name, shape, dt=F32):
        return nc.dram_tensor(name, list(shape), dt, kind="ExternalInput").ap()

    def dout(name, shape, dt=F32):
        return nc.dram_tensor(name, list(shape), dt, kind="ExternalOutput").ap()

    def dscr(name, shape, dt):
        return nc.dram_tensor(name, list(shape), dt).ap()

    xg = din("xg", [4, NCOL, D])
    corr = din("corr", [4, 64])
    vecs = din("vecs", [128, 56])
    ident_d = din("ident", [128, 128])
    w_up = din("w_up", [2, D, DFF])
    w_down = din("w_down", [2, DFF, D])
    pool_w = din("pool_w", [4, 256, 256])
    w_kv = din("w_kv", [D, KVW])
    w_qg = din("w_qg", [D, 1072])
    w_o = din("w_o", [D, D])
    b_gate = din("b_gate", [1, 48])
    cmp_w1 = din("cmp_w1", [2, 2048, 128])
    cmp_w2 = din("cmp_w2", [2, 128, 64])
    cmp_pe = din("cmp_pe", [2, 32, 64])
    kaug_d = din("kaug", [32, 6, 128], BF16)
    qaug_d = din("qaug", [6, 4, 4, 2048], BF16)
    caug_d = din("caug", [16, 6, 256], BF16)
    posq_d = din("posq", [2048])
    posk_d = din("posk", [128, 32])
    cend_d = din("cend", [128, 2])
    privn_d = din("privn", [2048, 64])
    pribias_d = din("pribias", [2048, 64])
    mc2s_d = din("mc2s", [128, 2, 64], BF16)
    esel_d = din("esel", [64, 4096], BF16)
    tri_d = din("tri", [128, 128], BF16)
    identb_d = din("identb", [128, 128], BF16)

    cache_d = din("cache", [NPAGES * 128, 1024])
    pt_d = din("pt", [NS, 16], mybir.dt.int32)
    iotap_d = din("iotap", [128, 1])
    kaugS_d = din("kaugS", [21, 6, 128], BF16)
    qaugS_d = din("qaugS", [6, 4, 4, NS], BF16)
    caugS_d = din("caugS", [6, 128], BF16)
    maskS_d = din("maskS", [128, 8])
    privnS_d = din("privnS", [NS, 64])
    pribiasS_d = din("pribiasS", [NS, 64])
    mc2sS_d = din("mc2sS", [128, 64], BF16)
    xs_d = din("xs", [NS, D])
    spool_d = din("spool", [NS * 15, D])
    swin_d = din("swin", [NS, 512, 512])
    selw_d = din("selw", [128, 2, 64])
    pool_s_out = dout("pool_s_out", [NS, 15, D])
    kv_s_out = dout("kv_s_out", [NS, 1024])
    win_s_out = dout("win_s_out", [NS, 512, 512])
    y_s_out = dout("y_s_out", [NS, D])
    kv_out = dout("kv_out", [2048, 1024])
    win_out = dout("win_out", [512, 512])
    pool_out = dout("pool_out", [16, D])
    y_out = dout("y_out", [2048, D])

    x1_d = dscr("x1_d", [2, 128, 8 * (GT + NS)], F32)
    KsT_d = dscr("KsT_d", [32, 70, 4, 128], BF16)
    KwT_d = dscr("KwT_d", [32, 70, 4, 128], BF16)
    Vs_d = dscr("Vs_d", [32, 128, 4, 65], BF16)
    Vw_d = dscr("Vw_d", [32, 128, 4, 65], BF16)
    KsTS_d = dscr("KsTS_d", [NS, 16, 70, 4, 128], BF16)
    VsS_d = dscr("VsS_d", [NS, 16, 128, 4, 65], BF16)
    KwTS_d = dscr("KwTS_d", [NS, 4, 70, 4, 128], BF16)
    VwS_d = dscr("VwS_d", [NS, 4, 128, 4, 65], BF16)
    kcTS_d = dscr("kcTS_d", [NS, 64, 4, 128], BF16)
    vcS_d = dscr("vcS_d", [NS, 128, 4, 64], BF16)

    es = contextlib.ExitStack()

    def sb(name, shape, dt):
        return es.enter_context(nc.sbuf_tensor(name, list(shape), dt))

    xT = sb("xT", [128, 8, NCX], F32)
    uT = sb("uT", [128, 8, NCX], BF16)
    hT = sb("hT", [128, 16, GT], BF16)
    wb = [sb("wb%d" % i, [128, 8 * 512], BF16) for i in range(NWB)]
    xin = [sb("xin%d" % i, [128, D], F32) for i in range(2)]
    tA = sb("tA", [128, NCX], F32)
    rstd = sb("rstd", [128, NCX], F32)
    kvst = [sb("kvst%d" % i, [128, KVW], F32) for i in range(2)]
    vec_sb = sb("vec_sb", [128, 56], F32)
    corr_sb = sb("corr_sb", [128, 4 * 64], F32)
    pw_sb = sb("pw_sb", [128, 4 * 2 * 256], BF16)
    ones_bf = sb("ones_bf", [128, 128], BF16)
    utail = sb("utail", [128, 8, 16], F32)
    selw = sb("selw_sb", [128, 2, 64], F32)
    diffS = sb("diffS", [128, 8, NS], BF16)
    hTs = sb("hTs", [128, 16, NS], BF16)
    rs_t = sb("rs_t", [128, NS], F32)
    QTs = sb("QTs", [70, 4, 4, NS], BF16)
    gate_s = sb("gate_s", [NS, 48], F32)
    knT = sb("knT", [70, 2, 4, NS], BF16)
    vnew = sb("vnew", [NS, 2, 4, 65], BF16)
    knew_bf = sb("knew_bf", [NS, 512], BF16)
    kcTs = [sb("kcTs%d" % i, [70, 4, 128], BF16) for i in range(2)]
    vcMs = [sb("vcMs%d" % i, [128, 4, 128], BF16) for i in range(2)]
    Pz = [sb("Pz%d" % i, [128, 4, NS], BF16) for i in range(4)]
    maskS = sb("maskS_sb", [128, 8], F32)
    privnS = sb("privnS_sb", [NS, 64], F32)
    pribiasS = sb("pribiasS_sb", [NS, 64], F32)
    selTs = sb("selTs", [64, 4, NS], BF16)
    epsb = sb("epsb", [128, 1], F32)
    ident = sb("ident_sb", [128, 128], F32)
    identb = sb("identb_sb", [128, 128], BF16)
    vst = [sb("vst%d" % i, [128, 2, 4, 65], BF16) for i in range(2)]
    ktst = [sb("ktst%d" % i, [64, GT], BF16) for i in range(2)]
    w1_sb = sb("w1_sb", [64, 2, 32, 128], BF16)
    w2_sb = sb("w2_sb", [128, 2, 64], BF16)
    peT = sb("peT", [64, 2, 32], BF16)
    pe_nat = sb("pe_nat", [64, 64], F32)
    ABs = sb("ABs", [128, 2, 4, 2, 4], F32)
    hs_all = sb("hs_all", [128, 2, 4, 256], BF16)
    hpreb = sb("hpreb", [128, 4], F32)
    cvec = sb("cvec", [128, 2], F32)
    kcT = sb("kcT", [70, 4, 256], BF16)
    vcM = sb("vcM", [128, 2, 4, 128], BF16)
    kbuf = [sb("kbuf%d" % i, [70, 4, 128], BF16) for i in range(3)]
    vbuf = [sb("vbuf%d" % i, [128, 4, 65], BF16) for i in range(3)]
    pT = [sb("pT%d" % i, [128, 4, 128], BF16) for i in range(4)]
    posq_bc = [sb("posq_bc%d" % i, [128, 128], F32) for i in range(2)]
    posk_sb = sb("posk_sb", [128, 32], F32)
    cend_sb = sb("cend_sb", [128, 2], F32)
    mtmp = [sb("mtmp%d" % i, [128, 128], F32) for i in range(2)]
    mask_sb = [sb("mask%d" % i, [128, 256], BF16) for i in range(3)]
    tri_sb = sb("tri_sb", [128, 128], BF16)
    esel_sb = sb("esel_sb", [64, 4096], BF16)
    privn_sb = [sb("privn%d" % i, [128, 64], F32) for i in range(2)]
    pribias_sb = [sb("pribias%d" % i, [128, 64], F32) for i in range(2)]
    pri = sb("pri", [128, 4, 64], F32)
    pri2 = sb("pri2", [128, 64], F32)
    m8a = sb("m8a", [128, 8], F32)
    m8b = sb("m8b", [128, 8], F32)
    sel_sb = sb("sel_sb", [128, 4, 64], BF16)
    selT = sb("selT", [64, 4, 128], BF16)
    gate_sb = sb("gate_sb", [128, 8, 48], F32)
    bg_sb = sb("bg_sb", [128, 48], F32)
    rsum = sb("rsum", [128, 4], F32)
    fsc = sb("fsc", [128, 4], F32)
    oacc = pw_sb[:, :].bitcast(F32).rearrange("p (h d) -> p h d", h=16)
    hs_flat = hs_all[:, :, :, :].rearrange("p a b c -> p (a b c)")
    obf = hs_flat[:, 0:1024]
    otmp = hs_flat[:, 1024:1536].bitcast(F32).rearrange("p (r d) -> p r d", r=4)
    yst = [kvst[i][:, 0:D] for i in range(2)]
    ptail = xin[0][0:16, :]
    tB = rstd
    xct = ktst
    psum = [es.enter_context(nc.psum_tensor("ps%d" % i, [128, 512], F32)) for i in range(8)]

    S = Sched(nslot=16)
    k_xT = [Tk("xT%d" % i) for i in range(8)]
    k_uT = Tk("uT")
    k_h = [Tk("h%d" % i) for i in range(16)]
    k_wb = [Tk() for _ in range(NWB)]
    k_xin = [Tk(), Tk()]
    k_tA, k_rstd = Tk(), Tk()
    k_tA1 = Tk()
    k_tB = k_rstd
    k_kvst = [Tk(), Tk()]
    k_ps = [Tk() for _ in range(8)]
    k_const = Tk("const")
    k_utail = Tk()
    k_diffS, k_hTs, k_rs = Tk(), Tk(), Tk()
    k_QTs, k_gates, k_knT, k_vnew, k_knew = Tk(), Tk(), Tk(), Tk(), Tk()
    k_kcTs = [Tk(), Tk()]
    k_vcMs = [Tk(), Tk()]
    k_Pz = [Tk() for _ in range(4)]
    k_selTs = Tk()
    k_G = [Tk() for _ in range(24)]
    k_X = [[Tk() for _ in range(4)] for _ in range(2)]
    k_idx = Tk()
    k_sscr = Tk("sample scratch")
    k_ptail = k_xin[0]
    k_out = Tk("out")
    k_vst = [Tk(), Tk()]
    k_ktst = [Tk(), Tk()]
    k_xct = k_ktst
    k_AB = Tk()
    k_scr = Tk("scratch")
    k_x1 = [Tk(), Tk()]
    k_cmp = Tk()
    k_cmp2 = Tk()
    k_kc = Tk()
    k_vc = Tk()
    k_kbuf = [Tk() for _ in range(3)]
    k_vbuf = [Tk() for _ in range(3)]
    k_pT = [Tk() for _ in range(4)]
    k_posq = [Tk(), Tk()]
    k_mtmp = [Tk(), Tk()]
    k_mask = [Tk() for _ in range(3)]
    k_privn = [Tk(), Tk()]
    k_pri, k_pri2, k_m8, k_sel, k_selT = Tk(), Tk(), Tk(), Tk(), Tk()
    k_gate, k_rsum, k_fsc, k_oacc, k_otmp, k_obf = Tk(), Tk(), Tk(), Tk(), Tk(), Tk()
    k_yst = k_kvst

    state = {"ps": 0, "wb": 0, "xin": 0, "kvst": 0, "vst": 0, "ktst": 0,
             "kb": 0, "pT": 0, "mask": 0, "pq": 0, "yst": 0, "psS": 0,
             "G": 0, "pz": 0, "kc2": 0}

    def rot(key, n):
        i = state[key]
        state[key] = (i + 1) % n
        return i

    def next_ps():
        return rot("ps", 8)

    pw4 = pw_sb[:, :].rearrange("p (g k c) -> p g k c", g=4, k=2)
    VG0, VG1, VGKV, VPS, VGQ, VG1B, VGF = 0, 1, 2, 3, 4, 5, 6

    def vcol(v, dc):
        return vec_sb[:, v * 8 + dc: v * 8 + dc + 1]

    def cload(eng, out_ap, in_ap):
        S.add(eng, lambda e: e.dma_start(out=out_ap, in_=in_ap), writes=[k_const], dma=True)

    cload("sp", vec_sb[:, :], vecs[:, :])
    cload("sp", corr_sb[:, :], corr.rearrange("g c -> (g c)").partition_broadcast(128))
    cload("pool", pw4, pool_w.rearrange("g (k p) c -> p g k c", p=128))
    cload("sp", ident[:, :], ident_d[:, :])
    cload("sp", identb[:, :], identb_d[:, :])
    cload("sp", selw[:, :, :], selw_d[:, :, :])
    S.add("sp", lambda e: e.dma_start(out=pool_s_out[:, 0:14, :], in_=spool_d.rearrange("(s r) d -> s r d", r=15)[:, 1:15, :]),
          writes=[k_out], dma=True)
    for s_ in range(NS):
        S.add("sp", lambda e, s_=s_: e.dma_start(out=win_s_out[s_, 0:511, :], in_=swin_d[s_, 1:512, :]), writes=[k_out], dma=True)
    if STAGE >= 3:
        cload("sp", tri_sb[:, :], tri_d[:, :])
        cload("sp", esel_sb[:, :], esel_d[:, :])
        cload("sp", posk_sb[:, :], posk_d[:, :])
        cload("sp", cend_sb[:, :], cend_d[:, :])
        cload("sp", bg_sb[:, :], b_gate[0, :].partition_broadcast(128))
        cload("pool", w1_sb[:, :, :, :], cmp_w1.rearrange("k (p d) h -> d k p h", d=64))
        cload("pool", w2_sb[:, :, :], cmp_w2.rearrange("k h d -> h k d"))
        cload("sp", pe_nat[:, :], cmp_pe.rearrange("k p d -> (k p) d"))
        for g in range(4):
            cload("sp", vcM[:, :, g, 64:128], mc2s_d[:, :, :])
    if STAGE >= 4:
        cload("sp", maskS[:, :], maskS_d[:, :])
        cload("sp", privnS[:, :], privnS_d[:, :])
        cload("sp", pribiasS[:, :], pribiasS_d[:, :])
        for i_ in range(2):
            cload("sp", kcTs[i_][64:70, :, :], caugS_d.unsqueeze(1).to_broadcast([6, 4, 128]))
            for g in range(4):
                cload("sp", vcMs[i_][:, g, 64:128], mc2sS_d[:, :])
        cload("sp", QTs[64:70, :, :, :], qaugS_d[:, :, :, :])
        for b_ in range(2):
            for g in range(4):
                cload("sp", knT[64:70, b_, g, :], kaugS_d[20, :, 0:NS])
        S.add("dve", lambda e: e.memset(vnew[:, :, :, 64:65], 1.0), writes=[k_vnew])
        for i_ in range(4):
            S.add("dve", lambda e, i_=i_: e.memset(Pz[i_][:, :, :], 0.0), writes=[k_Pz[i_]])
        for g in range(4):
            for s_ in range(NS):
                S.add("sp", lambda e, g=g, s_=s_: e.dma_start(
                    out=KsTS_d[s_, :, 64:70, g, :], in_=kaugS_d[0:16]), writes=[k_sscr], dma=True)
                S.add("sp", lambda e, g=g, s_=s_: e.dma_start(
                    out=KwTS_d[s_, :, 64:70, g, :], in_=kaugS_d[16:20]), writes=[k_sscr], dma=True)
    S.add("dve", lambda e: e.memset(ones_bf[:, :], 1.0), writes=[k_const])
    S.add("dve", lambda e: e.memset(epsb[:, :], EPS), writes=[k_const])
    if STAGE >= 3:
        for i in range(2):
            S.add("dve", lambda e, i=i: e.memset(vst[i][:, :, :, 64:65], 1.0), writes=[k_vst[i]])
        for g in range(4):
            S.add("sp", lambda e, g=g: e.dma_start(out=KsT_d[:, 64:70, g, :], in_=kaug_d[:, :, :]), writes=[k_scr], dma=True)
            S.add("sp", lambda e, g=g: e.dma_start(out=KwT_d[:, 64:70, g, :], in_=kaug_d[:, :, :]), writes=[k_scr], dma=True)
        pi = next_ps()
        S.add("pe", lambda e, pi=pi: e.transpose(psum[pi][0:64, 0:64], pe_nat[:, :], ident[0:64, 0:64]),
              reads=[k_const], writes=[k_ps[pi]])
        S.add("act", lambda e, pi=pi: e.activation(out=peT[:, :, :], in_=psum[pi][0:64, 0:64].rearrange("d (k p) -> d k p", k=2), func=AF.Copy),
              reads=[k_ps[pi]], writes=[k_const])
        for kv in range(2):
            pi = next_ps()
            for p in range(32):
                S.add("pe", lambda e, pi=pi, kv=kv, p=p: e.matmul(
                    psum[pi][:, 0:1], w1_sb[:, kv, p, :], peT[:, kv, p:p + 1], start=(p == 0), stop=(p == 31)),
                    reads=[k_const], writes=[k_ps[pi]])
            S.add("act", lambda e, pi=pi, kv=kv: e.activation(out=cvec[:, kv:kv + 1], in_=psum[pi][:, 0:1], func=AF.Copy),
                  reads=[k_ps[pi]], writes=[k_cmp])

    def wload(src_ap, shape3):
        i = rot("wb", NWB)
        a, b = shape3
        view = wb[i][:, 0:a * b].rearrange("p (a b) -> p a b", a=a)
        S.add("pool", lambda e: e.dma_start(out=view, in_=src_ap), writes=[k_wb[i]], dma=True)
        return i, view

    COLR = [(0, 16), (16, 528), (528, 1040)]
    MAINR = [(16, 528), (528, 1040)]
    sq = hT[:, :, :].rearrange("p a b -> p (a b)")[:, 0:8 * NCX].rearrange("p (a b) -> p a b", a=8)

    SR = (NCOL, NCX)

    def norm(vidx, ranges, with_tail=False, final=False, stail=False):
        for dc in range(8):
            S.add("act", lambda e, dc=dc: e.activation(out=sq[:, dc, :], in_=xT[:, dc, :], func=AF.Square),
                  reads=[k_xT[dc]], writes=list(k_h))
        for (c0, c1) in ranges:
            pi = next_ps()
            n = c1 - c0
            for dc in range(8):
                S.add("pe", lambda e, dc=dc, pi=pi, c0=c0, c1=c1, n=n: e.matmul(
                    psum[pi][:, 0:n], ones_bf[:, :], sq[:, dc, c0:c1], start=(dc == 0), stop=(dc == 7)),
                    reads=list(k_h) + [k_const], writes=[k_ps[pi]])
            S.add("act", lambda e, pi=pi, c0=c0, c1=c1, n=n: e.activation(
                out=rstd[:, c0:c1], in_=psum[pi][:, 0:n], func=AF.Sqrt, bias=epsb[:, 0:1], scale=1.0 / D),
                reads=[k_ps[pi], k_const], writes=[k_rstd])
            S.add("dve", lambda e, c0=c0, c1=c1: e.reciprocal(out=rstd[:, c0:c1], in_=rstd[:, c0:c1]),
                  reads=[k_rstd], writes=[k_rstd])
        lo = ranges[0][0]
        hi = ranges[-1][1]
        for dc in range(8):
            if final:
                S.add("dve", lambda e, dc=dc: e.scalar_tensor_tensor(
                    out=xT[:, dc, lo:hi], in0=xT[:, dc, lo:hi], scalar=vcol(vidx, dc), in1=rstd[:, lo:hi],
                    op0=ALU.mult, op1=ALU.mult), reads=[k_xT[dc], k_rstd, k_const], writes=[k_xT[dc]])
            else:
                S.add("dve", lambda e, dc=dc: e.scalar_tensor_tensor(
                    out=uT[:, dc, lo:hi], in0=xT[:, dc, lo:hi], scalar=vcol(vidx, dc), in1=rstd[:, lo:hi],
                    op0=ALU.mult, op1=ALU.mult), reads=[k_xT[dc], k_rstd, k_const], writes=[k_uT])
        if with_tail or stail:
            t0 = NCX - 16 if stail else NCOL - 16
            for dc in range(8):
                S.add("dve", lambda e, dc=dc, t0=t0: e.scalar_tensor_tensor(
                    out=utail[:, dc, :], in0=xT[:, dc, t0:t0 + 16], scalar=vcol(vidx, dc),
                    in1=rstd[:, t0:t0 + 16], op0=ALU.mult, op1=ALU.mult),
                    reads=[k_xT[dc], k_rstd, k_const], writes=[k_utail])

    def rows16_out(src3, ksrc, dst_ap):
        pi = next_ps()
        pi2 = next_ps()
        for dc in range(8):
            pp = pi if dc < 4 else pi2
            S.add("pe", lambda e, dc=dc, pp=pp: e.transpose(
                psum[pp][0:16, (dc % 4) * 128:(dc % 4 + 1) * 128], src3[:, dc, :], ident[:, :]),
                reads=ksrc + [k_const], writes=[k_ps[pp]])
        S.add("act", lambda e, pi=pi: e.activation(out=ptail[:, 0:512], in_=psum[pi][0:16, :], func=AF.Copy),
              reads=[k_ps[pi]], writes=[k_ptail])
        S.add("act", lambda e, pi2=pi2: e.activation(out=ptail[:, 512:1024], in_=psum[pi2][0:16, :], func=AF.Copy),
              reads=[k_ps[pi2]], writes=[k_ptail])
        S.add("sp", lambda e: e.dma_start(out=dst_ap, in_=ptail[:, :]), reads=[k_ptail], writes=[k_out], dma=True)

    def mlp(layer, has_s=False):
        for hh in range(2):
            for fb in range(4):
                col0 = hh * 2048 + fb * 512
                wi, wv = wload(w_up[layer].rearrange("(dc p) f -> p dc f", p=128)[:, :, col0:col0 + 512], (8, 512))
                for fc in range(4):
                    fcl = fb * 4 + fc
                    for th in range(2):
                        pi = next_ps()
                        c0 = 16 + th * 512
                        for dc in range(8):
                            S.add("pe", lambda e, pi=pi, wv=wv, dc=dc, fc=fc, c0=c0: e.matmul(
                                psum[pi][:, :], wv[:, dc, fc * 128:(fc + 1) * 128], uT[:, dc, c0:c0 + 512],
                                start=(dc == 0), stop=(dc == 7)),
                                reads=[k_wb[wi], k_uT], writes=[k_ps[pi]])
                        tt = tA[:, th * 512:(th + 1) * 512]
                        kt = k_tA if th == 0 else k_tA1
                        S.add("act", lambda e, pi=pi, tt=tt: e.activation(out=tt, in_=psum[pi][:, :], func=AF.Relu),
                              reads=[k_ps[pi]], writes=[kt])
                        S.add("dve", lambda e, tt=tt, fcl=fcl, th=th: e.tensor_tensor(
                            out=hT[:, fcl, th * 512:(th + 1) * 512], in0=tt, in1=tt, op=ALU.mult),
                            reads=[kt], writes=[k_h[fcl]])
                    if has_s:
                        pi = next_ps()
                        for dc in range(8):
                            S.add("pe", lambda e, pi=pi, wv=wv, dc=dc, fc=fc: e.matmul(
                                psum[pi][:, 0:NS], wv[:, dc, fc * 128:(fc + 1) * 128], uT[:, dc, NCOL:NCX],
                                start=(dc == 0), stop=(dc == 7)),
                                reads=[k_wb[wi], k_uT], writes=[k_ps[pi]])
                        S.add("act", lambda e, pi=pi: e.activation(out=rs_t[:, :], in_=psum[pi][:, 0:NS], func=AF.Relu),
                              reads=[k_ps[pi]], writes=[k_rs])
                        S.add("dve", lambda e, fcl=fcl: e.tensor_tensor(
                            out=hTs[:, fcl, :], in0=rs_t[:, :], in1=rs_t[:, :], op=ALU.mult),
                            reads=[k_rs], writes=[k_hTs])
            for dmp in range(4):
                wi, wv = wload(w_down[layer][hh * 2048:(hh + 1) * 2048, dmp * 256:(dmp + 1) * 256]
                               .rearrange("(f p) c -> p f c", p=128), (16, 256))
                for dmc in range(2):
                    dca = dmp * 2 + dmc
                    for th in range(2):
                        pi = next_ps()
                        for fcl in range(16):
                            S.add("pe", lambda e, pi=pi, wv=wv, fcl=fcl, dmc=dmc, th=th: e.matmul(
                                psum[pi][:, :], wv[:, fcl, dmc * 128:(dmc + 1) * 128], hT[:, fcl, th * 512:(th + 1) * 512],
                                start=(fcl == 0), stop=(fcl == 15)),
                                reads=[k_wb[wi], k_h[fcl]], writes=[k_ps[pi]])
                        c0 = 16 + th * 512
                        S.add("dve", lambda e, pi=pi, dca=dca, c0=c0: e.tensor_tensor(
                            out=xT[:, dca, c0:c0 + 512], in0=xT[:, dca, c0:c0 + 512], in1=psum[pi][:, :], op=ALU.add),
                            reads=[k_ps[pi], k_xT[dca]], writes=[k_xT[dca]])
                    if has_s:
                        pi = next_ps()
                        for fcl in range(16):
                            S.add("pe", lambda e, pi=pi, wv=wv, fcl=fcl, dmc=dmc: e.matmul(
                                psum[pi][:, 0:NS], wv[:, fcl, dmc * 128:(dmc + 1) * 128], hTs[:, fcl, :],
                                start=(fcl == 0), stop=(fcl == 15)),
                                reads=[k_wb[wi], k_hTs], writes=[k_ps[pi]])
                        S.add("dve", lambda e, pi=pi, dca=dca: e.tensor_tensor(
                            out=xT[:, dca, NCOL:NCX], in0=xT[:, dca, NCOL:NCX], in1=psum[pi][:, 0:NS], op=ALU.add),
                            reads=[k_ps[pi], k_xT[dca]], writes=[k_xT[dca]])

    if STAGE >= 4 and not _DEV.get("skip_prep"):
        I32 = mybir.dt.int32
        hflat = hT[:, :, :].rearrange("p a b -> p (a b)")
        XcT = hflat[0:64, :].rearrange("p (k g n) -> p k g n", k=2, g=4)
        uflat = uT[:, :, :].rearrange("p a b -> p (a b)")
        xflat_bf = xT[:, :, :].rearrange("p a b -> p (a b)").bitcast(BF16)
        Gb = [uflat[:, i * 1024:(i + 1) * 1024] for i in range(8)] + \
             [xflat_bf[:, i * 1024:(i + 1) * 1024] for i in range(16)]
        ptb_i = xin[0][:, 0:256].bitcast(I32)
        ptb_f = xin[0][:, 256:512]
        idx_f = xin[0][:, 512:768]
        idx_i = xin[1][:, 0:256].bitcast(I32)
        iotap = xin[1][:, 256:257]
        kcs_st = ktst[0][:, 0:512].rearrange("d (g n) -> d g n", g=4)
        vcs_st = xin[1][:, 512:640].bitcast(BF16).rearrange("n (g d) -> n g d", g=4)
        k_kcs, k_vcs = Tk(), Tk()
        hs_s = hs_all[:, 0, 0, 0:128]
        S.add("sp", lambda e: e.dma_start(out=ptb_i, in_=pt_d.rearrange("s k -> (s k)").partition_broadcast(128)), writes=[k_idx], dma=True)
        S.add("sp", lambda e: e.dma_start(out=iotap, in_=iotap_d[:, :]), writes=[k_idx], dma=True)
        S.add("dve", lambda e: e.tensor_copy(out=ptb_f, in_=ptb_i), reads=[k_idx], writes=[k_idx])
        S.add("dve", lambda e: e.tensor_scalar(out=idx_f, in0=ptb_f, scalar1=128.0, scalar2=iotap, op0=ALU.mult, op1=ALU.add),
              reads=[k_idx], writes=[k_idx])
        S.add("dve", lambda e: e.tensor_copy(out=idx_i, in_=idx_f), reads=[k_idx], writes=[k_idx])
        S.add("dve", lambda e: e.memset(kcs_st, 0.0), writes=[k_kcs])
        S.add("dve", lambda e: e.memset(vcs_st, 0.0), writes=[k_vcs])
        for s_ in range(_DEV.get("prep_ns", NS)):
            for kt in range(16):
                gi_ = rot("G", 24)
                col = s_ * 16 + kt
                S.add("pool", lambda e, gi_=gi_, col=col: e.indirect_dma_start(
                    out=Gb[gi_], out_offset=None, in_=cache_d[:, :],
                    in_offset=bass.IndirectOffsetOnAxis(ap=idx_i[:, col:col + 1], axis=0)),
                    reads=[k_idx], writes=[k_G[gi_]], dma=True)
                if _DEV.get("prep_level", 9) < 2:
                    continue
                pa, pb = next_ps(), next_ps()
                psa = psum[pa][:, :].bitcast(BF16)
                psb = psum[pb][:, :].bitcast(BF16)
                for j in range(4):
                    c_ = 512 + j * 64
                    S.add("pe", lambda e, psa=psa, j=j, c_=c_, gi_=gi_: e.transpose(
                        psa[0:64, j * 128:(j + 1) * 128], Gb[gi_][:, c_:c_ + 64], identb[:, :]),
                        reads=[k_G[gi_], k_const], writes=[k_ps[pa]])
                for j in range(8):
                    c_ = j * 64
                    S.add("pe", lambda e, psb=psb, j=j, c_=c_, gi_=gi_: e.transpose(
                        psb[0:64, j * 128:(j + 1) * 128], Gb[gi_][:, c_:c_ + 64], identb[:, :]),
                        reads=[k_G[gi_], k_const], writes=[k_ps[pb]])
                if _DEV.get("no_evac"):
                    continue
                kst = ktst[1][:, 0:512].rearrange("d (g n) -> d g n", g=4)
                if not _DEV.get("no_e1"):
                    S.add("act", lambda e, psa=psa, kst=kst: e.activation(
                        out=kst, in_=psa[0:64, 0:512].rearrange("d (g n) -> d g n", g=4), func=AF.Copy),
                        reads=[k_ps[pa]], writes=[k_ktst[1]])
                if not _DEV.get("no_store"):
                    S.add("sp", lambda e, kst=kst, s_=s_, kt=kt: e.dma_start(out=KsTS_d[s_, kt, 0:64, :, :], in_=kst),
                          reads=[k_ktst[1]], writes=[k_sscr], dma=True)
                if not _DEV.get("no_e2"):
                    S.add("dve", lambda e, psb=psb, kt=kt: e.tensor_copy(
                        out=XcT[:, :, :, kt * 128:(kt + 1) * 128], in_=psb[0:64, 0:1024].rearrange("d (k g n) -> d k g n", k=2, g=4)),
                        reads=[k_ps[pb]], writes=k_X[0] + k_X[1])
                vi = rot("vst", 2)
                if not _DEV.get("no_e4"):
                    S.add("act", lambda e, vi=vi, gi_=gi_: e.activation(
                        out=vst[vi][:, 0, :, 0:64], in_=Gb[gi_][:, 768:1024].rearrange("p (g d) -> p g d", g=4), func=AF.Copy),
                        reads=[k_G[gi_]], writes=[k_vst[vi]])
                if not _DEV.get("no_store"):
                    S.add("sp", lambda e, vi=vi, s_=s_, kt=kt: e.dma_start(out=VsS_d[s_, kt], in_=vst[vi][:, 0, :, :]),
                          reads=[k_vst[vi]], writes=[k_sscr], dma=True)
            for kv in range(2 if _DEV.get("prep_level", 9) >= 3 else 0):
                for g in range(4):
                    pi = next_ps()
                    for pos in range(32):
                        S.add("pe", lambda e, pi=pi, kv=kv, g=g, pos=pos: e.matmul(
                            psum[pi][:, 0:127], w1_sb[:, kv, pos, :], XcT[:, kv, g, pos:pos + 2017:16],
                            start=(pos == 0), stop=(pos == 31)),
                            reads=[k_X[kv][g], k_const], writes=[k_ps[pi]])
                    S.add("act", lambda e, pi=pi, kv=kv: e.activation(
                        out=hs_s[:, 0:127], in_=psum[pi][:, 0:127], func=AF.Silu, bias=cvec[:, kv:kv + 1]),
                        reads=[k_ps[pi], k_cmp], writes=[k_AB])
                    pj = next_ps()
                    if kv == 0:
                        S.add("pe", lambda e, pj=pj: e.matmul(psum[pj][0:64, 0:127], w2_sb[:, 0, :], hs_s[:, 0:127], start=True, stop=True),
                              reads=[k_AB, k_const], writes=[k_ps[pj]])
                        S.add("act", lambda e, pj=pj, g=g: e.activation(out=kcs_st[:, g, 0:127], in_=psum[pj][0:64, 0:127], func=AF.Copy),
                              reads=[k_ps[pj]], writes=[k_kcs])
                    else:
                        S.add("pe", lambda e, pj=pj: e.matmul(psum[pj][0:127, 0:64], hs_s[:, 0:127], w2_sb[:, 1, :], start=True, stop=True),
                              reads=[k_AB, k_const], writes=[k_ps[pj]])
                        S.add("act", lambda e, pj=pj, g=g: e.activation(out=vcs_st[0:127, g, :], in_=psum[pj][0:127, 0:64], func=AF.Copy),
                              reads=[k_ps[pj]], writes=[k_vcs])
            if _DEV.get("prep_level", 9) < 4:
                continue
            S.add("sp", lambda e, s_=s_: e.dma_start(out=kcTS_d[s_], in_=kcs_st), reads=[k_kcs], writes=[k_sscr], dma=True)
            S.add("sp", lambda e, s_=s_: e.dma_start(out=vcS_d[s_], in_=vcs_st), reads=[k_vcs], writes=[k_sscr], dma=True)
            for t_ in range(4):
                gi_ = rot("G", 24)
                S.add("pool", lambda e, gi_=gi_, s_=s_, t_=t_: e.dma_start(
                    out=Gb[gi_][:, 0:512], in_=swin_d[s_, t_ * 128:(t_ + 1) * 128, :]), writes=[k_G[gi_]], dma=True)
                pa = next_ps()
                psa = psum[pa][:, :].bitcast(BF16)
                for j in range(4):
                    S.add("pe", lambda e, psa=psa, j=j, gi_=gi_: e.transpose(
                        psa[0:64, j * 128:(j + 1) * 128], Gb[gi_][:, j * 64:(j + 1) * 64], identb[:, :]),
                        reads=[k_G[gi_], k_const], writes=[k_ps[pa]])
                kst = ktst[1][:, 0:512].rearrange("d (g n) -> d g n", g=4)
                S.add("act", lambda e, psa=psa, kst=kst: e.activation(
                    out=kst, in_=psa[0:64, 0:512].rearrange("d (g n) -> d g n", g=4), func=AF.Copy),
                    reads=[k_ps[pa]], writes=[k_ktst[1]])
                S.add("sp", lambda e, kst=kst, s_=s_, t_=t_: e.dma_start(out=KwTS_d[s_, t_, 0:64, :, :], in_=kst),
                      reads=[k_ktst[1]], writes=[k_sscr], dma=True)
                vi = rot("vst", 2)
                S.add("act", lambda e, vi=vi, gi_=gi_: e.activation(
                    out=vst[vi][:, 0, :, 0:64], in_=Gb[gi_][:, 256:512].rearrange("p (g d) -> p g d", g=4), func=AF.Copy),
                    reads=[k_G[gi_]], writes=[k_vst[vi]])
                S.add("sp", lambda e, vi=vi, s_=s_, t_=t_: e.dma_start(out=VwS_d[s_, t_], in_=vst[vi][:, 0, :, :]),
                      reads=[k_vst[vi]], writes=[k_sscr], dma=True)
        S.barrier()

    for gq, (slot0, is_own, ogi) in enumerate(L0_GROUPS):
        if _DEV.get("prep_only"):
            continue
        if STAGE < 3 and not is_own:
            continue
        last = is_own and ogi == 1
        has_s = is_own and ogi == 0
        RNG0 = COLR + ([SR] if has_s else [])
        RNG = MAINR + ([SR] if has_s else [])
        for ti in range(10 if has_s else 9):
            r0, nr = (0, 16) if ti == 0 else ((NCOL, NS) if ti == 9 else (16 + (ti - 1) * 128, 128))
            xi = rot("xin", 2)
            if ti == 9:
                S.add("sp", lambda e, xi=xi: e.dma_start(out=xin[xi][0:NS, :], in_=xs_d[:, :]),
                      writes=[k_xin[xi]], dma=True)
            else:
                S.add("sp", lambda e, xi=xi, r0=r0, nr=nr, gq=gq: e.dma_start(out=xin[xi][0:nr, :], in_=xg[gq, r0:r0 + nr, :]),
                      writes=[k_xin[xi]], dma=True)
            for hb in range(2):
                pi = next_ps()
                for j in range(4):
                    dc = hb * 4 + j
                    S.add("pe", lambda e, xi=xi, pi=pi, j=j, dc=dc, nr=nr: e.transpose(
                        psum[pi][:, j * 128:j * 128 + nr], xin[xi][0:nr, dc * 128:(dc + 1) * 128], ident[0:nr, 0:nr]),
                        reads=[k_xin[xi], k_const], writes=[k_ps[pi]])
                src = psum[pi][:, :].rearrange("p (j t) -> p j t", j=4)[:, :, 0:nr]
                if hb == 0:
                    S.add("act", lambda e, src=src, hb=hb, r0=r0, nr=nr: e.activation(
                        out=xT[:, hb * 4:hb * 4 + 4, r0:r0 + nr], in_=src, func=AF.Copy),
                        reads=[k_ps[pi]], writes=k_xT[hb * 4:hb * 4 + 4])
                else:
                    S.add("dve", lambda e, src=src, hb=hb, r0=r0, nr=nr: e.tensor_copy(
                        out=xT[:, hb * 4:hb * 4 + 4, r0:r0 + nr], in_=src),
                        reads=[k_ps[pi]], writes=k_xT[hb * 4:hb * 4 + 4])
        norm(VG0, RNG0, with_tail=last, stail=has_s)
        if last:
            rows16_out(utail, [k_utail], pool_out[:, :])
        if has_s:
            rows16_out(utail, [k_utail], pool_s_out[:, 14, :])
            stx = [rot("xin", 2), None]
            stx[1] = rot("xin", 2)
            for t_, (r0_, nr_) in enumerate(((0, 128), (128, 112))):
                S.add("sp", lambda e, xi=stx[t_], r0_=r0_, nr_=nr_: e.dma_start(out=xin[xi][0:nr_, :], in_=spool_d[r0_:r0_ + nr_, :]),
                      writes=[k_xin[stx[t_]]], dma=True)
        diffT = hT[:, 0:8, :]
        for dc in range(8):
            g = dc // 2
            nst = g + 1
            bufs = [tA, tB]
            kb_ = [k_tA, k_tB]
            kbw_ = [[k_tA, k_tA1], [k_tB]]
            cur = None
            for s in range(nst):
                sh = 1 << s
                lo = (1 << (s + 1))
                o = bufs[s % 2]
                ko = kb_[s % 2]
                kow = kbw_[s % 2]
                if s == 0:
                    S.add("dve", lambda e, o=o, dc=dc, lo=lo, sh=sh: e.tensor_tensor(
                        out=o[:, lo:NCOL], in0=uT[:, dc, lo:NCOL], in1=uT[:, dc, lo - sh:NCOL - sh], op=ALU.add),
                        reads=[k_uT], writes=kow)
                else:
                    i_ = bufs[(s - 1) % 2]
                    ki = kb_[(s - 1) % 2]
                    S.add("dve", lambda e, o=o, i_=i_, lo=lo, sh=sh: e.tensor_tensor(
                        out=o[:, lo:NCOL], in0=i_[:, lo:NCOL], in1=i_[:, lo - sh:NCOL - sh], op=ALU.add),
                        reads=[ki], writes=kow)
                cur = (o, ko, kow)
            o, ko, kow = cur
            w = 1 << (g + 1)
            if has_s:
                pi = next_ps()
                for t_, nr_ in enumerate((128, 112)):
                    S.add("pe", lambda e, pi=pi, t_=t_, nr_=nr_, dc=dc, g=g: e.matmul(
                        psum[pi][:, 0:NS], xin[stx[t_]][0:nr_, dc * 128:(dc + 1) * 128], selw[0:nr_, t_, g * 16:(g + 1) * 16],
                        start=(t_ == 0), stop=(t_ == 1)),
                        reads=[k_xin[stx[t_]], k_const], writes=[k_ps[pi]])
                S.add("dve", lambda e, o=o, pi=pi, dc=dc: e.tensor_tensor(
                    out=o[:, NCOL:NCX], in0=psum[pi][:, 0:NS], in1=uT[:, dc, NCOL:NCX], op=ALU.add),
                    reads=[k_ps[pi], k_uT, ko], writes=kow)
                S.add("dve", lambda e, o=o, dc=dc, w=w: e.scalar_tensor_tensor(
                    out=diffS[:, dc, :], in0=o[:, NCOL:NCX], scalar=1.0 / w, in1=uT[:, dc, NCOL:NCX],
                    op0=ALU.mult, op1=ALU.subtract), reads=[ko, k_uT], writes=[k_diffS])
            S.add("dve", lambda e, o=o, g=g, gq=gq: e.tensor_tensor(
                out=o[:, 16:32], in0=o[:, 16:32], in1=corr_sb[:, gq * 64 + g * 16: gq * 64 + g * 16 + 16], op=ALU.mult),
                reads=[ko, k_const], writes=kow)
            S.add("dve", lambda e, o=o, dc=dc, w=w: e.scalar_tensor_tensor(
                out=diffT[:, dc, :], in0=o[:, 16:NCOL], scalar=1.0 / w, in1=uT[:, dc, 16:NCOL],
                op0=ALU.mult, op1=ALU.subtract), reads=[ko, k_uT], writes=[k_h[dc]])
        for oc in range(8):
            g = oc // 2
            for th in range(2):
                pi = next_ps()
                for kk in range(2):
                    S.add("pe", lambda e, pi=pi, g=g, kk=kk, oc=oc, th=th: e.matmul(
                        psum[pi][:, :], pw4[:, g, kk, (oc % 2) * 128:(oc % 2) * 128 + 128],
                        diffT[:, 2 * g + kk, th * 512:(th + 1) * 512], start=(kk == 0), stop=(kk == 1)),
                        reads=[k_h[2 * g + kk], k_const], writes=[k_ps[pi]])
                c0 = 16 + th * 512
                S.add("dve", lambda e, pi=pi, oc=oc, c0=c0: e.scalar_tensor_tensor(
                    out=xT[:, oc, c0:c0 + 512], in0=psum[pi][:, :], scalar=vcol(VPS, oc), in1=xT[:, oc, c0:c0 + 512],
                    op0=ALU.mult, op1=ALU.add), reads=[k_ps[pi], k_xT[oc], k_const], writes=[k_xT[oc]])
            if has_s:
                pi = next_ps()
                for kk in range(2):
                    S.add("pe", lambda e, pi=pi, g=g, kk=kk, oc=oc: e.matmul(
                        psum[pi][:, 0:NS], pw4[:, g, kk, (oc % 2) * 128:(oc % 2) * 128 + 128],
                        diffS[:, 2 * g + kk, :], start=(kk == 0), stop=(kk == 1)),
                        reads=[k_diffS, k_const], writes=[k_ps[pi]])
                S.add("dve", lambda e, pi=pi, oc=oc: e.scalar_tensor_tensor(
                    out=xT[:, oc, NCOL:NCX], in0=psum[pi][:, 0:NS], scalar=vcol(VPS, oc), in1=xT[:, oc, NCOL:NCX],
                    op0=ALU.mult, op1=ALU.add), reads=[k_ps[pi], k_xT[oc], k_const], writes=[k_xT[oc]])
        norm(VG1, RNG)
        mlp(0, has_s)
        if is_own and STAGE >= 2:
            S.add("sp", lambda e, ogi=ogi: e.dma_start(
                out=x1_d[ogi].rearrange("p (a b) -> p a b", a=8), in_=xT[:, :, 16:NCX]),
                reads=list(k_xT), writes=[k_x1[ogi]], dma=True)
        norm(VGKV, RNG)
        kvw = []
        for cb in range(3):
            kvw.append(wload(w_kv.rearrange("(dc p) f -> p dc f", p=128)[:, :, cb * 512:(cb + 1) * 512], (8, 512)))
        for ti in range(8):
            ks = rot("kvst", 2)
            c0 = 16 + ti * 128
            for cb in range(3):
                wi, wv = kvw[cb]
                pi = next_ps()
                for dc in range(8):
                    S.add("pe", lambda e, pi=pi, wv=wv, dc=dc, c0=c0: e.matmul(
                        psum[pi][:, :], uT[:, dc, c0:c0 + 128], wv[:, dc, :], start=(dc == 0), stop=(dc == 7)),
                        reads=[k_wb[wi], k_uT], writes=[k_ps[pi]])
                S.add("act", lambda e, pi=pi, ks=ks, cb=cb: e.activation(
                    out=kvst[ks][:, cb * 512:(cb + 1) * 512], in_=psum[pi][:, :], func=AF.Copy),
                    reads=[k_ps[pi]], writes=[k_kvst[ks]])
            if is_own:
                row0 = ogi * GT + ti * 128
                S.add("sp", lambda e, ks=ks, row0=row0: e.dma_start(out=kv_out[row0:row0 + 128, :], in_=kvst[ks][:, 0:1024]),
                      reads=[k_kvst[ks]], writes=[k_out], dma=True)
                if last and ti >= 4:
                    S.add("sp", lambda e, ks=ks, ti=ti: e.dma_start(
                        out=win_out[(ti - 4) * 128:(ti - 3) * 128, :], in_=kvst[ks][:, 1024:1536]),
                        reads=[k_kvst[ks]], writes=[k_out], dma=True)
            if STAGE >= 3:
                vi = rot("vst", 2)
                S.add("dve", lambda e, ks=ks, vi=vi: e.tensor_copy(
                    out=vst[vi][:, 0, :, 0:64], in_=kvst[ks][:, 768:1024].rearrange("p (g d) -> p g d", g=4)),
                    reads=[k_kvst[ks]], writes=[k_vst[vi]])
                S.add("dve", lambda e, ks=ks, vi=vi: e.tensor_copy(
                    out=vst[vi][:, 1, :, 0:64], in_=kvst[ks][:, 1280:1536].rearrange("p (g d) -> p g d", g=4)),
                    reads=[k_kvst[ks]], writes=[k_vst[vi]])
                st_ = slot0 + ti
                S.add("sp", lambda e, vi=vi, st_=st_: e.dma_start(out=Vs_d[st_], in_=vst[vi][:, 0, :, :]),
                      reads=[k_vst[vi]], writes=[k_scr], dma=True)
                S.add("sp", lambda e, vi=vi, st_=st_: e.dma_start(out=Vw_d[st_], in_=vst[vi][:, 1, :, :]),
                      reads=[k_vst[vi]], writes=[k_scr], dma=True)
        if has_s:
            ks = rot("kvst", 2)
            for cb in range(3):
                wi, wv = kvw[cb]
                pi = next_ps()
                for dc in range(8):
                    S.add("pe", lambda e, pi=pi, wv=wv, dc=dc: e.matmul(
                        psum[pi][0:NS, :], uT[:, dc, NCOL:NCX], wv[:, dc, :], start=(dc == 0), stop=(dc == 7)),
                        reads=[k_wb[wi], k_uT], writes=[k_ps[pi]])
                S.add("act", lambda e, pi=pi, ks=ks, cb=cb: e.activation(
                    out=kvst[ks][0:NS, cb * 512:(cb + 1) * 512], in_=psum[pi][0:NS, :], func=AF.Copy),
                    reads=[k_ps[pi]], writes=[k_kvst[ks]])
            S.add("sp", lambda e, ks=ks: e.dma_start(out=kv_s_out[:, :], in_=kvst[ks][0:NS, 0:1024]),
                  reads=[k_kvst[ks]], writes=[k_out], dma=True)
            S.add("sp", lambda e, ks=ks: e.dma_start(out=win_s_out[:, 511, :], in_=kvst[ks][0:NS, 1024:1536]),
                  reads=[k_kvst[ks]], writes=[k_out], dma=True)
            if STAGE >= 4:
                for bi_, (kc0, vc0) in enumerate(((512, 768), (1024, 1280))):
                    S.add("dve", lambda e, ks=ks, bi_=bi_, vc0=vc0: e.tensor_copy(
                        out=vnew[:, bi_, :, 0:64], in_=kvst[ks][0:NS, vc0:vc0 + 256].rearrange("p (g d) -> p g d", g=4)),
                        reads=[k_kvst[ks]], writes=[k_vnew])
                    S.add("dve", lambda e, ks=ks, bi_=bi_, kc0=kc0: e.tensor_copy(
                        out=knew_bf[:, bi_ * 256:(bi_ + 1) * 256], in_=kvst[ks][0:NS, kc0:kc0 + 256]),
                        reads=[k_kvst[ks]], writes=[k_knew])
                pi = next_ps()
                pbf = psum[pi][:, :].bitcast(BF16)
                for j in range(8):
                    S.add("pe", lambda e, pbf=pbf, j=j: e.transpose(
                        pbf[0:64, j * NS:(j + 1) * NS], knew_bf[:, j * 64:(j + 1) * 64], identb[0:NS, 0:NS]),
                        reads=[k_knew, k_const], writes=[k_ps[pi]])
                S.add("act", lambda e, pbf=pbf: e.activation(
                    out=knT[0:64, :, :, :], in_=pbf[0:64, 0:8 * NS].rearrange("d (b g n) -> d b g n", b=2, g=4), func=AF.Copy),
                    reads=[k_ps[pi]], writes=[k_knT])
        if STAGE >= 3:
            for (cb, dst) in ((1, KsT_d), (2, KwT_d)):
                wi, wv = kvw[cb]
                for g in range(4):
                    ki = rot("ktst", 2)
                    for th in range(2):
                        pi = next_ps()
                        c0 = 16 + th * 512
                        for dc in range(8):
                            S.add("pe", lambda e, pi=pi, wv=wv, dc=dc, g=g, c0=c0: e.matmul(
                                psum[pi][0:64, :], wv[:, dc, g * 64:(g + 1) * 64], uT[:, dc, c0:c0 + 512],
                                start=(dc == 0), stop=(dc == 7)),
                                reads=[k_wb[wi], k_uT], writes=[k_ps[pi]])
                        S.add("act", lambda e, pi=pi, ki=ki, th=th: e.activation(
                            out=ktst[ki][:, th * 512:(th + 1) * 512], in_=psum[pi][0:64, :], func=AF.Copy),
                            reads=[k_ps[pi]], writes=[k_ktst[ki]])
                    S.add("sp", lambda e, ki=ki, g=g, dst=dst, slot0=slot0: e.dma_start(
                        out=dst[slot0:slot0 + 8, 0:64, g, :].rearrange("t d k -> d t k"),
                        in_=ktst[ki][:, :].rearrange("d (t k) -> d t k", t=8)),
                        reads=[k_ktst[ki]], writes=[k_scr], dma=True)
            wi, wv = kvw[0]
            for kv in range(2):
                for g in range(4):
                    xi = rot("ktst", 2)
                    for th in range(2):
                        pi = next_ps()
                        c0 = 16 + th * 512
                        for dc in range(8):
                            S.add("pe", lambda e, pi=pi, wv=wv, dc=dc, g=g, kv=kv, c0=c0: e.matmul(
                                psum[pi][0:64, :], wv[:, dc, kv * 256 + g * 64: kv * 256 + (g + 1) * 64], uT[:, dc, c0:c0 + 512],
                                start=(dc == 0), stop=(dc == 7)),
                                reads=[k_wb[wi], k_uT], writes=[k_ps[pi]])
                        S.add("act", lambda e, pi=pi, xi=xi, th=th: e.activation(
                            out=xct[xi][:, th * 512:(th + 1) * 512], in_=psum[pi][0:64, :], func=AF.Copy),
                            reads=[k_ps[pi]], writes=[k_xct[xi]])
                    pi = next_ps()
                    xb = xct[xi]
                    for pos in range(32):
                        S.add("pe", lambda e, pi=pi, kv=kv, pos=pos, xb=xb: e.matmul(
                            psum[pi][:, 0:63], w1_sb[:, kv, pos, :], xb[:, pos:pos + 993:16],
                            start=(pos == 0), stop=(pos == 31)),
                            reads=[k_xct[xi], k_const], writes=[k_ps[pi]])
                    for p in range(16):
                        S.add("pe", lambda e, pi=pi, kv=kv, p=p, xb=xb: e.matmul(
                            psum[pi][:, 64:65], w1_sb[:, kv, p, :], xb[:, 1008 + p:1009 + p],
                            start=(p == 0), stop=(p == 15)),
                            reads=[k_xct[xi], k_const], writes=[k_ps[pi]])
                    for p in range(16):
                        S.add("pe", lambda e, pi=pi, kv=kv, p=p, xb=xb: e.matmul(
                            psum[pi][:, 65:66], w1_sb[:, kv, 16 + p, :], xb[:, p:p + 1],
                            start=(p == 0), stop=(p == 15)),
                            reads=[k_xct[xi], k_const], writes=[k_ps[pi]])
                    sb0 = (slot0 * 8) % 256
                    po = sb0 // 64
                    S.add("act", lambda e, pi=pi, kv=kv, g=g, sb0=sb0: e.activation(
                        out=hs_all[:, kv, g, sb0:sb0 + 63], in_=psum[pi][:, 0:63], func=AF.Silu, bias=cvec[:, kv:kv + 1]),
                        reads=[k_ps[pi], k_cmp], writes=[k_AB])
                    S.add("act", lambda e, pi=pi, kv=kv, g=g, po=po: e.activation(
                        out=ABs[:, kv, g, :, po], in_=psum[pi][:, 64:66], func=AF.Copy),
                        reads=[k_ps[pi]], writes=[k_AB])

    if STAGE >= 3 and not _DEV.get("prep_only"):
        for kv in range(2):
            for g in range(4):
                S.add("dve", lambda e, kv=kv, g=g: e.tensor_tensor(
                    out=hpreb[:, 0:3], in0=ABs[:, kv, g, 0, 0:3], in1=ABs[:, kv, g, 1, 1:4], op=ALU.add),
                    reads=[k_AB], writes=[k_cmp2])
                S.add("dve", lambda e, kv=kv, g=g: e.tensor_tensor(
                    out=hpreb[:, 3:4], in0=ABs[:, kv, g, 0, 3:4], in1=ABs[:, kv, g, 1, 0:1], op=ALU.add),
                    reads=[k_AB], writes=[k_cmp2])
                S.add("act", lambda e, kv=kv, g=g: e.activation(
                    out=hs_all[:, kv, g, 63:256:64], in_=hpreb[:, :], func=AF.Silu, bias=cvec[:, kv:kv + 1]),
                    reads=[k_cmp2, k_cmp], writes=[k_AB])
                pi = next_ps()
                if kv == 0:
                    S.add("pe", lambda e, pi=pi, g=g: e.matmul(psum[pi][0:64, 0:256], w2_sb[:, 0, :], hs_all[:, 0, g, :], start=True, stop=True),
                          reads=[k_AB, k_const], writes=[k_ps[pi]])
                    S.add("act", lambda e, pi=pi, g=g: e.activation(out=kcT[0:64, g, :], in_=psum[pi][0:64, 0:256], func=AF.Copy),
                          reads=[k_ps[pi]], writes=[k_kc])
                else:
                    for nt in range(2):
                        S.add("pe", lambda e, pi=pi, nt=nt, g=g: e.matmul(
                            psum[pi][:, nt * 64:(nt + 1) * 64], hs_all[:, 1, g, nt * 128:(nt + 1) * 128], w2_sb[:, 1, :], start=True, stop=True),
                            reads=[k_AB, k_const], writes=[k_ps[pi]])
                    S.add("act", lambda e, pi=pi, g=g: e.activation(
                        out=vcM[:, :, g, 0:64], in_=psum[pi][:, 0:128].rearrange("n (t d) -> n t d", t=2), func=AF.Copy),
                        reads=[k_ps[pi]], writes=[k_vc])

    PS_S = [0, 1, 2, 3]
    PS_ACC = [4, 5]
    PS_M = [6, 7]
    QT = hT[:, :, :].rearrange("p a b -> p (a b)").rearrange("p (g r t) -> p g r t", g=4, r=4)

    def attention_tile(i, ogi):
        tl = i % 8
        c0 = tl * 128
        pq = rot("pq", 2)
        pqb = posq_bc[pq]
        kpq = k_posq[pq]
        S.add("sp", lambda e: e.dma_start(out=pqb[:, :], in_=posq_d[i * 128:(i + 1) * 128].partition_broadcast(128)),
              writes=[kpq], dma=True)
        S.add("sp", lambda e: e.dma_start(out=privn_sb[pq][:, :], in_=privn_d[i * 128:(i + 1) * 128, :]),
              writes=[k_privn[pq]], dma=True)
        S.add("sp", lambda e: e.dma_start(out=pribias_sb[pq][:, :], in_=pribias_d[i * 128:(i + 1) * 128, :]),
              writes=[k_privn[pq]], dma=True)
        S.add("sp", lambda e: e.dma_start(out=kcT[64:70, :, :], in_=caug_d[i].unsqueeze(1).to_broadcast([6, 4, 256])),
              writes=[k_kc], dma=True)

        def branch_core(gp, br, slots, kind):
            gs = [2 * gp, 2 * gp + 1]
            nkt = len(slots)

            def front(ki_, slot):
                cx = {"first": ki_ == 0, "lastk": ki_ == nkt - 1, "slot": slot, "pti": {}, "kb": None}
                kb = None
                if kind == "c":
                    kt_r = [k_kc]
                    cx["vt_r"] = [k_vc]
                else:
                    kb = rot("kb", 3)
                    ksrc = KsT_d if kind == "s" else KwT_d
                    vsrc = Vs_d if kind == "s" else Vw_d
                    S.add("sp", lambda e, kb=kb, ksrc=ksrc, slot=slot: e.dma_start(out=kbuf[kb][:, :, :], in_=ksrc[slot]),
                          reads=[k_scr], writes=[k_kbuf[kb]], dma=True)
                    S.add("sp", lambda e, kb=kb, vsrc=vsrc, slot=slot: e.dma_start(out=vbuf[kb][:, :, :], in_=vsrc[slot]),
                          reads=[k_scr], writes=[k_vbuf[kb]], dma=True)
                    kt_r = [k_kbuf[kb]]
                    cx["vt_r"] = [k_vbuf[kb]]
                cx["kb"] = kb
                psm = PS_M[ki_ % 2]
                if kind == "c":
                    mi = rot("mask", 3)
                    S.add("dve", lambda e, mi=mi, slot=slot: e.tensor_scalar(
                        out=mask_sb[mi][:, 0:128], in0=pqb[:, :], scalar1=cend_sb[:, slot:slot + 1], scalar2=None, op0=ALU.is_ge),
                        reads=[kpq, k_const], writes=[k_mask[mi]])
                    mk = ("sb", mi)
                elif kind == "w":
                    mi = rot("mask", 3)
                    S.add("dve", lambda e, slot=slot: e.tensor_scalar(
                        out=mtmp[0][:, :], in0=pqb[:, :], scalar1=posk_sb[:, slot:slot + 1], scalar2=0.0,
                        op0=ALU.subtract, op1=ALU.is_ge), reads=[kpq, k_const], writes=[k_mtmp[0]])
                    S.add("dve", lambda e, slot=slot: e.tensor_scalar(
                        out=mtmp[1][:, :], in0=pqb[:, :], scalar1=posk_sb[:, slot:slot + 1], scalar2=512.0,
                        op0=ALU.subtract, op1=ALU.is_lt), reads=[kpq, k_const], writes=[k_mtmp[1]])
                    S.add("dve", lambda e, mi=mi: e.tensor_tensor(
                        out=mask_sb[mi][:, 0:128], in0=mtmp[0][:, :], in1=mtmp[1][:, :], op=ALU.mult),
                        reads=[k_mtmp[0], k_mtmp[1]], writes=[k_mask[mi]])
                    mk = ("sb", mi)
                else:
                    for g in gs:
                        S.add("pe", lambda e, g=g, slot=slot, psm=psm: e.matmul(
                            psum[psm][:, g * 128:(g + 1) * 128], esel_sb[:, slot * 128:(slot + 1) * 128], selT[:, g, :],
                            start=True, stop=True), reads=[k_selT, k_const], writes=[k_ps[psm]])
                    mi = rot("mask", 3)
                    S.add("dve", lambda e, mi=mi, psm=psm, g0=gs[0]: e.tensor_copy(
                        out=mask_sb[mi][:, :], in_=psum[psm][:, g0 * 128:g0 * 128 + 256]),
                        reads=[k_ps[psm]], writes=[k_mask[mi]])
                    mk = ("sb2", mi)
                for g in gs:
                    si = PS_S[rot("psS", 4)]
                    pti = rot("pT", 4)
                    cx["pti"][g] = pti
                    if kind == "c":
                        lhs = kcT[:, g, slot * 128:(slot + 1) * 128]
                    else:
                        lhs = kbuf[kb][:, g, :]
                    S.add("pe", lambda e, si=si, lhs=lhs, g=g: e.matmul(
                        psum[si][:, :].rearrange("k (r t) -> k r t", r=4), lhs, QT[0:70, g, :, c0:c0 + 128], start=True, stop=True),
                        reads=kt_r + list(k_h), writes=[k_ps[si]])
                    S.add("act", lambda e, si=si, pti=pti: e.activation(
                        out=pT[pti][:, :, :], in_=psum[si][:, :].rearrange("k (r t) -> k r t", r=4), func=AF.Exp),
                        reads=[k_ps[si]], writes=[k_pT[pti]])
                    if mk[0] == "sb":
                        m_ap = mask_sb[mk[1]][:, 0:128].unsqueeze(1).to_broadcast([128, 4, 128])
                        m_r = [k_mask[mk[1]]]
                    else:
                        gl = g - gs[0]
                        m_ap = mask_sb[mk[1]][:, gl * 128:(gl + 1) * 128].unsqueeze(1).to_broadcast([128, 4, 128])
                        m_r = [k_mask[mk[1]]]
                    S.add("dve", lambda e, pti=pti, m_ap=m_ap: e.tensor_tensor(
                        out=pT[pti][:, :, :], in0=pT[pti][:, :, :], in1=m_ap, op=ALU.mult),
                        reads=[k_pT[pti]] + m_r, writes=[k_pT[pti]])
                    if kind == "s" and slot == i:
                        S.add("dve", lambda e, pti=pti: e.tensor_tensor(
                            out=pT[pti][:, :, :], in0=pT[pti][:, :, :],
                            in1=tri_sb[:, :].unsqueeze(1).to_broadcast([128, 4, 128]), op=ALU.mult),
                            reads=[k_pT[pti], k_const], writes=[k_pT[pti]])
                return cx

            def back(cx):
                slot, kb, first, lastk = cx["slot"], cx["kb"], cx["first"], cx["lastk"]
                for g in gs:
                    pti = cx["pti"][g]
                    ai = PS_ACC[g % 2]
                    ncol = 128 if kind == "c" else 65
                    for r in range(4):
                        if kind == "c":
                            rhs = vcM[:, slot, g, :]
                        else:
                            rhs = vbuf[kb][:, g, :]
                        S.add("pe", lambda e, ai=ai, pti=pti, r=r, rhs=rhs, ncol=ncol, first=first, lastk=lastk: e.matmul(
                            psum[ai][:, r * ncol:(r + 1) * ncol], pT[pti][:, r, :], rhs, start=(first and r == 0), stop=lastk,
                            skip_group_check=True),
                            reads=[k_pT[pti]] + cx["vt_r"], writes=[k_ps[ai]])

            pend = None
            for ki_, slot in enumerate(slots):
                cx = front(ki_, slot)
                if pend is not None:
                    back(pend)
                pend = cx
            back(pend)
            for g in gs:
                ai = PS_ACC[g % 2]
                if kind == "c":
                    acc3 = psum[ai][:, :].rearrange("t (r c) -> t r c", r=4)
                    S.add("dve", lambda e, acc3=acc3: e.tensor_reduce(
                        out=rsum[:, :], in_=acc3[:, :, 64:128], axis=AX.X, op=ALU.add),
                        reads=[k_ps[ai]], writes=[k_rsum])
                    S.add("dve", lambda e: e.tensor_scalar(out=rsum[:, :], in0=rsum[:, :], scalar1=0.5, scalar2=1e-30,
                                                           op0=ALU.mult, op1=ALU.max), reads=[k_rsum], writes=[k_rsum])
                else:
                    acc3 = psum[ai][:, 0:260].rearrange("t (r c) -> t r c", r=4)
                    S.add("dve", lambda e, acc3=acc3: e.tensor_scalar(
                        out=rsum[:, :], in0=acc3[:, :, 64], scalar1=1e-30, scalar2=None, op0=ALU.max),
                        reads=[k_ps[ai]], writes=[k_rsum])
                S.add("dve", lambda e: e.reciprocal(out=rsum[:, :], in_=rsum[:, :]), reads=[k_rsum], writes=[k_rsum])
                if kind == "c":
                    for r in range(4):
                        if r == 0:
                            S.add("dve", lambda e, acc3=acc3, g=g: e.tensor_scalar(
                                out=pri[:, g, :], in0=acc3[:, 0, 64:128], scalar1=rsum[:, 0:1], scalar2=None, op0=ALU.mult),
                                reads=[k_ps[ai], k_rsum], writes=[k_pri])
                        else:
                            S.add("dve", lambda e, acc3=acc3, g=g, r=r: e.scalar_tensor_tensor(
                                out=pri[:, g, :], in0=acc3[:, r, 64:128], scalar=rsum[:, r:r + 1], in1=pri[:, g, :],
                                op0=ALU.mult, op1=ALU.add), reads=[k_ps[ai], k_rsum, k_pri], writes=[k_pri])
                S.add("dve", lambda e, g=g, br=br: e.tensor_tensor(
                    out=fsc[:, :], in0=rsum[:, :],
                    in1=gate_sb[:, tl, :].rearrange("t (h b) -> t h b", b=3)[:, 4 * g:4 * g + 4, br], op=ALU.mult),
                    reads=[k_rsum, k_gate], writes=[k_fsc])
                if br == 0:
                    S.add("dve", lambda e, acc3=acc3, g=g: e.tensor_tensor(
                        out=oacc[:, 4 * g:4 * g + 4, :], in0=acc3[:, :, 0:64],
                        in1=fsc[:, :].unsqueeze(2).to_broadcast([128, 4, 64]), op=ALU.mult),
                        reads=[k_ps[ai], k_fsc], writes=[k_oacc])
                else:
                    S.add("dve", lambda e, acc3=acc3: e.tensor_tensor(
                        out=otmp[:, :, :], in0=acc3[:, :, 0:64],
                        in1=fsc[:, :].unsqueeze(2).to_broadcast([128, 4, 64]), op=ALU.mult),
                        reads=[k_ps[ai], k_fsc], writes=[k_otmp])
                    S.add("dve", lambda e, g=g: e.tensor_tensor(
                        out=oacc[:, 4 * g:4 * g + 4, :], in0=oacc[:, 4 * g:4 * g + 4, :], in1=otmp[:, :, :], op=ALU.add),
                        reads=[k_otmp, k_oacc], writes=[k_oacc])

        for gp in range(2):
            branch_core(gp, 0, [0, 1], "c")
        for g in range(4):
            S.add("dve", lambda e, g=g: e.tensor_tensor(out=pri[:, g, :], in0=pri[:, g, :], in1=privn_sb[pq][:, :], op=ALU.mult),
                  reads=[k_pri, k_privn[pq]], writes=[k_pri])
            S.add("dve", lambda e, g=g: e.tensor_tensor(out=pri[:, g, :], in0=pri[:, g, :], in1=pribias_sb[pq][:, :], op=ALU.add),
                  reads=[k_pri, k_privn[pq]], writes=[k_pri])
            S.add("dve", lambda e, g=g: e.max(out=m8a[:, :], in_=pri[:, g, :]), reads=[k_pri], writes=[k_m8])
            S.add("dve", lambda e, g=g: e.match_replace(out=pri2[:, :], in_to_replace=m8a[:, :], in_values=pri[:, g, :], imm_value=-1e30),
                  reads=[k_pri, k_m8], writes=[k_pri2])
            S.add("dve", lambda e: e.max(out=m8b[:, :], in_=pri2[:, :]), reads=[k_pri2], writes=[k_m8])
            S.add("dve", lambda e: e.tensor_scalar(out=m8b[:, 7:8], in0=m8b[:, 7:8], scalar1=0.0, scalar2=None, op0=ALU.max),
                  reads=[k_m8], writes=[k_m8])
            S.add("dve", lambda e, g=g: e.tensor_scalar(out=sel_sb[:, g, :], in0=pri[:, g, :], scalar1=m8b[:, 7:8], scalar2=None, op0=ALU.is_ge),
                  reads=[k_pri, k_m8], writes=[k_sel])
        PS_X = PS_S[rot("psS", 4)]
        psx_bf = psum[PS_X][:, :].bitcast(BF16)
        for g in range(4):
            S.add("pe", lambda e, g=g: e.transpose(psx_bf[0:64, g * 128:(g + 1) * 128], sel_sb[:, g, :], identb[:, :]),
                  reads=[k_sel, k_const], writes=[k_ps[PS_X]])
        S.add("act", lambda e: e.activation(out=selT[:, :, :], in_=psx_bf[0:64, 0:512].rearrange("s (g t) -> s g t", g=4), func=AF.Copy),
              reads=[k_ps[PS_X]], writes=[k_selT])
        sel_slots = list(range(0, i + 1)) + list(range(16, 32))
        for gp in range(2):
            branch_core(gp, 1, sel_slots, "s")
        if i >= 4:
            win_slots = list(range(i - 4, i + 1))
        else:
            win_slots = list(range(28 + i, 32)) + list(range(0, i + 1))
        for gp in range(2):
            branch_core(gp, 2, win_slots, "w")
        S.add("act", lambda e: e.activation(out=obf[:, :], in_=oacc[:, :, :].rearrange("t h d -> t (h d)"), func=AF.Copy),
              reads=[k_oacc], writes=[k_obf])
        PS_X2 = PS_S[rot("psS", 4)]
        psx2_bf = psum[PS_X2][:, :].bitcast(BF16)
        for fc in range(8):
            S.add("pe", lambda e, fc=fc: e.transpose(psx2_bf[:, fc * 128:(fc + 1) * 128], obf[:, fc * 128:(fc + 1) * 128], identb[:, :]),
                  reads=[k_obf, k_const], writes=[k_ps[PS_X2]])
        S.add("dve", lambda e: e.tensor_copy(out=uT[:, :, 16 + c0:16 + c0 + 128],
                                             in_=psx2_bf[:, :].rearrange("f (c t) -> f c t", c=8)),
              reads=[k_ps[PS_X2]], writes=[k_uT])

    def sample_attention():
        P_ = NS
        PS_X = PS_S[0]
        psx_bf = psum[PS_X][:, :].bitcast(BF16)

        def run_branch(gp, br, kind):
            gs = [2 * gp, 2 * gp + 1]
            started = {g: False for g in gs}
            steps = []
            for s_ in range(NS):
                if kind == "c":
                    tiles = [("c", 0)]
                elif kind == "s":
                    tiles = [("k", kt) for kt in range(16)] + [("n", 0)]
                else:
                    tiles = [("k", t_) for t_ in range(4)] + [("n", 1)]
                for (tk, ti_) in tiles:
                    steps.append((s_, tk, ti_))

            def front(n_, s_, tk, ti_):
                cx = {"s": s_, "nk": 128, "pz": {}}
                psm = PS_M[n_ % 2]
                if tk == "c":
                    ci = rot("kc2", 2)
                    S.add("sp", lambda e, ci=ci, s_=s_: e.dma_start(out=kcTs[ci][0:64, :, :], in_=kcTS_d[s_]),
                          reads=[k_sscr], writes=[k_kcTs[ci]], dma=True)
                    S.add("sp", lambda e, ci=ci, s_=s_: e.dma_start(out=vcMs[ci][:, :, 0:64], in_=vcS_d[s_]),
                          reads=[k_sscr], writes=[k_vcMs[ci]], dma=True)
                    kt_r, cx["vt_r"] = [k_kcTs[ci]], [k_vcMs[ci]]
                    klhs = lambda g, ci=ci: kcTs[ci][:, g, :]
                    cx["vrhs"] = lambda g, ci=ci: vcMs[ci][:, g, :]
                    mcol = maskS[:, 0:1]
                    m_r = [k_const]
                elif tk == "k":
                    kb = rot("kb", 3)
                    ksrc = KsTS_d if kind == "s" else KwTS_d
                    vsrc = VsS_d if kind == "s" else VwS_d
                    S.add("sp", lambda e, kb=kb, ksrc=ksrc, s_=s_, ti_=ti_: e.dma_start(out=kbuf[kb][:, :, :], in_=ksrc[s_, ti_]),
                          reads=[k_sscr], writes=[k_kbuf[kb]], dma=True)
                    S.add("sp", lambda e, kb=kb, vsrc=vsrc, s_=s_, ti_=ti_: e.dma_start(out=vbuf[kb][:, :, :], in_=vsrc[s_, ti_]),
                          reads=[k_sscr], writes=[k_vbuf[kb]], dma=True)
                    kt_r, cx["vt_r"] = [k_kbuf[kb]], [k_vbuf[kb]]
                    klhs = lambda g, kb=kb: kbuf[kb][:, g, :]
                    cx["vrhs"] = lambda g, kb=kb: vbuf[kb][:, g, :]
                    if kind == "w":
                        mcol = maskS[:, 1 + ti_:2 + ti_]
                        m_r = [k_const]
                    else:
                        for g in gs:
                            S.add("pe", lambda e, g=g, ti_=ti_, s_=s_, psm=psm: e.matmul(
                                psum[psm][:, g:g + 1], esel_sb[:, ti_ * 128:(ti_ + 1) * 128], selTs[:, g, s_:s_ + 1],
                                start=True, stop=True), reads=[k_selTs, k_const], writes=[k_ps[psm]])
                        mcol = None
                        m_r = [k_ps[psm]]
                else:
                    cx["nk"] = NS
                    bi_ = ti_
                    kt_r, cx["vt_r"] = [k_knT], [k_vnew]
                    klhs = lambda g, bi_=bi_: knT[:, bi_, g, :]
                    cx["vrhs"] = lambda g, bi_=bi_: vnew[:, bi_, g, :]
                    mcol = ident[0:NS, s_:s_ + 1]
                    m_r = [k_const]
                nk = cx["nk"]
                for g in gs:
                    si = PS_S[rot("psS", 4)]
                    pz = rot("pz", 4)
                    cx["pz"][g] = pz
                    S.add("pe", lambda e, si=si, g=g, klhs=klhs, nk=nk, s_=s_: e.matmul(
                        psum[si][0:nk, 0:4], klhs(g), QTs[0:70, g, :, s_], start=True, stop=True),
                        reads=kt_r + [k_QTs], writes=[k_ps[si]])
                    S.add("act", lambda e, si=si, pz=pz, nk=nk, s_=s_: e.activation(
                        out=Pz[pz][0:nk, :, s_], in_=psum[si][0:nk, 0:4], func=AF.Exp),
                        reads=[k_ps[si]], writes=[k_Pz[pz]])
                    mc = mcol if mcol is not None else psum[psm][:, g:g + 1]
                    S.add("dve", lambda e, pz=pz, nk=nk, s_=s_, mc=mc: e.tensor_scalar(
                        out=Pz[pz][0:nk, :, s_], in0=Pz[pz][0:nk, :, s_], scalar1=mc[0:nk, :] if mc.shape[0] != nk else mc, scalar2=None, op0=ALU.mult),
                        reads=[k_Pz[pz]] + m_r, writes=[k_Pz[pz]])
                return cx

            def back(cx):
                s_, nk = cx["s"], cx["nk"]
                for g in gs:
                    pz = cx["pz"][g]
                    vr = cx["vrhs"](g)
                    ai = PS_ACC[g % 2]
                    ncol = 128 if kind == "c" else 65
                    for r in range(4):
                        st_flag = (not started[g])
                        started[g] = True
                        S.add("pe", lambda e, ai=ai, pz=pz, r=r, vr=vr, ncol=ncol, nk=nk, st_flag=st_flag: e.matmul(
                            psum[ai][0:NS, r * ncol:(r + 1) * ncol], Pz[pz][0:nk, r, :], vr[0:nk, :] if nk != 128 else vr,
                            start=st_flag, stop=False, skip_group_check=True),
                            reads=[k_Pz[pz]] + cx["vt_r"], writes=[k_ps[ai]])
                    S.add("dve", lambda e, pz=pz, nk=nk, s_=s_: e.memset(Pz[pz][0:nk, :, s_], 0.0),
                          writes=[k_Pz[pz]])

            pend = None
            for n_, (s_, tk, ti_) in enumerate(steps):
                cx = front(n_, s_, tk, ti_)
                if pend is not None:
                    back(pend)
                pend = cx
            back(pend)
            for g in gs:
                ai = PS_ACC[g % 2]
                if kind == "c":
                    acc3 = psum[ai][0:P_, :].rearrange("t (r c) -> t r c", r=4)
                    S.add("dve", lambda e, acc3=acc3: e.tensor_reduce(
                        out=rsum[0:P_, :], in_=acc3[:, :, 64:128], axis=AX.X, op=ALU.add), reads=[k_ps[ai]], writes=[k_rsum])
                    S.add("dve", lambda e: e.tensor_scalar(out=rsum[0:P_, :], in0=rsum[0:P_, :], scalar1=0.5, scalar2=1e-30,
                                                           op0=ALU.mult, op1=ALU.max), reads=[k_rsum], writes=[k_rsum])
                else:
                    acc3 = psum[ai][0:P_, 0:260].rearrange("t (r c) -> t r c", r=4)
                    S.add("dve", lambda e, acc3=acc3: e.tensor_scalar(
                        out=rsum[0:P_, :], in0=acc3[:, :, 64], scalar1=1e-30, scalar2=None, op0=ALU.max),
                        reads=[k_ps[ai]], writes=[k_rsum])
                S.add("dve", lambda e: e.reciprocal(out=rsum[0:P_, :], in_=rsum[0:P_, :]), reads=[k_rsum], writes=[k_rsum])
                if kind == "c":
                    for r in range(4):
                        if r == 0:
                            S.add("dve", lambda e, acc3=acc3, g=g: e.tensor_scalar(
                                out=pri[0:P_, g, :], in0=acc3[:, 0, 64:128], scalar1=rsum[0:P_, 0:1], scalar2=None, op0=ALU.mult),
                                reads=[k_ps[ai], k_rsum], writes=[k_pri])
                        else:
                            S.add("dve", lambda e, acc3=acc3, g=g, r=r: e.scalar_tensor_tensor(
                                out=pri[0:P_, g, :], in0=acc3[:, r, 64:128], scalar=rsum[0:P_, r:r + 1], in1=pri[0:P_, g, :],
                                op0=ALU.mult, op1=ALU.add), reads=[k_ps[ai], k_rsum, k_pri], writes=[k_pri])
                S.add("dve", lambda e, g=g, br=br: e.tensor_tensor(
                    out=fsc[0:P_, :], in0=rsum[0:P_, :],
                    in1=gate_s[:, :].rearrange("t (h b) -> t h b", b=3)[:, 4 * g:4 * g + 4, br], op=ALU.mult),
                    reads=[k_rsum, k_gates], writes=[k_fsc])
                if br == 0:
                    S.add("dve", lambda e, acc3=acc3, g=g: e.tensor_tensor(
                        out=oacc[0:P_, 4 * g:4 * g + 4, :], in0=acc3[:, :, 0:64],
                        in1=fsc[0:P_, :].unsqueeze(2).to_broadcast([P_, 4, 64]), op=ALU.mult),
                        reads=[k_ps[ai], k_fsc], writes=[k_oacc])
                else:
                    S.add("dve", lambda e, acc3=acc3: e.tensor_tensor(
                        out=otmp[0:P_, :, :], in0=acc3[:, :, 0:64],
                        in1=fsc[0:P_, :].unsqueeze(2).to_broadcast([P_, 4, 64]), op=ALU.mult),
                        reads=[k_ps[ai], k_fsc], writes=[k_otmp])
                    S.add("dve", lambda e, g=g: e.tensor_tensor(
                        out=oacc[0:P_, 4 * g:4 * g + 4, :], in0=oacc[0:P_, 4 * g:4 * g + 4, :], in1=otmp[0:P_, :, :], op=ALU.add),
                        reads=[k_otmp, k_oacc], writes=[k_oacc])

        for gp in range(2):
            run_branch(gp, 0, "c")
        for g in range(4):
            S.add("dve", lambda e, g=g: e.tensor_tensor(out=pri[0:P_, g, :], in0=pri[0:P_, g, :], in1=privnS[:, :], op=ALU.mult),
                  reads=[k_pri, k_const], writes=[k_pri])
            S.add("dve", lambda e, g=g: e.tensor_tensor(out=pri[0:P_, g, :], in0=pri[0:P_, g, :], in1=pribiasS[:, :], op=ALU.add),
                  reads=[k_pri, k_const], writes=[k_pri])
            S.add("dve", lambda e, g=g: e.max(out=m8a[0:P_, :], in_=pri[0:P_, g, :]), reads=[k_pri], writes=[k_m8])
            S.add("dve", lambda e, g=g: e.match_replace(out=pri2[0:P_, :], in_to_replace=m8a[0:P_, :], in_values=pri[0:P_, g, :], imm_value=-1e30),
                  reads=[k_pri, k_m8], writes=[k_pri2])
            S.add("dve", lambda e: e.max(out=m8b[0:P_, :], in_=pri2[0:P_, :]), reads=[k_pri2], writes=[k_m8])
            S.add("dve", lambda e: e.tensor_scalar(out=m8b[0:P_, 7:8], in0=m8b[0:P_, 7:8], scalar1=0.0, scalar2=None, op0=ALU.max),
                  reads=[k_m8], writes=[k_m8])
            S.add("dve", lambda e, g=g: e.tensor_scalar(out=sel_sb[0:P_, g, :], in0=pri[0:P_, g, :], scalar1=m8b[0:P_, 7:8], scalar2=None, op0=ALU.is_ge),
                  reads=[k_pri, k_m8], writes=[k_sel])
        for g in range(4):
            S.add("pe", lambda e, g=g: e.transpose(psx_bf[0:64, g * NS:(g + 1) * NS], sel_sb[0:P_, g, :], identb[0:P_, 0:P_]),
                  reads=[k_sel, k_const], writes=[k_ps[PS_X]])
        S.add("act", lambda e: e.activation(out=selTs[:, :, :], in_=psx_bf[0:64, 0:4 * NS].rearrange("s (g t) -> s g t", g=4), func=AF.Copy),
              reads=[k_ps[PS_X]], writes=[k_selTs])
        for gp in range(2):
            run_branch(gp, 1, "s")
        for gp in range(2):
            run_branch(gp, 2, "w")
        S.add("act", lambda e: e.activation(out=obf[0:P_, :], in_=oacc[0:P_, :, :].rearrange("t h d -> t (h d)"), func=AF.Copy),
              reads=[k_oacc], writes=[k_obf])
        for fc in range(8):
            S.add("pe", lambda e, fc=fc: e.transpose(psx_bf[:, fc * NS:(fc + 1) * NS], obf[0:P_, fc * 128:(fc + 1) * 128], identb[0:P_, 0:P_]),
                  reads=[k_obf, k_const], writes=[k_ps[PS_X]])
        S.add("dve", lambda e: e.tensor_copy(out=uT[:, :, NCOL:NCX], in_=psx_bf[:, 0:8 * NS].rearrange("f (c t) -> f c t", c=8)),
              reads=[k_ps[PS_X]], writes=[k_uT])

    if STAGE >= 2 and not _DEV.get("prep_only"):
        for ogi in range(2):
            has_s = (ogi == 0)
            RNG = MAINR + ([SR] if has_s else [])
            S.add("sp", lambda e, ogi=ogi: e.dma_start(
                out=xT[:, :, 16:NCX], in_=x1_d[ogi].rearrange("p (a b) -> p a b", a=8)),
                reads=[k_x1[ogi]], writes=list(k_xT), dma=True)
            if STAGE >= 3:
                norm(VGQ, MAINR)
                wq = [wload(w_qg.rearrange("(dc p) f -> p dc f", p=128)[:, :, cb * 512:(cb + 1) * 512], (8, 512)) for cb in range(2)]
                wgi, wgv = wload(w_qg.rearrange("(dc p) f -> p dc f", p=128)[:, :, 1024:1072], (8, 48))
                for h in range(16):
                    g, r = h // 4, h % 4
                    wi, wv = wq[h // 8]
                    for th in range(2):
                        pi = next_ps()
                        cc = 16 + th * 512
                        for dc in range(8):
                            S.add("pe", lambda e, pi=pi, wv=wv, dc=dc, h=h, cc=cc: e.matmul(
                                psum[pi][0:64, :], wv[:, dc, (h % 8) * 64:(h % 8 + 1) * 64], uT[:, dc, cc:cc + 512],
                                start=(dc == 0), stop=(dc == 7)), reads=[k_wb[wi], k_uT], writes=[k_ps[pi]])
                        S.add("act", lambda e, pi=pi, g=g, r=r, th=th: e.activation(
                            out=QT[0:64, g, r, th * 512:(th + 1) * 512], in_=psum[pi][0:64, :], func=AF.Copy, scale=0.125),
                            reads=[k_ps[pi]], writes=list(k_h))
                    if has_s and STAGE >= 4:
                        pi = next_ps()
                        for dc in range(8):
                            S.add("pe", lambda e, pi=pi, wv=wv, dc=dc, h=h: e.matmul(
                                psum[pi][0:64, 0:NS], wv[:, dc, (h % 8) * 64:(h % 8 + 1) * 64], uT[:, dc, NCOL:NCX],
                                start=(dc == 0), stop=(dc == 7)), reads=[k_wb[wi], k_uT], writes=[k_ps[pi]])
                        S.add("act", lambda e, pi=pi, g=g, r=r: e.activation(
                            out=QTs[0:64, g, r, :], in_=psum[pi][0:64, 0:NS], func=AF.Copy, scale=0.125),
                            reads=[k_ps[pi]], writes=[k_QTs])
                S.add("sp", lambda e, ogi=ogi: e.dma_start(
                    out=QT[64:70, :, :, :], in_=qaug_d[:, :, :, ogi * GT:(ogi + 1) * GT]), writes=list(k_h), dma=True)
                for tl in range(8):
                    pi = next_ps()
                    cc = 16 + tl * 128
                    for dc in range(8):
                        S.add("pe", lambda e, pi=pi, dc=dc, cc=cc: e.matmul(
                            psum[pi][:, 0:48], uT[:, dc, cc:cc + 128], wgv[:, dc, :], start=(dc == 0), stop=(dc == 7)),
                            reads=[k_wb[wgi], k_uT], writes=[k_ps[pi]])
                    S.add("dve", lambda e, pi=pi, tl=tl: e.tensor_tensor(
                        out=gate_sb[:, tl, :], in0=psum[pi][:, 0:48], in1=bg_sb[:, :], op=ALU.add),
                        reads=[k_ps[pi], k_const], writes=[k_gate])
                S.add("act", lambda e: e.activation(out=gate_sb[:, :, :], in_=gate_sb[:, :, :], func=AF.Sigmoid),
                      reads=[k_gate], writes=[k_gate])
                if has_s and STAGE >= 4:
                    pi = next_ps()
                    for dc in range(8):
                        S.add("pe", lambda e, pi=pi, dc=dc: e.matmul(
                            psum[pi][0:NS, 0:48], uT[:, dc, NCOL:NCX], wgv[:, dc, :], start=(dc == 0), stop=(dc == 7)),
                            reads=[k_wb[wgi], k_uT], writes=[k_ps[pi]])
                    S.add("dve", lambda e, pi=pi: e.tensor_tensor(
                        out=gate_s[:, :], in0=psum[pi][0:NS, 0:48], in1=bg_sb[0:NS, :], op=ALU.add),
                        reads=[k_ps[pi], k_const], writes=[k_gates])
                    S.add("act", lambda e: e.activation(out=gate_s[:, :], in_=gate_s[:, :], func=AF.Sigmoid),
                          reads=[k_gates], writes=[k_gates])
                for tl in range(8):
                    attention_tile(ogi * 8 + tl, ogi)
                if has_s and STAGE >= 4 and not _DEV.get("skip_sa"):
                    sample_attention()
                wo = [wload(w_o.rearrange("(fc p) f -> p fc f", p=128)[:, :, cb * 512:(cb + 1) * 512], (8, 512)) for cb in range(2)]
                for dmc in range(8):
                    wi, wv = wo[dmc // 4]
                    for th in range(2):
                        pi = next_ps()
                        cc = 16 + th * 512
                        for fc in range(8):
                            S.add("pe", lambda e, pi=pi, wv=wv, fc=fc, dmc=dmc, cc=cc: e.matmul(
                                psum[pi][:, :], wv[:, fc, (dmc % 4) * 128:(dmc % 4 + 1) * 128], uT[:, fc, cc:cc + 512],
                                start=(fc == 0), stop=(fc == 7)), reads=[k_wb[wi], k_uT], writes=[k_ps[pi]])
                        S.add("dve", lambda e, pi=pi, dmc=dmc, cc=cc: e.tensor_tensor(
                            out=xT[:, dmc, cc:cc + 512], in0=xT[:, dmc, cc:cc + 512], in1=psum[pi][:, :], op=ALU.add),
                            reads=[k_ps[pi], k_xT[dmc]], writes=[k_xT[dmc]])
                    if has_s and STAGE >= 4:
                        pi = next_ps()
                        for fc in range(8):
                            S.add("pe", lambda e, pi=pi, wv=wv, fc=fc, dmc=dmc: e.matmul(
                                psum[pi][:, 0:NS], wv[:, fc, (dmc % 4) * 128:(dmc % 4 + 1) * 128], uT[:, fc, NCOL:NCX],
                                start=(fc == 0), stop=(fc == 7)), reads=[k_wb[wi], k_uT], writes=[k_ps[pi]])
                        S.add("dve", lambda e, pi=pi, dmc=dmc: e.tensor_tensor(
                            out=xT[:, dmc, NCOL:NCX], in0=xT[:, dmc, NCOL:NCX], in1=psum[pi][:, 0:NS], op=ALU.add),
                            reads=[k_ps[pi], k_xT[dmc]], writes=[k_xT[dmc]])
            norm(VG1B, RNG)
            mlp(1, has_s)
            norm(VGF, RNG, final=True)
            if has_s:
                rows16_out(xT[:, :, NCOL:NCX], list(k_xT), y_s_out[:, :])
            for tl in range(8):
                yi = rot("yst", 2)
                cc = 16 + tl * 128
                for hb in range(2):
                    pi = next_ps()
                    for j in range(4):
                        dc = hb * 4 + j
                        S.add("pe", lambda e, pi=pi, j=j, dc=dc, cc=cc: e.transpose(
                            psum[pi][:, j * 128:(j + 1) * 128], xT[:, dc, cc:cc + 128], ident[:, :]),
                            reads=[k_xT[dc], k_const], writes=[k_ps[pi]])
                    if hb == 0:
                        S.add("act", lambda e, pi=pi, yi=yi: e.activation(out=yst[yi][:, 0:512], in_=psum[pi][:, :], func=AF.Copy),
                              reads=[k_ps[pi]], writes=[k_yst[yi]])
                    else:
                        S.add("dve", lambda e, pi=pi, yi=yi: e.tensor_copy(out=yst[yi][:, 512:1024], in_=psum[pi][:, :]),
                              reads=[k_ps[pi]], writes=[k_yst[yi]])
                row0 = ogi * GT + tl * 128
                S.add("sp", lambda e, yi=yi, row0=row0: e.dma_start(out=y_out[row0:row0 + 128, :], in_=yst[yi][:, :]),
                      reads=[k_yst[yi]], writes=[k_out], dma=True)

    S.prepare(nc)
    with nc.Block() as block:
        S.emit(block)
    S._st.close()
    es.close()
    return nc


def _bf(x):
    return np.asarray(x, np.float32).astype(NPBF)


def _hilo(x):
    x = np.asarray(x, np.float32)
    hi = x.astype(NPBF).astype(np.float32)
    lo = (x - hi).astype(NPBF).astype(np.float32)
    return hi, lo


def core_meta(half):
    seqtile = np.concatenate([16 * half + np.arange(16), 16 * (1 - half) + np.arange(16)])
    posk = (seqtile[None, :] * 128 + np.arange(128)[:, None]).astype(np.float32)
    other_visible = (half == 1)
    eff = posk.copy()
    if not other_visible:
        eff[:, 16:] = NEGPOS
    hi, lo = _hilo(eff)
    kaug = np.zeros((32, 6, 128), np.float32)
    kaug[:, 0] = hi.T; kaug[:, 1] = lo.T; kaug[:, 2] = hi.T; kaug[:, 3] = lo.T; kaug[:, 4] = 1.0; kaug[:, 5] = 1.0
    posq = (half * 2048 + np.arange(2048)).astype(np.float32)
    slopes = np.exp2(-8.0 * (np.arange(16, dtype=np.float32) + 1.0) / 16).astype(np.float32)
    shi, slo = _hilo(slopes)
    tref = (half * 2048 + (np.arange(2048) // 128) * 128 + 64).astype(np.float32)
    qaug = np.zeros((6, 4, 4, 2048), np.float32)
    for g in range(4):
        for r in range(4):
            h = g * 4 + r
            c = (-slopes[h] * tref).astype(np.float32)
            chi, clo = _hilo(c)
            qaug[0, g, r] = shi[h]; qaug[1, g, r] = shi[h]; qaug[2, g, r] = slo[h]; qaug[3, g, r] = slo[h]
            qaug[4, g, r] = chi; qaug[5, g, r] = clo
    seqsub = (seqtile[:, None] * 8 + np.arange(8)[None, :]).reshape(-1)
    nxt = np.roll(seqsub, -1)
    valid = (nxt == seqsub + 1)
    cend_true = np.where(valid, seqsub * 16 + 31, 10 ** 9).astype(np.float64)
    cend = cend_true.astype(np.float32).reshape(2, 128).T.copy()
    caug = np.zeros((16, 6, 256), np.float32)
    for i in range(16):
        tr = half * 2048 + i * 128 + 64
        e = np.where(valid, np.minimum(cend_true, tr + 63), NEGPOS).astype(np.float32)
        ehi, elo = _hilo(e)
        caug[i, 0] = ehi; caug[i, 1] = elo; caug[i, 2] = ehi; caug[i, 3] = elo; caug[i, 4] = 1.0; caug[i, 5] = 1.0
    seqblk = (seqtile[:, None] * 2 + np.arange(2)[None, :]).reshape(-1)
    blk2slot = np.zeros(64, np.int64)
    blk2slot[seqblk] = np.arange(64)
    mc2s = np.zeros((256, 64), np.float32)
    for n in range(256):
        if valid[n]:
            nb = seqsub[n]
            for k in range(2):
                sb_ = ((nb + k) * 16) // 64
                mc2s[n, blk2slot[sb_]] += 1.0
    mc2s = mc2s.reshape(2, 128, 64).transpose(1, 0, 2).copy()
    tq = posq.astype(np.int64)
    cur = tq // 64
    bs = seqblk[None, :]
    validb = (bs * 64 <= tq[:, None])
    forced = (bs == 0) | (bs == cur[:, None]) | (bs == cur[:, None] - 1)
    privn = (validb & ~forced).astype(np.float32)
    pribias = np.where(validb, np.where(forced, 1e6, 0.0), -1.0).astype(np.float32)
    esel = (np.arange(4096)[None, :] // 64 == np.arange(64)[:, None]).astype(np.float32)
    tri = (np.arange(128)[:, None] <= np.arange(128)[None, :]).astype(np.float32)
    return {
        "kaug": _bf(kaug), "qaug": _bf(qaug), "caug": _bf(caug), "posq": posq, "posk": posk, "cend": cend,
        "privn": privn, "pribias": pribias, "mc2s": _bf(mc2s), "esel": _bf(esel), "tri": _bf(tri),
        "identb": _bf(np.eye(128)), "ident": np.eye(128, dtype=np.float32),
    }


def sample_meta():
    slopes = np.exp2(-8.0 * (np.arange(16, dtype=np.float32) + 1.0) / 16).astype(np.float32)
    shi, slo = _hilo(slopes)
    pos = np.zeros((21, 128), np.float32)
    for kt in range(16):
        pos[kt] = kt * 128 + np.arange(128)
    for t in range(4):
        pos[16 + t] = 1536 + t * 128 + np.arange(128)
    pos[20] = 2048.0
    hi, lo = _hilo(pos)
    kaugS = np.zeros((21, 6, 128), np.float32)
    kaugS[:, 0] = hi; kaugS[:, 1] = lo; kaugS[:, 2] = hi; kaugS[:, 3] = lo; kaugS[:, 4] = 1.0; kaugS[:, 5] = 1.0
    qaugS = np.zeros((6, 4, 4, 16), np.float32)
    for g in range(4):
        for r in range(4):
            h = g * 4 + r
            chi, clo = _hilo(np.float32(-slopes[h] * 2048.0))
            qaugS[0, g, r] = shi[h]; qaugS[1, g, r] = shi[h]; qaugS[2, g, r] = slo[h]; qaugS[3, g, r] = slo[h]
            qaugS[4, g, r] = chi; qaugS[5, g, r] = clo
    cend = np.where(np.arange(128) < 127, np.arange(128) * 16 + 31, NEGPOS).astype(np.float32)
    ehi, elo = _hilo(cend)
    caugS = np.zeros((6, 128), np.float32)
    caugS[0] = ehi; caugS[1] = elo; caugS[2] = ehi; caugS[3] = elo; caugS[4] = 1.0; caugS[5] = 1.0
    maskS = np.zeros((128, 8), np.float32)
    maskS[:127, 0] = 1.0
    for t in range(4):
        d = 2048 - (1536 + t * 128 + np.arange(128))
        maskS[:, 1 + t] = ((d >= 0) & (d < 512)).astype(np.float32)
    blk = np.arange(64)
    valid = blk <= 32
    forced = (blk == 0) | (blk == 31) | (blk == 32)
    privn = np.tile((valid & ~forced).astype(np.float32)[None], (16, 1))
    pribias = np.tile(np.where(valid, np.where(forced, 1e6, 0.0), -1.0).astype(np.float32)[None], (16, 1))
    mc2s = np.zeros((128, 64), np.float32)
    for n in range(127):
        for k in range(2):
            mc2s[n, ((n + k) * 16) // 64] += 1.0
    return {"kaugS": _bf(kaugS), "qaugS": _bf(qaugS), "caugS": _bf(caugS), "maskS": maskS, "privnS": privn,
            "pribiasS": pribias, "mc2sS": _bf(mc2s), "iotap": np.arange(128, dtype=np.float32).reshape(128, 1)}


_NC_CACHE = {}


def kernel(**inp):
    f32 = lambda a: np.ascontiguousarray(np.asarray(a, dtype=np.float32))
    x_prompt = f32(inp["x_prompt"])
    vec_list = [inp["norm_mix"][0], inp["norm_mlp"][0], inp["norm_kv"], inp["pool_scale"][0],
                inp["norm_mix"][1], inp["norm_mlp"][1], inp["norm_final"]]
    vecs = np.ascontiguousarray(np.concatenate([f32(v).reshape(8, 128).T for v in vec_list], axis=1))
    shared = {
        "vecs": vecs, "w_up": f32(inp["w_up"]), "w_down": f32(inp["w_down"]), "pool_w": f32(inp["pool_w"])[0],
        "w_kv": f32(inp["w_kv"]), "w_qg": f32(inp["w_qg"])[0], "w_o": f32(inp["w_o"])[0], "b_gate": f32(inp["b_gate"]),
        "cmp_w1": f32(inp["cmp_w1"]), "cmp_w2": f32(inp["cmp_w2"]), "cmp_pe": f32(inp["cmp_pe"]),
    }
    metas = [core_meta(0), core_meta(1)]
    x_sample = f32(inp["x_sample"]).reshape(128, D)
    state_pool = f32(inp["state_pool"]).reshape(128, 15, D)
    state_win = f32(inp["state_win"]).reshape(128, 512, 512)
    selw = np.zeros((256, 4, 16), np.float32)
    for s_ in range(16):
        for r_ in range(15):
            for g_ in range(4):
                if r_ >= 16 - (2 << g_):
                    selw[s_ * 15 + r_, g_, s_] = 1.0
    selw = np.ascontiguousarray(selw.reshape(2, 128, 64).transpose(1, 0, 2))
    smeta = sample_meta()
    cache2d = f32(inp["cache_kv_pages"]).reshape(2560 * 128, 1024)
    page_table = np.ascontiguousarray(np.asarray(inp["page_table"], dtype=np.int32))
    in_maps = []
    for c in range(NCORES):
        b, half = c // 2, c % 2
        xg = np.zeros((4, NCOL, D), np.float32)
        corr = np.ones((4, 4, 16), np.float32)
        for gq, (slot0, is_own, ogi) in enumerate(L0_GROUPS):
            hf = half if is_own else 1 - half
            st = hf * 2048 + (slot0 % 16) * 128
            if st > 0:
                xg[gq, 1:16] = x_prompt[b, st - 15:st]
            xg[gq, 16:] = x_prompt[b, st:st + GT]
            for g in range(4):
                w = 2 << g
                t = st + np.arange(16)
                corr[gq, g] = w / np.minimum(t + 1, w)
        m = dict(shared)
        m.update(metas[half])
        m["xs"] = np.ascontiguousarray(x_sample[16 * c:16 * c + 16])
        m["spool"] = np.ascontiguousarray(state_pool[16 * c:16 * c + 16].reshape(240, D))
        m["swin"] = np.ascontiguousarray(state_win[16 * c:16 * c + 16])
        m["selw"] = selw
        m.update(smeta)
        m["cache"] = cache2d
        m["pt"] = np.ascontiguousarray(page_table[16 * c:16 * c + 16])
        m["xg"] = xg
        m["corr"] = corr.reshape(4, 64)
        in_maps.append(m)
    if "nc" not in _NC_CACHE:
        _NC_CACHE["nc"] = build_nc()
    nc = _NC_CACHE["nc"]
    res = run_bass_kernel_spmd(nc, in_maps, core_ids=list(range(NCORES)))
    R = res.results
    y_prompt = np.zeros((NB, SEQ, D), np.float32)
    y_sample = np.zeros((128, 1, D), np.float32)
    pool_prompt = np.zeros((NB, 1, 15, D), np.float32)
    pool_sample = np.zeros((128, 1, 15, D), np.float32)
    kv_rows_prompt = np.zeros((NB, SEQ, 2, 2, 4, 64), np.float32)
    kv_rows_sample = np.zeros((128, 1, 2, 2, 4, 64), np.float32)
    win_prompt = np.zeros((NB, 512, 2, 4, 64), np.float32)
    win_sample = np.zeros((128, 512, 2, 4, 64), np.float32)
    for c in range(NCORES):
        b, half = c // 2, c % 2
        kv_rows_prompt[b, half * 2048:(half + 1) * 2048] = R[c]["kv_out"].reshape(2048, 2, 2, 4, 64)
        y_sample[16 * c:16 * c + 16, 0] = R[c]["y_s_out"]
        pool_sample[16 * c:16 * c + 16, 0] = R[c]["pool_s_out"]
        kv_rows_sample[16 * c:16 * c + 16, 0] = R[c]["kv_s_out"].reshape(16, 2, 2, 4, 64)
        win_sample[16 * c:16 * c + 16] = R[c]["win_s_out"].reshape(16, 512, 2, 4, 64)
        y_prompt[b, half * 2048:(half + 1) * 2048] = R[c]["y_out"]
        if half == 1:
            win_prompt[b] = R[c]["win_out"].reshape(512, 2, 4, 64)
            pool_prompt[b, 0] = R[c]["pool_out"][1:16]
    return (y_prompt, y_sample, pool_prompt, pool_sample, kv_rows_prompt, kv_rows_sample, win_prompt, win_sample)
```

```python
import contextlib
import numpy as np
import ml_dtypes
import concourse.bass as bass
import concourse.mybir as mybir
from concourse.bass_utils import run_bass_kernel_spmd

F32 = mybir.dt.float32
BF16 = mybir.dt.bfloat16
AF = mybir.ActivationFunctionType
ALU = mybir.AluOpType
AX = mybir.AxisListType
NPBF = ml_dtypes.bfloat16

NCORES = 8
D = 1024
DFF = 4096
SEQ = 4096
NB = 4
GT = 1024
HALO = 16
NCOL = HALO + GT
NS = 16
NCX = NCOL + NS
EPS = 1e-6
KVW = 1536
NWB = 3
SIG_EPOCH = 30000
NEGPOS = -8192.0
NPAGES = 2560
_DEV = {}
STAGE = 4


class Tk:
    __slots__ = ("w", "r", "name")

    def __init__(self, name=""):
        self.w = None
        self.r = []
        self.name = name


class Op:
    __slots__ = ("eng", "fn", "deps", "dma", "sig", "signum", "slot", "slotval", "idx")

    def __init__(self, eng, fn, dma):
        self.eng = eng
        self.fn = fn
        self.deps = []
        self.dma = dma
        self.sig = False
        self.signum = None
        self.slot = None
        self.slotval = None


class Sched:
    ENGS = ("pe", "act", "dve", "pool", "sp")

    def __init__(self, nslot=8):
        self.q = {e: [] for e in self.ENGS}
        self.nslot = nslot
        self.pending = {}

    def barrier(self):
        fr = []
        for e in self.ENGS:
            comp = [o for o in self.q[e] if not o.dma]
            if comp:
                fr.append(comp[-1])
            dm = [o for o in self.q[e] if o.dma]
            fr.extend(dm[-self.nslot:])
        self.pending = {e: list(fr) for e in self.ENGS}

    def add(self, eng, fn, reads=(), writes=(), dma=False, extra=()):
        op = Op(eng, fn, dma)
        deps = []
        seen = set()
        if self.pending.get(eng):
            extra = list(extra) + self.pending.pop(eng)
        for d in extra:
            if id(d) not in seen:
                seen.add(id(d)); deps.append(d)
        for t in reads:
            if t.w is not None and id(t.w) not in seen:
                seen.add(id(t.w)); deps.append(t.w)
        for t in writes:
            for r in t.r:
                if id(r) not in seen:
                    seen.add(id(r)); deps.append(r)
            if t.w is not None and id(t.w) not in seen:
                seen.add(id(t.w)); deps.append(t.w)
        op.deps = [d for d in deps if d is not op]
        for t in reads:
            t.r.append(op)
        for t in writes:
            t.w = op
            t.r = []
        op.idx = len(self.q[eng])
        self.q[eng].append(op)
        return op

    def prepare(self, nc):
        nslot = self.nslot
        for e in self.ENGS:
            for op in self.q[e]:
                for d in op.deps:
                    if d.dma:
                        continue
                    if d.eng == "pe" and op.eng == "pe" and not op.dma:
                        continue
                    d.sig = True
        nsig = {}
        for e in self.ENGS:
            k = 0
            i = 0
            for op in self.q[e]:
                if op.dma:
                    op.slot = i % nslot
                    op.slotval = 16 * (i // nslot + 1)
                    i += 1
                elif op.sig:
                    k += 1
                    op.signum = k
            nsig[e] = k
        st = contextlib.ExitStack()
        csem = {}
        for e in self.ENGS:
            nep = nsig[e] // SIG_EPOCH + 1
            csem[e] = [st.enter_context(nc.semaphore("c_%s_%d" % (e, j))) for j in range(nep)]
        dsem = {}
        for e in ("sp", "pool"):
            dsem[e] = [st.enter_context(nc.semaphore("d_%s_%d" % (e, j))) for j in range(nslot)]
        self._st = st
        self.csem = csem
        self.dsem = dsem

    def emit(self, block):
        csem, dsem = self.csem, self.dsem

        def run(e, eng):
            waited = {}

            def wait(sem, key, val):
                if waited.get(key, -1) >= val:
                    return
                waited[key] = val
                eng.wait_ge(sem, val)

            for op in self.q[e]:
                need = {}
                for d in op.deps:
                    if d.dma:
                        key, sem, val = ("d", d.eng, d.slot), dsem[d.eng][d.slot], d.slotval
                    else:
                        if d.eng == "pe" and e == "pe" and not op.dma:
                            continue
                        ep = (d.signum - 1) // SIG_EPOCH
                        key, sem, val = ("c", d.eng, ep), csem[d.eng][ep], d.signum - ep * SIG_EPOCH
                    if key not in need or need[key][1] < val:
                        need[key] = (sem, val)
                for key, (sem, val) in need.items():
                    wait(sem, key, val)
                if op.dma:
                    if op.slotval > 16:
                        wait(dsem[e][op.slot], ("d", e, op.slot), op.slotval - 16)
                    ins = op.fn(eng)
                    ins.then_inc(dsem[e][op.slot], 16)
                else:
                    ins = op.fn(eng)
                    if op.sig:
                        ep = (op.signum - 1) // SIG_EPOCH
                        ins.then_inc(csem[e][ep], 1)
            if e in dsem:
                last = {}
                for op in self.q[e]:
                    if op.dma:
                        last[op.slot] = op.slotval
                for s, v in last.items():
                    wait(dsem[e][s], ("d", e, s), v)

        @block.tensor
        def _(eng):
            run("pe", eng)

        @block.scalar
        def _(eng):
            run("act", eng)

        @block.vector
        def _(eng):
            run("dve", eng)

        @block.gpsimd
        def _(eng):
            run("pool", eng)

        @block.sync
        def _(eng):
            run("sp", eng)


L0_GROUPS = [(16, False, None), (24, False, None), (0, True, 0), (8, True, 1)]


def build_nc():
    nc = bass.Bass("TRN2", target_bir_lowering=False)

    def din(name, shape, dt=F32):
        return nc.dram_tensor(name, list(shape), dt, kind="ExternalInput").ap()

    def dout(name, shape, dt=F32):
        return nc.dram_tensor(name, list(shape), dt, kind="ExternalOutput").ap()

    def dscr(name, shape, dt):
        return nc.dram_tensor(name, list(shape), dt).ap()

    xg = din("xg", [4, NCOL, D])
    corr = din("corr", [4, 64])
    vecs = din("vecs", [128, 56])
    ident_d = din("ident", [128, 128])
    w_up = din("w_up", [2, D, DFF])
    w_down = din("w_down", [2, DFF, D])
    pool_w = din("pool_w", [4, 256, 256])
    w_kv = din("w_kv", [D, KVW])
    w_qg = din("w_qg", [D, 1072])
    w_o = din("w_o", [D, D])
    b_gate = din("b_gate", [1, 48])
    cmp_w1 = din("cmp_w1", [2, 2048, 128])
    cmp_w2 = din("cmp_w2", [2, 128, 64])
    cmp_pe = din("cmp_pe", [2, 32, 64])
    kaug_d = din("kaug", [32, 6, 128], BF16)
    qaug_d = din("qaug", [6, 4, 4, 2048], BF16)
    caug_d = din("caug", [16, 6, 256], BF16)
    posq_d = din("posq", [2048])
    posk_d = din("posk", [128, 32])
    cend_d = din("cend", [128, 2])
    privn_d = din("privn", [2048, 64])
    pribias_d = din("pribias", [2048, 64])
    mc2s_d = din("mc2s", [128, 2, 64], BF16)
    esel_d = din("esel", [64, 4096], BF16)
    tri_d = din("tri", [128, 128], BF16)
    identb_d = din("identb", [128, 128], BF16)

    cache_d = din("cache", [NPAGES * 128, 1024])
    pt_d = din("pt", [NS, 16], mybir.dt.int32)
    iotap_d = din("iotap", [128, 1])
    kaugS_d = din("kaugS", [21, 6, 128], BF16)
    qaugS_d = din("qaugS", [6, 4, 4, NS], BF16)
    caugS_d = din("caugS", [6, 128], BF16)
    maskS_d = din("maskS", [128, 8])
    privnS_d = din("privnS", [NS, 64])
    pribiasS_d = din("pribiasS", [NS, 64])
    mc2sS_d = din("mc2sS", [128, 64], BF16)
    xs_d = din("xs", [NS, D])
    spool_d = din("spool", [NS * 15, D])
    swin_d = din("swin", [NS, 512, 512])
    selw_d = din("selw", [128, 2, 64])
    pool_s_out = dout("pool_s_out", [NS, 15, D])
    kv_s_out = dout("kv_s_out", [NS, 1024])
    win_s_out = dout("win_s_out", [NS, 512, 512])
    y_s_out = dout("y_s_out", [NS, D])
    kv_out = dout("kv_out", [2048, 1024])
    win_out = dout("win_out", [512, 512])
    pool_out = dout("pool_out", [16, D])
    y_out = dout("y_out", [2048, D])

    x1_d = dscr("x1_d", [2, 128, 8 * (GT + NS)], F32)
    KsT_d = dscr("KsT_d", [32, 70, 4, 128], BF16)
    KwT_d = dscr("KwT_d", [32, 70, 4, 128], BF16)
    Vs_d = dscr("Vs_d", [32, 128, 4, 65], BF16)
    Vw_d = dscr("Vw_d", [32, 128, 4, 65], BF16)
    KsTS_d = dscr("KsTS_d", [NS, 16, 70, 4, 128], BF16)
    VsS_d = dscr("VsS_d", [NS, 16, 128, 4, 65], BF16)
    KwTS_d = dscr("KwTS_d", [NS, 4, 70, 4, 128], BF16)
    VwS_d = dscr("VwS_d", [NS, 4, 128, 4, 65], BF16)
    kcTS_d = dscr("kcTS_d", [NS, 64, 4, 128], BF16)
    vcS_d = dscr("vcS_d", [NS, 128, 4, 64], BF16)

    es = contextlib.ExitStack()

    def sb(name, shape, dt):
        return es.enter_context(nc.sbuf_tensor(name, list(shape), dt))

    xT = sb("xT", [128, 8, NCX], F32)
    uT = sb("uT", [128, 8, NCX], BF16)
    hT = sb("hT", [128, 16, GT], BF16)
    wb = [sb("wb%d" % i, [128, 8 * 512], BF16) for i in range(NWB)]
    xin = [sb("xin%d" % i, [128, D], F32) for i in range(2)]
    tA = sb("tA", [128, NCX], F32)
    rstd = sb("rstd", [128, NCX], F32)
    kvst = [sb("kvst%d" % i, [128, KVW], F32) for i in range(2)]
    vec_sb = sb("vec_sb", [128, 56], F32)
    corr_sb = sb("corr_sb", [128, 4 * 64], F32)
    pw_sb = sb("pw_sb", [128, 4 * 2 * 256], BF16)
    ones_bf = sb("ones_bf", [128, 128], BF16)
    utail = sb("utail", [128, 8, 16], F32)
    selw = sb("selw_sb", [128, 2, 64], F32)
    diffS = sb("diffS", [128, 8, NS], BF16)
    hTs = sb("hTs", [128, 16, NS], BF16)
    rs_t = sb("rs_t", [128, NS], F32)
    QTs = sb("QTs", [70, 4, 4, NS], BF16)
    gate_s = sb("gate_s", [NS, 48], F32)
    knT = sb("knT", [70, 2, 4, NS], BF16)
    vnew = sb("vnew", [NS, 2, 4, 65], BF16)
    knew_bf = sb("knew_bf", [NS, 512], BF16)
    kcTs = [sb("kcTs%d" % i, [70, 4, 128], BF16) for i in range(2)]
    vcMs = [sb("vcMs%d" % i, [128, 4, 128], BF16) for i in range(2)]
    Pz = [sb("Pz%d" % i, [128, 4, NS], BF16) for i in range(4)]
    maskS = sb("maskS_sb", [128, 8], F32)
    privnS = sb("privnS_sb", [NS, 64], F32)
    pribiasS = sb("pribiasS_sb", [NS, 64], F32)
    selTs = sb("selTs", [64, 4, NS], BF16)
    epsb = sb("epsb", [128, 1], F32)
    ident = sb("ident_sb", [128, 128], F32)
    identb = sb("identb_sb", [128, 128], BF16)
    vst = [sb("vst%d" % i, [128, 2, 4, 65], BF16) for i in range(2)]
    ktst = [sb("ktst%d" % i, [64, GT], BF16) for i in range(2)]
    w1_sb = sb("w1_sb", [64, 2, 32, 128], BF16)
    w2_sb = sb("w2_sb", [128, 2, 64], BF16)
    peT = sb("peT", [64, 2, 32], BF16)
    pe_nat = sb("pe_nat", [64, 64], F32)
    ABs = sb("ABs", [128, 2, 4, 2, 4], F32)
    hs_all = sb("hs_all", [128, 2, 4, 256], BF16)
    hpreb = sb("hpreb", [128, 4], F32)
    cvec = sb("cvec", [128, 2], F32)
    kcT = sb("kcT", [70, 4, 256], BF16)
    vcM = sb("vcM", [128, 2, 4, 128], BF16)
    kbuf = [sb("kbuf%d" % i, [70, 4, 128], BF16) for i in range(3)]
    vbuf = [sb("vbuf%d" % i, [128, 4, 65], BF16) for i in range(3)]
    pT = [sb("pT%d" % i, [128, 4, 128], BF16) for i in range(4)]
    posq_bc = [sb("posq_bc%d" % i, [128, 128], F32) for i in range(2)]
    posk_sb = sb("posk_sb", [128, 32], F32)
    cend_sb = sb("cend_sb", [128, 2], F32)
    mtmp = [sb("mtmp%d" % i, [128, 128], F32) for i in range(2)]
    mask_sb = [sb("mask%d" % i, [128, 256], BF16) for i in range(3)]
    tri_sb = sb("tri_sb", [128, 128], BF16)
    esel_sb = sb("esel_sb", [64, 4096], BF16)
    privn_sb = [sb("privn%d" % i, [128, 64], F32) for i in range(2)]
    pribias_sb = [sb("pribias%d" % i, [128, 64], F32) for i in range(2)]
    pri = sb("pri", [128, 4, 64], F32)
    pri2 = sb("pri2", [128, 64], F32)
    m8a = sb("m8a", [128, 8], F32)
    m8b = sb("m8b", [128, 8], F32)
    sel_sb = sb("sel_sb", [128, 4, 64], BF16)
    selT = sb("selT", [64, 4, 128], BF16)
    gate_sb = sb("gate_sb", [128, 8, 48], F32)
    bg_sb = sb("bg_sb", [128, 48], F32)
    rsum = sb("rsum", [128, 4], F32)
    fsc = sb("fsc", [128, 4], F32)
    oacc = pw_sb[:, :].bitcast(F32).rearrange("p (h d) -> p h d", h=16)
    hs_flat = hs_all[:, :, :, :].rearrange("p a b c -> p (a b c)")
    obf = hs_flat[:, 0:1024]
    otmp = hs_flat[:, 1024:1536].bitcast(F32).rearrange("p (r d) -> p r d", r=4)
    kbuf.append(hs_flat[0:70, 1536:2048].rearrange("p (g k) -> p g k", g=4))
    vbuf.append(sb("vbuf3", [128, 4, 65], BF16))
    yst = [kvst[i][:, 0:D] for i in range(2)]
    ptail = xin[0][0:16, :]
    tB = rstd
    xct = ktst
    psum = [es.enter_context(nc.psum_tensor("ps%d" % i, [128, 512], F32)) for i in range(8)]

    S = Sched(nslot=16)
    k_xT = [Tk("xT%d" % i) for i in range(8)]
    k_uT = Tk("uT")
    k_h = [Tk("h%d" % i) for i in range(16)]
    k_wb = [Tk() for _ in range(NWB)]
    k_xin = [Tk(), Tk()]
    k_tA, k_rstd = Tk(), Tk()
    k_tA1 = Tk()
    k_tB = k_rstd
    k_kvst = [Tk(), Tk()]
    k_ps = [Tk() for _ in range(8)]
    k_const = Tk("const")
    k_utail = Tk()
    k_diffS, k_hTs, k_rs = Tk(), Tk(), Tk()
    k_QTs, k_gates, k_knT, k_vnew, k_knew = Tk(), Tk(), Tk(), Tk(), Tk()
    k_kcTs = [Tk(), Tk()]
    k_vcMs = [Tk(), Tk()]
    k_Pz = [Tk() for _ in range(4)]
    k_selTs = Tk()
    k_G = [Tk() for _ in range(24)]
    k_X = [[Tk() for _ in range(4)] for _ in range(2)]
    k_idx = Tk()
    k_sscr = Tk("sample scratch")
    k_ptail = k_xin[0]
    k_out = Tk("out")
    k_vst = [Tk(), Tk()]
    k_ktst = [Tk(), Tk()]
    k_xct = k_ktst
    k_AB = Tk()
    k_scr = Tk("scratch")
    k_x1 = [Tk(), Tk()]
    k_cmp = Tk()
    k_cmp2 = Tk()
    k_kc = Tk()
    k_vc = Tk()
    k_kbuf = [Tk() for _ in range(4)]
    k_vbuf = [Tk() for _ in range(4)]
    k_pT = [Tk() for _ in range(4)]
    k_posq = [Tk(), Tk()]
    k_mtmp = [Tk(), Tk()]
    k_mask = [Tk() for _ in range(3)]
    k_privn = [Tk(), Tk()]
    k_pri, k_pri2, k_m8, k_sel, k_selT = Tk(), Tk(), Tk(), Tk(), Tk()
    k_gate, k_rsum, k_fsc, k_oacc, k_otmp, k_obf = Tk(), Tk(), Tk(), Tk(), Tk(), Tk()
    k_yst = k_kvst

    state = {"ps": 0, "wb": 0, "xin": 0, "kvst": 0, "vst": 0, "ktst": 0,
             "kb": 0, "pT": 0, "mask": 0, "pq": 0, "yst": 0, "psS": 0,
             "G": 0, "pz": 0, "kc2": 0}

    def rot(key, n):
        i = state[key]
        state[key] = (i + 1) % n
        return i

    def next_ps():
        return rot("ps", 8)

    pw4 = pw_sb[:, :].rearrange("p (g k c) -> p g k c", g=4, k=2)
    VG0, VG1, VGKV, VPS, VGQ, VG1B, VGF = 0, 1, 2, 3, 4, 5, 6

    def vcol(v, dc):
        return vec_sb[:, v * 8 + dc: v * 8 + dc + 1]

    def cload(eng, out_ap, in_ap):
        S.add(eng, lambda e: e.dma_start(out=out_ap, in_=in_ap), writes=[k_const], dma=True)

    cload("sp", vec_sb[:, :], vecs[:, :])
    cload("sp", corr_sb[:, :], corr.rearrange("g c -> (g c)").partition_broadcast(128))
    cload("pool", pw4, pool_w.rearrange("g (k p) c -> p g k c", p=128))
    cload("sp", ident[:, :], ident_d[:, :])
    cload("sp", identb[:, :], identb_d[:, :])
    cload("sp", selw[:, :, :], selw_d[:, :, :])
    S.add("sp", lambda e: e.dma_start(out=pool_s_out[:, 0:14, :], in_=spool_d.rearrange("(s r) d -> s r d", r=15)[:, 1:15, :]),
          writes=[k_out], dma=True)
    for s_ in range(NS):
        S.add("sp", lambda e, s_=s_: e.dma_start(out=win_s_out[s_, 0:511, :], in_=swin_d[s_, 1:512, :]), writes=[k_out], dma=True)
    if STAGE >= 3:
        cload("sp", tri_sb[:, :], tri_d[:, :])
        cload("sp", esel_sb[:, :], esel_d[:, :])
        cload("sp", posk_sb[:, :], posk_d[:, :])
        cload("sp", cend_sb[:, :], cend_d[:, :])
        cload("sp", bg_sb[:, :], b_gate[0, :].partition_broadcast(128))
        cload("pool", w1_sb[:, :, :, :], cmp_w1.rearrange("k (p d) h -> d k p h", d=64))
        cload("pool", w2_sb[:, :, :], cmp_w2.rearrange("k h d -> h k d"))
        cload("sp", pe_nat[:, :], cmp_pe.rearrange("k p d -> (k p) d"))
        for g in range(4):
            cload("sp", vcM[:, :, g, 64:128], mc2s_d[:, :, :])
    if STAGE >= 4:
        cload("sp", maskS[:, :], maskS_d[:, :])
        cload("sp", privnS[:, :], privnS_d[:, :])
        cload("sp", pribiasS[:, :], pribiasS_d[:, :])
        for i_ in range(2):
            cload("sp", kcTs[i_][64:70, :, :], caugS_d.unsqueeze(1).to_broadcast([6, 4, 128]))
            for g in range(4):
                cload("sp", vcMs[i_][:, g, 64:128], mc2sS_d[:, :])
        cload("sp", QTs[64:70, :, :, :], qaugS_d[:, :, :, :])
        for b_ in range(2):
            for g in range(4):
                cload("sp", knT[64:70, b_, g, :], kaugS_d[20, :, 0:NS])
        S.add("dve", lambda e: e.memset(vnew[:, :, :, 64:65], 1.0), writes=[k_vnew])
        for i_ in range(4):
            S.add("dve", lambda e, i_=i_: e.memset(Pz[i_][:, :, :], 0.0), writes=[k_Pz[i_]])
        for g in range(4):
            for s_ in range(NS):
                S.add("sp", lambda e, g=g, s_=s_: e.dma_start(
                    out=KsTS_d[s_, :, 64:70, g, :], in_=kaugS_d[0:16]), writes=[k_sscr], dma=True)
                S.add("sp", lambda e, g=g, s_=s_: e.dma_start(
                    out=KwTS_d[s_, :, 64:70, g, :], in_=kaugS_d[16:20]), writes=[k_sscr], dma=True)
    S.add("dve", lambda e: e.memset(ones_bf[:, :], 1.0), writes=[k_const])
    S.add("dve", lambda e: e.memset(epsb[:, :], EPS), writes=[k_const])
    if STAGE >= 3:
        for i in range(2):
            S.add("dve", lambda e, i=i: e.memset(vst[i][:, :, :, 64:65], 1.0), writes=[k_vst[i]])
        for g in range(4):
            S.add("sp", lambda e, g=g: e.dma_start(out=KsT_d[:, 64:70, g, :], in_=kaug_d[:, :, :]), writes=[k_scr], dma=True)
            S.add("sp", lambda e, g=g: e.dma_start(out=KwT_d[:, 64:70, g, :], in_=kaug_d[:, :, :]), writes=[k_scr], dma=True)
        pi = next_ps()
        S.add("pe", lambda e, pi=pi: e.transpose(psum[pi][0:64, 0:64], pe_nat[:, :], ident[0:64, 0:64]),
              reads=[k_const], writes=[k_ps[pi]])
        S.add("act", lambda e, pi=pi: e.activation(out=peT[:, :, :], in_=psum[pi][0:64, 0:64].rearrange("d (k p) -> d k p", k=2), func=AF.Copy),
              reads=[k_ps[pi]], writes=[k_const])
        for kv in range(2):
            pi = next_ps()
            for p in range(32):
                S.add("pe", lambda e, pi=pi, kv=kv, p=p: e.matmul(
                    psum[pi][:, 0:1], w1_sb[:, kv, p, :], peT[:, kv, p:p + 1], start=(p == 0), stop=(p == 31)),
                    reads=[k_const], writes=[k_ps[pi]])
            S.add("act", lambda e, pi=pi, kv=kv: e.activation(out=cvec[:, kv:kv + 1], in_=psum[pi][:, 0:1], func=AF.Copy),
                  reads=[k_ps[pi]], writes=[k_cmp])

    def wload(src_ap, shape3):
        i = rot("wb", NWB)
        a, b = shape3
        view = wb[i][:, 0:a * b].rearrange("p (a b) -> p a b", a=a)
        S.add("pool", lambda e: e.dma_start(out=view, in_=src_ap), writes=[k_wb[i]], dma=True)
        return i, view

    COLR = [(0, 16), (16, 528), (528, 1040)]
    MAINR = [(16, 528), (528, 1040)]
    sq = hT[:, :, :].rearrange("p a b -> p (a b)")[:, 0:8 * NCX].rearrange("p (a b) -> p a b", a=8)

    SR = (NCOL, NCX)

    def norm(vidx, ranges, with_tail=False, final=False, stail=False):
        for dc in range(8):
            S.add("act", lambda e, dc=dc: e.activation(out=sq[:, dc, :], in_=xT[:, dc, :], func=AF.Square),
                  reads=[k_xT[dc]], writes=list(k_h))
        for (c0, c1) in ranges:
            pi = next_ps()
            n = c1 - c0
            for dc in range(8):
                S.add("pe", lambda e, dc=dc, pi=pi, c0=c0, c1=c1, n=n: e.matmul(
                    psum[pi][:, 0:n], ones_bf[:, :], sq[:, dc, c0:c1], start=(dc == 0), stop=(dc == 7)),
                    reads=list(k_h) + [k_const], writes=[k_ps[pi]])
            S.add("act", lambda e, pi=pi, c0=c0, c1=c1, n=n: e.activation(
                out=rstd[:, c0:c1], in_=psum[pi][:, 0:n], func=AF.Sqrt, bias=epsb[:, 0:1], scale=1.0 / D),
                reads=[k_ps[pi], k_const], writes=[k_rstd])
            S.add("dve", lambda e, c0=c0, c1=c1: e.reciprocal(out=rstd[:, c0:c1], in_=rstd[:, c0:c1]),
                  reads=[k_rstd], writes=[k_rstd])
        lo = ranges[0][0]
        hi = ranges[-1][1]
        for dc in range(8):
            if final:
                S.add("dve", lambda e, dc=dc: e.scalar_tensor_tensor(
                    out=xT[:, dc, lo:hi], in0=xT[:, dc, lo:hi], scalar=vcol(vidx, dc), in1=rstd[:, lo:hi],
                    op0=ALU.mult, op1=ALU.mult), reads=[k_xT[dc], k_rstd, k_const], writes=[k_xT[dc]])
            else:
                S.add("dve", lambda e, dc=dc: e.scalar_tensor_tensor(
                    out=uT[:, dc, lo:hi], in0=xT[:, dc, lo:hi], scalar=vcol(vidx, dc), in1=rstd[:, lo:hi],
                    op0=ALU.mult, op1=ALU.mult), reads=[k_xT[dc], k_rstd, k_const], writes=[k_uT])
        if with_tail or stail:
            t0 = NCX - 16 if stail else NCOL - 16
            for dc in range(8):
                S.add("dve", lambda e, dc=dc, t0=t0: e.scalar_tensor_tensor(
                    out=utail[:, dc, :], in0=xT[:, dc, t0:t0 + 16], scalar=vcol(vidx, dc),
                    in1=rstd[:, t0:t0 + 16], op0=ALU.mult, op1=ALU.mult),
                    reads=[k_xT[dc], k_rstd, k_const], writes=[k_utail])

    def rows16_out(src3, ksrc, dst_ap):
        pi = next_ps()
        pi2 = next_ps()
        for dc in range(8):
            pp = pi if dc < 4 else pi2
            S.add("pe", lambda e, dc=dc, pp=pp: e.transpose(
                psum[pp][0:16, (dc % 4) * 128:(dc % 4 + 1) * 128], src3[:, dc, :], ident[:, :]),
                reads=ksrc + [k_const], writes=[k_ps[pp]])
        S.add("act", lambda e, pi=pi: e.activation(out=ptail[:, 0:512], in_=psum[pi][0:16, :], func=AF.Copy),
              reads=[k_ps[pi]], writes=[k_ptail])
        S.add("act", lambda e, pi2=pi2: e.activation(out=ptail[:, 512:1024], in_=psum[pi2][0:16, :], func=AF.Copy),
              reads=[k_ps[pi2]], writes=[k_ptail])
        S.add("sp", lambda e: e.dma_start(out=dst_ap, in_=ptail[:, :]), reads=[k_ptail], writes=[k_out], dma=True)

    def mlp(layer, has_s=False):
        for hh in range(2):
            for fb in range(4):
                col0 = hh * 2048 + fb * 512
                wi, wv = wload(w_up[layer].rearrange("(dc p) f -> p dc f", p=128)[:, :, col0:col0 + 512], (8, 512))
                for fc in range(4):
                    fcl = fb * 4 + fc
                    for th in range(2):
                        pi = next_ps()
                        c0 = 16 + th * 512
                        for dc in range(8):
                            S.add("pe", lambda e, pi=pi, wv=wv, dc=dc, fc=fc, c0=c0: e.matmul(
                                psum[pi][:, :], wv[:, dc, fc * 128:(fc + 1) * 128], uT[:, dc, c0:c0 + 512],
                                start=(dc == 0), stop=(dc == 7)),
                                reads=[k_wb[wi], k_uT], writes=[k_ps[pi]])
                        tt = tA[:, th * 512:(th + 1) * 512]
                        kt = k_tA if th == 0 else k_tA1
                        S.add("act", lambda e, pi=pi, tt=tt: e.activation(out=tt, in_=psum[pi][:, :], func=AF.Relu),
                              reads=[k_ps[pi]], writes=[kt])
                        S.add("dve", lambda e, tt=tt, fcl=fcl, th=th: e.tensor_tensor(
                            out=hT[:, fcl, th * 512:(th + 1) * 512], in0=tt, in1=tt, op=ALU.mult),
                            reads=[kt], writes=[k_h[fcl]])
                    if has_s:
                        pi = next_ps()
                        for dc in range(8):
                            S.add("pe", lambda e, pi=pi, wv=wv, dc=dc, fc=fc: e.matmul(
                                psum[pi][:, 0:NS], wv[:, dc, fc * 128:(fc + 1) * 128], uT[:, dc, NCOL:NCX],
                                start=(dc == 0), stop=(dc == 7)),
                                reads=[k_wb[wi], k_uT], writes=[k_ps[pi]])
                        S.add("act", lambda e, pi=pi: e.activation(out=rs_t[:, :], in_=psum[pi][:, 0:NS], func=AF.Relu),
                              reads=[k_ps[pi]], writes=[k_rs])
                        S.add("dve", lambda e, fcl=fcl: e.tensor_tensor(
                            out=hTs[:, fcl, :], in0=rs_t[:, :], in1=rs_t[:, :], op=ALU.mult),
                            reads=[k_rs], writes=[k_hTs])
            for dmp in range(4):
                wi, wv = wload(w_down[layer][hh * 2048:(hh + 1) * 2048, dmp * 256:(dmp + 1) * 256]
                               .rearrange("(f p) c -> p f c", p=128), (16, 256))
                for dmc in range(2):
                    dca = dmp * 2 + dmc
                    for th in range(2):
                        pi = next_ps()
                        for fcl in range(16):
                            S.add("pe", lambda e, pi=pi, wv=wv, fcl=fcl, dmc=dmc, th=th: e.matmul(
                                psum[pi][:, :], wv[:, fcl, dmc * 128:(dmc + 1) * 128], hT[:, fcl, th * 512:(th + 1) * 512],
                                start=(fcl == 0), stop=(fcl == 15)),
                                reads=[k_wb[wi], k_h[fcl]], writes=[k_ps[pi]])
                        c0 = 16 + th * 512
                        S.add("dve", lambda e, pi=pi, dca=dca, c0=c0: e.tensor_tensor(
                            out=xT[:, dca, c0:c0 + 512], in0=xT[:, dca, c0:c0 + 512], in1=psum[pi][:, :], op=ALU.add),
                            reads=[k_ps[pi], k_xT[dca]], writes=[k_xT[dca]])
                    if has_s:
                        pi = next_ps()
                        for fcl in range(16):
                            S.add("pe", lambda e, pi=pi, wv=wv, fcl=fcl, dmc=dmc: e.matmul(
                                psum[pi][:, 0:NS], wv[:, fcl, dmc * 128:(dmc + 1) * 128], hTs[:, fcl, :],
                                start=(fcl == 0), stop=(fcl == 15)),
                                reads=[k_wb[wi], k_hTs], writes=[k_ps[pi]])
                        S.add("dve", lambda e, pi=pi, dca=dca: e.tensor_tensor(
                            out=xT[:, dca, NCOL:NCX], in0=xT[:, dca, NCOL:NCX], in1=psum[pi][:, 0:NS], op=ALU.add),
                            reads=[k_ps[pi], k_xT[dca]], writes=[k_xT[dca]])

    if STAGE >= 4 and not _DEV.get("skip_prep"):
        I32 = mybir.dt.int32
        hflat = hT[:, :, :].rearrange("p a b -> p (a b)")
        XcT = hflat[0:64, :].rearrange("p (k g n) -> p k g n", k=2, g=4)
        uflat = uT[:, :, :].rearrange("p a b -> p (a b)")
        xflat_bf = xT[:, :, :].rearrange("p a b -> p (a b)").bitcast(BF16)
        Gb = [uflat[:, i * 1024:(i + 1) * 1024] for i in range(8)] + \
             [xflat_bf[:, i * 1024:(i + 1) * 1024] for i in range(16)]
        ptb_i = xin[0][:, 0:256].bitcast(I32)
        ptb_f = xin[0][:, 256:512]
        idx_f = xin[0][:, 512:768]
        idx_i = xin[1][:, 0:256].bitcast(I32)
        iotap = xin[1][:, 256:257]
        kcs_st = ktst[0][:, 0:512].rearrange("d (g n) -> d g n", g=4)
        vcs_st = xin[1][:, 512:640].bitcast(BF16).rearrange("n (g d) -> n g d", g=4)
        k_kcs, k_vcs = Tk(), Tk()
        hs_s = hs_all[:, 0, 0, 0:128]
        S.add("sp", lambda e: e.dma_start(out=ptb_i, in_=pt_d.rearrange("s k -> (s k)").partition_broadcast(128)), writes=[k_idx], dma=True)
        S.add("sp", lambda e: e.dma_start(out=iotap, in_=iotap_d[:, :]), writes=[k_idx], dma=True)
        S.add("dve", lambda e: e.tensor_copy(out=ptb_f, in_=ptb_i), reads=[k_idx], writes=[k_idx])
        S.add("dve", lambda e: e.tensor_scalar(out=idx_f, in0=ptb_f, scalar1=128.0, scalar2=iotap, op0=ALU.mult, op1=ALU.add),
              reads=[k_idx], writes=[k_idx])
        S.add("dve", lambda e: e.tensor_copy(out=idx_i, in_=idx_f), reads=[k_idx], writes=[k_idx])
        S.add("dve", lambda e: e.memset(kcs_st, 0.0), writes=[k_kcs])
        S.add("dve", lambda e: e.memset(vcs_st, 0.0), writes=[k_vcs])
        for s_ in range(_DEV.get("prep_ns", NS)):
            for kt in range(16):
                gi_ = rot("G", 24)
                col = s_ * 16 + kt
                S.add("pool", lambda e, gi_=gi_, col=col: e.indirect_dma_start(
                    out=Gb[gi_], out_offset=None, in_=cache_d[:, :],
                    in_offset=bass.IndirectOffsetOnAxis(ap=idx_i[:, col:col + 1], axis=0)),
                    reads=[k_idx], writes=[k_G[gi_]], dma=True)
                if _DEV.get("prep_level", 9) < 2:
                    continue
                pa, pb = next_ps(), next_ps()
                psa = psum[pa][:, :].bitcast(BF16)
                psb = psum[pb][:, :].bitcast(BF16)
                for j in range(4):
                    c_ = 512 + j * 64
                    S.add("pe", lambda e, psa=psa, j=j, c_=c_, gi_=gi_: e.transpose(
                        psa[0:64, j * 128:(j + 1) * 128], Gb[gi_][:, c_:c_ + 64], identb[:, :]),
                        reads=[k_G[gi_], k_const], writes=[k_ps[pa]])
                for j in range(8):
                    c_ = j * 64
                    S.add("pe", lambda e, psb=psb, j=j, c_=c_, gi_=gi_: e.transpose(
                        psb[0:64, j * 128:(j + 1) * 128], Gb[gi_][:, c_:c_ + 64], identb[:, :]),
                        reads=[k_G[gi_], k_const], writes=[k_ps[pb]])
                if _DEV.get("no_evac"):
                    continue
                kst = ktst[1][:, 0:512].rearrange("d (g n) -> d g n", g=4)
                if not _DEV.get("no_e1"):
                    S.add("act", lambda e, psa=psa, kst=kst: e.activation(
                        out=kst, in_=psa[0:64, 0:512].rearrange("d (g n) -> d g n", g=4), func=AF.Copy),
                        reads=[k_ps[pa]], writes=[k_ktst[1]])
                if not _DEV.get("no_store"):
                    S.add("sp", lambda e, kst=kst, s_=s_, kt=kt: e.dma_start(out=KsTS_d[s_, kt, 0:64, :, :], in_=kst),
                          reads=[k_ktst[1]], writes=[k_sscr], dma=True)
                if not _DEV.get("no_e2"):
                    S.add("dve", lambda e, psb=psb, kt=kt: e.tensor_copy(
                        out=XcT[:, :, :, kt * 128:(kt + 1) * 128], in_=psb[0:64, 0:1024].rearrange("d (k g n) -> d k g n", k=2, g=4)),
                        reads=[k_ps[pb]], writes=k_X[0] + k_X[1])
                vi = rot("vst", 2)
                if not _DEV.get("no_e4"):
                    S.add("act", lambda e, vi=vi, gi_=gi_: e.activation(
                        out=vst[vi][:, 0, :, 0:64], in_=Gb[gi_][:, 768:1024].rearrange("p (g d) -> p g d", g=4), func=AF.Copy),
                        reads=[k_G[gi_]], writes=[k_vst[vi]])
                if not _DEV.get("no_store"):
                    S.add("sp", lambda e, vi=vi, s_=s_, kt=kt: e.dma_start(out=VsS_d[s_, kt], in_=vst[vi][:, 0, :, :]),
                          reads=[k_vst[vi]], writes=[k_sscr], dma=True)
            for kv in range(2 if _DEV.get("prep_level", 9) >= 3 else 0):
                for g in range(4):
                    pi = next_ps()
                    for pos in range(32):
                        S.add("pe", lambda e, pi=pi, kv=kv, g=g, pos=pos: e.matmul(
                            psum[pi][:, 0:127], w1_sb[:, kv, pos, :], XcT[:, kv, g, pos:pos + 2017:16],
                            start=(pos == 0), stop=(pos == 31)),
                            reads=[k_X[kv][g], k_const], writes=[k_ps[pi]])
                    S.add("act", lambda e, pi=pi, kv=kv: e.activation(
                        out=hs_s[:, 0:127], in_=psum[pi][:, 0:127], func=AF.Silu, bias=cvec[:, kv:kv + 1]),
                        reads=[k_ps[pi], k_cmp], writes=[k_AB])
                    pj = next_ps()
                    if kv == 0:
                        S.add("pe", lambda e, pj=pj: e.matmul(psum[pj][0:64, 0:127], w2_sb[:, 0, :], hs_s[:, 0:127], start=True, stop=True),
                              reads=[k_AB, k_const], writes=[k_ps[pj]])
                        S.add("act", lambda e, pj=pj, g=g: e.activation(out=kcs_st[:, g, 0:127], in_=psum[pj][0:64, 0:127], func=AF.Copy),
                              reads=[k_ps[pj]], writes=[k_kcs])
                    else:
                        S.add("pe", lambda e, pj=pj: e.matmul(psum[pj][0:127, 0:64], hs_s[:, 0:127], w2_sb[:, 1, :], start=True, stop=True),
                              reads=[k_AB, k_const], writes=[k_ps[pj]])
                        S.add("act", lambda e, pj=pj, g=g: e.activation(out=vcs_st[0:127, g, :], in_=psum[pj][0:127, 0:64], func=AF.Copy),
                              reads=[k_ps[pj]], writes=[k_vcs])
            if _DEV.get("prep_level", 9) < 4:
                continue
            S.add("sp", lambda e, s_=s_: e.dma_start(out=kcTS_d[s_], in_=kcs_st), reads=[k_kcs], writes=[k_sscr], dma=True)
            S.add("sp", lambda e, s_=s_: e.dma_start(out=vcS_d[s_], in_=vcs_st), reads=[k_vcs], writes=[k_sscr], dma=True)
            for t_ in range(4):
                gi_ = rot("G", 24)
                S.add("pool", lambda e, gi_=gi_, s_=s_, t_=t_: e.dma_start(
                    out=Gb[gi_][:, 0:512], in_=swin_d[s_, t_ * 128:(t_ + 1) * 128, :]), writes=[k_G[gi_]], dma=True)
                pa = next_ps()
                psa = psum[pa][:, :].bitcast(BF16)
                for j in range(4):
                    S.add("pe", lambda e, psa=psa, j=j, gi_=gi_: e.transpose(
                        psa[0:64, j * 128:(j + 1) * 128], Gb[gi_][:, j * 64:(j + 1) * 64], identb[:, :]),
                        reads=[k_G[gi_], k_const], writes=[k_ps[pa]])
                kst = ktst[1][:, 0:512].rearrange("d (g n) -> d g n", g=4)
                S.add("act", lambda e, psa=psa, kst=kst: e.activation(
                    out=kst, in_=psa[0:64, 0:512].rearrange("d (g n) -> d g n", g=4), func=AF.Copy),
                    reads=[k_ps[pa]], writes=[k_ktst[1]])
                S.add("sp", lambda e, kst=kst, s_=s_, t_=t_: e.dma_start(out=KwTS_d[s_, t_, 0:64, :, :], in_=kst),
                      reads=[k_ktst[1]], writes=[k_sscr], dma=True)
                vi = rot("vst", 2)
                S.add("act", lambda e, vi=vi, gi_=gi_: e.activation(
                    out=vst[vi][:, 0, :, 0:64], in_=Gb[gi_][:, 256:512].rearrange("p (g d) -> p g d", g=4), func=AF.Copy),
                    reads=[k_G[gi_]], writes=[k_vst[vi]])
                S.add("sp", lambda e, vi=vi, s_=s_, t_=t_: e.dma_start(out=VwS_d[s_, t_], in_=vst[vi][:, 0, :, :]),
                      reads=[k_vst[vi]], writes=[k_sscr], dma=True)
        S.barrier()

    for gq, (slot0, is_own, ogi) in enumerate(L0_GROUPS):
        if _DEV.get("prep_only"):
            continue
        if STAGE < 3 and not is_own:
            continue
        last = is_own and ogi == 1
        has_s = is_own and ogi == 0
        RNG0 = COLR + ([SR] if has_s else [])
        RNG = MAINR + ([SR] if has_s else [])
        for ti in range(10 if has_s else 9):
            r0, nr = (0, 16) if ti == 0 else ((NCOL, NS) if ti == 9 else (16 + (ti - 1) * 128, 128))
            xi = rot("xin", 2)
            if ti == 9:
                S.add("sp", lambda e, xi=xi: e.dma_start(out=xin[xi][0:NS, :], in_=xs_d[:, :]),
                      writes=[k_xin[xi]], dma=True)
            else:
                S.add("sp", lambda e, xi=xi, r0=r0, nr=nr, gq=gq: e.dma_start(out=xin[xi][0:nr, :], in_=xg[gq, r0:r0 + nr, :]),
                      writes=[k_xin[xi]], dma=True)
            for hb in range(2):
                pi = next_ps()
                for j in range(4):
                    dc = hb * 4 + j
                    S.add("pe", lambda e, xi=xi, pi=pi, j=j, dc=dc, nr=nr: e.transpose(
                        psum[pi][:, j * 128:j * 128 + nr], xin[xi][0:nr, dc * 128:(dc + 1) * 128], ident[0:nr, 0:nr]),
                        reads=[k_xin[xi], k_const], writes=[k_ps[pi]])
                src = psum[pi][:, :].rearrange("p (j t) -> p j t", j=4)[:, :, 0:nr]
                if hb == 0:
                    S.add("act", lambda e, src=src, hb=hb, r0=r0, nr=nr: e.activation(
                        out=xT[:, hb * 4:hb * 4 + 4, r0:r0 + nr], in_=src, func=AF.Copy),
                        reads=[k_ps[pi]], writes=k_xT[hb * 4:hb * 4 + 4])
                else:
                    S.add("dve", lambda e, src=src, hb=hb, r0=r0, nr=nr: e.tensor_copy(
                        out=xT[:, hb * 4:hb * 4 + 4, r0:r0 + nr], in_=src),
                        reads=[k_ps[pi]], writes=k_xT[hb * 4:hb * 4 + 4])
        norm(VG0, RNG0, with_tail=last, stail=has_s)
        if last:
            rows16_out(utail, [k_utail], pool_out[:, :])
        if has_s:
            rows16_out(utail, [k_utail], pool_s_out[:, 14, :])
            stx = [rot("xin", 2), None]
            stx[1] = rot("xin", 2)
            for t_, (r0_, nr_) in enumerate(((0, 128), (128, 112))):
                S.add("sp", lambda e, xi=stx[t_], r0_=r0_, nr_=nr_: e.dma_start(out=xin[xi][0:nr_, :], in_=spool_d[r0_:r0_ + nr_, :]),
                      writes=[k_xin[stx[t_]]], dma=True)
        diffT = hT[:, 0:8, :]
        for dc in range(8):
            g = dc // 2
            nst = g + 1
            bufs = [tA, tB]
            kb_ = [k_tA, k_tB]
            kbw_ = [[k_tA, k_tA1], [k_tB]]
            cur = None
            for s in range(nst):
                sh = 1 << s
                lo = (1 << (s + 1))
                o = bufs[s % 2]
                ko = kb_[s % 2]
                kow = kbw_[s % 2]
                if s == 0:
                    S.add("dve", lambda e, o=o, dc=dc, lo=lo, sh=sh: e.tensor_tensor(
                        out=o[:, lo:NCOL], in0=uT[:, dc, lo:NCOL], in1=uT[:, dc, lo - sh:NCOL - sh], op=ALU.add),
                        reads=[k_uT], writes=kow)
                else:
                    i_ = bufs[(s - 1) % 2]
                    ki = kb_[(s - 1) % 2]
                    S.add("dve", lambda e, o=o, i_=i_, lo=lo, sh=sh: e.tensor_tensor(
                        out=o[:, lo:NCOL], in0=i_[:, lo:NCOL], in1=i_[:, lo - sh:NCOL - sh], op=ALU.add),
                        reads=[ki], writes=kow)
                cur = (o, ko, kow)
            o, ko, kow = cur
            w = 1 << (g + 1)
            if has_s:
                pi = next_ps()
                for t_, nr_ in enumerate((128, 112)):
                    S.add("pe", lambda e, pi=pi, t_=t_, nr_=nr_, dc=dc, g=g: e.matmul(
                        psum[pi][:, 0:NS], xin[stx[t_]][0:nr_, dc * 128:(dc + 1) * 128], selw[0:nr_, t_, g * 16:(g + 1) * 16],
                        start=(t_ == 0), stop=(t_ == 1)),
                        reads=[k_xin[stx[t_]], k_const], writes=[k_ps[pi]])
                S.add("dve", lambda e, o=o, pi=pi, dc=dc: e.tensor_tensor(
                    out=o[:, NCOL:NCX], in0=psum[pi][:, 0:NS], in1=uT[:, dc, NCOL:NCX], op=ALU.add),
                    reads=[k_ps[pi], k_uT, ko], writes=kow)
                S.add("dve", lambda e, o=o, dc=dc, w=w: e.scalar_tensor_tensor(
                    out=diffS[:, dc, :], in0=o[:, NCOL:NCX], scalar=1.0 / w, in1=uT[:, dc, NCOL:NCX],
                    op0=ALU.mult, op1=ALU.subtract), reads=[ko, k_uT], writes=[k_diffS])
            S.add("dve", lambda e, o=o, g=g, gq=gq: e.tensor_tensor(
                out=o[:, 16:32], in0=o[:, 16:32], in1=corr_sb[:, gq * 64 + g * 16: gq * 64 + g * 16 + 16], op=ALU.mult),
                reads=[ko, k_const], writes=kow)
            S.add("dve", lambda e, o=o, dc=dc, w=w: e.scalar_tensor_tensor(
                out=diffT[:, dc, :], in0=o[:, 16:NCOL], scalar=1.0 / w, in1=uT[:, dc, 16:NCOL],
                op0=ALU.mult, op1=ALU.subtract), reads=[ko, k_uT], writes=[k_h[dc]])
        for oc in range(8):
            g = oc // 2
            for th in range(2):
                pi = next_ps()
                for kk in range(2):
                    S.add("pe", lambda e, pi=pi, g=g, kk=kk, oc=oc, th=th: e.matmul(
                        psum[pi][:, :], pw4[:, g, kk, (oc % 2) * 128:(oc % 2) * 128 + 128],
                        diffT[:, 2 * g + kk, th * 512:(th + 1) * 512], start=(kk == 0), stop=(kk == 1)),
                        reads=[k_h[2 * g + kk], k_const], writes=[k_ps[pi]])
                c0 = 16 + th * 512
                S.add("dve", lambda e, pi=pi, oc=oc, c0=c0: e.scalar_tensor_tensor(
                    out=xT[:, oc, c0:c0 + 512], in0=psum[pi][:, :], scalar=vcol(VPS, oc), in1=xT[:, oc, c0:c0 + 512],
                    op0=ALU.mult, op1=ALU.add), reads=[k_ps[pi], k_xT[oc], k_const], writes=[k_xT[oc]])
            if has_s:
                pi = next_ps()
                for kk in range(2):
                    S.add("pe", lambda e, pi=pi, g=g, kk=kk, oc=oc: e.matmul(
                        psum[pi][:, 0:NS], pw4[:, g, kk, (oc % 2) * 128:(oc % 2) * 128 + 128],
                        diffS[:, 2 * g + kk, :], start=(kk == 0), stop=(kk == 1)),
                        reads=[k_diffS, k_const], writes=[k_ps[pi]])
                S.add("dve", lambda e, pi=pi, oc=oc: e.scalar_tensor_tensor(
                    out=xT[:, oc, NCOL:NCX], in0=psum[pi][:, 0:NS], scalar=vcol(VPS, oc), in1=xT[:, oc, NCOL:NCX],
                    op0=ALU.mult, op1=ALU.add), reads=[k_ps[pi], k_xT[oc], k_const], writes=[k_xT[oc]])
        norm(VG1, RNG)
        mlp(0, has_s)
        if is_own and STAGE >= 2:
            S.add("sp", lambda e, ogi=ogi: e.dma_start(
                out=x1_d[ogi].rearrange("p (a b) -> p a b", a=8), in_=xT[:, :, 16:NCX]),
                reads=list(k_xT), writes=[k_x1[ogi]], dma=True)
        norm(VGKV, RNG)
        kvw = []
        for cb in range(3):
            kvw.append(wload(w_kv.rearrange("(dc p) f -> p dc f", p=128)[:, :, cb * 512:(cb + 1) * 512], (8, 512)))
        for ti in range(8):
            ks = rot("kvst", 2)
            c0 = 16 + ti * 128
            for cb in range(3):
                wi, wv = kvw[cb]
                pi = next_ps()
                for dc in range(8):
                    S.add("pe", lambda e, pi=pi, wv=wv, dc=dc, c0=c0: e.matmul(
                        psum[pi][:, :], uT[:, dc, c0:c0 + 128], wv[:, dc, :], start=(dc == 0), stop=(dc == 7)),
                        reads=[k_wb[wi], k_uT], writes=[k_ps[pi]])
                S.add("act", lambda e, pi=pi, ks=ks, cb=cb: e.activation(
                    out=kvst[ks][:, cb * 512:(cb + 1) * 512], in_=psum[pi][:, :], func=AF.Copy),
                    reads=[k_ps[pi]], writes=[k_kvst[ks]])
            if is_own:
                row0 = ogi * GT + ti * 128
                S.add("sp", lambda e, ks=ks, row0=row0: e.dma_start(out=kv_out[row0:row0 + 128, :], in_=kvst[ks][:, 0:1024]),
                      reads=[k_kvst[ks]], writes=[k_out], dma=True)
                if last and ti >= 4:
                    S.add("sp", lambda e, ks=ks, ti=ti: e.dma_start(
                        out=win_out[(ti - 4) * 128:(ti - 3) * 128, :], in_=kvst[ks][:, 1024:1536]),
                        reads=[k_kvst[ks]], writes=[k_out], dma=True)
            if STAGE >= 3:
                vi = rot("vst", 2)
                S.add("dve", lambda e, ks=ks, vi=vi: e.tensor_copy(
                    out=vst[vi][:, 0, :, 0:64], in_=kvst[ks][:, 768:1024].rearrange("p (g d) -> p g d", g=4)),
                    reads=[k_kvst[ks]], writes=[k_vst[vi]])
                S.add("dve", lambda e, ks=ks, vi=vi: e.tensor_copy(
                    out=vst[vi][:, 1, :, 0:64], in_=kvst[ks][:, 1280:1536].rearrange("p (g d) -> p g d", g=4)),
                    reads=[k_kvst[ks]], writes=[k_vst[vi]])
                st_ = slot0 + ti
                S.add("sp", lambda e, vi=vi, st_=st_: e.dma_start(out=Vs_d[st_], in_=vst[vi][:, 0, :, :]),
                      reads=[k_vst[vi]], writes=[k_scr], dma=True)
                S.add("sp", lambda e, vi=vi, st_=st_: e.dma_start(out=Vw_d[st_], in_=vst[vi][:, 1, :, :]),
                      reads=[k_vst[vi]], writes=[k_scr], dma=True)
        if has_s:
            ks = rot("kvst", 2)
            for cb in range(3):
                wi, wv = kvw[cb]
                pi = next_ps()
                for dc in range(8):
                    S.add("pe", lambda e, pi=pi, wv=wv, dc=dc: e.matmul(
                        psum[pi][0:NS, :], uT[:, dc, NCOL:NCX], wv[:, dc, :], start=(dc == 0), stop=(dc == 7)),
                        reads=[k_wb[wi], k_uT], writes=[k_ps[pi]])
                S.add("act", lambda e, pi=pi, ks=ks, cb=cb: e.activation(
                    out=kvst[ks][0:NS, cb * 512:(cb + 1) * 512], in_=psum[pi][0:NS, :], func=AF.Copy),
                    reads=[k_ps[pi]], writes=[k_kvst[ks]])
            S.add("sp", lambda e, ks=ks: e.dma_start(out=kv_s_out[:, :], in_=kvst[ks][0:NS, 0:1024]),
                  reads=[k_kvst[ks]], writes=[k_out], dma=True)
            S.add("sp", lambda e, ks=ks: e.dma_start(out=win_s_out[:, 511, :], in_=kvst[ks][0:NS, 1024:1536]),
                  reads=[k_kvst[ks]], writes=[k_out], dma=True)
            if STAGE >= 4:
                for bi_, (kc0, vc0) in enumerate(((512, 768), (1024, 1280))):
                    S.add("dve", lambda e, ks=ks, bi_=bi_, vc0=vc0: e.tensor_copy(
                        out=vnew[:, bi_, :, 0:64], in_=kvst[ks][0:NS, vc0:vc0 + 256].rearrange("p (g d) -> p g d", g=4)),
                        reads=[k_kvst[ks]], writes=[k_vnew])
                    S.add("dve", lambda e, ks=ks, bi_=bi_, kc0=kc0: e.tensor_copy(
                        out=knew_bf[:, bi_ * 256:(bi_ + 1) * 256], in_=kvst[ks][0:NS, kc0:kc0 + 256]),
                        reads=[k_kvst[ks]], writes=[k_knew])
                pi = next_ps()
                pbf = psum[pi][:, :].bitcast(BF16)
                for j in range(8):
                    S.add("pe", lambda e, pbf=pbf, j=j: e.transpose(
                        pbf[0:64, j * NS:(j + 1) * NS], knew_bf[:, j * 64:(j + 1) * 64], identb[0:NS, 0:NS]),
                        reads=[k_knew, k_const], writes=[k_ps[pi]])
                S.add("act", lambda e, pbf=pbf: e.activation(
                    out=knT[0:64, :, :, :], in_=pbf[0:64, 0:8 * NS].rearrange("d (b g n) -> d b g n", b=2, g=4), func=AF.Copy),
                    reads=[k_ps[pi]], writes=[k_knT])
        if STAGE >= 3:
            for (cb, dst) in ((1, KsT_d), (2, KwT_d)):
                wi, wv = kvw[cb]
                for g in range(4):
                    ki = rot("ktst", 2)
                    for th in range(2):
                        pi = next_ps()
                        c0 = 16 + th * 512
                        for dc in range(8):
                            S.add("pe", lambda e, pi=pi, wv=wv, dc=dc, g=g, c0=c0: e.matmul(
                                psum[pi][0:64, :], wv[:, dc, g * 64:(g + 1) * 64], uT[:, dc, c0:c0 + 512],
                                start=(dc == 0), stop=(dc == 7)),
                                reads=[k_wb[wi], k_uT], writes=[k_ps[pi]])
                        S.add("act", lambda e, pi=pi, ki=ki, th=th: e.activation(
                            out=ktst[ki][:, th * 512:(th + 1) * 512], in_=psum[pi][0:64, :], func=AF.Copy),
                            reads=[k_ps[pi]], writes=[k_ktst[ki]])
                    S.add("sp", lambda e, ki=ki, g=g, dst=dst, slot0=slot0: e.dma_start(
                        out=dst[slot0:slot0 + 8, 0:64, g, :].rearrange("t d k -> d t k"),
                        in_=ktst[ki][:, :].rearrange("d (t k) -> d t k", t=8)),
                        reads=[k_ktst[ki]], writes=[k_scr], dma=True)
            wi, wv = kvw[0]
            for kv in range(2):
                for g in range(4):
                    xi = rot("ktst", 2)
                    for th in range(2):
                        pi = next_ps()
                        c0 = 16 + th * 512
                        for dc in range(8):
                            S.add("pe", lambda e, pi=pi, wv=wv, dc=dc, g=g, kv=kv, c0=c0: e.matmul(
                                psum[pi][0:64, :], wv[:, dc, kv * 256 + g * 64: kv * 256 + (g + 1) * 64], uT[:, dc, c0:c0 + 512],
                                start=(dc == 0), stop=(dc == 7)),
                                reads=[k_wb[wi], k_uT], writes=[k_ps[pi]])
                        S.add("act", lambda e, pi=pi, xi=xi, th=th: e.activation(
                            out=xct[xi][:, th * 512:(th + 1) * 512], in_=psum[pi][0:64, :], func=AF.Copy),
                            reads=[k_ps[pi]], writes=[k_xct[xi]])
                    pi = next_ps()
                    xb = xct[xi]
                    for pos in range(32):
                        S.add("pe", lambda e, pi=pi, kv=kv, pos=pos, xb=xb: e.matmul(
                            psum[pi][:, 0:63], w1_sb[:, kv, pos, :], xb[:, pos:pos + 993:16],
                            start=(pos == 0), stop=(pos == 31)),
                            reads=[k_xct[xi], k_const], writes=[k_ps[pi]])
                    for p in range(16):
                        S.add("pe", lambda e, pi=pi, kv=kv, p=p, xb=xb: e.matmul(
                            psum[pi][:, 64:65], w1_sb[:, kv, p, :], xb[:, 1008 + p:1009 + p],
                            start=(p == 0), stop=(p == 15)),
                            reads=[k_xct[xi], k_const], writes=[k_ps[pi]])
                    for p in range(16):
                        S.add("pe", lambda e, pi=pi, kv=kv, p=p, xb=xb: e.matmul(
                            psum[pi][:, 65:66], w1_sb[:, kv, 16 + p, :], xb[:, p:p + 1],
                            start=(p == 0), stop=(p == 15)),
                            reads=[k_xct[xi], k_const], writes=[k_ps[pi]])
                    sb0 = (slot0 * 8) % 256
                    po = sb0 // 64
                    S.add("act", lambda e, pi=pi, kv=kv, g=g, sb0=sb0: e.activation(
                        out=hs_all[:, kv, g, sb0:sb0 + 63], in_=psum[pi][:, 0:63], func=AF.Silu, bias=cvec[:, kv:kv + 1]),
                        reads=[k_ps[pi], k_cmp], writes=[k_AB])
                    S.add("act", lambda e, pi=pi, kv=kv, g=g, po=po: e.activation(
                        out=ABs[:, kv, g, :, po], in_=psum[pi][:, 64:66], func=AF.Copy),
                        reads=[k_ps[pi]], writes=[k_AB])

    if STAGE >= 3 and not _DEV.get("prep_only"):
        for kv in range(2):
            for g in range(4):
                S.add("dve", lambda e, kv=kv, g=g: e.tensor_tensor(
                    out=hpreb[:, 0:3], in0=ABs[:, kv, g, 0, 0:3], in1=ABs[:, kv, g, 1, 1:4], op=ALU.add),
                    reads=[k_AB], writes=[k_cmp2])
                S.add("dve", lambda e, kv=kv, g=g: e.tensor_tensor(
                    out=hpreb[:, 3:4], in0=ABs[:, kv, g, 0, 3:4], in1=ABs[:, kv, g, 1, 0:1], op=ALU.add),
                    reads=[k_AB], writes=[k_cmp2])
                S.add("act", lambda e, kv=kv, g=g: e.activation(
                    out=hs_all[:, kv, g, 63:256:64], in_=hpreb[:, :], func=AF.Silu, bias=cvec[:, kv:kv + 1]),
                    reads=[k_cmp2, k_cmp], writes=[k_AB])
                pi = next_ps()
                if kv == 0:
                    S.add("pe", lambda e, pi=pi, g=g: e.matmul(psum[pi][0:64, 0:256], w2_sb[:, 0, :], hs_all[:, 0, g, :], start=True, stop=True),
                          reads=[k_AB, k_const], writes=[k_ps[pi]])
                    S.add("act", lambda e, pi=pi, g=g: e.activation(out=kcT[0:64, g, :], in_=psum[pi][0:64, 0:256], func=AF.Copy),
                          reads=[k_ps[pi]], writes=[k_kc])
                else:
                    for nt in range(2):
                        S.add("pe", lambda e, pi=pi, nt=nt, g=g: e.matmul(
                            psum[pi][:, nt * 64:(nt + 1) * 64], hs_all[:, 1, g, nt * 128:(nt + 1) * 128], w2_sb[:, 1, :], start=True, stop=True),
                            reads=[k_AB, k_const], writes=[k_ps[pi]])
                    S.add("act", lambda e, pi=pi, g=g: e.activation(
                        out=vcM[:, :, g, 0:64], in_=psum[pi][:, 0:128].rearrange("n (t d) -> n t d", t=2), func=AF.Copy),
                        reads=[k_ps[pi]], writes=[k_vc])

    PS_S = [0, 1, 2, 3]
    PS_ACC = [4, 5]
    PS_M = [6, 7]
    QT = hT[:, :, :].rearrange("p a b -> p (a b)").rearrange("p (g r t) -> p g r t", g=4, r=4)

    def attention_tile(i, ogi):
        tl = i % 8
        c0 = tl * 128
        pq = rot("pq", 2)
        pqb = posq_bc[pq]
        kpq = k_posq[pq]
        S.add("sp", lambda e: e.dma_start(out=pqb[:, :], in_=posq_d[i * 128:(i + 1) * 128].partition_broadcast(128)),
              writes=[kpq], dma=True)
        S.add("sp", lambda e: e.dma_start(out=privn_sb[pq][:, :], in_=privn_d[i * 128:(i + 1) * 128, :]),
              writes=[k_privn[pq]], dma=True)
        S.add("sp", lambda e: e.dma_start(out=pribias_sb[pq][:, :], in_=pribias_d[i * 128:(i + 1) * 128, :]),
              writes=[k_privn[pq]], dma=True)
        S.add("sp", lambda e: e.dma_start(out=kcT[64:70, :, :], in_=caug_d[i].unsqueeze(1).to_broadcast([6, 4, 256])),
              writes=[k_kc], dma=True)

        def branch_core(gp, br, slots, kind):
            gs = [2 * gp, 2 * gp + 1]
            nkt = len(slots)

            def front(ki_, slot):
                cx = {"first": ki_ == 0, "lastk": ki_ == nkt - 1, "slot": slot, "pti": {}, "kb": None}
                kb = None
                if kind == "c":
                    kt_r = [k_kc]
                    cx["vt_r"] = [k_vc]
                else:
                    kb = rot("kb", 4)
                    ksrc = KsT_d if kind == "s" else KwT_d
                    vsrc = Vs_d if kind == "s" else Vw_d
                    S.add("sp", lambda e, kb=kb, ksrc=ksrc, slot=slot: e.dma_start(out=kbuf[kb][:, :, :], in_=ksrc[slot]),
                          reads=[k_scr], writes=[k_kbuf[kb]], dma=True)
                    S.add("pool", lambda e, kb=kb, vsrc=vsrc, slot=slot: e.dma_start(out=vbuf[kb][:, :, :], in_=vsrc[slot]),
                          reads=[k_scr], writes=[k_vbuf[kb]], dma=True)
                    kt_r = [k_kbuf[kb]]
                    cx["vt_r"] = [k_vbuf[kb]]
                cx["kb"] = kb
                psm = PS_M[ki_ % 2]
                if kind == "c":
                    mi = rot("mask", 3)
                    S.add("dve", lambda e, mi=mi, slot=slot: e.tensor_scalar(
                        out=mask_sb[mi][:, 0:128], in0=pqb[:, :], scalar1=cend_sb[:, slot:slot + 1], scalar2=None, op0=ALU.is_ge),
                        reads=[kpq, k_const], writes=[k_mask[mi]])
                    mk = ("sb", mi)
                elif kind == "w":
                    mi = rot("mask", 3)
                    S.add("dve", lambda e, slot=slot: e.tensor_scalar(
                        out=mtmp[0][:, :], in0=pqb[:, :], scalar1=posk_sb[:, slot:slot + 1], scalar2=0.0,
                        op0=ALU.subtract, op1=ALU.is_ge), reads=[kpq, k_const], writes=[k_mtmp[0]])
                    S.add("dve", lambda e, slot=slot: e.tensor_scalar(
                        out=mtmp[1][:, :], in0=pqb[:, :], scalar1=posk_sb[:, slot:slot + 1], scalar2=512.0,
                        op0=ALU.subtract, op1=ALU.is_lt), reads=[kpq, k_const], writes=[k_mtmp[1]])
                    S.add("dve", lambda e, mi=mi: e.tensor_tensor(
                        out=mask_sb[mi][:, 0:128], in0=mtmp[0][:, :], in1=mtmp[1][:, :], op=ALU.mult),
                        reads=[k_mtmp[0], k_mtmp[1]], writes=[k_mask[mi]])
                    mk = ("sb", mi)
                else:
                    for g in gs:
                        S.add("pe", lambda e, g=g, slot=slot, psm=psm: e.matmul(
                            psum[psm][:, g * 128:(g + 1) * 128], esel_sb[:, slot * 128:(slot + 1) * 128], selT[:, g, :],
                            start=True, stop=True), reads=[k_selT, k_const], writes=[k_ps[psm]])
                    mi = rot("mask", 3)
                    S.add("dve", lambda e, mi=mi, psm=psm, g0=gs[0]: e.tensor_copy(
                        out=mask_sb[mi][:, :], in_=psum[psm][:, g0 * 128:g0 * 128 + 256]),
                        reads=[k_ps[psm]], writes=[k_mask[mi]])
                    mk = ("sb2", mi)
                for g in gs:
                    si = PS_S[rot("psS", 4)]
                    pti = rot("pT", 4)
                    cx["pti"][g] = pti
                    if kind == "c":
                        lhs = kcT[:, g, slot * 128:(slot + 1) * 128]
                    else:
                        lhs = kbuf[kb][:, g, :]
                    S.add("pe", lambda e, si=si, lhs=lhs, g=g: e.matmul(
                        psum[si][:, :].rearrange("k (r t) -> k r t", r=4), lhs, QT[0:70, g, :, c0:c0 + 128], start=True, stop=True),
                        reads=kt_r + list(k_h), writes=[k_ps[si]])
                    S.add("act", lambda e, si=si, pti=pti: e.activation(
                        out=pT[pti][:, :, :], in_=psum[si][:, :].rearrange("k (r t) -> k r t", r=4), func=AF.Exp),
                        reads=[k_ps[si]], writes=[k_pT[pti]])
                    if mk[0] == "sb":
                        m_ap = mask_sb[mk[1]][:, 0:128].unsqueeze(1).to_broadcast([128, 4, 128])
                        m_r = [k_mask[mk[1]]]
                    else:
                        gl = g - gs[0]
                        m_ap = mask_sb[mk[1]][:, gl * 128:(gl + 1) * 128].unsqueeze(1).to_broadcast([128, 4, 128])
                        m_r = [k_mask[mk[1]]]
                    S.add("dve", lambda e, pti=pti, m_ap=m_ap: e.tensor_tensor(
                        out=pT[pti][:, :, :], in0=pT[pti][:, :, :], in1=m_ap, op=ALU.mult),
                        reads=[k_pT[pti]] + m_r, writes=[k_pT[pti]])
                    if kind == "s" and slot == i:
                        S.add("dve", lambda e, pti=pti: e.tensor_tensor(
                            out=pT[pti][:, :, :], in0=pT[pti][:, :, :],
                            in1=tri_sb[:, :].unsqueeze(1).to_broadcast([128, 4, 128]), op=ALU.mult),
                            reads=[k_pT[pti], k_const], writes=[k_pT[pti]])
                return cx

            def back(cx):
                slot, kb, first, lastk = cx["slot"], cx["kb"], cx["first"], cx["lastk"]
                for g in gs:
                    pti = cx["pti"][g]
                    ai = PS_ACC[g % 2]
                    ncol = 128 if kind == "c" else 65
                    for r in range(4):
                        if kind == "c":
                            rhs = vcM[:, slot, g, :]
                        else:
                            rhs = vbuf[kb][:, g, :]
                        S.add("pe", lambda e, ai=ai, pti=pti, r=r, rhs=rhs, ncol=ncol, first=first, lastk=lastk: e.matmul(
                            psum[ai][:, r * ncol:(r + 1) * ncol], pT[pti][:, r, :], rhs, start=(first and r == 0), stop=lastk,
                            skip_group_check=True),
                            reads=[k_pT[pti]] + cx["vt_r"], writes=[k_ps[ai]])

            pend = None
            for ki_, slot in enumerate(slots):
                cx = front(ki_, slot)
                if pend is not None:
                    back(pend)
                pend = cx
            back(pend)
            for g in gs:
                ai = PS_ACC[g % 2]
                if kind == "c":
                    acc3 = psum[ai][:, :].rearrange("t (r c) -> t r c", r=4)
                    S.add("dve", lambda e, acc3=acc3: e.tensor_reduce(
                        out=rsum[:, :], in_=acc3[:, :, 64:128], axis=AX.X, op=ALU.add),
                        reads=[k_ps[ai]], writes=[k_rsum])
                    S.add("dve", lambda e: e.tensor_scalar(out=rsum[:, :], in0=rsum[:, :], scalar1=0.5, scalar2=1e-30,
                                                           op0=ALU.mult, op1=ALU.max), reads=[k_rsum], writes=[k_rsum])
                else:
                    acc3 = psum[ai][:, 0:260].rearrange("t (r c) -> t r c", r=4)
                    S.add("dve", lambda e, acc3=acc3: e.tensor_scalar(
                        out=rsum[:, :], in0=acc3[:, :, 64], scalar1=1e-30, scalar2=None, op0=ALU.max),
                        reads=[k_ps[ai]], writes=[k_rsum])
                S.add("dve", lambda e: e.reciprocal(out=rsum[:, :], in_=rsum[:, :]), reads=[k_rsum], writes=[k_rsum])
                if kind == "c":
                    for r in range(4):
                        if r == 0:
                            S.add("dve", lambda e, acc3=acc3, g=g: e.tensor_scalar(
                                out=pri[:, g, :], in0=acc3[:, 0, 64:128], scalar1=rsum[:, 0:1], scalar2=None, op0=ALU.mult),
                                reads=[k_ps[ai], k_rsum], writes=[k_pri])
                        else:
                            S.add("dve", lambda e, acc3=acc3, g=g, r=r: e.scalar_tensor_tensor(
                                out=pri[:, g, :], in0=acc3[:, r, 64:128], scalar=rsum[:, r:r + 1], in1=pri[:, g, :],
                                op0=ALU.mult, op1=ALU.add), reads=[k_ps[ai], k_rsum, k_pri], writes=[k_pri])
                S.add("dve", lambda e, g=g, br=br: e.tensor_tensor(
                    out=fsc[:, :], in0=rsum[:, :],
                    in1=gate_sb[:, tl, :].rearrange("t (h b) -> t h b", b=3)[:, 4 * g:4 * g + 4, br], op=ALU.mult),
                    reads=[k_rsum, k_gate], writes=[k_fsc])
                if br == 0:
                    S.add("dve", lambda e, acc3=acc3, g=g: e.tensor_tensor(
                        out=oacc[:, 4 * g:4 * g + 4, :], in0=acc3[:, :, 0:64],
                        in1=fsc[:, :].unsqueeze(2).to_broadcast([128, 4, 64]), op=ALU.mult),
                        reads=[k_ps[ai], k_fsc], writes=[k_oacc])
                else:
                    S.add("dve", lambda e, acc3=acc3: e.tensor_tensor(
                        out=otmp[:, :, :], in0=acc3[:, :, 0:64],
                        in1=fsc[:, :].unsqueeze(2).to_broadcast([128, 4, 64]), op=ALU.mult),
                        reads=[k_ps[ai], k_fsc], writes=[k_otmp])
                    S.add("dve", lambda e, g=g: e.tensor_tensor(
                        out=oacc[:, 4 * g:4 * g + 4, :], in0=oacc[:, 4 * g:4 * g + 4, :], in1=otmp[:, :, :], op=ALU.add),
                        reads=[k_otmp, k_oacc], writes=[k_oacc])

        for gp in range(2):
            branch_core(gp, 0, [0, 1], "c")
        for g in range(4):
            S.add("dve", lambda e, g=g: e.tensor_tensor(out=pri[:, g, :], in0=pri[:, g, :], in1=privn_sb[pq][:, :], op=ALU.mult),
                  reads=[k_pri, k_privn[pq]], writes=[k_pri])
            S.add("dve", lambda e, g=g: e.tensor_tensor(out=pri[:, g, :], in0=pri[:, g, :], in1=pribias_sb[pq][:, :], op=ALU.add),
                  reads=[k_pri, k_privn[pq]], writes=[k_pri])
            S.add("dve", lambda e, g=g: e.max(out=m8a[:, :], in_=pri[:, g, :]), reads=[k_pri], writes=[k_m8])
            S.add("dve", lambda e, g=g: e.match_replace(out=pri2[:, :], in_to_replace=m8a[:, :], in_values=pri[:, g, :], imm_value=-1e30),
                  reads=[k_pri, k_m8], writes=[k_pri2])
            S.add("dve", lambda e: e.max(out=m8b[:, :], in_=pri2[:, :]), reads=[k_pri2], writes=[k_m8])
            S.add("dve", lambda e: e.tensor_scalar(out=m8b[:, 7:8], in0=m8b[:, 7:8], scalar1=0.0, scalar2=None, op0=ALU.max),
                  reads=[k_m8], writes=[k_m8])
            S.add("dve", lambda e, g=g: e.tensor_scalar(out=sel_sb[:, g, :], in0=pri[:, g, :], scalar1=m8b[:, 7:8], scalar2=None, op0=ALU.is_ge),
                  reads=[k_pri, k_m8], writes=[k_sel])
        PS_X = PS_S[rot("psS", 4)]
        psx_bf = psum[PS_X][:, :].bitcast(BF16)
        for g in range(4):
            S.add("pe", lambda e, g=g: e.transpose(psx_bf[0:64, g * 128:(g + 1) * 128], sel_sb[:, g, :], identb[:, :]),
                  reads=[k_sel, k_const], writes=[k_ps[PS_X]])
        S.add("act", lambda e: e.activation(out=selT[:, :, :], in_=psx_bf[0:64, 0:512].rearrange("s (g t) -> s g t", g=4), func=AF.Copy),
              reads=[k_ps[PS_X]], writes=[k_selT])
        sel_slots = list(range(0, i + 1)) + list(range(16, 32))
        for gp in range(2):
            branch_core(gp, 1, sel_slots, "s")
        if i >= 4:
            win_slots = list(range(i - 4, i + 1))
        else:
            win_slots = list(range(28 + i, 32)) + list(range(0, i + 1))
        for gp in range(2):
            branch_core(gp, 2, win_slots, "w")
        S.add("act", lambda e: e.activation(out=obf[:, :], in_=oacc[:, :, :].rearrange("t h d -> t (h d)"), func=AF.Copy),
              reads=[k_oacc], writes=[k_obf])
        PS_X2 = PS_S[rot("psS", 4)]
        psx2_bf = psum[PS_X2][:, :].bitcast(BF16)
        for fc in range(8):
            S.add("pe", lambda e, fc=fc: e.transpose(psx2_bf[:, fc * 128:(fc + 1) * 128], obf[:, fc * 128:(fc + 1) * 128], identb[:, :]),
                  reads=[k_obf, k_const], writes=[k_ps[PS_X2]])
        S.add("dve", lambda e: e.tensor_copy(out=uT[:, :, 16 + c0:16 + c0 + 128],
                                             in_=psx2_bf[:, :].rearrange("f (c t) -> f c t", c=8)),
              reads=[k_ps[PS_X2]], writes=[k_uT])

    def sample_attention():
        P_ = NS
        PS_X = PS_S[0]
        psx_bf = psum[PS_X][:, :].bitcast(BF16)

        def run_branch(gp, br, kind):
            gs = [2 * gp, 2 * gp + 1]
            started = {g: False for g in gs}
            steps = []
            for s_ in range(NS):
                if kind == "c":
                    tiles = [("c", 0)]
                elif kind == "s":
                    tiles = [("k", kt) for kt in range(16)] + [("n", 0)]
                else:
                    tiles = [("k", t_) for t_ in range(4)] + [("n", 1)]
                for (tk, ti_) in tiles:
                    steps.append((s_, tk, ti_))

            def front(n_, s_, tk, ti_):
                cx = {"s": s_, "nk": 128, "pz": {}}
                psm = PS_M[n_ % 2]
                if tk == "c":
                    ci = rot("kc2", 2)
                    S.add("sp", lambda e, ci=ci, s_=s_: e.dma_start(out=kcTs[ci][0:64, :, :], in_=kcTS_d[s_]),
                          reads=[k_sscr], writes=[k_kcTs[ci]], dma=True)
                    S.add("sp", lambda e, ci=ci, s_=s_: e.dma_start(out=vcMs[ci][:, :, 0:64], in_=vcS_d[s_]),
                          reads=[k_sscr], writes=[k_vcMs[ci]], dma=True)
                    kt_r, cx["vt_r"] = [k_kcTs[ci]], [k_vcMs[ci]]
                    klhs = lambda g, ci=ci: kcTs[ci][:, g, :]
                    cx["vrhs"] = lambda g, ci=ci: vcMs[ci][:, g, :]
                    mcol = maskS[:, 0:1]
                    m_r = [k_const]
                elif tk == "k":
                    kb = rot("kb", 4)
                    ksrc = KsTS_d if kind == "s" else KwTS_d
                    vsrc = VsS_d if kind == "s" else VwS_d
                    S.add("sp", lambda e, kb=kb, ksrc=ksrc, s_=s_, ti_=ti_: e.dma_start(out=kbuf[kb][:, :, :], in_=ksrc[s_, ti_]),
                          reads=[k_sscr], writes=[k_kbuf[kb]], dma=True)
                    S.add("pool", lambda e, kb=kb, vsrc=vsrc, s_=s_, ti_=ti_: e.dma_start(out=vbuf[kb][:, :, :], in_=vsrc[s_, ti_]),
                          reads=[k_sscr], writes=[k_vbuf[kb]], dma=True)
                    kt_r, cx["vt_r"] = [k_kbuf[kb]], [k_vbuf[kb]]
                    klhs = lambda g, kb=kb: kbuf[kb][:, g, :]
                    cx["vrhs"] = lambda g, kb=kb: vbuf[kb][:, g, :]
                    if kind == "w":
                        mcol = maskS[:, 1 + ti_:2 + ti_]
                        m_r = [k_const]
                    else:
                        for g in gs:
                            S.add("pe", lambda e, g=g, ti_=ti_, s_=s_, psm=psm: e.matmul(
                                psum[psm][:, g:g + 1], esel_sb[:, ti_ * 128:(ti_ + 1) * 128], selTs[:, g, s_:s_ + 1],
                                start=True, stop=True), reads=[k_selTs, k_const], writes=[k_ps[psm]])
                        mcol = None
                        m_r = [k_ps[psm]]
                else:
                    cx["nk"] = NS
                    bi_ = ti_
                    kt_r, cx["vt_r"] = [k_knT], [k_vnew]
                    klhs = lambda g, bi_=bi_: knT[:, bi_, g, :]
                    cx["vrhs"] = lambda g, bi_=bi_: vnew[:, bi_, g, :]
                    mcol = ident[0:NS, s_:s_ + 1]
                    m_r = [k_const]
                nk = cx["nk"]
                for g in gs:
                    si = PS_S[rot("psS", 4)]
                    pz = rot("pz", 4)
                    cx["pz"][g] = pz
                    S.add("pe", lambda e, si=si, g=g, klhs=klhs, nk=nk, s_=s_: e.matmul(
                        psum[si][0:nk, 0:4], klhs(g), QTs[0:70, g, :, s_], start=True, stop=True),
                        reads=kt_r + [k_QTs], writes=[k_ps[si]])
                    S.add("act", lambda e, si=si, pz=pz, nk=nk, s_=s_: e.activation(
                        out=Pz[pz][0:nk, :, s_], in_=psum[si][0:nk, 0:4], func=AF.Exp),
                        reads=[k_ps[si]], writes=[k_Pz[pz]])
                    mc = mcol if mcol is not None else psum[psm][:, g:g + 1]
                    S.add("dve", lambda e, pz=pz, nk=nk, s_=s_, mc=mc: e.tensor_scalar(
                        out=Pz[pz][0:nk, :, s_], in0=Pz[pz][0:nk, :, s_], scalar1=mc[0:nk, :] if mc.shape[0] != nk else mc, scalar2=None, op0=ALU.mult),
                        reads=[k_Pz[pz]] + m_r, writes=[k_Pz[pz]])
                return cx

            def back(cx):
                s_, nk = cx["s"], cx["nk"]
                for g in gs:
                    pz = cx["pz"][g]
                    vr = cx["vrhs"](g)
                    ai = PS_ACC[g % 2]
                    ncol = 128 if kind == "c" else 65
                    for r in range(4):
                        st_flag = (not started[g])
                        started[g] = True
                        S.add("pe", lambda e, ai=ai, pz=pz, r=r, vr=vr, ncol=ncol, nk=nk, st_flag=st_flag: e.matmul(
                            psum[ai][0:NS, r * ncol:(r + 1) * ncol], Pz[pz][0:nk, r, :], vr[0:nk, :] if nk != 128 else vr,
                            start=st_flag, stop=False, skip_group_check=True),
                            reads=[k_Pz[pz]] + cx["vt_r"], writes=[k_ps[ai]])
                    S.add("dve", lambda e, pz=pz, nk=nk, s_=s_: e.memset(Pz[pz][0:nk, :, s_], 0.0),
                          writes=[k_Pz[pz]])

            pend = None
            for n_, (s_, tk, ti_) in enumerate(steps):
                cx = front(n_, s_, tk, ti_)
                if pend is not None:
                    back(pend)
                pend = cx
            back(pend)
            for g in gs:
                ai = PS_ACC[g % 2]
                if kind == "c":
                    acc3 = psum[ai][0:P_, :].rearrange("t (r c) -> t r c", r=4)
                    S.add("dve", lambda e, acc3=acc3: e.tensor_reduce(
                        out=rsum[0:P_, :], in_=acc3[:, :, 64:128], axis=AX.X, op=ALU.add), reads=[k_ps[ai]], writes=[k_rsum])
                    S.add("dve", lambda e: e.tensor_scalar(out=rsum[0:P_, :], in0=rsum[0:P_, :], scalar1=0.5, scalar2=1e-30,
                                                           op0=ALU.mult, op1=ALU.max), reads=[k_rsum], writes=[k_rsum])
                else:
                    acc3 = psum[ai][0:P_, 0:260].rearrange("t (r c) -> t r c", r=4)
                    S.add("dve", lambda e, acc3=acc3: e.tensor_scalar(
                        out=rsum[0:P_, :], in0=acc3[:, :, 64], scalar1=1e-30, scalar2=None, op0=ALU.max),
                        reads=[k_ps[ai]], writes=[k_rsum])
                S.add("dve", lambda e: e.reciprocal(out=rsum[0:P_, :], in_=rsum[0:P_, :]), reads=[k_rsum], writes=[k_rsum])
                if kind == "c":
                    for r in range(4):
                        if r == 0:
                            S.add("dve", lambda e, acc3=acc3, g=g: e.tensor_scalar(
                                out=pri[0:P_, g, :], in0=acc3[:, 0, 64:128], scalar1=rsum[0:P_, 0:1], scalar2=None, op0=ALU.mult),
                                reads=[k_ps[ai], k_rsum], writes=[k_pri])
                        else:
                            S.add("dve", lambda e, acc3=acc3, g=g, r=r: e.scalar_tensor_tensor(
                                out=pri[0:P_, g, :], in0=acc3[:, r, 64:128], scalar=rsum[0:P_, r:r + 1], in1=pri[0:P_, g, :],
                                op0=ALU.mult, op1=ALU.add), reads=[k_ps[ai], k_rsum, k_pri], writes=[k_pri])
                S.add("dve", lambda e, g=g, br=br: e.tensor_tensor(
                    out=fsc[0:P_, :], in0=rsum[0:P_, :],
                    in1=gate_s[:, :].rearrange("t (h b) -> t h b", b=3)[:, 4 * g:4 * g + 4, br], op=ALU.mult),
                    reads=[k_rsum, k_gates], writes=[k_fsc])
                if br == 0:
                    S.add("dve", lambda e, acc3=acc3, g=g: e.tensor_tensor(
                        out=oacc[0:P_, 4 * g:4 * g + 4, :], in0=acc3[:, :, 0:64],
                        in1=fsc[0:P_, :].unsqueeze(2).to_broadcast([P_, 4, 64]), op=ALU.mult),
                        reads=[k_ps[ai], k_fsc], writes=[k_oacc])
                else:
                    S.add("dve", lambda e, acc3=acc3: e.tensor_tensor(
                        out=otmp[0:P_, :, :], in0=acc3[:, :, 0:64],
                        in1=fsc[0:P_, :].unsqueeze(2).to_broadcast([P_, 4, 64]), op=ALU.mult),
                        reads=[k_ps[ai], k_fsc], writes=[k_otmp])
                    S.add("dve", lambda e, g=g: e.tensor_tensor(
                        out=oacc[0:P_, 4 * g:4 * g + 4, :], in0=oacc[0:P_, 4 * g:4 * g + 4, :], in1=otmp[0:P_, :, :], op=ALU.add),
                        reads=[k_otmp, k_oacc], writes=[k_oacc])

        for gp in range(2):
            run_branch(gp, 0, "c")
        for g in range(4):
            S.add("dve", lambda e, g=g: e.tensor_tensor(out=pri[0:P_, g, :], in0=pri[0:P_, g, :], in1=privnS[:, :], op=ALU.mult),
                  reads=[k_pri, k_const], writes=[k_pri])
            S.add("dve", lambda e, g=g: e.tensor_tensor(out=pri[0:P_, g, :], in0=pri[0:P_, g, :], in1=pribiasS[:, :], op=ALU.add),
                  reads=[k_pri, k_const], writes=[k_pri])
            S.add("dve", lambda e, g=g: e.max(out=m8a[0:P_, :], in_=pri[0:P_, g, :]), reads=[k_pri], writes=[k_m8])
            S.add("dve", lambda e, g=g: e.match_replace(out=pri2[0:P_, :], in_to_replace=m8a[0:P_, :], in_values=pri[0:P_, g, :], imm_value=-1e30),
                  reads=[k_pri, k_m8], writes=[k_pri2])
            S.add("dve", lambda e: e.max(out=m8b[0:P_, :], in_=pri2[0:P_, :]), reads=[k_pri2], writes=[k_m8])
            S.add("dve", lambda e: e.tensor_scalar(out=m8b[0:P_, 7:8], in0=m8b[0:P_, 7:8], scalar1=0.0, scalar2=None, op0=ALU.max),
                  reads=[k_m8], writes=[k_m8])
            S.add("dve", lambda e, g=g: e.tensor_scalar(out=sel_sb[0:P_, g, :], in0=pri[0:P_, g, :], scalar1=m8b[0:P_, 7:8], scalar2=None, op0=ALU.is_ge),
                  reads=[k_pri, k_m8], writes=[k_sel])
        for g in range(4):
            S.add("pe", lambda e, g=g: e.transpose(psx_bf[0:64, g * NS:(g + 1) * NS], sel_sb[0:P_, g, :], identb[0:P_, 0:P_]),
                  reads=[k_sel, k_const], writes=[k_ps[PS_X]])
        S.add("act", lambda e: e.activation(out=selTs[:, :, :], in_=psx_bf[0:64, 0:4 * NS].rearrange("s (g t) -> s g t", g=4), func=AF.Copy),
              reads=[k_ps[PS_X]], writes=[k_selTs])
        for gp in range(2):
            run_branch(gp, 1, "s")
        for gp in range(2):
            run_branch(gp, 2, "w")
        S.add("act", lambda e: e.activation(out=obf[0:P_, :], in_=oacc[0:P_, :, :].rearrange("t h d -> t (h d)"), func=AF.Copy),
              reads=[k_oacc], writes=[k_obf])
        for fc in range(8):
            S.add("pe", lambda e, fc=fc: e.transpose(psx_bf[:, fc * NS:(fc + 1) * NS], obf[0:P_, fc * 128:(fc + 1) * 128], identb[0:P_, 0:P_]),
                  reads=[k_obf, k_const], writes=[k_ps[PS_X]])
        S.add("dve", lambda e: e.tensor_copy(out=uT[:, :, NCOL:NCX], in_=psx_bf[:, 0:8 * NS].rearrange("f (c t) -> f c t", c=8)),
              reads=[k_ps[PS_X]], writes=[k_uT])

    if STAGE >= 2 and not _DEV.get("prep_only"):
        for ogi in range(2):
            has_s = (ogi == 0)
            RNG = MAINR + ([SR] if has_s else [])
            S.add("sp", lambda e, ogi=ogi: e.dma_start(
                out=xT[:, :, 16:NCX], in_=x1_d[ogi].rearrange("p (a b) -> p a b", a=8)),
                reads=[k_x1[ogi]], writes=list(k_xT), dma=True)
            if STAGE >= 3:
                norm(VGQ, MAINR)
                wq = [wload(w_qg.rearrange("(dc p) f -> p dc f", p=128)[:, :, cb * 512:(cb + 1) * 512], (8, 512)) for cb in range(2)]
                wgi, wgv = wload(w_qg.rearrange("(dc p) f -> p dc f", p=128)[:, :, 1024:1072], (8, 48))
                for h in range(16):
                    g, r = h // 4, h % 4
                    wi, wv = wq[h // 8]
                    for th in range(2):
                        pi = next_ps()
                        cc = 16 + th * 512
                        for dc in range(8):
                            S.add("pe", lambda e, pi=pi, wv=wv, dc=dc, h=h, cc=cc: e.matmul(
                                psum[pi][0:64, :], wv[:, dc, (h % 8) * 64:(h % 8 + 1) * 64], uT[:, dc, cc:cc + 512],
                                start=(dc == 0), stop=(dc == 7)), reads=[k_wb[wi], k_uT], writes=[k_ps[pi]])
                        S.add("act", lambda e, pi=pi, g=g, r=r, th=th: e.activation(
                            out=QT[0:64, g, r, th * 512:(th + 1) * 512], in_=psum[pi][0:64, :], func=AF.Copy, scale=0.125),
                            reads=[k_ps[pi]], writes=list(k_h))
                    if has_s and STAGE >= 4:
                        pi = next_ps()
                        for dc in range(8):
                            S.add("pe", lambda e, pi=pi, wv=wv, dc=dc, h=h: e.matmul(
                                psum[pi][0:64, 0:NS], wv[:, dc, (h % 8) * 64:(h % 8 + 1) * 64], uT[:, dc, NCOL:NCX],
                                start=(dc == 0), stop=(dc == 7)), reads=[k_wb[wi], k_uT], writes=[k_ps[pi]])
                        S.add("act", lambda e, pi=pi, g=g, r=r: e.activation(
                            out=QTs[0:64, g, r, :], in_=psum[pi][0:64, 0:NS], func=AF.Copy, scale=0.125),
                            reads=[k_ps[pi]], writes=[k_QTs])
                S.add("sp", lambda e, ogi=ogi: e.dma_start(
                    out=QT[64:70, :, :, :], in_=qaug_d[:, :, :, ogi * GT:(ogi + 1) * GT]), writes=list(k_h), dma=True)
                for tl in range(8):
                    pi = next_ps()
                    cc = 16 + tl * 128
                    for dc in range(8):
                        S.add("pe", lambda e, pi=pi, dc=dc, cc=cc: e.matmul(
                            psum[pi][:, 0:48], uT[:, dc, cc:cc + 128], wgv[:, dc, :], start=(dc == 0), stop=(dc == 7)),
                            reads=[k_wb[wgi], k_uT], writes=[k_ps[pi]])
                    S.add("dve", lambda e, pi=pi, tl=tl: e.tensor_tensor(
                        out=gate_sb[:, tl, :], in0=psum[pi][:, 0:48], in1=bg_sb[:, :], op=ALU.add),
                        reads=[k_ps[pi], k_const], writes=[k_gate])
                S.add("act", lambda e: e.activation(out=gate_sb[:, :, :], in_=gate_sb[:, :, :], func=AF.Sigmoid),
                      reads=[k_gate], writes=[k_gate])
                if has_s and STAGE >= 4:
                    pi = next_ps()
                    for dc in range(8):
                        S.add("pe", lambda e, pi=pi, dc=dc: e.matmul(
                            psum[pi][0:NS, 0:48], uT[:, dc, NCOL:NCX], wgv[:, dc, :], start=(dc == 0), stop=(dc == 7)),
                            reads=[k_wb[wgi], k_uT], writes=[k_ps[pi]])
                    S.add("dve", lambda e, pi=pi: e.tensor_tensor(
                        out=gate_s[:, :], in0=psum[pi][0:NS, 0:48], in1=bg_sb[0:NS, :], op=ALU.add),
                        reads=[k_ps[pi], k_const], writes=[k_gates])
                    S.add("act", lambda e: e.activation(out=gate_s[:, :], in_=gate_s[:, :], func=AF.Sigmoid),
                          reads=[k_gates], writes=[k_gates])
                for tl in range(8):
                    attention_tile(ogi * 8 + tl, ogi)
                if has_s and STAGE >= 4 and not _DEV.get("skip_sa"):
                    sample_attention()
                wo = [wload(w_o.rearrange("(fc p) f -> p fc f", p=128)[:, :, cb * 512:(cb + 1) * 512], (8, 512)) for cb in range(2)]
                for dmc in range(8):
                    wi, wv = wo[dmc // 4]
                    for th in range(2):
                        pi = next_ps()
                        cc = 16 + th * 512
                        for fc in range(8):
                            S.add("pe", lambda e, pi=pi, wv=wv, fc=fc, dmc=dmc, cc=cc: e.matmul(
                                psum[pi][:, :], wv[:, fc, (dmc % 4) * 128:(dmc % 4 + 1) * 128], uT[:, fc, cc:cc + 512],
                                start=(fc == 0), stop=(fc == 7)), reads=[k_wb[wi], k_uT], writes=[k_ps[pi]])
                        S.add("dve", lambda e, pi=pi, dmc=dmc, cc=cc: e.tensor_tensor(
                            out=xT[:, dmc, cc:cc + 512], in0=xT[:, dmc, cc:cc + 512], in1=psum[pi][:, :], op=ALU.add),
                            reads=[k_ps[pi], k_xT[dmc]], writes=[k_xT[dmc]])
                    if has_s and STAGE >= 4:
                        pi = next_ps()
                        for fc in range(8):
                            S.add("pe", lambda e, pi=pi, wv=wv, fc=fc, dmc=dmc: e.matmul(
                                psum[pi][:, 0:NS], wv[:, fc, (dmc % 4) * 128:(dmc % 4 + 1) * 128], uT[:, fc, NCOL:NCX],
                                start=(fc == 0), stop=(fc == 7)), reads=[k_wb[wi], k_uT], writes=[k_ps[pi]])
                        S.add("dve", lambda e, pi=pi, dmc=dmc: e.tensor_tensor(
                            out=xT[:, dmc, NCOL:NCX], in0=xT[:, dmc, NCOL:NCX], in1=psum[pi][:, 0:NS], op=ALU.add),
                            reads=[k_ps[pi], k_xT[dmc]], writes=[k_xT[dmc]])
            norm(VG1B, RNG)
            mlp(1, has_s)
            norm(VGF, RNG, final=True)
            if has_s:
                rows16_out(xT[:, :, NCOL:NCX], list(k_xT), y_s_out[:, :])
            for tl in range(8):
                yi = rot("yst", 2)
                cc = 16 + tl * 128
                for hb in range(2):
                    pi = next_ps()
                    for j in range(4):
                        dc = hb * 4 + j
                        S.add("pe", lambda e, pi=pi, j=j, dc=dc, cc=cc: e.transpose(
                            psum[pi][:, j * 128:(j + 1) * 128], xT[:, dc, cc:cc + 128], ident[:, :]),
                            reads=[k_xT[dc], k_const], writes=[k_ps[pi]])
                    if hb == 0:
                        S.add("act", lambda e, pi=pi, yi=yi: e.activation(out=yst[yi][:, 0:512], in_=psum[pi][:, :], func=AF.Copy),
                              reads=[k_ps[pi]], writes=[k_yst[yi]])
                    else:
                        S.add("dve", lambda e, pi=pi, yi=yi: e.tensor_copy(out=yst[yi][:, 512:1024], in_=psum[pi][:, :]),
                              reads=[k_ps[pi]], writes=[k_yst[yi]])
                row0 = ogi * GT + tl * 128
                S.add("sp", lambda e, yi=yi, row0=row0: e.dma_start(out=y_out[row0:row0 + 128, :], in_=yst[yi][:, :]),
                      reads=[k_yst[yi]], writes=[k_out], dma=True)

    S.prepare(nc)
    with nc.Block() as block:
        S.emit(block)
    S._st.close()
    es.close()
    return nc


def _bf(x):
    return np.asarray(x, np.float32).astype(NPBF)


def _hilo(x):
    x = np.asarray(x, np.float32)
    hi = x.astype(NPBF).astype(np.float32)
    lo = (x - hi).astype(NPBF).astype(np.float32)
    return hi, lo


def core_meta(half):
    seqtile = np.concatenate([16 * half + np.arange(16), 16 * (1 - half) + np.arange(16)])
    posk = (seqtile[None, :] * 128 + np.arange(128)[:, None]).astype(np.float32)
    other_visible = (half == 1)
    eff = posk.copy()
    if not other_visible:
        eff[:, 16:] = NEGPOS
    hi, lo = _hilo(eff)
    kaug = np.zeros((32, 6, 128), np.float32)
    kaug[:, 0] = hi.T; kaug[:, 1] = lo.T; kaug[:, 2] = hi.T; kaug[:, 3] = lo.T; kaug[:, 4] = 1.0; kaug[:, 5] = 1.0
    posq = (half * 2048 + np.arange(2048)).astype(np.float32)
    slopes = np.exp2(-8.0 * (np.arange(16, dtype=np.float32) + 1.0) / 16).astype(np.float32)
    shi, slo = _hilo(slopes)
    tref = (half * 2048 + (np.arange(2048) // 128) * 128 + 64).astype(np.float32)
    qaug = np.zeros((6, 4, 4, 2048), np.float32)
    for g in range(4):
        for r in range(4):
            h = g * 4 + r
            c = (-slopes[h] * tref).astype(np.float32)
            chi, clo = _hilo(c)
            qaug[0, g, r] = shi[h]; qaug[1, g, r] = shi[h]; qaug[2, g, r] = slo[h]; qaug[3, g, r] = slo[h]
            qaug[4, g, r] = chi; qaug[5, g, r] = clo
    seqsub = (seqtile[:, None] * 8 + np.arange(8)[None, :]).reshape(-1)
    nxt = np.roll(seqsub, -1)
    valid = (nxt == seqsub + 1)
    cend_true = np.where(valid, seqsub * 16 + 31, 10 ** 9).astype(np.float64)
    cend = cend_true.astype(np.float32).reshape(2, 128).T.copy()
    caug = np.zeros((16, 6, 256), np.float32)
    for i in range(16):
        tr = half * 2048 + i * 128 + 64
        e = np.where(valid, np.minimum(cend_true, tr + 63), NEGPOS).astype(np.float32)
        ehi, elo = _hilo(e)
        caug[i, 0] = ehi; caug[i, 1] = elo; caug[i, 2] = ehi; caug[i, 3] = elo; caug[i, 4] = 1.0; caug[i, 5] = 1.0
    seqblk = (seqtile[:, None] * 2 + np.arange(2)[None, :]).reshape(-1)
    blk2slot = np.zeros(64, np.int64)
    blk2slot[seqblk] = np.arange(64)
    mc2s = np.zeros((256, 64), np.float32)
    for n in range(256):
        if valid[n]:
            nb = seqsub[n]
            for k in range(2):
                sb_ = ((nb + k) * 16) // 64
                mc2s[n, blk2slot[sb_]] += 1.0
    mc2s = mc2s.reshape(2, 128, 64).transpose(1, 0, 2).copy()
    tq = posq.astype(np.int64)
    cur = tq // 64
    bs = seqblk[None, :]
    validb = (bs * 64 <= tq[:, None])
    forced = (bs == 0) | (bs == cur[:, None]) | (bs == cur[:, None] - 1)
    privn = (validb & ~forced).astype(np.float32)
    pribias = np.where(validb, np.where(forced, 1e6, 0.0), -1.0).astype(np.float32)
    esel = (np.arange(4096)[None, :] // 64 == np.arange(64)[:, None]).astype(np.float32)
    tri = (np.arange(128)[:, None] <= np.arange(128)[None, :]).astype(np.float32)
    return {
        "kaug": _bf(kaug), "qaug": _bf(qaug), "caug": _bf(caug), "posq": posq, "posk": posk, "cend": cend,
        "privn": privn, "pribias": pribias, "mc2s": _bf(mc2s), "esel": _bf(esel), "tri": _bf(tri),
        "identb": _bf(np.eye(128)), "ident": np.eye(128, dtype=np.float32),
    }


def sample_meta():
    slopes = np.exp2(-8.0 * (np.arange(16, dtype=np.float32) + 1.0) / 16).astype(np.float32)
    shi, slo = _hilo(slopes)
    pos = np.zeros((21, 128), np.float32)
    for kt in range(16):
        pos[kt] = kt * 128 + np.arange(128)
    for t in range(4):
        pos[16 + t] = 1536 + t * 128 + np.arange(128)
    pos[20] = 2048.0
    hi, lo = _hilo(pos)
    kaugS = np.zeros((21, 6, 128), np.float32)
    kaugS[:, 0] = hi; kaugS[:, 1] = lo; kaugS[:, 2] = hi; kaugS[:, 3] = lo; kaugS[:, 4] = 1.0; kaugS[:, 5] = 1.0
    qaugS = np.zeros((6, 4, 4, 16), np.float32)
    for g in range(4):
        for r in range(4):
            h = g * 4 + r
            chi, clo = _hilo(np.float32(-slopes[h] * 2048.0))
            qaugS[0, g, r] = shi[h]; qaugS[1, g, r] = shi[h]; qaugS[2, g, r] = slo[h]; qaugS[3, g, r] = slo[h]
            qaugS[4, g, r] = chi; qaugS[5, g, r] = clo
    cend = np.where(np.arange(128) < 127, np.arange(128) * 16 + 31, NEGPOS).astype(np.float32)
    ehi, elo = _hilo(cend)
    caugS = np.zeros((6, 128), np.float32)
    caugS[0] = ehi; caugS[1] = elo; caugS[2] = ehi; caugS[3] = elo; caugS[4] = 1.0; caugS[5] = 1.0
    maskS = np.zeros((128, 8), np.float32)
    maskS[:127, 0] = 1.0
    for t in range(4):
        d = 2048 - (1536 + t * 128 + np.arange(128))
        maskS[:, 1 + t] = ((d >= 0) & (d < 512)).astype(np.float32)
    blk = np.arange(64)
    valid = blk <= 32
    forced = (blk == 0) | (blk == 31) | (blk == 32)
    privn = np.tile((valid & ~forced).astype(np.float32)[None], (16, 1))
    pribias = np.tile(np.where(valid, np.where(forced, 1e6, 0.0), -1.0).astype(np.float32)[None], (16, 1))
    mc2s = np.zeros((128, 64), np.float32)
    for n in range(127):
        for k in range(2):
            mc2s[n, ((n + k) * 16) // 64] += 1.0
    return {"kaugS": _bf(kaugS), "qaugS": _bf(qaugS), "caugS": _bf(caugS), "maskS": maskS, "privnS": privn,
            "pribiasS": pribias, "mc2sS": _bf(mc2s), "iotap": np.arange(128, dtype=np.float32).reshape(128, 1)}


_NC_CACHE = {}


def kernel(**inp):
    f32 = lambda a: np.ascontiguousarray(np.asarray(a, dtype=np.float32))
    x_prompt = f32(inp["x_prompt"])
    vec_list = [inp["norm_mix"][0], inp["norm_mlp"][0], inp["norm_kv"], inp["pool_scale"][0],
                inp["norm_mix"][1], inp["norm_mlp"][1], inp["norm_final"]]
    vecs = np.ascontiguousarray(np.concatenate([f32(v).reshape(8, 128).T for v in vec_list], axis=1))
    shared = {
        "vecs": vecs, "w_up": f32(inp["w_up"]), "w_down": f32(inp["w_down"]), "pool_w": f32(inp["pool_w"])[0],
        "w_kv": f32(inp["w_kv"]), "w_qg": f32(inp["w_qg"])[0], "w_o": f32(inp["w_o"])[0], "b_gate": f32(inp["b_gate"]),
        "cmp_w1": f32(inp["cmp_w1"]), "cmp_w2": f32(inp["cmp_w2"]), "cmp_pe": f32(inp["cmp_pe"]),
    }
    metas = [core_meta(0), core_meta(1)]
    x_sample = f32(inp["x_sample"]).reshape(128, D)
    state_pool = f32(inp["state_pool"]).reshape(128, 15, D)
    state_win = f32(inp["state_win"]).reshape(128, 512, 512)
    selw = np.zeros((256, 4, 16), np.float32)
    for s_ in range(16):
        for r_ in range(15):
            for g_ in range(4):
                if r_ >= 16 - (2 << g_):
                    selw[s_ * 15 + r_, g_, s_] = 1.0
    selw = np.ascontiguousarray(selw.reshape(2, 128, 64).transpose(1, 0, 2))
    smeta = sample_meta()
    cache2d = f32(inp["cache_kv_pages"]).reshape(2560 * 128, 1024)
    page_table = np.ascontiguousarray(np.asarray(inp["page_table"], dtype=np.int32))
    in_maps = []
    for c in range(NCORES):
        b, half = c // 2, c % 2
        xg = np.zeros((4, NCOL, D), np.float32)
        corr = np.ones((4, 4, 16), np.float32)
        for gq, (slot0, is_own, ogi) in enumerate(L0_GROUPS):
            hf = half if is_own else 1 - half
            st = hf * 2048 + (slot0 % 16) * 128
            if st > 0:
                xg[gq, 1:16] = x_prompt[b, st - 15:st]
            xg[gq, 16:] = x_prompt[b, st:st + GT]
            for g in range(4):
                w = 2 << g
                t = st + np.arange(16)
                corr[gq, g] = w / np.minimum(t + 1, w)
        m = dict(shared)
        m.update(metas[half])
        m["xs"] = np.ascontiguousarray(x_sample[16 * c:16 * c + 16])
        m["spool"] = np.ascontiguousarray(state_pool[16 * c:16 * c + 16].reshape(240, D))
        m["swin"] = np.ascontiguousarray(state_win[16 * c:16 * c + 16])
        m["selw"] = selw
        m.update(smeta)
        m["cache"] = cache2d
        m["pt"] = np.ascontiguousarray(page_table[16 * c:16 * c + 16])
        m["xg"] = xg
        m["corr"] = corr.reshape(4, 64)
        in_maps.append(m)
    if "nc" not in _NC_CACHE:
        _NC_CACHE["nc"] = build_nc()
    nc = _NC_CACHE["nc"]
    res = run_bass_kernel_spmd(nc, in_maps, core_ids=list(range(NCORES)))
    R = res.results
    y_prompt = np.zeros((NB, SEQ, D), np.float32)
    y_sample = np.zeros((128, 1, D), np.float32)
    pool_prompt = np.zeros((NB, 1, 15, D), np.float32)
    pool_sample = np.zeros((128, 1, 15, D), np.float32)
    kv_rows_prompt = np.zeros((NB, SEQ, 2, 2, 4, 64), np.float32)
    kv_rows_sample = np.zeros((128, 1, 2, 2, 4, 64), np.float32)
    win_prompt = np.zeros((NB, 512, 2, 4, 64), np.float32)
    win_sample = np.zeros((128, 512, 2, 4, 64), np.float32)
    for c in range(NCORES):
        b, half = c // 2, c % 2
        kv_rows_prompt[b, half * 2048:(half + 1) * 2048] = R[c]["kv_out"].reshape(2048, 2, 2, 4, 64)
        y_sample[16 * c:16 * c + 16, 0] = R[c]["y_s_out"]
        pool_sample[16 * c:16 * c + 16, 0] = R[c]["pool_s_out"]
        kv_rows_sample[16 * c:16 * c + 16, 0] = R[c]["kv_s_out"].reshape(16, 2, 2, 4, 64)
        win_sample[16 * c:16 * c + 16] = R[c]["win_s_out"].reshape(16, 512, 2, 4, 64)
        y_prompt[b, half * 2048:(half + 1) * 2048] = R[c]["y_out"]
        if half == 1:
            win_prompt[b] = R[c]["win_out"].reshape(512, 2, 4, 64)
            pool_prompt[b, 0] = R[c]["pool_out"][1:16]
    return (y_prompt, y_sample, pool_prompt, pool_sample, kv_rows_prompt, kv_rows_sample, win_prompt, win_sample)
```
